# Optimizing a Trainium2 kernel written in Bass

```python
import jax, jax.numpy as jnp
from jax import lax
import numpy as np

D_MODEL = 1024
BATCH = 4
SEQ = 4096
DEPTH = 1
DEC_BATCH = 32
DEC_SEQ = 1
PAST_LEN = 16384
PAGE_SIZE = 128

N_HEADS = 8
HEAD_DIM = 64
N_KV = 2
Q_PER_KV = N_HEADS // N_KV
CMP_LEN = 32
CMP_STRIDE = 16
CMP_HALVES = CMP_LEN // CMP_STRIDE
CMP_HID = 128
SEL_BLOCK = 64
N_SEL = 16
N_LOCAL_SEL = 2
WINDOW = 512
Q_BLOCK = 128
GMLP_WIDTH = 512
GMLP_GROUPS = 4
GMLP_GROUP_DIM = GMLP_WIDTH // GMLP_GROUPS
CHUNK = 128
D_FF = 2816
CONV_W = 3

EPS = 1e-6
NEG = -1e30
SEL_BONUS = 1e6
ATTN_SCALE = HEAD_DIM ** -0.5
Q_COLS = N_HEADS * HEAD_DIM
KV_COLS = N_KV * HEAD_DIM
IN_COLS = Q_COLS + 6 * KV_COLS + 3 * N_HEADS + 2 * GMLP_WIDTH + 2 * D_MODEL

kernel_name = 'nsa_gmlp_convffn_hybrid_step'


def rmsnorm(x, g):
    xf = x.astype(jnp.float32)
    xf = xf * lax.rsqrt(jnp.mean(xf * xf, axis=-1, keepdims=True) + EPS)
    return xf.astype(x.dtype) * g


def masked_softmax(s, mask):
    s = jnp.where(mask, s, NEG)
    m = jnp.max(s, axis=-1, keepdims=True)
    e = jnp.where(mask, jnp.exp(s - m), 0.0)
    d = jnp.sum(e, axis=-1, keepdims=True)
    return e / jnp.where(d > 0, d, 1.0)


def project(h, w_in):
    B, T = h.shape[:2]
    z = h @ w_in
    q = z[..., :Q_COLS].reshape(B, T, N_HEADS, HEAD_DIM)
    off = Q_COLS
    kv = z[..., off:off + 6 * KV_COLS].reshape(B, T, 3, 2, N_KV, HEAD_DIM)
    off += 6 * KV_COLS
    nsa_g = jax.nn.sigmoid(z[..., off:off + 3 * N_HEADS].reshape(B, T, N_HEADS, 3))
    off += 3 * N_HEADS
    uv = jax.nn.gelu(z[..., off:off + 2 * GMLP_WIDTH])
    u, v = uv[..., :GMLP_WIDTH], uv[..., GMLP_WIDTH:]
    off += 2 * GMLP_WIDTH
    gates = jax.nn.sigmoid(z[..., off:].reshape(B, T, 2, D_MODEL))
    return q, kv, nsa_g, u, v, gates


def dense_gqa(q, k, v, mask):
    B, Tq, H, D = q.shape
    G = k.shape[2]
    qg = q.reshape(B, Tq, G, H // G, D)
    s = jnp.einsum('btgrd,bsgd->bgrts', qg, k).astype(jnp.float32) * ATTN_SCALE
    p = masked_softmax(s, mask)
    o = jnp.einsum('bgrts,bsgd->btgrd', p.astype(v.dtype), v).reshape(B, Tq, H, D)
    return o, p


def compress(x, pe, w1, b1, w2):
    B, T, G, D = x.shape
    n_seg = T // CMP_STRIDE
    n_cmp = n_seg - CMP_HALVES + 1
    seg = x[:, :n_seg * CMP_STRIDE].reshape(B, n_seg, CMP_STRIDE, G, D)
    pe_h = pe.reshape(CMP_HALVES, CMP_STRIDE, D)
    w1_h = w1.reshape(CMP_HALVES, CMP_STRIDE, D, CMP_HID)
    hid = b1
    for r in range(CMP_HALVES):
        part = jnp.einsum('bnsgd,sdh->bngh', seg + pe_h[r][:, None, :], w1_h[r])
        hid = hid + part[:, r:r + n_cmp]
    return jnp.einsum('bngh,hd->bngd', jax.nn.gelu(hid), w2)


def select_blocks(p_cmp, q_pos, n_blocks):
    imp_c = jnp.sum(p_cmp, axis=2)
    n_cmp = imp_c.shape[-1]
    ratio = SEL_BLOCK // CMP_STRIDE
    n_back = CMP_HALVES - 1
    padded = jnp.pad(imp_c, ((0, 0), (0, 0), (0, 0), (n_back, ratio * (n_blocks + 1) - n_cmp)))
    imp = padded[..., 0::ratio][..., :n_blocks]
    for m in range(1, ratio + n_back):
        imp = imp + padded[..., m::ratio][..., :n_blocks]
    blk = jnp.arange(n_blocks, dtype=jnp.int32)[None, :]
    cur = (q_pos // SEL_BLOCK)[:, None]
    valid = blk * SEL_BLOCK <= q_pos[:, None]
    forced = (blk == 0) | ((blk <= cur) & (blk > cur - N_LOCAL_SEL))
    score = jnp.where(valid, imp + jnp.where(forced, SEL_BONUS, 0.0), NEG)
    vals, idx = lax.top_k(score, min(N_SEL, n_blocks))
    return idx, vals > NEG / 2


def to_blocks(x, n_blocks):
    B, T, G, D = x.shape
    xp = jnp.pad(x, ((0, 0), (0, n_blocks * SEL_BLOCK - T), (0, 0), (0, 0)))
    return xp.reshape(B, n_blocks, SEL_BLOCK, G, D).transpose(0, 3, 1, 2, 4)


def nsa_global(q, q_pos, kv_cmp, kv_slc, n_blocks, cmp_pe, cmp_w1, cmp_b1, cmp_w2):
    kc = compress(kv_cmp[:, :, 0], cmp_pe[0], cmp_w1[0], cmp_b1[0], cmp_w2[0])
    vc = compress(kv_cmp[:, :, 1], cmp_pe[1], cmp_w1[1], cmp_b1[1], cmp_w2[1])
    ends = jnp.arange(kc.shape[1], dtype=jnp.int32) * CMP_STRIDE + CMP_LEN - 1
    o_cmp, p_cmp = dense_gqa(q, kc, vc, ends[None, :] <= q_pos[:, None])
    idx, ok = select_blocks(p_cmp, q_pos, n_blocks)
    kb = to_blocks(kv_slc[:, :, 0], n_blocks)
    vb = to_blocks(kv_slc[:, :, 1], n_blocks)
    return o_cmp, idx, ok, kb, vb


def sel_attention(q, kb, vb, idx, ok, q_pos):
    B, Tq, H, D = q.shape
    G = kb.shape[1]
    K = idx.shape[-1]
    take = jax.vmap(jax.vmap(lambda a, i: a[i]))
    kg = take(kb, idx)
    vg = take(vb, idx)
    qg = q.reshape(B, Tq, G, H // G, D)
    s = jnp.einsum('btgrd,bgtksd->bgrtks', qg, kg).astype(jnp.float32) * ATTN_SCALE
    kpos = idx[..., None] * SEL_BLOCK + jnp.arange(SEL_BLOCK, dtype=jnp.int32)
    mask = (kpos <= q_pos[:, None, None]) & ok[..., None]
    mask = mask.reshape(B, G, Tq, K * SEL_BLOCK)[:, :, None]
    p = masked_softmax(s.reshape(B, G, H // G, Tq, K * SEL_BLOCK), mask)
    o = jnp.einsum('bgrtn,bgtnd->btgrd', p.astype(vg.dtype), vg.reshape(B, G, Tq, K * SEL_BLOCK, D))
    return o.reshape(B, Tq, H, D)


def sel_attention_blocked(q, kb, vb, idx, ok, q_pos):
    B, T, H, D = q.shape
    G, K = idx.shape[1], idx.shape[-1]
    nb = T // Q_BLOCK
    qb = q.reshape(B, nb, Q_BLOCK, H, D).transpose(1, 0, 2, 3, 4)
    ib = idx.reshape(B, G, nb, Q_BLOCK, K).transpose(2, 0, 1, 3, 4)
    okb = ok.reshape(B, G, nb, Q_BLOCK, K).transpose(2, 0, 1, 3, 4)
    pb = q_pos.reshape(nb, Q_BLOCK)
    out = lax.map(lambda a: sel_attention(a[0], kb, vb, a[1], a[2], a[3]), (qb, ib, okb, pb))
    return out.transpose(1, 0, 2, 3, 4).reshape(B, T, H, D)


def window_prompt(q, k, v):
    B, T, H, D = q.shape
    G = k.shape[2]
    nb = T // Q_BLOCK
    nband = WINDOW // Q_BLOCK
    pad = nband * Q_BLOCK

    def band(x):
        xp = jnp.pad(x, ((0, 0), (pad, 0), (0, 0), (0, 0))).reshape(B, nb + nband, Q_BLOCK, G, D)
        return jnp.concatenate([xp[:, j:j + nb] for j in range(nband + 1)], axis=2)

    kb, vb = band(k), band(v)
    qb = q.reshape(B, nb, Q_BLOCK, G, H // G, D)
    s = jnp.einsum('bctgrd,bcsgd->bgrcts', qb, kb).astype(jnp.float32) * ATTN_SCALE
    base = jnp.arange(nb, dtype=jnp.int32)[:, None] * Q_BLOCK
    qpos = base + jnp.arange(Q_BLOCK, dtype=jnp.int32)[None, :]
    kpos = base - pad + jnp.arange((nband + 1) * Q_BLOCK, dtype=jnp.int32)[None, :]
    dlt = qpos[:, :, None] - kpos[:, None, :]
    mask = (kpos[:, None, :] >= 0) & (dlt >= 0) & (dlt <= WINDOW)
    p = masked_softmax(s, mask)
    o = jnp.einsum('bgrcts,bcsgd->bctgrd', p.astype(v.dtype), vb)
    return o.reshape(B, T, H, D)


def chunk_mix(u, v, ws, bs):
    B, T, W = v.shape
    nc = -(-T // CHUNK)
    vp = jnp.pad(v, ((0, 0), (0, nc * CHUNK - T), (0, 0))).reshape(B, nc, CHUNK, GMLP_GROUPS, GMLP_GROUP_DIM)
    tri = jnp.tril(jnp.ones((CHUNK, CHUNK), dtype=bool))
    wm = jnp.where(tri, ws, 0.0)
    z = jnp.einsum('gij,bcjgd->bcigd', wm, vp) + bs.T[None, None, :, :, None]
    return u * z.reshape(B, nc * CHUNK, W)[:, :T]


def conv_ffn(h, prev, w_up, conv_w, conv_b, w_down):
    ab = h @ w_up
    a, b = ab[..., :D_FF], ab[..., D_FF:]
    T = a.shape[1]
    ext = jnp.concatenate([prev.astype(a.dtype), a], axis=1)
    c = conv_b + ext[:, 0:T] * conv_w[0]
    for j in range(1, CONV_W):
        c = c + ext[:, j:j + T] * conv_w[j]
    y = (jax.nn.gelu(c) * b) @ w_down
    return y, ext[:, -(CONV_W - 1):]


def front(x, norm_mix, w_in, gmlp_norm):
    h = rmsnorm(x, norm_mix)
    q, kv, nsa_g, u, v, gates = project(h, w_in)
    return q, kv, nsa_g, u, rmsnorm(v, gmlp_norm), gates


def back(x, o_cmp, o_slc, o_win, nsa_g, u, vn, gates, conv_prev,
         gmlp_ws, gmlp_bs, w_proj_a, w_proj_b, w_out, norm_ffn, w_up, conv_w, conv_b, w_down):
    B, T = x.shape[:2]
    o_nsa = nsa_g[..., 0:1] * o_cmp + nsa_g[..., 1:2] * o_slc + nsa_g[..., 2:3] * o_win
    br_a = o_nsa.reshape(B, T, Q_COLS) @ w_proj_a
    br_b = chunk_mix(u, vn, gmlp_ws, gmlp_bs) @ w_proj_b
    x1 = x + (gates[:, :, 0] * br_a + gates[:, :, 1] * br_b) @ w_out
    f, conv_state = conv_ffn(rmsnorm(x1, norm_ffn), conv_prev, w_up, conv_w, conv_b, w_down)
    return x1 + f, conv_state


def setup_inputs(seed: int = 0) -> dict:
    key = jax.random.key(seed)
    ks = jax.random.split(key, 32)
    f32 = jnp.float32

    def nrm(k, shape, scale):
        return jax.random.normal(k, shape, f32) * scale

    n_pages = PAST_LEN // PAGE_SIZE
    n_used = DEC_BATCH * n_pages
    n_pool = n_used + (n_used + 3) // 4
    win_buf = min(WINDOW, PAST_LEN)
    page_table = jax.random.permutation(ks[0], n_pool)[:n_used].reshape(DEC_BATCH, n_pages).astype(jnp.int32)
    return dict(
        x_prompt=nrm(ks[1], (BATCH, SEQ, D_MODEL), 1.0),
        x_sample=nrm(ks[2], (DEC_BATCH, DEC_SEQ, D_MODEL), 1.0),
        cache_cmp=nrm(ks[3], (DEPTH, n_pool, PAGE_SIZE, 2, N_KV, HEAD_DIM), 1.0),
        cache_slc=nrm(ks[4], (DEPTH, n_pool, PAGE_SIZE, 2, N_KV, HEAD_DIM), 1.0),
        cache_win=nrm(ks[5], (DEPTH, DEC_BATCH, win_buf, 2, N_KV, HEAD_DIM), 1.0),
        state_conv=nrm(ks[6], (DEPTH, DEC_BATCH, CONV_W - 1, D_FF), 1.0),
        page_table=page_table,
        norm_mix=1.0 + nrm(ks[7], (DEPTH, D_MODEL), 0.02),
        w_in=nrm(ks[8], (DEPTH, D_MODEL, IN_COLS), D_MODEL ** -0.5),
        cmp_pe=nrm(ks[9], (DEPTH, 2, CMP_LEN, HEAD_DIM), 0.1),
        cmp_w1=nrm(ks[10], (DEPTH, 2, CMP_LEN * HEAD_DIM, CMP_HID), (CMP_LEN * HEAD_DIM) ** -0.5),
        cmp_b1=nrm(ks[11], (DEPTH, 2, CMP_HID), 0.02),
        cmp_w2=nrm(ks[12], (DEPTH, 2, CMP_HID, HEAD_DIM), CMP_HID ** -0.5),
        gmlp_norm=1.0 + nrm(ks[13], (DEPTH, GMLP_WIDTH), 0.02),
        gmlp_ws=nrm(ks[14], (DEPTH, GMLP_GROUPS, CHUNK, CHUNK), CHUNK ** -0.5),
        gmlp_bs=1.0 + nrm(ks[15], (DEPTH, GMLP_GROUPS, CHUNK), 0.02),
        w_proj_a=nrm(ks[16], (DEPTH, Q_COLS, D_MODEL), Q_COLS ** -0.5),
        w_proj_b=nrm(ks[17], (DEPTH, GMLP_WIDTH, D_MODEL), GMLP_WIDTH ** -0.5),
        w_out=nrm(ks[18], (DEPTH, D_MODEL, D_MODEL), D_MODEL ** -0.5),
        norm_ffn=1.0 + nrm(ks[19], (DEPTH, D_MODEL), 0.02),
        w_up=nrm(ks[20], (DEPTH, D_MODEL, 2 * D_FF), D_MODEL ** -0.5),
        conv_w=nrm(ks[21], (DEPTH, CONV_W, D_FF), CONV_W ** -0.5),
        conv_b=nrm(ks[22], (DEPTH, D_FF), 0.02),
        w_down=nrm(ks[23], (DEPTH, D_FF, D_MODEL), D_FF ** -0.5),
        norm_final=1.0 + nrm(ks[24], (D_MODEL,), 0.02),
    )


def reference(x_prompt, x_sample, cache_cmp, cache_slc, cache_win, state_conv, page_table,
              norm_mix, w_in, cmp_pe, cmp_w1, cmp_b1, cmp_w2, gmlp_norm, gmlp_ws, gmlp_bs,
              w_proj_a, w_proj_b, w_out, norm_ffn, w_up, conv_w, conv_b, w_down, norm_final):
    xp, xs = x_prompt, x_sample
    Bp, T = xp.shape[:2]
    Bd, Tn = xs.shape[:2]
    n_pages = page_table.shape[1]
    past_len = n_pages * cache_cmp.shape[2]
    pos_p = jnp.arange(T, dtype=jnp.int32)
    pos_s = past_len + jnp.arange(Tn, dtype=jnp.int32)
    nblk_p = -(-T // SEL_BLOCK)
    nblk_s = -(-(past_len + Tn) // SEL_BLOCK)
    st = {n: [] for n in ('cmp_p', 'slc_p', 'win_p', 'v_p', 'conv_p', 'cmp_s', 'slc_s', 'win_s', 'v_s', 'conv_s')}
    for l in range(DEPTH):
        cw = (cmp_pe[l], cmp_w1[l], cmp_b1[l], cmp_w2[l])
        bw = (gmlp_ws[l], gmlp_bs[l], w_proj_a[l], w_proj_b[l], w_out[l],
              norm_ffn[l], w_up[l], conv_w[l], conv_b[l], w_down[l])
        q, kv, g, u, vn, gates = front(xp, norm_mix[l], w_in[l], gmlp_norm[l])
        kv_cmp, kv_slc, kv_win = kv[:, :, 0], kv[:, :, 1], kv[:, :, 2]
        o_cmp, idx, ok, kb, vb = nsa_global(q, pos_p, kv_cmp, kv_slc, nblk_p, *cw)
        o_slc = sel_attention_blocked(q, kb, vb, idx, ok, pos_p)
        o_win = window_prompt(q, kv_win[:, :, 0], kv_win[:, :, 1])
        conv0 = jnp.zeros((Bp, CONV_W - 1, D_FF), xp.dtype)
        xp, conv_p = back(xp, o_cmp, o_slc, o_win, g, u, vn, gates, conv0, *bw)
        st['cmp_p'].append(kv_cmp)
        st['slc_p'].append(kv_slc)
        st['win_p'].append(kv_win[:, -min(WINDOW, T):])
        st['v_p'].append(vn[:, ((T - 1) // CHUNK) * CHUNK:])
        st['conv_p'].append(conv_p)
        q, kv, g, u, vn, gates = front(xs, norm_mix[l], w_in[l], gmlp_norm[l])
        kv_cmp, kv_slc, kv_win = kv[:, :, 0], kv[:, :, 1], kv[:, :, 2]
        rows = (Bd, past_len) + cache_cmp.shape[3:]
        full_cmp = jnp.concatenate([cache_cmp[l][page_table].reshape(rows), kv_cmp], axis=1)
        full_slc = jnp.concatenate([cache_slc[l][page_table].reshape(rows), kv_slc], axis=1)
        o_cmp, idx, ok, kb, vb = nsa_global(q, pos_s, full_cmp, full_slc, nblk_s, *cw)
        o_slc = sel_attention(q, kb, vb, idx, ok, pos_s)
        win_all = jnp.concatenate([cache_win[l], kv_win], axis=1)
        lb = cache_win.shape[2]
        kpos = past_len - lb + jnp.arange(lb + Tn, dtype=jnp.int32)
        dlt = pos_s[:, None] - kpos[None, :]
        o_win, _ = dense_gqa(q, win_all[:, :, 0], win_all[:, :, 1], (dlt >= 0) & (dlt <= WINDOW))
        xs, conv_s = back(xs, o_cmp, o_slc, o_win, g, u, vn, gates, state_conv[l], *bw)
        st['cmp_s'].append(kv_cmp)
        st['slc_s'].append(kv_slc)
        st['win_s'].append(win_all[:, -min(WINDOW, lb + Tn):])
        st['v_s'].append(vn[:, ((Tn - 1) // CHUNK) * CHUNK:])
        st['conv_s'].append(conv_s)
    y_prompt = rmsnorm(xp, norm_final)
    y_sample = rmsnorm(xs, norm_final)
    return (y_prompt, y_sample,
            jnp.stack(st['cmp_p']), jnp.stack(st['slc_p']), jnp.stack(st['win_p']),
            jnp.stack(st['v_p']), jnp.stack(st['conv_p']),
            jnp.stack(st['cmp_s']), jnp.stack(st['slc_s']), jnp.stack(st['win_s']),
            jnp.stack(st['v_s']), jnp.stack(st['conv_s']))
```

```python
import numpy as np
from contextlib import ExitStack
import concourse.bass as bass
import concourse.mybir as mybir
from concourse.bass_utils import run_bass_kernel_spmd

F32 = mybir.dt.float32
BF16 = mybir.dt.bfloat16
I32 = mybir.dt.int32
AF = mybir.ActivationFunctionType
ALU = mybir.AluOpType
AX = mybir.AxisListType

NEGM = -30000.0
EPS = 1e-6
D = 1024
INC = 4376
DFF = 2816
NSLOT = 288


class Buf:
    __slots__ = ("name", "w", "r")

    def __init__(self, name):
        self.name = name
        self.w = {}
        self.r = {}


def _merge(d, k, v):
    if d.get(k, 0) < v:
        d[k] = v


class KB:
    NDMA = 32

    def __init__(self, nc):
        self.nc = nc
        self.eng = {"pe": nc.tensor, "act": nc.scalar, "dve": nc.vector, "pool": nc.gpsimd, "sp": nc.sync}
        self.sem = {e: nc.alloc_semaphore("cs_" + e) for e in ("pe", "act", "dve", "pool")}
        self.cnt = {e: 0 for e in self.sem}
        self.waited = {e: {} for e in self.eng}
        self.dsem = [nc.alloc_semaphore("ds%d" % i) for i in range(self.NDMA)]
        self.dtgt = [0] * self.NDMA
        self.dring = {"sp": list(range(0, 24)), "pool": list(range(24, 28)), "act": list(range(28, 32))}
        self.dpos = {"sp": 0, "pool": 0, "act": 0}
        self.nb = 0

    def buf(self, name=None):
        self.nb += 1
        return Buf(name or ("b%d" % self.nb))

    def _deps(self, reads, writes):
        deps = {}
        for b in reads:
            for k, v in b.w.items():
                _merge(deps, k, v)
        for b in writes:
            for k, v in b.w.items():
                _merge(deps, k, v)
            for k, v in b.r.items():
                _merge(deps, k, v)
        return deps

    def _wait(self, e, deps):
        eng = self.eng[e]
        wd = self.waited[e]
        for k, v in deps.items():
            if k == e and e == "pe":
                continue
            if wd.get(k, 0) >= v:
                continue
            sem = self.dsem[k[1]] if isinstance(k, tuple) else self.sem[k]
            eng.wait_ge(sem, v)
            wd[k] = v

    def _mark(self, key, val, reads, writes):
        for b in reads:
            _merge(b.r, key, val)
        for b in writes:
            b.w = {key: val}
            b.r = {}

    def op(self, e, fn, reads=(), writes=(), sig=True):
        self._wait(e, self._deps(reads, writes))
        ins = fn(self.eng[e])
        if sig:
            self.cnt[e] += 1
            ins.then_inc(self.sem[e], 1)
            val = self.cnt[e]
        else:
            val = self.cnt[e] + 1
        self._mark(e, val, reads, writes)

    def dma(self, q, out, in_, reads=(), writes=(), **kw):
        ring = self.dring[q]
        si = ring[self.dpos[q] % len(ring)]
        self.dpos[q] += 1
        deps = self._deps(reads, writes)
        if self.dtgt[si] > 0:
            _merge(deps, ("d", si), self.dtgt[si])
        self._wait(q, deps)
        ins = self.eng[q].dma_start(out=out, in_=in_, **kw)
        self.dtgt[si] += 16
        ins.then_inc(self.dsem[si], 16)
        self._mark(("d", si), self.dtgt[si], reads, writes)

    def barrier(self):
        deps = {}
        for e, c in self.cnt.items():
            if c > 0:
                deps[e] = c
        for si, t in enumerate(self.dtgt):
            if t > 0:
                deps[("d", si)] = t
        for e in self.eng:
            self._wait(e, deps)

    def finish(self):
        self.barrier()


class _Stop(Exception):
    pass


def build_program(do_sample=True, stage=99, sub=99):
    st = {}
    try:
        return _build_program(do_sample, stage, sub, st)
    except _Stop:
        st["K"].finish()
        return st["nc"]


def _build_program(do_sample, stage, sub, _st):
    nc = bass.Bass("TRN2", target_bir_lowering=False)
    K = KB(nc)
    _st["K"], _st["nc"] = K, nc
    es0 = ExitStack()

    def ck(n):
        if stage == 4 and sub == n:
            raise _Stop()

    def din(name, shape, dt=F32):
        return nc.dram_tensor(name, list(shape), dt, kind="ExternalInput").ap()

    def dout(name, shape, dt=F32):
        return nc.dram_tensor(name, list(shape), dt, kind="ExternalOutput").ap()

    def sbt(es, name, shape, dt):
        return es.enter_context(nc.sbuf_tensor(name, list(shape), dt)).ap()

    xh = din("xh", [2048, D])
    xo = din("xo", [2048, D])
    selb = din("selb", [1, 64])
    cmpb = din("cmpb", [1, NSLOT])
    hsc = din("hsc", [1, 2])
    w_in = din("w_in", [D, INC])
    cmp_pe = din("cmp_pe", [2, 32, 64])
    cmp_w1 = din("cmp_w1", [2, 2048, 128])
    cmp_b1 = din("cmp_b1", [2, 128])
    cmp_w2 = din("cmp_w2", [2, 128, 64])
    gmlp_norm = din("gmlp_norm", [1, 512])
    gmlp_ws = din("gmlp_ws", [4, 128, 128])
    gmlp_bs = din("gmlp_bs", [1, 512])
    w_proj_a = din("w_proj_a", [512, D])
    w_proj_b = din("w_proj_b", [512, D])
    w_out = din("w_out", [D, D])
    norm_mix = din("norm_mix", [8, 128])
    norm_ffn = din("norm_ffn", [8, 128])
    w_up = din("w_up", [D, 2 * DFF])
    conv_w = din("conv_w", [3 * 22, 128])
    conv_b = din("conv_b", [22, 128])
    w_down = din("w_down", [DFF, D])
    norm_final = din("norm_final", [1, D])

    xs = din("xs", [4, D])
    cache_cmp = din("cache_cmp", [40960, 4096])
    cache_slc = din("cache_slc", [40960, 4096])
    cache_win = din("cache_win", [4, 512, 256])
    state_conv = din("state_conv", [8, DFF])
    ptx = din("ptx", [128, 32], I32)
    pmod = din("pmod", [128, 1])
    ys_o = dout("ys", [4, D])
    kvs_o = dout("kvs", [4, 768])
    wins_o = dout("wins", [4, 512, 256])
    vns_o = dout("vns", [4, 512])
    convs_o = dout("convs", [4, 2, DFF])

    y_o = dout("y", [2048, D])
    kv_o = dout("kvo", [3, 2048, 256])
    vn_o = dout("vno", [128, 512])
    conv_o = dout("convo", [2, DFF])

    scr_om = nc.dram_tensor("scr_om", [5, 128, 8, 512], BF16, kind="Internal").ap()
    scr_x1 = nc.dram_tensor("scr_x1", [17, 128, D], F32, kind="Internal").ap()
    B_scr_om = [K.buf() for _ in range(5)]
    B_scr_x1 = [K.buf() for _ in range(17)]

    psS = nc.alloc_psum_tensor("psS", [128, 1024], F32).ap()
    banks = [nc.alloc_psum_tensor("pb%d" % i, [128, 512], F32).ap() for i in range(6)]
    B_bank = [K.buf("bank%d" % i) for i in range(6)]
    B_S = [K.buf("bankS0"), K.buf("bankS1")]
    allb = [(banks[i], B_bank[i]) for i in range(6)] + [(psS[:, 0:512], B_S[0]), (psS[:, 512:1024], B_S[1])]
    rot = {"g": 0, "o": 0, "a": 0}

    def ps_g():
        i = rot["g"]
        rot["g"] = (i + 1) % 4
        return allb[i]

    def ps_o():
        i = rot["o"]
        rot["o"] = (i + 1) % 2
        return allb[4 + i]

    def ps_a():
        i = rot["a"]
        rot["a"] = (i + 1) % 8
        return allb[i]

    c = es0
    ident_f = sbt(c, "ident_f", [128, 128], F32)
    ident_b = sbt(c, "ident_b", [128, 128], BF16)
    ones_f = sbt(c, "ones_f", [128, 128], F32)
    ones_b = sbt(c, "ones_b", [128, 128], BF16)
    zeros_b = sbt(c, "zeros_b", [128, 512], BF16)
    negs_b = sbt(c, "negs_b", [128, 512], BF16)
    B_const = K.buf("const")

    K.op("pool", lambda e: e.memset(ones_f, 1.0), writes=[B_const])
    K.op("pool", lambda e: e.memset(ones_b, 1.0), writes=[B_const])
    K.op("pool", lambda e: e.memset(zeros_b, 0.0), writes=[B_const])
    K.op("pool", lambda e: e.memset(negs_b, NEGM), writes=[B_const])
    K.op("pool", lambda e: e.affine_select(out=ident_f, in_=ones_f, pattern=[[-1, 128]], compare_op=ALU.is_equal,
                                           fill=0.0, base=0, channel_multiplier=1), writes=[B_const])
    K.op("pool", lambda e: e.affine_select(out=ident_b, in_=ones_b, pattern=[[-1, 128]], compare_op=ALU.is_equal,
                                           fill=0.0, base=0, channel_multiplier=1), writes=[B_const])

    def r4(ap):
        return ap.rearrange("p (a b) -> p a b", a=4)

    def load_cols(dst, src_rows, nrows):
        tmp = sbt(c, "lc_%d" % K.nb, [nrows, 128], F32)
        tb = K.buf()
        K.dma("sp", tmp, src_rows, writes=[tb])
        pt, pb = ps_g()
        K.op("pe", lambda e: e.transpose(out=pt[:, 0:nrows], in_=tmp, identity=ident_f[0:nrows, 0:nrows]),
             reads=[tb, B_const], writes=[pb])
        K.op("dve", lambda e: e.tensor_copy(out=dst, in_=pt[:, 0:nrows]), reads=[pb], writes=[B_const])

    gcol_mix = sbt(c, "gcol_mix", [128, 8], F32)
    gcol_ffn = sbt(c, "gcol_ffn", [128, 8], F32)
    load_cols(gcol_mix, norm_mix, 8)
    load_cols(gcol_ffn, norm_ffn, 8)
    halfcol = sbt(c, "halfcol", [128, 1], F32)
    K.dma("sp", halfcol, hsc[:, 1:2].partition_broadcast(128), writes=[B_const])
    hsc_sb = sbt(c, "hsc_sb", [1, 2], F32)
    K.dma("sp", hsc_sb, hsc, writes=[B_const])

    hT_s = sbt(c, "hT_s", [128, 8, 4], BF16)
    B_hTs = K.buf()
    omT_s = sbt(c, "omT_s", [128, 8, 4], BF16)
    B_omTs = K.buf()
    scr_x1s = nc.dram_tensor("scr_x1s", [4, D], F32, kind="Internal").ap()
    B_scr_x1s = K.buf()

    e1 = ExitStack()
    w_in_sb = sbt(e1, "w_in_sb", [128, 8, 2328], BF16)
    B_win = K.buf("w_in")
    for k in range(8):
        for g in range(2):
            K.dma("pool", w_in_sb[:, k, 0:512].rearrange("p (r g d) -> p g r d", r=4, g=2, d=64)[:, g],
                  w_in[k * 128:(k + 1) * 128, g * 256:(g + 1) * 256].rearrange("p (r d) -> p r d", d=64),
                  writes=[B_win])
        K.dma("pool", w_in_sb[:, k, 512:2328], w_in[k * 128:(k + 1) * 128, 512:2328], writes=[B_win])
    w1sb = sbt(e1, "w1sb", [128, 2, 32, 128], BF16)
    for kv in range(2):
        for hf in range(2):
            K.dma("pool", w1sb[hf * 64:(hf + 1) * 64, kv], cmp_w1[kv].rearrange("(p d) h -> d p h", d=64),
                  writes=[B_win])
    w2pad = sbt(e1, "w2pad", [128, 2, 2, 128], BF16)
    K.op("pool", lambda e: e.memset(w2pad, 0.0), writes=[B_win])
    for kv in range(2):
        for g in range(2):
            K.dma("pool", w2pad[:, kv, g, g * 64:(g + 1) * 64], cmp_w2[kv], writes=[B_win])
    b1col = sbt(e1, "b1col", [128, 2], F32)
    for kv in range(2):
        K.dma("sp", b1col[:, kv:kv + 1], cmp_b1[kv].rearrange("(p o) -> p o", o=1), writes=[B_win])
    peTok = sbt(e1, "peTok", [32, 2, 64], F32)
    K.dma("sp", peTok, cmp_pe.rearrange("k p d -> p k d"), writes=[B_win])
    peT = sbt(e1, "peT", [64, 2, 32], BF16)
    b1eff = sbt(e1, "b1eff", [128, 2], F32)
    for kv in range(2):
        pt, pb = ps_g()
        K.op("pe", lambda e: e.transpose(out=pt[0:64, 0:32], in_=peTok[:, kv, :], identity=ident_f[0:32, 0:32]),
             reads=[B_win, B_const], writes=[pb])
        K.op("dve", lambda e: e.tensor_copy(out=peT[:, kv, :], in_=pt[0:64, 0:32]), reads=[pb], writes=[B_win])
    for kv in range(2):
        pt, pb = ps_g()
        for pos in range(32):
            K.op("pe", lambda e: e.matmul(pt[:, 0:1], lhsT=w1sb[0:64, kv, pos, :], rhs=peT[:, kv, pos:pos + 1],
                                          start=(pos == 0), stop=(pos == 31)), reads=[B_win], writes=[pb], sig=(pos == 31))
        K.op("dve", lambda e: e.tensor_tensor(out=b1eff[:, kv:kv + 1], in0=pt[:, 0:1], in1=b1col[:, kv:kv + 1],
                                              op=ALU.add), reads=[pb, B_win], writes=[B_win])
    ws_f = sbt(e1, "ws_f", [128, 4, 128], F32)
    K.dma("sp", ws_f, gmlp_ws.rearrange("g i j -> i g j"), writes=[B_win])
    K.op("pool", lambda e: e.affine_select(out=ws_f, in_=ws_f, pattern=[[0, 4], [-1, 128]], compare_op=ALU.is_ge,
                                           fill=0.0, base=0, channel_multiplier=1), reads=[B_win], writes=[B_win])
    WsT = sbt(e1, "WsT", [128, 4, 128], BF16)
    for g in range(4):
        pt, pb = ps_g()
        K.op("pe", lambda e: e.transpose(out=pt[:, 0:128], in_=ws_f[:, g, :], identity=ident_f),
             reads=[B_win, B_const], writes=[pb])
        K.op("dve", lambda e: e.tensor_copy(out=WsT[:, g, :], in_=pt[:, 0:128]), reads=[pb], writes=[B_win])
    bsB = sbt(e1, "bsB", [128, 512], F32)
    K.dma("sp", bsB, gmlp_bs.partition_broadcast(128), writes=[B_win])
    gnB = sbt(e1, "gnB", [128, 512], F32)
    K.dma("sp", gnB, gmlp_norm.partition_broadcast(128), writes=[B_win])

    mask_diag4 = sbt(e1, "mask_diag4", [128, 4, 128], BF16)
    mask_band4 = sbt(e1, "mask_band4", [128, 4, 128], BF16)
    K.op("pool", lambda e: e.affine_select(out=mask_diag4, in_=r4(zeros_b), pattern=[[0, 4], [1, 128]],
                                           compare_op=ALU.is_ge, fill=NEGM, base=0, channel_multiplier=-1),
         reads=[B_const], writes=[B_win])
    K.op("pool", lambda e: e.affine_select(out=mask_band4, in_=r4(zeros_b), pattern=[[0, 4], [-1, 128]],
                                           compare_op=ALU.is_ge, fill=NEGM, base=0, channel_multiplier=1),
         reads=[B_const], writes=[B_win])
    Eb = sbt(e1, "Eb", [64, 4096], BF16)
    K.op("pool", lambda e: e.memset(Eb, 1.0), writes=[B_win])
    K.op("pool", lambda e: e.affine_select(out=Eb, in_=Eb, pattern=[[1, 4096]], compare_op=ALU.is_ge, fill=0.0,
                                           base=0, channel_multiplier=-64), reads=[B_win], writes=[B_win])
    K.op("pool", lambda e: e.affine_select(out=Eb, in_=Eb, pattern=[[-1, 4096]], compare_op=ALU.is_ge, fill=0.0,
                                           base=63, channel_multiplier=64), reads=[B_win], writes=[B_win])
    Mst = sbt(e1, "Mst", [128, 128], F32)
    K.op("pool", lambda e: e.memset(Mst, 0.0), writes=[B_win])
    K.op("pool", lambda e: e.memset(Mst[:, 66:128], -1e30), writes=[B_win])
    K.op("pool", lambda e: e.memset(Mst[0:64, 65:66], -1e30), writes=[B_win])
    K.op("pool", lambda e: e.memset(Mst[64:128, 65:66], 1e6), writes=[B_win])
    K.op("pool", lambda e: e.memset(Mst[:, 64:65], 1e6), writes=[B_win])
    K.op("pool", lambda e: e.memset(Mst[0:64, 63:64], 1e6), writes=[B_win])
    vis8x2 = sbt(e1, "vis8x2", [128, 2, 8], BF16)
    K.op("pool", lambda e: e.affine_select(out=vis8x2, in_=zeros_b[:, 0:16].rearrange("p (a b) -> p a b", a=2),
                                           pattern=[[0, 2], [-16, 8]], compare_op=ALU.is_ge, fill=NEGM, base=-15,
                                           channel_multiplier=1), reads=[B_const], writes=[B_win])
    selbB = sbt(e1, "selbB", [128, 64], F32)
    K.dma("sp", selbB, selb.partition_broadcast(128), writes=[B_win])
    cmpb_row2 = sbt(e1, "cmpb_row2", [1, 2, 256], BF16)
    for a in range(2):
        K.dma("pool", cmpb_row2[:, a, :], cmpb[:, 0:256], writes=[B_win])
    cmpbT = sbt(e1, "cmpbT", [128, 2], F32)
    for ch in range(2):
        K.dma("sp", cmpbT[:, ch:ch + 1], cmpb[0, ch * 128:(ch + 1) * 128].rearrange("(p o) -> p o", o=1),
              writes=[B_win])
    histneg4 = sbt(e1, "histneg4", [1, 512], BF16)
    K.op("dve", lambda e: e.tensor_scalar(out=histneg4, in0=zeros_b[0:1, :], scalar1=hsc_sb[0:1, 0:1], scalar2=None,
                                          op0=ALU.add), reads=[B_const], writes=[B_win])

    if stage <= 1:
        K.finish()
        return nc
    e1p = ExitStack()
    kslcT = sbt(e1p, "kslcT", [128, 4096], BF16)
    Vslc = sbt(e1p, "Vslc", [128, 32, 2, 65], BF16)
    kwinT = sbt(e1p, "kwinT", [128, 8, 128], BF16)
    Vwin = sbt(e1p, "Vwin", [128, 8, 2, 65], BF16)
    kvcT = sbt(e1p, "kvcT", [128, 2, 16 + 512], BF16)
    kcT = sbt(e1p, "kcT", [128, NSLOT], BF16)
    vcT = sbt(e1p, "vcT", [128, NSLOT], BF16)
    vcaug = sbt(e1p, "vcaug", [128, 2, 2, 65], BF16)
    B_kslc = [K.buf() for _ in range(32)]
    B_vslc = [K.buf() for _ in range(32)]
    B_kwin = [K.buf() for _ in range(8)]
    B_vwin = [K.buf() for _ in range(8)]
    B_kvc = [K.buf(), K.buf()]
    B_kc = K.buf()
    B_vc = K.buf()
    B_vcaug = [K.buf(), K.buf()]
    K.op("pool", lambda e: e.memset(Vslc, 1.0), writes=B_vslc)
    K.op("pool", lambda e: e.memset(Vwin, 1.0), writes=B_vwin)
    K.op("pool", lambda e: e.memset(vcaug, 1.0), writes=B_vcaug)
    K.op("pool", lambda e: e.memset(kvcT, 0.0), writes=B_kvc)
    K.op("pool", lambda e: e.memset(kcT, 0.0), writes=[B_kc])
    K.op("pool", lambda e: e.memset(vcT, 0.0), writes=[B_vc])

    xbuf = [sbt(e1p, "xbuf%d" % i, [128, D], F32) for i in range(2)]
    B_x = [K.buf(), K.buf()]
    xn = [sbt(e1p, "xn%d" % i, [128, D], BF16) for i in range(2)]
    B_xn = [K.buf(), K.buf()]
    st4 = sbt(e1p, "st4", [128, 8], F32)
    B_st = K.buf()
    hT = sbt(e1p, "hT", [128, 8, 512], BF16)
    B_hT = [K.buf() for _ in range(4)]
    qTz = [sbt(e1p, "qTz%d" % i, [128, 4, 512], BF16) for i in range(2)]
    B_qT = K.buf()
    for i in range(2):
        K.op("pool", lambda e: e.memset(qTz[i], 0.0), writes=[B_qT])
    uT = sbt(e1p, "uT", [128, 4, 512], BF16)
    B_uT = K.buf()
    kvtok = [sbt(e1p, "kvtok%d" % i, [128, 768], F32) for i in range(2)]
    B_kvtok = [K.buf(), K.buf()]
    sg = sbt(e1p, "sg", [128, 4, 24], F32)
    B_sg = [K.buf() for _ in range(4)]
    gv = sbt(e1p, "gv", [128, 512], F32)
    B_gv = K.buf()
    vn_f = sbt(e1p, "vn_f", [128, 512], F32)
    B_vnf = K.buf()
    vn_b = sbt(e1p, "vn_b", [128, 512], BF16)
    B_vnb = K.buf()
    tmpm = sbt(e1p, "tmpm", [128, 512], F32)
    B_tmpm = K.buf()
    mixT = sbt(e1p, "mixT", [128, 4, 512], BF16)
    B_mixT = K.buf()
    gh = sbt(e1p, "gh", [128, 2, 2, 32], BF16)
    B_gh = K.buf()
    ET = [sbt(e1p, "ET%d" % i, [128, 512], BF16) for i in range(3)]
    B_ET = [K.buf() for _ in range(3)]
    etr = [0]
    Eh = [sbt(e1p, "Eh%d" % i, [128, 256], F32) for i in range(2)]
    B_Eh = [K.buf(), K.buf()]
    impbuf = sbt(e1p, "impbuf", [128, 272], F32)
    B_imp = K.buf()
    K.op("pool", lambda e: e.memset(impbuf, 0.0), writes=[B_imp])
    sc = sbt(e1p, "sc", [128, 64], F32)
    sc2 = sbt(e1p, "sc2", [128, 64], F32)
    mx8 = sbt(e1p, "mx8", [128, 16], F32)
    selt = sbt(e1p, "selt", [128, 64], BF16)
    den4 = sbt(e1p, "den4", [128, 8], F32)
    B_sel = K.buf()
    cmask = [sbt(e1p, "cmask%d" % i, [128, 4, 128], BF16) for i in range(2)]
    B_cmask = [K.buf(), K.buf()]
    negselT4 = [sbt(e1p, "negselT4_%d" % i, [64, 4, 128], BF16) for i in range(2)]
    B_negsel = [K.buf(), K.buf()]
    o_brs = [sbt(e1p, "o_br%d" % i, [128, 3, 4, 65], F32) for i in range(2)]
    B_obrs = [[K.buf() for _ in range(3)] for _ in range(2)]
    coef = sbt(e1p, "coef", [128, 3, 4], F32)
    B_coef = K.buf()
    otmp = sbt(e1p, "otmp", [128, 3, 4, 64], F32)
    B_otmp = K.buf()
    o_nsa = sbt(e1p, "o_nsa", [128, 512], BF16)
    B_onsa = K.buf()
    o_nsaT = sbt(e1p, "o_nsaT", [128, 4, 512], BF16)
    B_onsaT = K.buf()

    print("SBUF remaining after pass1a alloc:", nc.sbuf_bytes_remaining)
    def rmsnorm_T(src_dram, xb, Bxb, xnb, Bxnb, gcol, hT_dst, B_hT_dst, x_preloaded=False):
        if not x_preloaded:
            K.dma("sp", xb, src_dram, writes=[Bxb])
        K.op("act", lambda e: e.activation(out=xnb, in_=xb, func=AF.Square, accum_out=st4[:, 0:1]),
             reads=[Bxb], writes=[Bxnb, B_st])
        K.op("act", lambda e: e.activation(out=st4[:, 1:2], in_=st4[:, 0:1], func=AF.Ln, scale=1.0 / D, bias=EPS), reads=[B_st], writes=[B_st])
        K.op("act", lambda e: e.activation(out=st4[:, 2:3], in_=st4[:, 1:2], func=AF.Exp, scale=-0.5), reads=[B_st], writes=[B_st])
        K.op("dve", lambda e: e.tensor_scalar(out=xnb, in0=xb, scalar1=st4[:, 2:3], scalar2=None, op0=ALU.mult),
             reads=[Bxb, B_st], writes=[Bxnb])
        pt, pb = ps_g()
        ptb = pt.bitcast(BF16).rearrange("p (k t) -> p k t", k=8)
        for k in range(8):
            K.op("pe", lambda e: e.transpose(out=ptb[:, k, :], in_=xnb[:, k * 128:(k + 1) * 128], identity=ident_b),
                 reads=[Bxnb, B_const], writes=[pb], sig=(k == 7))
        K.op("dve", lambda e: e.tensor_tensor(out=hT_dst, in0=ptb, in1=gcol.unsqueeze(2).to_broadcast([128, 8, 128]),
                                              op=ALU.mult), reads=[pb, B_const], writes=[B_hT_dst])

    def projT(cols_ap_fn, nt, B_hts, evac):
        pt, pb = ps_g()
        for k in range(8):
            K.op("pe", lambda e: e.matmul(pt[:, 0:nt], lhsT=cols_ap_fn(k), rhs=hT[:, k, 0:nt], start=(k == 0),
                                          stop=(k == 7)), reads=[B_win] + B_hts, writes=[pb], sig=(k == 7))
        evac(pt[:, 0:nt], pb)

    def tile_src(T):
        return xh[T * 128:(T + 1) * 128, :] if T < 16 else xo[(T - 16) * 128:(T - 15) * 128, :]

    def compress_st(seg0, nseg):
        for kv in range(2):
            for g in range(2):
                pt, pb = ps_g()
                i = 0
                for r in range(2):
                    for s in range(16):
                        c0 = 16 * r + s
                        K.op("pe", lambda e: e.matmul(pt[:, 0:nseg], lhsT=w1sb[g * 64:(g + 1) * 64, kv, r * 16 + s, :],
                                                      rhs=kvcT[g * 64:(g + 1) * 64, kv, c0:c0 + 16 * (nseg - 1) + 1:16],
                                                      start=(i == 0), stop=(i == 31)),
                             reads=[B_win, B_kvc[kv]], writes=[pb], sig=(i == 31))
                        i += 1
                K.op("act", lambda e: e.activation(out=gh[:, kv, g, 0:nseg], in_=pt[:, 0:nseg], func=AF.Gelu_apprx_tanh,
                                                   bias=b1eff[:, kv:kv + 1]), reads=[pb, B_win], writes=[B_gh])
            pt, pb = ps_g()
            for g in range(2):
                K.op("pe", lambda e: e.matmul(pt[:, 0:nseg], lhsT=w2pad[:, kv, g, :], rhs=gh[:, kv, g, 0:nseg],
                                              start=(g == 0), stop=(g == 1)), reads=[B_win, B_gh], writes=[pb], sig=(g == 1))
            dst, Bd = (kcT, B_kc) if kv == 0 else (vcT, B_vc)
            K.op("dve", lambda e: e.tensor_copy(out=dst[:, seg0:seg0 + nseg], in_=pt[:, 0:nseg]), reads=[pb],
                 writes=[Bd])
            K.op("pool", lambda e: e.tensor_copy(out=kvcT[:, kv, 0:16], in_=kvcT[:, kv, 16 * nseg:16 * nseg + 16]),
                 reads=[B_kvc[kv]], writes=[B_kvc[kv]])

    def vc_transpose(ch):
        pt, pb = ps_g()
        ptb = pt.bitcast(BF16)
        K.op("pe", lambda e: e.transpose(out=ptb[:, 0:128], in_=vcT[:, ch * 128:(ch + 1) * 128], identity=ident_b),
             reads=[B_vc, B_const], writes=[pb])
        K.op("dve", lambda e: e.tensor_copy(out=vcaug[:, ch, :, 0:64],
                                            in_=ptb[:, 0:128].rearrange("p (g d) -> p g d", g=2)),
             reads=[pb], writes=[B_vcaug[ch]])

    def front_st(T0, ntl, full):
        nt = ntl * 128
        for j in range(ntl):
            T = T0 + j
            rmsnorm_T(tile_src(T), xbuf[j % 2], B_x[j % 2], xn[j % 2], B_xn[j % 2], gcol_mix,
                      hT[:, :, j * 128:(j + 1) * 128], B_hT[j])
        Bh = B_hT[0:ntl]
        for kv in range(2):
            projT(lambda k: w_in_sb[:, k, 512 + kv * 128:640 + kv * 128], nt, Bh,
                  lambda p, pb: K.op("act", lambda e: e.copy(out=kvcT[:, kv, 16:16 + nt], in_=p), reads=[pb],
                                     writes=[B_kvc[kv]]))
        projT(lambda k: w_in_sb[:, k, 768:896], nt, Bh,
              lambda p, pb: K.op("dve", lambda e: e.tensor_copy(out=kslcT[:, T0 * 128:T0 * 128 + nt], in_=p),
                                 reads=[pb], writes=B_kslc[T0:T0 + ntl]))
        if T0 + ntl > 8:
            pt, pb = ps_g()
            for k in range(8):
                K.op("pe", lambda e: e.matmul(pt[:, 0:nt], lhsT=w_in_sb[:, k, 1024:1152], rhs=hT[:, k, 0:nt],
                                              start=(k == 0), stop=(k == 7)), reads=[B_win] + Bh, writes=[pb], sig=(k == 7))
            for j in range(ntl):
                sl = (T0 + j) % 8
                K.op("act", lambda e: e.copy(out=kwinT[:, sl, :], in_=pt[:, j * 128:(j + 1) * 128]), reads=[pb],
                     writes=[B_kwin[sl]])
        if full:
            for r in range(4):
                projT(lambda k: w_in_sb[:, k, 128 * r:128 * r + 128], nt, Bh,
                      lambda p, pb: (K.op("act", lambda e: e.copy(out=qTz[0][0:64, r, 0:nt], in_=p[0:64]), reads=[pb],
                                          writes=[B_qT]),
                                     K.op("dve", lambda e: e.tensor_copy(out=qTz[1][64:128, r, 0:nt], in_=p[64:128]),
                                          reads=[pb], writes=[B_qT])))
            for cc in range(4):
                projT(lambda k: w_in_sb[:, k, 1304 + cc * 128:1432 + cc * 128], nt, Bh,
                      lambda p, pb: K.op("act", lambda e: e.activation(out=uT[:, cc, 0:nt], in_=p,
                                                                       func=AF.Gelu_apprx_tanh), reads=[pb],
                                         writes=[B_uT]))
        for j in range(ntl):
            T = T0 + j
            kt_, Bkt = kvtok[j % 2], B_kvtok[j % 2]
            pa, pab = ps_g()
            for k in range(8):
                K.op("pe", lambda e: e.matmul(pa, lhsT=hT[:, k, j * 128:(j + 1) * 128], rhs=w_in_sb[:, k, 512:1024],
                                              start=(k == 0), stop=(k == 7)), reads=[B_win, B_hT[j]], writes=[pab], sig=(k == 7))
            pbk, pbb = ps_g()
            for k in range(8):
                K.op("pe", lambda e: e.matmul(pbk[:, 0:280], lhsT=hT[:, k, j * 128:(j + 1) * 128],
                                              rhs=w_in_sb[:, k, 1024:1304], start=(k == 0), stop=(k == 7)),
                     reads=[B_win, B_hT[j]], writes=[pbb], sig=(k == 7))
            K.op("act", lambda e: e.copy(out=kt_[:, 0:512], in_=pa), reads=[pab], writes=[Bkt])
            K.op("dve", lambda e: e.tensor_copy(out=kt_[:, 512:768], in_=pbk[:, 0:256]), reads=[pbb], writes=[Bkt])
            if full:
                K.op("act", lambda e: e.activation(out=sg[:, j, :], in_=pbk[:, 256:280], func=AF.Sigmoid),
                     reads=[pbb], writes=[B_sg[j]])
            K.op("pool", lambda e: e.tensor_copy(out=Vslc[:, T, :, 0:64],
                                                 in_=kt_[:, 384:512].rearrange("p (g d) -> p g d", g=2)),
                 reads=[Bkt], writes=[B_vslc[T]])
            if T >= 8:
                K.op("pool", lambda e: e.tensor_copy(out=Vwin[:, T % 8, :, 0:64],
                                                     in_=kt_[:, 640:768].rearrange("p (g d) -> p g d", g=2)),
                     reads=[Bkt], writes=[B_vwin[T % 8]])
            if T >= 16:
                for br in range(3):
                    K.dma("sp", kv_o[br, (T - 16) * 128:(T - 15) * 128, :], kt_[:, br * 256:(br + 1) * 256],
                          reads=[Bkt])
            if full:
                pc, pcb = ps_g()
                for k in range(8):
                    K.op("pe", lambda e: e.matmul(pc, lhsT=hT[:, k, j * 128:(j + 1) * 128],
                                                  rhs=w_in_sb[:, k, 1816:2328], start=(k == 0), stop=(k == 7)),
                         reads=[B_win, B_hT[j]], writes=[pcb], sig=(k == 7))
                K.op("act", lambda e: e.activation(out=gv, in_=pc, func=AF.Gelu_apprx_tanh), reads=[pcb],
                     writes=[B_gv])
                K.op("act", lambda e: e.activation(out=vn_b, in_=gv, func=AF.Square,
                                                   accum_out=st4[:, 4:5]), reads=[B_gv], writes=[B_vnb, B_st])
                K.op("act", lambda e: e.activation(out=st4[:, 5:6], in_=st4[:, 4:5], func=AF.Ln, scale=1.0 / 512, bias=EPS), reads=[B_st], writes=[B_st])
                K.op("act", lambda e: e.activation(out=st4[:, 6:7], in_=st4[:, 5:6], func=AF.Exp, scale=-0.5), reads=[B_st], writes=[B_st])
                K.op("dve", lambda e: e.scalar_tensor_tensor(out=vn_f, in0=gv, scalar=st4[:, 6:7], in1=gnB,
                                                             op0=ALU.mult, op1=ALU.mult),
                     reads=[B_gv, B_st, B_win], writes=[B_vnf])
                K.op("pool", lambda e: e.tensor_copy(out=vn_b, in_=vn_f), reads=[B_vnf], writes=[B_vnb])
                if T == 31:
                    K.dma("sp", vn_o, vn_f, reads=[B_vnf])
                pm, pmb = ps_g()
                for g in range(4):
                    K.op("pe", lambda e: e.matmul(pm[:, g * 128:(g + 1) * 128], lhsT=vn_b[:, g * 128:(g + 1) * 128],
                                                  rhs=WsT[:, g, :], start=(g == 0), stop=(g == 3),
                                                  skip_group_check=True), reads=[B_vnb, B_win], writes=[pmb], sig=(g == 3))
                K.op("dve", lambda e: e.tensor_tensor(out=tmpm, in0=pm, in1=bsB, op=ALU.add), reads=[pmb, B_win],
                     writes=[B_tmpm])
                K.op("pool", lambda e: e.tensor_tensor(out=mixT[:, :, j * 128:(j + 1) * 128], in0=r4(tmpm),
                                                       in1=uT[:, :, j * 128:(j + 1) * 128], op=ALU.mult),
                     reads=[B_tmpm, B_uT], writes=[B_mixT])
        compress_st(T0 * 8, ntl * 8)

    def attn_branch(T, g, j, kind, accB):
        po, pob = ps_o()
        po3 = po[:, 0:260].rearrange("p (h c) -> p h c", h=4)
        qTg = qTz[g][:, :, j * 128:(j + 1) * 128]
        if kind == "slc":
            kts = list(range(0, T + 1))
        elif kind == "win":
            kts = list(range(T - 4, T + 1))
        else:
            kts = [0] if T < 16 else [0, 1]
        first = True
        for ki, kt in enumerate(kts):
            ps, psb = ps_g()
            ex = []
            if kind == "slc":
                lk, Bk = kslcT[:, kt * 128:(kt + 1) * 128], B_kslc[kt]
                ex.append((Eb[:, kt * 128:(kt + 1) * 128], negselT4[g].rearrange("p a b -> p (a b)"), [B_negsel[g]]))
                if kt == T:
                    ex.append((ident_b, mask_diag4.rearrange("p a b -> p (a b)"), []))
                va, Bv = Vslc[:, kt, g, :], B_vslc[kt]
            elif kind == "win":
                lk, Bk = kwinT[:, kt % 8, :], B_kwin[kt % 8]
                if kt == T:
                    ex.append((ident_b, mask_diag4.rearrange("p a b -> p (a b)"), []))
                if kt == T - 4:
                    ex.append((ident_b, mask_band4.rearrange("p a b -> p (a b)"), []))
                if kt < 16:
                    ex.append((ones_b[0:1, 0:128], histneg4, []))
                va, Bv = Vwin[:, kt % 8, g, :], B_vwin[kt % 8]
            else:
                lk, Bk = kcT[:, kt * 128:(kt + 1) * 128], B_kc
                if (T < 16 and kt == 0) or (T >= 16 and kt == 1):
                    ex.append((ident_b, cmask[T % 2].rearrange("p a b -> p (a b)"), [B_cmask[T % 2]]))
                va, Bv = vcaug[:, kt, g, :], B_vcaug[kt]
            K.op("pe", lambda e: e.matmul(ps, lhsT=lk, rhs=qTg, start=True, stop=(len(ex) == 0)),
                 reads=[Bk, B_qT], writes=[psb], sig=(len(ex) == 0))
            for xi, (l2, r2, Bs) in enumerate(ex):
                K.op("pe", lambda e: e.matmul(ps, lhsT=l2, rhs=r2, start=False, stop=(xi == len(ex) - 1)),
                     reads=[B_win, B_const] + Bs, writes=[psb], sig=(xi == len(ex) - 1))
            ei = etr[0]
            etr[0] = (ei + 1) % 3
            if kind == "cmp":
                K.op("act", lambda e: e.activation(out=ET[ei], in_=ps, func=AF.Exp, scale=0.125,
                                                   bias=cmpbT[:, kt:kt + 1]), reads=[psb, B_win], writes=[B_ET[ei]])
            else:
                K.op("act", lambda e: e.activation(out=ET[ei], in_=ps, func=AF.Exp, scale=0.125), reads=[psb],
                     writes=[B_ET[ei]])
            for h in range(4):
                K.op("pe", lambda e: e.matmul(po3[:, h, :], lhsT=ET[ei][:, h * 128:(h + 1) * 128], rhs=va,
                                              start=(first and h == 0), stop=(ki == len(kts) - 1 and h == 3),
                                              skip_group_check=True), reads=[B_ET[ei], Bv], writes=[pob], sig=(h == 3))
            first = False
        bi = {"cmp": 0, "slc": 1, "win": 2}[kind]
        K.op("act", lambda e: e.copy(out=o_brs[g][:, bi], in_=po3), reads=[pob], writes=[B_obrs[g][bi]])

    def select_blocks(T, g, j):
        L = 8 * T + 8
        S3 = psS.rearrange("p (h c) -> p h c", h=4)
        for bk in range(2):
            Bb = B_S[bk]
            for hh in range(2):
                h = bk * 2 + hh
                K.op("pe", lambda e: e.matmul(S3[:, h, 0:L], lhsT=qTz[g][:, h, j * 128:(j + 1) * 128],
                                              rhs=kcT[:, 0:L], start=(hh == 0), stop=False,
                                              skip_group_check=True), reads=[B_qT, B_kc], writes=[Bb])
            K.op("pe", lambda e: e.matmul(S3[:, bk * 2:bk * 2 + 2, 0:L], lhsT=ones_b[0:1, 0:128],
                                          rhs=cmpb_row2[:, :, 0:L], start=False, stop=False, skip_group_check=True),
                 reads=[B_win, B_const], writes=[Bb])
            K.op("pe", lambda e: e.matmul(S3[:, bk * 2:bk * 2 + 2, L - 8:L], lhsT=ident_b, rhs=vis8x2, start=False,
                                          stop=True, skip_group_check=True), reads=[B_win, B_const], writes=[Bb])
        ck(2)
        for h in range(4):
            K.op("act", lambda e: e.activation(out=Eh[h % 2][:, 0:L], in_=S3[:, h, 0:L], func=AF.Exp, scale=0.125,
                                               accum_out=den4[:, h:h + 1]), reads=[B_S[h // 2]],
                 writes=[B_Eh[h % 2], B_sel])
            K.op("dve", lambda e: e.tensor_scalar(out=den4[:, 4 + h:5 + h], in0=den4[:, h:h + 1], scalar1=1e-30,
                                                  scalar2=None, op0=ALU.max), reads=[B_sel], writes=[B_sel])
            K.op("dve", lambda e: e.reciprocal(out=den4[:, 4 + h:5 + h], in_=den4[:, 4 + h:5 + h]), reads=[B_sel],
                 writes=[B_sel])
            if h == 0:
                K.op("dve", lambda e: e.tensor_scalar(out=impbuf[:, 0:L], in0=Eh[0][:, 0:L], scalar1=den4[:, 4:5],
                                                      scalar2=None, op0=ALU.mult), reads=[B_Eh[0], B_sel],
                     writes=[B_imp])
            else:
                K.op("dve", lambda e: e.scalar_tensor_tensor(out=impbuf[:, 0:L], in0=Eh[h % 2][:, 0:L],
                                                             scalar=den4[:, 4 + h:5 + h], in1=impbuf[:, 0:L],
                                                             op0=ALU.mult, op1=ALU.add),
                     reads=[B_Eh[h % 2], B_sel, B_imp], writes=[B_imp])
        ck(3)
        K.op("dve", lambda e: e.tensor_tensor(out=sc, in0=impbuf[:, 0:253:4], in1=impbuf[:, 1:254:4], op=ALU.add),
             reads=[B_imp], writes=[B_sel])
        for m in (2, 3, 4):
            K.op("dve", lambda e: e.tensor_tensor(out=sc, in0=sc, in1=impbuf[:, m:m + 253:4], op=ALU.add),
                 reads=[B_imp, B_sel], writes=[B_sel])
        K.op("dve", lambda e: e.tensor_tensor(out=sc, in0=sc, in1=selbB, op=ALU.add), reads=[B_sel, B_win],
             writes=[B_sel])
        K.op("dve", lambda e: e.tensor_tensor(out=sc, in0=sc, in1=Mst[:, 64 - 2 * T:128 - 2 * T], op=ALU.add),
             reads=[B_sel, B_win], writes=[B_sel])
        ck(4)
        K.op("dve", lambda e: e.max(out=mx8[:, 0:8], in_=sc), reads=[B_sel], writes=[B_sel])
        K.op("dve", lambda e: e.match_replace(out=sc2, in_to_replace=mx8[:, 0:8], in_values=sc, imm_value=-3e38),
             reads=[B_sel], writes=[B_sel])
        K.op("dve", lambda e: e.max(out=mx8[:, 8:16], in_=sc2), reads=[B_sel], writes=[B_sel])
        K.op("dve", lambda e: e.tensor_scalar(out=mx8[:, 15:16], in0=mx8[:, 15:16], scalar1=-1e29, scalar2=None,
                                              op0=ALU.max), reads=[B_sel], writes=[B_sel])
        K.op("dve", lambda e: e.tensor_scalar(out=sc2, in0=sc, scalar1=mx8[:, 15:16], scalar2=None, op0=ALU.is_ge),
             reads=[B_sel], writes=[B_sel])
        K.op("dve", lambda e: e.tensor_scalar(out=selt, in0=sc2, scalar1=-1.0, scalar2=-NEGM, op0=ALU.add,
                                              op1=ALU.mult), reads=[B_sel], writes=[B_sel])
        ck(5)
        pt, pb = ps_g()
        ptb = pt.bitcast(BF16)
        K.op("pe", lambda e: e.transpose(out=ptb[0:64, 0:128], in_=selt, identity=ident_b), reads=[B_sel, B_const],
             writes=[pb])
        K.op("dve", lambda e: e.tensor_copy(out=negselT4[g], in_=ptb[0:64, 0:128].unsqueeze(1).to_broadcast(
            [64, 4, 128])), reads=[pb], writes=[B_negsel[g]])

    def attention_tile(T, j):
        ch = 0 if T < 16 else 1
        base = 2048 * ch + 15 - 128 * T
        K.op("pool", lambda e: e.affine_select(out=cmask[T % 2], in_=r4(negs_b), pattern=[[0, 4], [-1, 128]],
                                               compare_op=ALU.is_gt, fill=0.0, base=base, channel_multiplier=16),
             reads=[B_const], writes=[B_cmask[T % 2]])
        for g in range(2):
            select_blocks(T, g, j)
        for kind in ("win", "cmp", "slc"):
            for g in range(2):
                attn_branch(T, g, j, kind, None)
        for g in range(2):
            o_br, B_obr = o_brs[g], B_obrs[g]
            K.op("dve", lambda e: e.tensor_scalar(out=coef, in0=o_br[:, :, :, 64], scalar1=1e-30, scalar2=None,
                                                  op0=ALU.max), reads=B_obr, writes=[B_coef])
            ck(10)
            K.op("dve", lambda e: e.reciprocal(out=coef, in_=coef), reads=[B_coef], writes=[B_coef])
            ck(11)
            K.op("dve", lambda e: e.tensor_tensor(out=coef, in0=coef,
                                                  in1=sg[:, j, g * 12:(g + 1) * 12].rearrange("p (h b) -> p b h", b=3),
                                                  op=ALU.mult), reads=[B_coef, B_sg[j]], writes=[B_coef])
            ck(12)
            K.op("dve", lambda e: e.tensor_tensor(out=otmp, in0=o_br[:, :, :, 0:64],
                                                  in1=coef.unsqueeze(3).to_broadcast([128, 3, 4, 64]), op=ALU.mult),
                 reads=B_obr + [B_coef], writes=[B_otmp])
            ck(13)
            K.op("pool", lambda e: e.tensor_tensor(out=otmp[:, 0], in0=otmp[:, 0], in1=otmp[:, 1], op=ALU.add),
                 reads=[B_otmp], writes=[B_otmp])
            K.op("pool", lambda e: e.tensor_tensor(out=o_nsa[:, g * 256:(g + 1) * 256].rearrange(
                "p (h d) -> p h d", h=4), in0=otmp[:, 0], in1=otmp[:, 2], op=ALU.add), reads=[B_otmp],
                writes=[B_onsa])
        pt, pb = ps_g()
        ptb = pt.bitcast(BF16).rearrange("p (k t) -> p k t", k=8)
        for cc in range(4):
            K.op("pe", lambda e: e.transpose(out=ptb[:, cc, :], in_=o_nsa[:, cc * 128:(cc + 1) * 128],
                                             identity=ident_b), reads=[B_onsa, B_const], writes=[pb])
        K.op("dve", lambda e: e.tensor_copy(out=o_nsaT[:, :, j * 128:(j + 1) * 128], in_=ptb[:, 0:4, :]), reads=[pb],
             writes=[B_onsaT])

    hist_sts = [(0, 4), (4, 4), (8, 4), (12, 3)]
    own_sts = [(15, 4), (19, 4), (23, 4), (27, 4), (31, 1)]
    for (T0, ntl) in hist_sts:
        front_st(T0, ntl, False)
        if stage <= 2:
            K.finish()
            return nc
    vc_transpose(0)
    for si, (T0, ntl) in enumerate(own_sts):
        front_st(T0, ntl, True)
        vc_transpose(1)
        if si == 0:
            vc_transpose(0)
        if stage <= 3:
            K.finish()
            return nc
        for j in range(ntl):
            attention_tile(T0 + j, j)
            if stage <= 4:
                K.finish()
                return nc
        if stage <= 5:
            K.finish()
            return nc
        nt = ntl * 128
        K.dma("sp", scr_om[si, :, 0:4, 0:nt], o_nsaT[:, :, 0:nt], reads=[B_onsaT], writes=[B_scr_om[si]])
        K.dma("sp", scr_om[si, :, 4:8, 0:nt], mixT[:, :, 0:nt], reads=[B_mixT], writes=[B_scr_om[si]])

    K.barrier()
    e1p.close()
    if do_sample:
        s1 = ExitStack()
        xs_sb = sbt(s1, "xs_sb", [4, D], F32)
        B_xs = K.buf()
        K.dma("sp", xs_sb, xs, writes=[B_xs])
        sqs = sbt(s1, "sqs", [4, D], BF16)
        sts = sbt(s1, "sts", [4, 8], F32)
        xns = sbt(s1, "xns", [4, D], BF16)
        B_s = K.buf()
        K.op("act", lambda e: e.activation(out=sqs, in_=xs_sb, func=AF.Square, accum_out=sts[:, 0:1]), reads=[B_xs],
             writes=[B_s])
        K.op("act", lambda e: e.activation(out=sts[:, 1:2], in_=sts[:, 0:1], func=AF.Ln, scale=1.0 / D, bias=EPS),
             reads=[B_s], writes=[B_s])
        K.op("act", lambda e: e.activation(out=sts[:, 2:3], in_=sts[:, 1:2], func=AF.Exp, scale=-0.5), reads=[B_s],
             writes=[B_s])
        K.op("dve", lambda e: e.tensor_scalar(out=xns, in0=xs_sb, scalar1=sts[:, 2:3], scalar2=None, op0=ALU.mult),
             reads=[B_xs, B_s], writes=[B_s])
        pt, pb = ps_g()
        ptb = pt.bitcast(BF16)[:, 0:32].rearrange("p (k t) -> p k t", k=8)
        for k in range(8):
            K.op("pe", lambda e: e.transpose(out=ptb[:, k, :], in_=xns[:, k * 128:(k + 1) * 128],
                                             identity=ident_b[0:4, 0:4]), reads=[B_s, B_const], writes=[pb])
        K.op("dve", lambda e: e.tensor_tensor(out=hT_s, in0=ptb, in1=gcol_mix.unsqueeze(2).to_broadcast([128, 8, 4]),
                                              op=ALU.mult), reads=[pb, B_const], writes=[B_hTs])
        zs = sbt(s1, "zs", [4, 1816], F32)
        B_zs = K.buf()
        for c0 in range(512, 2328, 512):
            c1 = min(c0 + 512, 2328)
            pt, pb = ps_g()
            for k in range(8):
                K.op("pe", lambda e: e.matmul(pt[0:4, 0:c1 - c0], lhsT=hT_s[:, k, :], rhs=w_in_sb[:, k, c0:c1],
                                              start=(k == 0), stop=(k == 7)), reads=[B_win, B_hTs], writes=[pb], sig=(k == 7))
            K.op("dve", lambda e: e.tensor_copy(out=zs[:, c0 - 512:c1 - 512], in_=pt[0:4, 0:c1 - c0]), reads=[pb],
                 writes=[B_zs])
        K.dma("sp", kvs_o, zs[:, 0:768], reads=[B_zs])
        sg_s = sbt(s1, "sg_s", [4, 24], F32)
        u_s = sbt(s1, "u_s", [4, 512], F32)
        v_s = sbt(s1, "v_s", [4, 512], F32)
        vn_s = sbt(s1, "vn_s", [4, 512], F32)
        mixs = sbt(s1, "mixs", [4, 512], F32)
        mixsb = sbt(s1, "mixsb", [4, 512], BF16)
        B_sm = K.buf()
        K.op("act", lambda e: e.activation(out=sg_s, in_=zs[:, 768:792], func=AF.Sigmoid), reads=[B_zs], writes=[B_sm])
        K.op("act", lambda e: e.activation(out=u_s, in_=zs[:, 792:1304], func=AF.Gelu_apprx_tanh), reads=[B_zs],
             writes=[B_sm])
        K.op("act", lambda e: e.activation(out=v_s, in_=zs[:, 1304:1816], func=AF.Gelu_apprx_tanh), reads=[B_zs],
             writes=[B_sm])
        K.op("act", lambda e: e.activation(out=sqs[:, 0:512], in_=v_s, func=AF.Square, accum_out=sts[:, 4:5]),
             reads=[B_sm], writes=[B_s])
        K.op("act", lambda e: e.activation(out=sts[:, 5:6], in_=sts[:, 4:5], func=AF.Ln, scale=1.0 / 512, bias=EPS),
             reads=[B_s], writes=[B_s])
        K.op("act", lambda e: e.activation(out=sts[:, 6:7], in_=sts[:, 5:6], func=AF.Exp, scale=-0.5), reads=[B_s],
             writes=[B_s])
        K.op("dve", lambda e: e.scalar_tensor_tensor(out=vn_s, in0=v_s, scalar=sts[:, 6:7], in1=gnB[0:4, :],
                                                     op0=ALU.mult, op1=ALU.mult), reads=[B_sm, B_s, B_win],
             writes=[B_sm])
        K.dma("sp", vns_o, vn_s, reads=[B_sm])
        ws00 = sbt(s1, "ws00", [4, 8], F32)
        with nc.allow_non_contiguous_dma(reason="4 scalars"):
            K.dma("sp", ws00[:, 0:4], gmlp_ws[:, 0, 0:1].rearrange("g o -> o g").partition_broadcast(4), writes=[B_sm])
            K.dma("sp", ws00[:, 4:8], gmlp_bs[:, 0:512:128].partition_broadcast(4), writes=[B_sm])
        K.op("dve", lambda e: e.tensor_tensor(out=r4(mixs), in0=r4(vn_s), in1=ws00[:, 0:4].unsqueeze(2).to_broadcast(
            [4, 4, 128]), op=ALU.mult), reads=[B_sm], writes=[B_sm])
        K.op("dve", lambda e: e.tensor_tensor(out=r4(mixs), in0=r4(mixs), in1=ws00[:, 4:8].unsqueeze(2).to_broadcast(
            [4, 4, 128]), op=ALU.add), reads=[B_sm], writes=[B_sm])
        K.op("dve", lambda e: e.tensor_tensor(out=mixsb, in0=mixs, in1=u_s, op=ALU.mult), reads=[B_sm], writes=[B_sm])
        pt, pb = ps_g()
        ptb = pt.bitcast(BF16)[:, 0:16].rearrange("p (k t) -> p k t", k=4)
        for k in range(4):
            K.op("pe", lambda e: e.transpose(out=ptb[:, k, :], in_=mixsb[:, k * 128:(k + 1) * 128],
                                             identity=ident_b[0:4, 0:4]), reads=[B_sm, B_const], writes=[pb])
        K.op("dve", lambda e: e.tensor_copy(out=omT_s[:, 4:8, :], in_=ptb), reads=[pb], writes=[B_omTs])
        qTs = [sbt(s1, "qTs%d" % i, [128, 4, 4], BF16) for i in range(2)]
        B_qTs = K.buf()
        for i in range(2):
            K.op("pool", lambda e: e.memset(qTs[i], 0.0), writes=[B_qTs])
        for r in range(4):
            pt, pb = ps_g()
            for k in range(8):
                K.op("pe", lambda e: e.matmul(pt[:, 0:4], lhsT=w_in_sb[:, k, r * 128:(r + 1) * 128], rhs=hT_s[:, k, :],
                                              start=(k == 0), stop=(k == 7)), reads=[B_win, B_hTs], writes=[pb], sig=(k == 7))
            K.op("dve", lambda e: e.tensor_copy(out=qTs[0][0:64, r, :], in_=pt[0:64, 0:4]), reads=[pb], writes=[B_qTs])
            K.op("dve", lambda e: e.tensor_copy(out=qTs[1][64:128, r, :], in_=pt[64:128, 0:4]), reads=[pb],
                 writes=[B_qTs])
        kTn = sbt(s1, "kTn", [128, 2, 4], BF16)
        B_kTn = K.buf()
        for bi_, c0 in enumerate((768, 1024)):
            pt, pb = ps_g()
            for k in range(8):
                K.op("pe", lambda e: e.matmul(pt[:, 0:4], lhsT=w_in_sb[:, k, c0:c0 + 128], rhs=hT_s[:, k, :],
                                              start=(k == 0), stop=(k == 7)), reads=[B_win, B_hTs], writes=[pb], sig=(k == 7))
            K.op("dve", lambda e: e.tensor_copy(out=kTn[:, bi_, :], in_=pt[:, 0:4]), reads=[pb], writes=[B_kTn])
        Vn = sbt(s1, "Vn", [4, 2, 2, 65], BF16)
        B_Vn = K.buf()
        K.op("pool", lambda e: e.memset(Vn, 1.0), writes=[B_Vn])
        for bi_, c0 in enumerate((384, 640)):
            K.op("dve", lambda e: e.tensor_copy(out=Vn[:, bi_, :, 0:64],
                                                in_=zs[:, c0:c0 + 128].rearrange("p (g d) -> p g d", g=2)),
                 reads=[B_zs], writes=[B_Vn])
        dmask = sbt(s1, "dmask", [4, 8, 4], BF16)
        K.op("pool", lambda e: e.affine_select(out=dmask, in_=zeros_b[0:4, 0:32].rearrange("p (a b) -> p a b", a=8),
                                               pattern=[[0, 8], [1, 4]], compare_op=ALU.is_equal, fill=NEGM, base=0,
                                               channel_multiplier=-1), reads=[B_const], writes=[B_Vn])
        ptx_sb = sbt(s1, "ptx_sb", [128, 32], I32)
        ptf = sbt(s1, "ptf", [128, 32], F32)
        idx_i = sbt(s1, "idx_i", [128, 32], I32)
        pmod_sb = sbt(s1, "pmod_sb", [128, 1], F32)
        B_idx = K.buf()
        K.dma("sp", ptx_sb, ptx, writes=[B_idx])
        K.dma("sp", pmod_sb, pmod, writes=[B_idx])
        K.op("dve", lambda e: e.tensor_copy(out=ptf, in_=ptx_sb), reads=[B_idx], writes=[B_idx])
        K.op("dve", lambda e: e.tensor_scalar(out=ptf, in0=ptf, scalar1=8.0, scalar2=pmod_sb[:, 0:1], op0=ALU.mult,
                                              op1=ALU.add), reads=[B_idx], writes=[B_idx])
        K.op("dve", lambda e: e.tensor_copy(out=idx_i, in_=ptf), reads=[B_idx], writes=[B_idx])

        pg = [sbt(s1, "pg%d" % i, [128, 16, 256], F32) for i in range(2)]
        B_pg = [K.buf(), K.buf()]
        kTs = sbt(s1, "kTs", [128, 2, 16, 129], BF16)
        B_kTs = K.buf()
        kcs = sbt(s1, "kcs", [128, 2, 1024], BF16)
        B_kcs = K.buf()
        ghs = sbt(s1, "ghs", [128, 2, 128], BF16)
        B_ghs = K.buf()
        vcas = sbt(s1, "vcas", [128, 8, 2, 65], BF16)
        B_vcas = K.buf()
        K.op("pool", lambda e: e.memset(vcas, 1.0), writes=[B_vcas])
        KTt = [sbt(s1, "KTt%d" % i, [128, 4, 128], BF16) for i in range(2)]
        B_KTt = [K.buf(), K.buf()]
        Vas = [sbt(s1, "Vas%d" % i, [128, 16, 2, 65], BF16) for i in range(2)]
        B_Vas = [K.buf(), K.buf()]
        for i in range(2):
            K.op("pool", lambda e: e.memset(Vas[i], 1.0), writes=[B_Vas[i]])
        ETs = [sbt(s1, "ETs%d" % i, [128, 16, 8, 4], BF16) for i in range(2)]
        B_ETs = [K.buf(), K.buf()]
        etc = [0]
        Oacc = sbt(s1, "Oacc", [4, 3, 8, 65], F32)
        B_Oacc = K.buf()
        K.op("pool", lambda e: e.memset(Oacc, 0.0), writes=[B_Oacc])
        Es = sbt(s1, "Es", [4, 1024], F32)
        imps = sbt(s1, "imps", [1, 1032], F32)
        scs = sbt(s1, "scs", [1, 264], F32)
        scs2 = sbt(s1, "scs2", [1, 264], F32)
        mxs = sbt(s1, "mxs", [1, 16], F32)
        negx = sbt(s1, "negx", [1, 256, 4], BF16)
        mcol = sbt(s1, "mcol", [128, 2, 8], F32)
        dens = sbt(s1, "dens", [4, 4], F32)
        B_sel = K.buf()
        B_mcol = K.buf()
        K.op("pool", lambda e: e.memset(imps, 0.0), writes=[B_sel])
        s0col = sbt(s1, "s0col", [128, 1], F32)
        K.op("pool", lambda e: e.memset(s0col, 0.0), writes=[B_sel])
        K.op("pool", lambda e: e.memset(s0col[0:1, :], NEGM), writes=[B_sel])
        s0row = sbt(s1, "s0row", [1, 512], BF16)
        K.op("pool", lambda e: e.memset(s0row, 0.0), writes=[B_sel])
        K.op("pool", lambda e: e.memset(s0row[:, 0:1], NEGM), writes=[B_sel])
        ones4f = sbt(s1, "ones4f", [4, 1], F32)
        K.op("pool", lambda e: e.memset(ones4f, 1.0), writes=[B_sel])
        bon = sbt(s1, "bon", [1, 264], F32)
        K.op("pool", lambda e: e.memset(bon, 0.0), writes=[B_sel])
        K.op("pool", lambda e: e.memset(bon[:, 0:1], 1e6), writes=[B_sel])
        K.op("pool", lambda e: e.memset(bon[:, 255:257], 1e6), writes=[B_sel])
        K.op("pool", lambda e: e.memset(bon[:, 257:264], -1e30), writes=[B_sel])

        def acc_banks():
            return allb[4], allb[5]

        def evac_acc(bi):
            (pA, pAb), (pB, pBb) = acc_banks()
            for g, (pp, ppb) in enumerate(((pA, pAb), (pB, pBb))):
                K.op("dve", lambda e: e.tensor_tensor(out=Oacc[:, bi, g * 4:(g + 1) * 4, :],
                                                      in0=Oacc[:, bi, g * 4:(g + 1) * 4, :],
                                                      in1=pp[0:4, 0:260].rearrange("p (h c) -> p h c", h=4),
                                                      op=ALU.add), reads=[ppb, B_Oacc], writes=[B_Oacc])

        def pv_tile(ETt, Bet, s_idx, Vt, Bv, first, kparts=128):
            (pA, pAb), (pB, pBb) = acc_banks()
            for g, (pp, ppb) in enumerate(((pA, pAb), (pB, pBb))):
                p3 = pp[0:4, 0:260].rearrange("p (h c) -> p h c", h=4)
                for r in range(4):
                    K.op("pe", lambda e: e.matmul(p3[:, r, :], lhsT=ETt[0:kparts, s_idx, g * 4 + r, :],
                                                  rhs=Vt[0:kparts, g, :], start=(first and r == 0), stop=False,
                                                  skip_group_check=True), reads=[Bet, Bv], writes=[ppb], sig=(r == 3))

        def gather(pgb, Bpg, cache, col):
            K._wait("pool", K._deps([B_idx], [Bpg]))
            ring = K.dring["pool"]
            si = ring[K.dpos["pool"] % len(ring)]
            K.dpos["pool"] += 1
            if K.dtgt[si] > 0:
                K._wait("pool", {("d", si): K.dtgt[si]})
            ins = nc.gpsimd.indirect_dma_start(out=pgb.rearrange("p s c -> p (s c)"), out_offset=None, in_=cache,
                                               in_offset=bass.IndirectOffsetOnAxis(ap=idx_i[:, col:col + 1], axis=0))
            K.dtgt[si] += 16
            ins.then_inc(K.dsem[si], 16)
            K._mark(("d", si), K.dtgt[si], [B_idx], [Bpg])

        gi = [0]
        for b in range(4):
            for i in range(2):
                K.op("pool", lambda e: e.memset(ETs[i], 0.0), writes=[B_ETs[i]])
            K.op("pool", lambda e: e.memset(kTs, 0.0), writes=[B_kTs])
            for grp in range(8):
                pgb, Bpg = pg[gi[0] % 2], B_pg[gi[0] % 2]
                gi[0] += 1
                gather(pgb, Bpg, cache_cmp, b * 8 + grp)
                for kv in range(2):
                    for s4 in range(4):
                        pt, pb = ps_g()
                        for q_ in range(4):
                            s = s4 * 4 + q_
                            K.op("pe", lambda e: e.transpose(out=pt[:, q_ * 128:(q_ + 1) * 128],
                                                             in_=pgb[:, s, kv * 128:(kv + 1) * 128], identity=ident_f),
                                 reads=[Bpg, B_const], writes=[pb])
                        eng = "act" if (s4 % 2 == 0) else "dve"
                        if eng == "act":
                            K.op("act", lambda e: e.copy(out=kTs[:, kv, s4 * 4:(s4 + 1) * 4, 1:129],
                                                         in_=pt.rearrange("p (a b) -> p a b", a=4)), reads=[pb],
                                 writes=[B_kTs])
                        else:
                            K.op("dve", lambda e: e.tensor_copy(out=kTs[:, kv, s4 * 4:(s4 + 1) * 4, 1:129],
                                                                in_=pt.rearrange("p (a b) -> p a b", a=4)),
                                 reads=[pb], writes=[B_kTs])
                for kv in range(2):
                    for g in range(2):
                        pt, pb = ps_g()
                        i = 0
                        for r in range(2):
                            for s in range(16):
                                K.op("pe", lambda e: e.matmul(pt[:, 0:128],
                                                              lhsT=w1sb[g * 64:(g + 1) * 64, kv, r * 16 + s, :],
                                                              rhs=kTs[g * 64:(g + 1) * 64, kv, s, r:r + 128],
                                                              start=(i == 0), stop=(i == 31)),
                                     reads=[B_win, B_kTs], writes=[pb], sig=(i == 31))
                                i += 1
                        K.op("act", lambda e: e.activation(out=ghs[:, g, :], in_=pt[:, 0:128],
                                                           func=AF.Gelu_apprx_tanh, bias=b1eff[:, kv:kv + 1]),
                             reads=[pb, B_win], writes=[B_ghs])
                    pt, pb = ps_g()
                    for g in range(2):
                        K.op("pe", lambda e: e.matmul(pt[:, 0:128], lhsT=w2pad[:, kv, g, :], rhs=ghs[:, g, :],
                                                      start=(g == 0), stop=(g == 1)), reads=[B_win, B_ghs],
                             writes=[pb], sig=(g == 1))
                    K.op("dve", lambda e: e.tensor_copy(out=kcs[:, kv, grp * 128:(grp + 1) * 128], in_=pt[:, 0:128]),
                         reads=[pb], writes=[B_kcs])
                    K.op("pool", lambda e: e.tensor_copy(out=kTs[:, kv, :, 0:1], in_=kTs[:, kv, :, 128:129]),
                         reads=[B_kTs], writes=[B_kTs])
            for ch in range(8):
                pt, pb = ps_g()
                ptb = pt.bitcast(BF16)
                K.op("pe", lambda e: e.transpose(out=ptb[:, 0:128], in_=kcs[:, 1, ch * 128:(ch + 1) * 128],
                                                 identity=ident_b), reads=[B_kcs, B_const], writes=[pb])
                K.op("dve", lambda e: e.tensor_copy(out=vcas[:, ch, :, 0:64],
                                                    in_=ptb[:, 0:128].rearrange("p (g d) -> p g d", g=2)),
                     reads=[pb], writes=[B_vcas])
            ei = etc[0] % 2
            etc[0] += 1
            pt, pb = ps_g()
            p3 = pt[:, 0:64].rearrange("p (c h) -> p c h", c=8)
            for ch in range(8):
                for g in range(2):
                    K.op("pe", lambda e: e.matmul(p3[:, ch, g * 4:(g + 1) * 4], lhsT=kcs[:, 0, ch * 128:(ch + 1) * 128],
                                                  rhs=qTs[g][:, :, b], start=(ch == 0 and g == 0), stop=False,
                                                  skip_group_check=True), reads=[B_kcs, B_qTs], writes=[pb],
                         sig=(ch == 7 and g == 1))
            K.op("act", lambda e: e.activation(out=ETs[ei][:, 0:1, :, b], in_=p3[:, 0:1, :], func=AF.Exp, scale=0.125,
                                               bias=s0col[:, 0:1]), reads=[pb, B_sel], writes=[B_ETs[ei]])
            K.op("act", lambda e: e.activation(out=ETs[ei][:, 1:8, :, b], in_=p3[:, 1:8, :], func=AF.Exp, scale=0.125),
                 reads=[pb], writes=[B_ETs[ei]])
            for ch in range(8):
                pv_tile(ETs[ei], B_ETs[ei], ch, vcas[:, ch], B_vcas, first=(ch == 0))
            evac_acc(0)
            for g in range(2):
                for hf2 in range(2):
                    bk, bkb = allb[6 + hf2]
                    K.op("pe", lambda e: e.matmul(bk[0:4, :], lhsT=qTs[g][:, :, b],
                                                  rhs=kcs[:, 0, hf2 * 512:(hf2 + 1) * 512], start=True,
                                                  stop=(hf2 == 1)), reads=[B_qTs, B_kcs], writes=[bkb], sig=(hf2 == 1))
                    if hf2 == 0:
                        K.op("pe", lambda e: e.matmul(bk[0:4, :], lhsT=ones_b[0:1, 0:4], rhs=s0row, start=False,
                                                      stop=True), reads=[B_sel, B_const], writes=[bkb])
                K.op("act", lambda e: e.activation(out=Es, in_=psS[0:4, :], func=AF.Exp, scale=0.125,
                                                   accum_out=dens[:, 0:1]), reads=B_S, writes=[B_sel])
                K.op("dve", lambda e: e.tensor_scalar(out=dens[:, 1:2], in0=dens[:, 0:1], scalar1=1e-30, scalar2=None,
                                                      op0=ALU.max), reads=[B_sel], writes=[B_sel])
                K.op("dve", lambda e: e.reciprocal(out=dens[:, 1:2], in_=dens[:, 1:2]), reads=[B_sel], writes=[B_sel])
                K.op("dve", lambda e: e.tensor_scalar(out=Es, in0=Es, scalar1=dens[:, 1:2], scalar2=None,
                                                      op0=ALU.mult), reads=[B_sel], writes=[B_sel])
                for hf2 in range(2):
                    bk, bkb = allb[hf2]
                    K.op("pe", lambda e: e.matmul(bk[0:1, :], lhsT=ones4f, rhs=Es[:, hf2 * 512:(hf2 + 1) * 512],
                                                  start=True, stop=True), reads=[B_sel], writes=[bkb])
                    K.op("dve", lambda e: e.tensor_copy(out=imps[:, hf2 * 512:(hf2 + 1) * 512], in_=bk[0:1, :]),
                         reads=[bkb], writes=[B_sel])
                K.op("dve", lambda e: e.tensor_tensor(out=scs[:, 0:257], in0=imps[:, 0:1025:4], in1=imps[:, 1:1026:4],
                                                      op=ALU.add), reads=[B_sel], writes=[B_sel])
                for m in (2, 3, 4):
                    K.op("dve", lambda e: e.tensor_tensor(out=scs[:, 0:257], in0=scs[:, 0:257],
                                                          in1=imps[:, m:m + 1025:4], op=ALU.add), reads=[B_sel],
                         writes=[B_sel])
                K.op("dve", lambda e: e.tensor_tensor(out=scs[:, 0:257], in0=scs[:, 0:257], in1=bon[:, 0:257],
                                                      op=ALU.add), reads=[B_sel], writes=[B_sel])
                K.op("dve", lambda e: e.tensor_copy(out=scs[:, 257:264], in_=bon[:, 257:264]), reads=[B_sel],
                     writes=[B_sel])
                K.op("dve", lambda e: e.max(out=mxs[:, 0:8], in_=scs), reads=[B_sel], writes=[B_sel])
                K.op("dve", lambda e: e.match_replace(out=scs2, in_to_replace=mxs[:, 0:8], in_values=scs,
                                                      imm_value=-3e38), reads=[B_sel], writes=[B_sel])
                K.op("dve", lambda e: e.max(out=mxs[:, 8:16], in_=scs2), reads=[B_sel], writes=[B_sel])
                K.op("dve", lambda e: e.tensor_scalar(out=scs2[:, 0:256], in0=scs[:, 0:256], scalar1=mxs[:, 15:16],
                                                      scalar2=None, op0=ALU.is_ge), reads=[B_sel], writes=[B_sel])
                K.op("dve", lambda e: e.tensor_scalar(out=scs2[:, 0:256], in0=scs2[:, 0:256], scalar1=-1.0,
                                                      scalar2=-NEGM, op0=ALU.add, op1=ALU.mult), reads=[B_sel],
                     writes=[B_sel])
                K.op("dve", lambda e: e.tensor_copy(out=negx, in_=scs2[:, 0:256].unsqueeze(2).to_broadcast(
                    [1, 256, 4])), reads=[B_sel], writes=[B_sel])
                pt, pb = ps_g()
                nx = negx.rearrange("o (a j) r -> o a (j r)", a=8)
                for grp in range(8):
                    K.op("pe", lambda e: e.matmul(pt[:, grp:grp + 1], lhsT=nx[:, grp, :], rhs=ones_b[0:1, 0:1],
                                                  start=(grp == 0), stop=(grp == 7), skip_group_check=True),
                         reads=[B_sel, B_const], writes=[pb], sig=(grp == 7))
                K.op("dve", lambda e: e.tensor_copy(out=mcol[:, g, :], in_=pt[:, 0:8]), reads=[pb], writes=[B_mcol])
            for grp in range(8):
                pgb, Bpg = pg[gi[0] % 2], B_pg[gi[0] % 2]
                gi[0] += 1
                gather(pgb, Bpg, cache_slc, b * 8 + grp)
                vi = grp % 2
                K.op("pool", lambda e: e.tensor_copy(out=Vas[vi][:, :, :, 0:64],
                                                     in_=pgb[:, :, 128:256].rearrange("p s (g d) -> p s g d", g=2)),
                     reads=[Bpg], writes=[B_Vas[vi]])
                ei = etc[0] % 2
                etc[0] += 1
                ps, psb = ps_g()
                ps4 = ps[:, 0:128].rearrange("p (s h) -> p s h", s=16)
                for s4 in range(4):
                    pt, pb = ps_g()
                    for q_ in range(4):
                        s = s4 * 4 + q_
                        K.op("pe", lambda e: e.transpose(out=pt[:, q_ * 128:(q_ + 1) * 128], in_=pgb[:, s, 0:128],
                                                         identity=ident_f), reads=[Bpg, B_const], writes=[pb])
                    kt_, Bkt_ = KTt[s4 % 2], B_KTt[s4 % 2]
                    if s4 % 2 == 0:
                        K.op("act", lambda e: e.copy(out=kt_, in_=pt.rearrange("p (a b) -> p a b", a=4)), reads=[pb],
                             writes=[Bkt_])
                    else:
                        K.op("dve", lambda e: e.tensor_copy(out=kt_, in_=pt.rearrange("p (a b) -> p a b", a=4)),
                             reads=[pb], writes=[Bkt_])
                    for q_ in range(4):
                        s = s4 * 4 + q_
                        for g in range(2):
                            K.op("pe", lambda e: e.matmul(ps4[:, s, g * 4:(g + 1) * 4], lhsT=kt_[:, q_, :],
                                                          rhs=qTs[g][:, :, b], start=(s == 0 and g == 0), stop=False,
                                                          skip_group_check=True), reads=[Bkt_, B_qTs], writes=[psb],
                                 sig=(q_ == 3 and g == 1))
                for g in range(2):
                    K.op("act", lambda e: e.activation(out=ETs[ei][:, :, g * 4:(g + 1) * 4, b],
                                                       in_=ps4[:, :, g * 4:(g + 1) * 4], func=AF.Exp, scale=0.125,
                                                       bias=mcol[:, g, grp:grp + 1]), reads=[psb, B_mcol],
                         writes=[B_ETs[ei]])
                for s in range(16):
                    pv_tile(ETs[ei], B_ETs[ei], s, Vas[vi][:, s], B_Vas[vi], first=(grp == 0 and s == 0))
            def new_token(bi_, first):
                pt, pb = ps_g()
                for g in range(2):
                    K.op("pe", lambda e: e.matmul(pt[0:4, g * 16:(g + 1) * 16], lhsT=kTn[:, bi_, :],
                                                  rhs=qTs[g].rearrange("p r b -> p (r b)"), start=(g == 0), stop=False,
                                                  skip_group_check=True), reads=[B_kTn, B_qTs], writes=[pb])
                K.op("pe", lambda e: e.matmul(pt[0:4, 0:32], lhsT=ident_b[0:4, 0:4],
                                              rhs=dmask.rearrange("p a b -> p (a b)"), start=False, stop=True,
                                              skip_group_check=True), reads=[B_Vn, B_const], writes=[pb])
                en = sbt(s1, "en_%d_%d" % (b, bi_), [4, 8, 4], BF16)
                Ben = K.buf()
                K.op("act", lambda e: e.activation(out=en, in_=pt[0:4, 0:32].rearrange("p (a b) -> p a b", a=8),
                                                   func=AF.Exp, scale=0.125), reads=[pb], writes=[Ben])
                (pA, pAb), (pB, pBb) = acc_banks()
                for g, (pp, ppb) in enumerate(((pA, pAb), (pB, pBb))):
                    p3_ = pp[0:4, 0:260].rearrange("p (h c) -> p h c", h=4)
                    for r in range(4):
                        K.op("pe", lambda e: e.matmul(p3_[:, r, :], lhsT=en[:, g * 4 + r, :], rhs=Vn[:, bi_, g, :],
                                                      start=(first and r == 0), stop=False, skip_group_check=True),
                             reads=[Ben, B_Vn], writes=[ppb])
            if b == 0:
                new_token(0, False)
            evac_acc(1)
            for wt in range(4):
                pgb, Bpg = pg[gi[0] % 2], B_pg[gi[0] % 2]
                gi[0] += 1
                K.dma("sp", pgb[:, 0, :], cache_win[b, wt * 128:(wt + 1) * 128, :], writes=[Bpg])
                vi = wt % 2
                K.op("pool", lambda e: e.tensor_copy(out=Vas[vi][:, 0, :, 0:64],
                                                     in_=pgb[:, 0, 128:256].rearrange("p (g d) -> p g d", g=2)),
                     reads=[Bpg], writes=[B_Vas[vi]])
                pt, pb = ps_g()
                K.op("pe", lambda e: e.transpose(out=pt[:, 0:128], in_=pgb[:, 0, 0:128], identity=ident_f),
                     reads=[Bpg, B_const], writes=[pb])
                kt_, Bkt_ = KTt[wt % 2], B_KTt[wt % 2]
                K.op("dve", lambda e: e.tensor_copy(out=kt_[:, 0, :], in_=pt[:, 0:128]), reads=[pb], writes=[Bkt_])
                ei = etc[0] % 2
                etc[0] += 1
                ps, psb = ps_g()
                for g in range(2):
                    K.op("pe", lambda e: e.matmul(ps[:, g * 4:(g + 1) * 4], lhsT=kt_[:, 0, :], rhs=qTs[g][:, :, b],
                                                  start=(g == 0), stop=(g == 1), skip_group_check=True),
                         reads=[Bkt_, B_qTs], writes=[psb], sig=(g == 1))
                K.op("act", lambda e: e.activation(out=ETs[ei][:, 0, :, b], in_=ps[:, 0:8], func=AF.Exp, scale=0.125),
                     reads=[psb], writes=[B_ETs[ei]])
                pv_tile(ETs[ei], B_ETs[ei], 0, Vas[vi][:, 0], B_Vas[vi], first=(wt == 0))
            if b == 0:
                new_token(1, False)
            evac_acc(2)
        for b in range(4):
            K.dma("sp", wins_o[b, 0:511, :], cache_win[b, 1:512, :])
        K.dma("sp", wins_o[:, 511, :], zs[:, 512:768], reads=[B_zs])
        coefs = sbt(s1, "coefs", [4, 3, 8], F32)
        otm = sbt(s1, "otm", [4, 3, 8, 64], F32)
        onsas = sbt(s1, "onsas", [4, 512], BF16)
        B_cb = K.buf()
        K.op("dve", lambda e: e.tensor_scalar(out=coefs, in0=Oacc[:, :, :, 64], scalar1=1e-30, scalar2=None,
                                              op0=ALU.max), reads=[B_Oacc], writes=[B_cb])
        K.op("dve", lambda e: e.reciprocal(out=coefs, in_=coefs), reads=[B_cb], writes=[B_cb])
        K.op("dve", lambda e: e.tensor_tensor(out=coefs, in0=coefs, in1=sg_s.rearrange("p (h b) -> p b h", b=3),
                                              op=ALU.mult), reads=[B_cb, B_sm], writes=[B_cb])
        K.op("dve", lambda e: e.tensor_tensor(out=otm, in0=Oacc[:, :, :, 0:64],
                                              in1=coefs.unsqueeze(3).to_broadcast([4, 3, 8, 64]), op=ALU.mult),
             reads=[B_Oacc, B_cb], writes=[B_cb])
        K.op("dve", lambda e: e.tensor_tensor(out=otm[:, 0], in0=otm[:, 0], in1=otm[:, 1], op=ALU.add), reads=[B_cb],
             writes=[B_cb])
        K.op("dve", lambda e: e.tensor_tensor(out=onsas.rearrange("p (h d) -> p h d", h=8), in0=otm[:, 0],
                                              in1=otm[:, 2], op=ALU.add), reads=[B_cb], writes=[B_cb])
        pt, pb = ps_g()
        ptb = pt.bitcast(BF16)[:, 0:16].rearrange("p (k t) -> p k t", k=4)
        for k in range(4):
            K.op("pe", lambda e: e.transpose(out=ptb[:, k, :], in_=onsas[:, k * 128:(k + 1) * 128],
                                             identity=ident_b[0:4, 0:4]), reads=[B_cb, B_const], writes=[pb])
        K.op("dve", lambda e: e.tensor_copy(out=omT_s[:, 0:4, :], in_=ptb), reads=[pb], writes=[B_omTs])
        K.barrier()
        s1.close()
    K.barrier()
    e1.close()
    if stage <= 6:
        K.finish()
        return nc

    e2 = ExitStack()
    wg_sb = sbt(e2, "wg_sb", [128, 8, 2048], BF16)
    B_w2 = K.buf("w_pass1b")
    for k in range(8):
        K.dma("pool", wg_sb[:, k, :], w_in[k * 128:(k + 1) * 128, 2328:4376], writes=[B_w2])
    wpa = sbt(e2, "wpa", [128, 4, D], BF16)
    wpb = sbt(e2, "wpb", [128, 4, D], BF16)
    wout = sbt(e2, "wout", [128, 8, D], BF16)
    for k in range(4):
        K.dma("pool", wpa[:, k, :], w_proj_a[k * 128:(k + 1) * 128, :], writes=[B_w2])
        K.dma("pool", wpb[:, k, :], w_proj_b[k * 128:(k + 1) * 128, :], writes=[B_w2])
    for k in range(8):
        K.dma("pool", wout[:, k, :], w_out[k * 128:(k + 1) * 128, :], writes=[B_w2])
    xst = sbt(e2, "xst", [128, 4, D], F32)
    B_xst = [K.buf() for _ in range(4)]
    xn = [sbt(e2, "xn2_%d" % i, [128, D], BF16) for i in range(2)]
    st4 = sbt(e2, "st4b", [128, 8], F32)
    hT = sbt(e2, "hT2", [128, 8, 512], BF16)
    omT = sbt(e2, "omT", [128, 8, 512], BF16)
    B_omT = K.buf()
    g0s = sbt(e2, "g0s", [128, 512], F32)
    g1s = sbt(e2, "g1s", [128, 512], F32)
    B_gs = [K.buf(), K.buf()]
    mergedT = sbt(e2, "mergedT", [128, 8, 512], BF16)
    B_merged = K.buf()
    x1b = [sbt(e2, "x1b%d" % i, [128, D], F32) for i in range(2)]
    B_x1b = [K.buf(), K.buf()]
    B_st = K.buf()
    B_xn = [K.buf(), K.buf()]
    B_hT = [K.buf() for _ in range(4)]

    def p1b_merge(nt, om_ap, B_om_l, h_ap, B_h_l):
        for cc in range(8):
            pa, pab = ps_a()
            pbb_, pbb = ps_a()
            pg0, pg0b = ps_a()
            pg1, pg1b = ps_a()
            for k in range(4):
                K.op("pe", lambda e: e.matmul(pa[:, 0:nt], lhsT=wpa[:, k, cc * 128:(cc + 1) * 128], rhs=om_ap[:, k, 0:nt],
                                              start=(k == 0), stop=(k == 3)), reads=[B_w2] + B_om_l, writes=[pab], sig=(k == 3))
            for k in range(4):
                K.op("pe", lambda e: e.matmul(pbb_[:, 0:nt], lhsT=wpb[:, k, cc * 128:(cc + 1) * 128],
                                              rhs=om_ap[:, 4 + k, 0:nt], start=(k == 0), stop=(k == 3)),
                     reads=[B_w2] + B_om_l, writes=[pbb], sig=(k == 3))
            for k in range(8):
                K.op("pe", lambda e: e.matmul(pg0[:, 0:nt], lhsT=wg_sb[:, k, cc * 128:(cc + 1) * 128],
                                              rhs=h_ap[:, k, 0:nt], start=(k == 0), stop=(k == 7)),
                     reads=[B_w2] + B_h_l, writes=[pg0b], sig=(k == 7))
            for k in range(8):
                K.op("pe", lambda e: e.matmul(pg1[:, 0:nt], lhsT=wg_sb[:, k, 1024 + cc * 128:1152 + cc * 128],
                                              rhs=h_ap[:, k, 0:nt], start=(k == 0), stop=(k == 7)),
                     reads=[B_w2] + B_h_l, writes=[pg1b], sig=(k == 7))
            K.op("act", lambda e: e.activation(out=g0s[:, 0:nt], in_=pg0[:, 0:nt], func=AF.Sigmoid), reads=[pg0b],
                 writes=[B_gs[0]])
            K.op("act", lambda e: e.activation(out=g1s[:, 0:nt], in_=pg1[:, 0:nt], func=AF.Sigmoid), reads=[pg1b],
                 writes=[B_gs[1]])
            K.op("dve", lambda e: e.tensor_tensor(out=g0s[:, 0:nt], in0=g0s[:, 0:nt], in1=pa[:, 0:nt], op=ALU.mult),
                 reads=[B_gs[0], pab], writes=[B_gs[0]])
            K.op("dve", lambda e: e.tensor_tensor(out=g1s[:, 0:nt], in0=g1s[:, 0:nt], in1=pbb_[:, 0:nt], op=ALU.mult),
                 reads=[B_gs[1], pbb], writes=[B_gs[1]])
            K.op("pool", lambda e: e.tensor_tensor(out=mergedT[:, cc, 0:nt], in0=g0s[:, 0:nt], in1=g1s[:, 0:nt],
                                                   op=ALU.add), reads=B_gs, writes=[B_merged])

    for si, (T0, ntl) in enumerate(own_sts):
        nt = ntl * 128
        K.dma("sp", omT[:, :, 0:nt], scr_om[si, :, :, 0:nt], reads=[B_scr_om[si]], writes=[B_omT])
        for j in range(ntl):
            rmsnorm_T(tile_src(T0 + j), xst[:, j, :], B_xst[j], xn[j % 2], B_xn[j % 2], gcol_mix,
                      hT[:, :, j * 128:(j + 1) * 128], B_hT[j])
        Bh = B_hT[0:ntl]
        p1b_merge(nt, omT, [B_omT], hT, Bh)
        for j in range(ntl):
            T = T0 + j
            for n in range(2):
                px, pxb = ps_a()
                for k in range(8):
                    K.op("pe", lambda e: e.matmul(px, lhsT=mergedT[:, k, j * 128:(j + 1) * 128],
                                                  rhs=wout[:, k, n * 512:(n + 1) * 512], start=(k == 0), stop=(k == 7)),
                         reads=[B_merged, B_w2], writes=[pxb], sig=(k == 7))
                K.op("dve", lambda e: e.tensor_tensor(out=x1b[j % 2][:, n * 512:(n + 1) * 512], in0=px,
                                                      in1=xst[:, j, n * 512:(n + 1) * 512], op=ALU.add),
                     reads=[pxb, B_xst[j]], writes=[B_x1b[j % 2]])
            K.dma("sp", scr_x1[T - 15], x1b[j % 2], reads=[B_x1b[j % 2]], writes=[B_scr_x1[T - 15]])

    if do_sample:
        K.dma("sp", xst[0:4, 0, :], xs, writes=[B_xst[0]])
        p1b_merge(4, omT_s, [B_omTs], hT_s, [B_hTs])
        for n in range(2):
            px, pxb = ps_a()
            for k in range(8):
                K.op("pe", lambda e: e.matmul(px[0:4, :], lhsT=mergedT[:, k, 0:4], rhs=wout[:, k, n * 512:(n + 1) * 512],
                                              start=(k == 0), stop=(k == 7)), reads=[B_merged, B_w2], writes=[pxb], sig=(k == 7))
            K.op("dve", lambda e: e.tensor_tensor(out=x1b[0][0:4, n * 512:(n + 1) * 512], in0=px[0:4, :],
                                                  in1=xst[0:4, 0, n * 512:(n + 1) * 512], op=ALU.add),
                 reads=[pxb, B_xst[0]], writes=[B_x1b[0]])
        K.dma("sp", scr_x1s, x1b[0][0:4, :], reads=[B_x1b[0]], writes=[B_scr_x1s])

    K.barrier()
    e2.close()
    if stage <= 7:
        K.finish()
        return nc

    e3 = ExitStack()
    wup = sbt(e3, "wup", [128, 8, 2 * DFF], BF16)
    wdn = sbt(e3, "wdn", [128, 22, D], BF16)
    B_w3 = K.buf("w_pass2")
    for k in range(8):
        for c0 in range(0, 2 * DFF, 1408):
            K.dma("pool", wup[:, k, c0:c0 + 1408], w_up[k * 128:(k + 1) * 128, c0:c0 + 1408], writes=[B_w3])
    for f in range(22):
        K.dma("pool", wdn[:, f, :], w_down[f * 128:(f + 1) * 128, :], writes=[B_w3])
    cw = sbt(e3, "cw", [128, 66], F32)
    cb = sbt(e3, "cb", [128, 22], F32)
    tmpc = sbt(e3, "tmpc", [66, 128], F32)
    K.dma("sp", tmpc, conv_w, writes=[B_w3])
    pt, pb = ps_a()
    K.op("pe", lambda e: e.transpose(out=pt[:, 0:66], in_=tmpc, identity=ident_f[0:66, 0:66]), reads=[B_w3, B_const],
         writes=[pb])
    K.op("dve", lambda e: e.tensor_copy(out=cw, in_=pt[:, 0:66]), reads=[pb], writes=[B_w3])
    tmpb = sbt(e3, "tmpb", [22, 128], F32)
    K.dma("sp", tmpb, conv_b, writes=[B_w3])
    pt, pb = ps_a()
    K.op("pe", lambda e: e.transpose(out=pt[:, 0:22], in_=tmpb, identity=ident_f[0:22, 0:22]), reads=[B_w3, B_const],
         writes=[pb])
    K.op("dve", lambda e: e.tensor_copy(out=cb, in_=pt[:, 0:22]), reads=[pb], writes=[B_w3])
    gfinB = sbt(e3, "gfinB", [128, D], F32)
    K.dma("sp", gfinB, norm_final.partition_broadcast(128), writes=[B_w3])

    xld = [sbt(e3, "xld%d" % i, [128, D], F32) for i in range(2)]
    B_xld = [K.buf(), K.buf()]
    xn = [sbt(e3, "xn3_%d" % i, [128, D], BF16) for i in range(2)]
    st4 = sbt(e3, "st4c", [128, 8], F32)
    hT = sbt(e3, "hT3", [128, 8, 512], BF16)
    B_st = K.buf()
    B_xn = [K.buf(), K.buf()]
    B_hT = [K.buf() for _ in range(4)]
    abuf = [sbt(e3, "abuf%d" % i, [128, 2 + 512], F32) for i in range(2)]
    B_abuf = [K.buf(), K.buf()]
    acar = sbt(e3, "acar", [128, 22, 2], F32)
    B_acar = K.buf()
    K.op("pool", lambda e: e.memset(acar, 0.0), writes=[B_acar])
    cbuf = [sbt(e3, "cbuf0", [128, 512], F32)] * 2
    B_cbuf = [K.buf()] * 2
    actT = sbt(e3, "actT", [128, 22, 512], BF16)
    B_actT = K.buf()
    ybuf = [sbt(e3, "ybuf0", [128, D], F32)] * 2
    B_ybuf = [K.buf()] * 2
    convsb = sbt(e3, "convsb", [2, 512], F32)
    B_convsb = K.buf()

    for si, (T0, ntl) in enumerate(own_sts):
        nt = ntl * 128
        for j in range(ntl):
            K.dma("sp", xld[j % 2], scr_x1[T0 + j - 15], reads=[B_scr_x1[T0 + j - 15]], writes=[B_xld[j % 2]])
            rmsnorm_T(None, xld[j % 2], B_xld[j % 2], xn[j % 2], B_xn[j % 2], gcol_ffn,
                      hT[:, :, j * 128:(j + 1) * 128], B_hT[j], x_preloaded=True)
        Bh = B_hT[0:ntl]
        for f in range(22):
            pa, pab = ps_a()
            pbb_, pbb = ps_a()
            ab, Bab = abuf[f % 2], B_abuf[f % 2]
            cbf, Bcb = cbuf[f % 2], B_cbuf[f % 2]
            for k in range(8):
                K.op("pe", lambda e: e.matmul(pa[:, 0:nt], lhsT=wup[:, k, f * 128:(f + 1) * 128], rhs=hT[:, k, 0:nt],
                                              start=(k == 0), stop=(k == 7)), reads=[B_w3] + Bh, writes=[pab], sig=(k == 7))
            for k in range(8):
                K.op("pe", lambda e: e.matmul(pbb_[:, 0:nt], lhsT=wup[:, k, DFF + f * 128:DFF + (f + 1) * 128],
                                              rhs=hT[:, k, 0:nt], start=(k == 0), stop=(k == 7)),
                     reads=[B_w3] + Bh, writes=[pbb], sig=(k == 7))
            K.op("pool", lambda e: e.tensor_copy(out=ab[:, 0:2], in_=acar[:, f, :]), reads=[B_acar], writes=[Bab])
            K.op("act", lambda e: e.copy(out=ab[:, 2:2 + nt], in_=pa[:, 0:nt]), reads=[pab], writes=[Bab])
            if si == 0:
                K.op("dve", lambda e: e.tensor_scalar(out=ab[:, 2:130], in0=ab[:, 2:130], scalar1=halfcol[:, 0:1],
                                                      scalar2=None, op0=ALU.mult), reads=[Bab, B_const], writes=[Bab])
            K.op("pool", lambda e: e.tensor_copy(out=acar[:, f, :], in_=ab[:, nt:nt + 2]), reads=[Bab],
                 writes=[B_acar])
            K.op("dve", lambda e: e.tensor_scalar(out=cbf[:, 0:nt], in0=ab[:, 0:nt], scalar1=cw[:, f:f + 1],
                                                  scalar2=cb[:, f:f + 1], op0=ALU.mult, op1=ALU.add),
                 reads=[Bab, B_w3], writes=[Bcb])
            K.op("dve", lambda e: e.scalar_tensor_tensor(out=cbf[:, 0:nt], in0=ab[:, 1:1 + nt],
                                                         scalar=cw[:, 22 + f:23 + f], in1=cbf[:, 0:nt], op0=ALU.mult,
                                                         op1=ALU.add), reads=[Bab, B_w3, Bcb], writes=[Bcb])
            K.op("dve", lambda e: e.scalar_tensor_tensor(out=cbf[:, 0:nt], in0=ab[:, 2:2 + nt],
                                                         scalar=cw[:, 44 + f:45 + f], in1=cbf[:, 0:nt], op0=ALU.mult,
                                                         op1=ALU.add), reads=[Bab, B_w3, Bcb], writes=[Bcb])
            K.op("act", lambda e: e.activation(out=cbf[:, 0:nt], in_=cbf[:, 0:nt], func=AF.Gelu_apprx_tanh),
                 reads=[Bcb], writes=[Bcb])
            K.op("dve", lambda e: e.tensor_tensor(out=actT[:, f, 0:nt], in0=cbf[:, 0:nt], in1=pbb_[:, 0:nt],
                                                  op=ALU.mult), reads=[Bcb, pbb], writes=[B_actT])
        if si == 4:
            for r0 in range(0, 22, 4):
                nn = min(4, 22 - r0)
                pt, pb = ps_a()
                for f in range(r0, r0 + nn):
                    K.op("pe", lambda e: e.transpose(out=pt[0:2, (f - r0) * 128:(f - r0 + 1) * 128], in_=acar[:, f, :],
                                                     identity=ident_f), reads=[B_acar, B_const], writes=[pb])
                K.op("dve", lambda e: e.tensor_copy(out=convsb[:, 0:nn * 128], in_=pt[0:2, 0:nn * 128]),
                     reads=[pb], writes=[B_convsb])
                K.dma("sp", conv_o[:, r0 * 128:(r0 + nn) * 128], convsb[:, 0:nn * 128], reads=[B_convsb])
        for j in range(ntl):
            T = T0 + j
            yb, Byb = ybuf[j % 2], B_ybuf[j % 2]
            K.dma("sp", xld[j % 2], scr_x1[T - 15], reads=[B_scr_x1[T - 15]], writes=[B_xld[j % 2]])
            for n in range(2):
                py, pyb = ps_a()
                for f in range(22):
                    K.op("pe", lambda e: e.matmul(py, lhsT=actT[:, f, j * 128:(j + 1) * 128],
                                                  rhs=wdn[:, f, n * 512:(n + 1) * 512], start=(f == 0), stop=(f == 21)),
                         reads=[B_actT, B_w3], writes=[pyb], sig=(f == 21))
                K.op("dve", lambda e: e.tensor_tensor(out=yb[:, n * 512:(n + 1) * 512], in0=py,
                                                      in1=xld[j % 2][:, n * 512:(n + 1) * 512], op=ALU.add),
                     reads=[pyb, B_xld[j % 2]], writes=[Byb])
            if T < 16:
                continue
            K.op("act", lambda e: e.activation(out=xn[0], in_=yb, func=AF.Square, accum_out=st4[:, 4:5]), reads=[Byb],
                 writes=[B_xn[0], B_st])
            K.op("act", lambda e: e.activation(out=st4[:, 5:6], in_=st4[:, 4:5], func=AF.Ln, scale=1.0 / D, bias=EPS), reads=[B_st], writes=[B_st])
            K.op("act", lambda e: e.activation(out=st4[:, 6:7], in_=st4[:, 5:6], func=AF.Exp, scale=-0.5), reads=[B_st], writes=[B_st])
            K.op("dve", lambda e: e.scalar_tensor_tensor(out=yb, in0=yb, scalar=st4[:, 6:7], in1=gfinB,
                                                          op0=ALU.mult, op1=ALU.mult), reads=[Byb, B_st, B_w3],
                 writes=[Byb])
            K.dma("sp", y_o[(T - 16) * 128:(T - 15) * 128, :], yb, reads=[Byb])

    if do_sample:
        x1s = xld[0][0:4, :]
        K.dma("sp", x1s, scr_x1s, reads=[B_scr_x1s], writes=[B_xld[0]])
        K.op("act", lambda e: e.activation(out=xn[0][0:4, :], in_=x1s, func=AF.Square, accum_out=st4[0:4, 0:1]),
             reads=[B_xld[0]], writes=[B_xn[0], B_st])
        K.op("act", lambda e: e.activation(out=st4[0:4, 1:2], in_=st4[0:4, 0:1], func=AF.Ln, scale=1.0 / D, bias=EPS),
             reads=[B_st], writes=[B_st])
        K.op("act", lambda e: e.activation(out=st4[0:4, 2:3], in_=st4[0:4, 1:2], func=AF.Exp, scale=-0.5),
             reads=[B_st], writes=[B_st])
        K.op("dve", lambda e: e.tensor_scalar(out=xn[0][0:4, :], in0=x1s, scalar1=st4[0:4, 2:3], scalar2=None,
                                              op0=ALU.mult), reads=[B_xld[0], B_st], writes=[B_xn[0]])
        pt, pb = ps_a()
        ptb = pt.bitcast(BF16)[:, 0:32].rearrange("p (k t) -> p k t", k=8)
        for k in range(8):
            K.op("pe", lambda e: e.transpose(out=ptb[:, k, :], in_=xn[0][0:4, k * 128:(k + 1) * 128],
                                             identity=ident_b[0:4, 0:4]), reads=[B_xn[0], B_const], writes=[pb])
        K.op("dve", lambda e: e.tensor_tensor(out=hT[:, :, 0:4], in0=ptb,
                                              in1=gcol_ffn.unsqueeze(2).to_broadcast([128, 8, 4]), op=ALU.mult),
             reads=[pb, B_const], writes=[B_hT[0]])
        pab_, pabb = ps_a()
        p3 = pab_[:, 0:176].rearrange("p (f t) -> p f t", f=44)
        for f in range(44):
            for k in range(8):
                K.op("pe", lambda e: e.matmul(p3[:, f, :], lhsT=wup[:, k, f * 128:(f + 1) * 128], rhs=hT[:, k, 0:4],
                                              start=(f == 0 and k == 0), stop=(f == 43 and k == 7),
                                              skip_group_check=True), reads=[B_w3, B_hT[0]], writes=[pabb], sig=(f == 43 and k == 7))
        aTs = abuf[0][:, 0:88].rearrange("p (f t) -> p f t", f=22)
        K.op("act", lambda e: e.copy(out=abuf[0][:, 0:88], in_=pab_[:, 0:88]), reads=[pabb], writes=[B_abuf[0]])
        stt = abuf[1][0:8, 0:DFF // 8 * 0 + 514]
        stT = cbuf[0][:, 0:176].rearrange("p (f t) -> p f t", f=22)
        pst, pstb = ps_a()
        for f0 in range(0, 22, 4):
            nn = min(4, 22 - f0)
            K.dma("sp", abuf[1][0:8, 0:nn * 128], state_conv[:, f0 * 128:(f0 + nn) * 128], writes=[B_abuf[1]])
            for f in range(f0, f0 + nn):
                K.op("pe", lambda e: e.transpose(out=pst[:, f * 8:(f + 1) * 8],
                                                 in_=abuf[1][0:8, (f - f0) * 128:(f - f0 + 1) * 128],
                                                 identity=ident_f[0:8, 0:8]), reads=[B_abuf[1], B_const],
                     writes=[pstb])
        K.op("dve", lambda e: e.tensor_copy(out=cbuf[0][:, 0:176], in_=pst[:, 0:176]), reads=[pstb],
             writes=[B_cbuf[0]])
        stT4 = cbuf[0][:, 0:176].rearrange("p (f b j) -> p f b j", f=22, b=4)
        cS = cbuf[0][:, 256:344].rearrange("p (f t) -> p f t", f=22)
        tS = cbuf[0][:, 384:472].rearrange("p (f t) -> p f t", f=22)

        def bc(ap):
            return ap.unsqueeze(2).to_broadcast([128, 22, 4])
        K.op("dve", lambda e: e.tensor_tensor(out=cS, in0=stT4[:, :, :, 0], in1=bc(cw[:, 0:22]), op=ALU.mult),
             reads=[B_cbuf[0], B_w3], writes=[B_cbuf[0]])
        K.op("dve", lambda e: e.tensor_tensor(out=cS, in0=cS, in1=bc(cb), op=ALU.add), reads=[B_cbuf[0], B_w3],
             writes=[B_cbuf[0]])
        K.op("dve", lambda e: e.tensor_tensor(out=tS, in0=stT4[:, :, :, 1], in1=bc(cw[:, 22:44]), op=ALU.mult),
             reads=[B_cbuf[0], B_w3], writes=[B_cbuf[0]])
        K.op("dve", lambda e: e.tensor_tensor(out=cS, in0=cS, in1=tS, op=ALU.add), reads=[B_cbuf[0]],
             writes=[B_cbuf[0]])
        K.op("dve", lambda e: e.tensor_tensor(out=tS, in0=aTs, in1=bc(cw[:, 44:66]), op=ALU.mult),
             reads=[B_abuf[0], B_w3], writes=[B_cbuf[0]])
        K.op("dve", lambda e: e.tensor_tensor(out=cS, in0=cS, in1=tS, op=ALU.add), reads=[B_cbuf[0]],
             writes=[B_cbuf[0]])
        K.op("act", lambda e: e.activation(out=cS, in_=cS, func=AF.Gelu_apprx_tanh), reads=[B_cbuf[0]],
             writes=[B_cbuf[0]])
        K.op("dve", lambda e: e.tensor_tensor(out=actT[:, :, 0:4], in0=cS, in1=p3[:, 22:44, :], op=ALU.mult),
             reads=[B_cbuf[0], pabb], writes=[B_actT])
        K.dma("sp", convs_o[:, 0, :], state_conv.rearrange("(b j) f -> b j f", j=2)[:, 1, :])
        for r0 in range(0, 22, 4):
            nn = min(4, 22 - r0)
            pt, pb = ps_a()
            for f in range(r0, r0 + nn):
                K.op("pe", lambda e: e.transpose(out=pt[0:4, (f - r0) * 128:(f - r0 + 1) * 128], in_=aTs[:, f, :],
                                                 identity=ident_f), reads=[B_abuf[0], B_const], writes=[pb])
            K.op("dve", lambda e: e.tensor_copy(out=ybuf[0][0:4, 0:nn * 128], in_=pt[0:4, 0:nn * 128]), reads=[pb],
                 writes=[B_ybuf[0]])
            K.dma("sp", convs_o[:, 1, r0 * 128:(r0 + nn) * 128], ybuf[0][0:4, 0:nn * 128], reads=[B_ybuf[0]])
        yb = ybuf[0]
        for n in range(2):
            py, pyb = ps_a()
            for f in range(22):
                K.op("pe", lambda e: e.matmul(py[0:4, :], lhsT=actT[:, f, 0:4], rhs=wdn[:, f, n * 512:(n + 1) * 512],
                                              start=(f == 0), stop=(f == 21)), reads=[B_actT, B_w3], writes=[pyb], sig=(f == 21))
            K.op("dve", lambda e: e.tensor_tensor(out=yb[0:4, n * 512:(n + 1) * 512], in0=py[0:4, :],
                                                  in1=x1s[:, n * 512:(n + 1) * 512], op=ALU.add),
                 reads=[pyb, B_xld[0]], writes=[B_ybuf[0]])
        K.op("act", lambda e: e.activation(out=xn[0][0:4, :], in_=yb[0:4, :], func=AF.Square,
                                           accum_out=st4[0:4, 4:5]), reads=[B_ybuf[0]], writes=[B_xn[0], B_st])
        K.op("act", lambda e: e.activation(out=st4[0:4, 5:6], in_=st4[0:4, 4:5], func=AF.Ln, scale=1.0 / D, bias=EPS),
             reads=[B_st], writes=[B_st])
        K.op("act", lambda e: e.activation(out=st4[0:4, 6:7], in_=st4[0:4, 5:6], func=AF.Exp, scale=-0.5),
             reads=[B_st], writes=[B_st])
        K.op("dve", lambda e: e.scalar_tensor_tensor(out=yb[0:4, :], in0=yb[0:4, :], scalar=st4[0:4, 6:7],
                                                     in1=gfinB[0:4, :], op0=ALU.mult, op1=ALU.mult),
             reads=[B_ybuf[0], B_st, B_w3], writes=[B_ybuf[0]])
        K.dma("sp", ys_o, yb[0:4, :], reads=[B_ybuf[0]])

    K.finish()
    e3.close()
    es0.close()
    return nc


_NC_CACHE = {}


def _get_nc():
    if "nc" not in _NC_CACHE:
        _NC_CACHE["nc"] = build_program()
    return _NC_CACHE["nc"]


def kernel(x_prompt, x_sample, cache_cmp, cache_slc, cache_win, state_conv, page_table, norm_mix, w_in, cmp_pe,
           cmp_w1, cmp_b1, cmp_w2, gmlp_norm, gmlp_ws, gmlp_bs, w_proj_a, w_proj_b, w_out, norm_ffn, w_up, conv_w,
           conv_b, w_down, norm_final):
    f = np.float32
    x_prompt = np.asarray(x_prompt, f)
    nc = _get_nc()
    shared = {
        "w_in": np.ascontiguousarray(np.asarray(w_in, f)[0]),
        "cmp_pe": np.ascontiguousarray(np.asarray(cmp_pe, f)[0]),
        "cmp_w1": np.ascontiguousarray(np.asarray(cmp_w1, f)[0]),
        "cmp_b1": np.ascontiguousarray(np.asarray(cmp_b1, f)[0]),
        "cmp_w2": np.ascontiguousarray(np.asarray(cmp_w2, f)[0]),
        "gmlp_norm": np.ascontiguousarray(np.asarray(gmlp_norm, f).reshape(1, 512)),
        "gmlp_ws": np.ascontiguousarray(np.asarray(gmlp_ws, f)[0]),
        "gmlp_bs": np.ascontiguousarray(np.asarray(gmlp_bs, f).reshape(1, 512)),
        "w_proj_a": np.ascontiguousarray(np.asarray(w_proj_a, f)[0]),
        "w_proj_b": np.ascontiguousarray(np.asarray(w_proj_b, f)[0]),
        "w_out": np.ascontiguousarray(np.asarray(w_out, f)[0]),
        "norm_mix": np.ascontiguousarray(np.asarray(norm_mix, f).reshape(8, 128)),
        "norm_ffn": np.ascontiguousarray(np.asarray(norm_ffn, f).reshape(8, 128)),
        "w_up": np.ascontiguousarray(np.asarray(w_up, f)[0]),
        "conv_w": np.ascontiguousarray(np.asarray(conv_w, f).reshape(66, 128)),
        "conv_b": np.ascontiguousarray(np.asarray(conv_b, f).reshape(22, 128)),
        "w_down": np.ascontiguousarray(np.asarray(w_down, f)[0]),
        "norm_final": np.ascontiguousarray(np.asarray(norm_final, f).reshape(1, D)),
    }
    x_sample = np.asarray(x_sample, f)
    cc_flat = np.ascontiguousarray(np.asarray(cache_cmp, f)).reshape(40960, 4096)
    cs_flat = np.ascontiguousarray(np.asarray(cache_slc, f)).reshape(40960, 4096)
    cache_win = np.asarray(cache_win, f)
    state_conv = np.asarray(state_conv, f)
    page_table = np.asarray(page_table).astype(np.int32)
    pmod = (np.arange(128) % 8).astype(f).reshape(128, 1)
    in_maps = []
    for c in range(8):
        b, hf = c // 2, c % 2
        m = dict(shared)
        m["xo"] = np.ascontiguousarray(x_prompt[b, hf * 2048:(hf + 1) * 2048])
        m["xh"] = np.ascontiguousarray(x_prompt[b, (1 - hf) * 2048:(2 - hf) * 2048])
        selb = np.zeros((1, 64), f)
        cmpb = np.zeros((1, NSLOT), f)
        if hf == 0:
            selb[0, :32] = -1e30
            selb[0, 32] = 1e6
            cmpb[0, :129] = NEGM
            hsc = np.array([[NEGM, 0.0]], f)
        else:
            selb[0, 0] = 1e6
            cmpb[0, 0] = NEGM
            hsc = np.array([[0.0, 1.0]], f)
        m["selb"], m["cmpb"], m["hsc"] = selb, cmpb, hsc
        sb = slice(4 * c, 4 * c + 4)
        m["xs"] = np.ascontiguousarray(x_sample[sb, 0, :])
        m["cache_cmp"] = cc_flat
        m["cache_slc"] = cs_flat
        m["cache_win"] = np.ascontiguousarray(cache_win[0, sb].reshape(4, 512, 256))
        m["state_conv"] = np.ascontiguousarray(state_conv[0, sb].reshape(8, DFF))
        ptb_ = page_table[sb].reshape(4, 8, 16)
        ptx = np.repeat(ptb_, 8, axis=2)
        m["ptx"] = np.ascontiguousarray(ptx.transpose(2, 0, 1).reshape(128, 32)).astype(np.int32)
        m["pmod"] = pmod
        in_maps.append(m)
    res = run_bass_kernel_spmd(nc, in_maps, core_ids=list(range(8)))
    R = res.results
    y_prompt = np.stack([np.concatenate([R[2 * b]["y"], R[2 * b + 1]["y"]], 0) for b in range(4)]).astype(f)
    kvs = []
    for br in range(3):
        kvs.append(np.stack([np.concatenate([R[2 * b]["kvo"][br], R[2 * b + 1]["kvo"][br]], 0)
                             for b in range(4)]).reshape(1, 4, 4096, 2, 2, 64).astype(f))
    new_win_p = np.ascontiguousarray(kvs[2][:, :, -512:])
    new_v_p = np.stack([R[2 * b + 1]["vno"] for b in range(4)]).reshape(1, 4, 128, 512).astype(f)
    new_conv_p = np.stack([R[2 * b + 1]["convo"] for b in range(4)]).reshape(1, 4, 2, DFF).astype(f)
    y_s = np.concatenate([R[c]["ys"] for c in range(8)], 0).reshape(32, 1, D).astype(f)
    kvs_s = np.concatenate([R[c]["kvs"] for c in range(8)], 0).astype(f)
    cmp_s = np.ascontiguousarray(kvs_s[:, 0:256]).reshape(1, 32, 1, 2, 2, 64)
    slc_s = np.ascontiguousarray(kvs_s[:, 256:512]).reshape(1, 32, 1, 2, 2, 64)
    win_s = np.concatenate([R[c]["wins"] for c in range(8)], 0).reshape(1, 32, 512, 2, 2, 64).astype(f)
    v_s = np.concatenate([R[c]["vns"] for c in range(8)], 0).reshape(1, 32, 1, 512).astype(f)
    conv_s = np.concatenate([R[c]["convs"] for c in range(8)], 0).reshape(1, 32, 2, DFF).astype(f)
    return (y_prompt, y_s, kvs[0], kvs[1], new_win_p, new_v_p, new_conv_p, cmp_s, slc_s, win_s, v_s, conv_s)
```

```python
import numpy as np
from contextlib import ExitStack
import concourse.bass as bass
import concourse.mybir as mybir
from concourse.bass_utils import run_bass_kernel_spmd

F32 = mybir.dt.float32
BF16 = mybir.dt.bfloat16
I32 = mybir.dt.int32
AF = mybir.ActivationFunctionType
ALU = mybir.AluOpType
AX = mybir.AxisListType

NEGM = -30000.0
EPS = 1e-6
D = 1024
INC = 4376
DFF = 2816
NSLOT = 288


class Buf:
    __slots__ = ("name", "w", "r")

    def __init__(self, name):
        self.name = name
        self.w = {}
        self.r = {}


def _merge(d, k, v):
    if d.get(k, 0) < v:
        d[k] = v


class KB:
    NDMA = 32

    def __init__(self, nc):
        self.nc = nc
        self.eng = {"pe": nc.tensor, "act": nc.scalar, "dve": nc.vector, "pool": nc.gpsimd, "sp": nc.sync}
        self.sem = {e: nc.alloc_semaphore("cs_" + e) for e in ("pe", "act", "dve", "pool")}
        self.cnt = {e: 0 for e in self.sem}
        self.waited = {e: {} for e in self.eng}
        self.dsem = [nc.alloc_semaphore("ds%d" % i) for i in range(self.NDMA)]
        self.dtgt = [0] * self.NDMA
        self.dring = {"sp": list(range(0, 24)), "pool": list(range(24, 28)), "act": list(range(28, 32))}
        self.dpos = {"sp": 0, "pool": 0, "act": 0}
        self.nb = 0

    def buf(self, name=None):
        self.nb += 1
        return Buf(name or ("b%d" % self.nb))

    def _deps(self, reads, writes):
        deps = {}
        for b in reads:
            for k, v in b.w.items():
                _merge(deps, k, v)
        for b in writes:
            for k, v in b.w.items():
                _merge(deps, k, v)
            for k, v in b.r.items():
                _merge(deps, k, v)
        return deps

    def _wait(self, e, deps):
        eng = self.eng[e]
        wd = self.waited[e]
        for k, v in deps.items():
            if k == e and e == "pe":
                continue
            if wd.get(k, 0) >= v:
                continue
            sem = self.dsem[k[1]] if isinstance(k, tuple) else self.sem[k]
            eng.wait_ge(sem, v)
            wd[k] = v

    def _mark(self, key, val, reads, writes):
        for b in reads:
            _merge(b.r, key, val)
        for b in writes:
            b.w = {key: val}
            b.r = {}

    def op(self, e, fn, reads=(), writes=(), sig=True):
        self._wait(e, self._deps(reads, writes))
        ins = fn(self.eng[e])
        if sig:
            self.cnt[e] += 1
            ins.then_inc(self.sem[e], 1)
            val = self.cnt[e]
        else:
            val = self.cnt[e] + 1
        self._mark(e, val, reads, writes)

    def dma(self, q, out, in_, reads=(), writes=(), **kw):
        ring = self.dring[q]
        si = ring[self.dpos[q] % len(ring)]
        self.dpos[q] += 1
        deps = self._deps(reads, writes)
        if self.dtgt[si] > 0:
            _merge(deps, ("d", si), self.dtgt[si])
        self._wait(q, deps)
        ins = self.eng[q].dma_start(out=out, in_=in_, **kw)
        self.dtgt[si] += 16
        ins.then_inc(self.dsem[si], 16)
        self._mark(("d", si), self.dtgt[si], reads, writes)

    def barrier(self):
        deps = {}
        for e, c in self.cnt.items():
            if c > 0:
                deps[e] = c
        for si, t in enumerate(self.dtgt):
            if t > 0:
                deps[("d", si)] = t
        for e in self.eng:
            self._wait(e, deps)

    def finish(self):
        self.barrier()


class _Stop(Exception):
    pass


def build_program(do_sample=True, stage=99, sub=99):
    st = {}
    try:
        return _build_program(do_sample, stage, sub, st)
    except _Stop:
        st["K"].finish()
        return st["nc"]


def _build_program(do_sample, stage, sub, _st):
    nc = bass.Bass("TRN2", target_bir_lowering=False)
    K = KB(nc)
    _st["K"], _st["nc"] = K, nc
    es0 = ExitStack()

    def ck(n):
        if stage == 4 and sub == n:
            raise _Stop()

    def din(name, shape, dt=F32):
        return nc.dram_tensor(name, list(shape), dt, kind="ExternalInput").ap()

    def dout(name, shape, dt=F32):
        return nc.dram_tensor(name, list(shape), dt, kind="ExternalOutput").ap()

    def sbt(es, name, shape, dt):
        return es.enter_context(nc.sbuf_tensor(name, list(shape), dt)).ap()

    xh = din("xh", [2048, D])
    xo = din("xo", [2048, D])
    selb = din("selb", [1, 64])
    cmpb = din("cmpb", [1, NSLOT])
    hsc = din("hsc", [1, 2])
    w_in = din("w_in", [D, INC])
    cmp_pe = din("cmp_pe", [2, 32, 64])
    cmp_w1 = din("cmp_w1", [2, 2048, 128])
    cmp_b1 = din("cmp_b1", [2, 128])
    cmp_w2 = din("cmp_w2", [2, 128, 64])
    gmlp_norm = din("gmlp_norm", [1, 512])
    gmlp_ws = din("gmlp_ws", [4, 128, 128])
    gmlp_bs = din("gmlp_bs", [1, 512])
    w_proj_a = din("w_proj_a", [512, D])
    w_proj_b = din("w_proj_b", [512, D])
    w_out = din("w_out", [D, D])
    norm_mix = din("norm_mix", [8, 128])
    norm_ffn = din("norm_ffn", [8, 128])
    w_up = din("w_up", [D, 2 * DFF])
    conv_w = din("conv_w", [3 * 22, 128])
    conv_b = din("conv_b", [22, 128])
    w_down = din("w_down", [DFF, D])
    norm_final = din("norm_final", [1, D])

    xs = din("xs", [4, D])
    cache_cmp = din("cache_cmp", [40960, 4096])
    cache_slc = din("cache_slc", [40960, 4096])
    cache_win = din("cache_win", [4, 512, 256])
    state_conv = din("state_conv", [8, DFF])
    ptx = din("ptx", [128, 32], I32)
    pmod = din("pmod", [128, 1])
    ys_o = dout("ys", [4, D])
    kvs_o = dout("kvs", [4, 768])
    wins_o = dout("wins", [4, 512, 256])
    vns_o = dout("vns", [4, 512])
    convs_o = dout("convs", [4, 2, DFF])

    y_o = dout("y", [2048, D])
    kv_o = dout("kvo", [3, 2048, 256])
    vn_o = dout("vno", [128, 512])
    conv_o = dout("convo", [2, DFF])

    scr_om = nc.dram_tensor("scr_om", [5, 128, 8, 512], BF16, kind="Internal").ap()
    scr_x1 = nc.dram_tensor("scr_x1", [17, 128, D], F32, kind="Internal").ap()
    B_scr_om = [K.buf() for _ in range(5)]
    B_scr_x1 = [K.buf() for _ in range(17)]

    psS = nc.alloc_psum_tensor("psS", [128, 1024], F32).ap()
    banks = [nc.alloc_psum_tensor("pb%d" % i, [128, 512], F32).ap() for i in range(6)]
    B_bank = [K.buf("bank%d" % i) for i in range(6)]
    B_S = [K.buf("bankS0"), K.buf("bankS1")]
    allb = [(banks[i], B_bank[i]) for i in range(6)] + [(psS[:, 0:512], B_S[0]), (psS[:, 512:1024], B_S[1])]
    rot = {"g": 0, "o": 0, "a": 0}

    def ps_g():
        i = rot["g"]
        rot["g"] = (i + 1) % 4
        return allb[i]

    def ps_o():
        i = rot["o"]
        rot["o"] = (i + 1) % 2
        return allb[4 + i]

    def ps_a():
        i = rot["a"]
        rot["a"] = (i + 1) % 8
        return allb[i]

    c = es0
    ident_f = sbt(c, "ident_f", [128, 128], F32)
    ident_b = sbt(c, "ident_b", [128, 128], BF16)
    ones_f = sbt(c, "ones_f", [128, 128], F32)
    ones_b = sbt(c, "ones_b", [128, 128], BF16)
    zeros_b = sbt(c, "zeros_b", [128, 512], BF16)
    negs_b = sbt(c, "negs_b", [128, 512], BF16)
    B_const = K.buf("const")

    K.op("pool", lambda e: e.memset(ones_f, 1.0), writes=[B_const])
    K.op("pool", lambda e: e.memset(ones_b, 1.0), writes=[B_const])
    K.op("pool", lambda e: e.memset(zeros_b, 0.0), writes=[B_const])
    K.op("pool", lambda e: e.memset(negs_b, NEGM), writes=[B_const])
    K.op("pool", lambda e: e.affine_select(out=ident_f, in_=ones_f, pattern=[[-1, 128]], compare_op=ALU.is_equal,
                                           fill=0.0, base=0, channel_multiplier=1), writes=[B_const])
    K.op("pool", lambda e: e.affine_select(out=ident_b, in_=ones_b, pattern=[[-1, 128]], compare_op=ALU.is_equal,
                                           fill=0.0, base=0, channel_multiplier=1), writes=[B_const])

    def r4(ap):
        return ap.rearrange("p (a b) -> p a b", a=4)

    def load_cols(dst, src_rows, nrows):
        tmp = sbt(c, "lc_%d" % K.nb, [nrows, 128], F32)
        tb = K.buf()
        K.dma("sp", tmp, src_rows, writes=[tb])
        pt, pb = ps_g()
        K.op("pe", lambda e: e.transpose(out=pt[:, 0:nrows], in_=tmp, identity=ident_f[0:nrows, 0:nrows]),
             reads=[tb, B_const], writes=[pb])
        K.op("dve", lambda e: e.tensor_copy(out=dst, in_=pt[:, 0:nrows]), reads=[pb], writes=[B_const])

    gcol_mix = sbt(c, "gcol_mix", [128, 8], F32)
    gcol_ffn = sbt(c, "gcol_ffn", [128, 8], F32)
    load_cols(gcol_mix, norm_mix, 8)
    load_cols(gcol_ffn, norm_ffn, 8)
    halfcol = sbt(c, "halfcol", [128, 1], F32)
    K.dma("sp", halfcol, hsc[:, 1:2].partition_broadcast(128), writes=[B_const])
    hsc_sb = sbt(c, "hsc_sb", [1, 2], F32)
    K.dma("sp", hsc_sb, hsc, writes=[B_const])

    hT_s = sbt(c, "hT_s", [128, 8, 4], BF16)
    B_hTs = K.buf()
    omT_s = sbt(c, "omT_s", [128, 8, 4], BF16)
    B_omTs = K.buf()
    scr_x1s = nc.dram_tensor("scr_x1s", [4, D], F32, kind="Internal").ap()
    B_scr_x1s = K.buf()

    e1 = ExitStack()
    w_in_sb = sbt(e1, "w_in_sb", [128, 8, 2328], BF16)
    B_win = K.buf("w_in")
    for k in range(8):
        for g in range(2):
            K.dma("pool", w_in_sb[:, k, 0:512].rearrange("p (r g d) -> p g r d", r=4, g=2, d=64)[:, g],
                  w_in[k * 128:(k + 1) * 128, g * 256:(g + 1) * 256].rearrange("p (r d) -> p r d", d=64),
                  writes=[B_win])
        K.dma("pool", w_in_sb[:, k, 512:2328], w_in[k * 128:(k + 1) * 128, 512:2328], writes=[B_win])
    w1sb = sbt(e1, "w1sb", [128, 2, 32, 128], BF16)
    for kv in range(2):
        for hf in range(2):
            K.dma("pool", w1sb[hf * 64:(hf + 1) * 64, kv], cmp_w1[kv].rearrange("(p d) h -> d p h", d=64),
                  writes=[B_win])
    w2pad = sbt(e1, "w2pad", [128, 2, 2, 128], BF16)
    K.op("pool", lambda e: e.memset(w2pad, 0.0), writes=[B_win])
    for kv in range(2):
        for g in range(2):
            K.dma("pool", w2pad[:, kv, g, g * 64:(g + 1) * 64], cmp_w2[kv], writes=[B_win])
    b1col = sbt(e1, "b1col", [128, 2], F32)
    for kv in range(2):
        K.dma("sp", b1col[:, kv:kv + 1], cmp_b1[kv].rearrange("(p o) -> p o", o=1), writes=[B_win])
    peTok = sbt(e1, "peTok", [32, 2, 64], F32)
    K.dma("sp", peTok, cmp_pe.rearrange("k p d -> p k d"), writes=[B_win])
    peT = sbt(e1, "peT", [64, 2, 32], BF16)
    b1eff = sbt(e1, "b1eff", [128, 2], F32)
    for kv in range(2):
        pt, pb = ps_g()
        K.op("pe", lambda e: e.transpose(out=pt[0:64, 0:32], in_=peTok[:, kv, :], identity=ident_f[0:32, 0:32]),
             reads=[B_win, B_const], writes=[pb])
        K.op("dve", lambda e: e.tensor_copy(out=peT[:, kv, :], in_=pt[0:64, 0:32]), reads=[pb], writes=[B_win])
    for kv in range(2):
        pt, pb = ps_g()
        for pos in range(32):
            K.op("pe", lambda e: e.matmul(pt[:, 0:1], lhsT=w1sb[0:64, kv, pos, :], rhs=peT[:, kv, pos:pos + 1],
                                          start=(pos == 0), stop=(pos == 31)), reads=[B_win], writes=[pb], sig=(pos == 31))
        K.op("dve", lambda e: e.tensor_tensor(out=b1eff[:, kv:kv + 1], in0=pt[:, 0:1], in1=b1col[:, kv:kv + 1],
                                              op=ALU.add), reads=[pb, B_win], writes=[B_win])
    ws_f = sbt(e1, "ws_f", [128, 4, 128], F32)
    K.dma("sp", ws_f, gmlp_ws.rearrange("g i j -> i g j"), writes=[B_win])
    K.op("pool", lambda e: e.affine_select(out=ws_f, in_=ws_f, pattern=[[0, 4], [-1, 128]], compare_op=ALU.is_ge,
                                           fill=0.0, base=0, channel_multiplier=1), reads=[B_win], writes=[B_win])
    WsT = sbt(e1, "WsT", [128, 4, 128], BF16)
    for g in range(4):
        pt, pb = ps_g()
        K.op("pe", lambda e: e.transpose(out=pt[:, 0:128], in_=ws_f[:, g, :], identity=ident_f),
             reads=[B_win, B_const], writes=[pb])
        K.op("dve", lambda e: e.tensor_copy(out=WsT[:, g, :], in_=pt[:, 0:128]), reads=[pb], writes=[B_win])
    bsB = sbt(e1, "bsB", [128, 512], F32)
    K.dma("sp", bsB, gmlp_bs.partition_broadcast(128), writes=[B_win])
    gnB = sbt(e1, "gnB", [128, 512], F32)
    K.dma("sp", gnB, gmlp_norm.partition_broadcast(128), writes=[B_win])

    mask_diag4 = sbt(e1, "mask_diag4", [128, 4, 128], BF16)
    mask_band4 = sbt(e1, "mask_band4", [128, 4, 128], BF16)
    K.op("pool", lambda e: e.affine_select(out=mask_diag4, in_=r4(zeros_b), pattern=[[0, 4], [1, 128]],
                                           compare_op=ALU.is_ge, fill=NEGM, base=0, channel_multiplier=-1),
         reads=[B_const], writes=[B_win])
    K.op("pool", lambda e: e.affine_select(out=mask_band4, in_=r4(zeros_b), pattern=[[0, 4], [-1, 128]],
                                           compare_op=ALU.is_ge, fill=NEGM, base=0, channel_multiplier=1),
         reads=[B_const], writes=[B_win])
    Mst = sbt(e1, "Mst", [128, 128], F32)
    K.op("pool", lambda e: e.memset(Mst, 0.0), writes=[B_win])
    K.op("pool", lambda e: e.memset(Mst[:, 66:128], -1e30), writes=[B_win])
    K.op("pool", lambda e: e.memset(Mst[0:64, 65:66], -1e30), writes=[B_win])
    K.op("pool", lambda e: e.memset(Mst[64:128, 65:66], 1e6), writes=[B_win])
    K.op("pool", lambda e: e.memset(Mst[:, 64:65], 1e6), writes=[B_win])
    K.op("pool", lambda e: e.memset(Mst[0:64, 63:64], 1e6), writes=[B_win])
    vis8x2 = sbt(e1, "vis8x2", [128, 2, 8], BF16)
    K.op("pool", lambda e: e.affine_select(out=vis8x2, in_=zeros_b[:, 0:16].rearrange("p (a b) -> p a b", a=2),
                                           pattern=[[0, 2], [-16, 8]], compare_op=ALU.is_ge, fill=NEGM, base=-15,
                                           channel_multiplier=1), reads=[B_const], writes=[B_win])
    selbB = sbt(e1, "selbB", [128, 64], F32)
    K.dma("sp", selbB, selb.partition_broadcast(128), writes=[B_win])
    cmpb_row2 = sbt(e1, "cmpb_row2", [1, 2, 256], BF16)
    for a in range(2):
        K.dma("pool", cmpb_row2[:, a, :], cmpb[:, 0:256], writes=[B_win])
    cmpbT = sbt(e1, "cmpbT", [128, 2], F32)
    for ch in range(2):
        K.dma("sp", cmpbT[:, ch:ch + 1], cmpb[0, ch * 128:(ch + 1) * 128].rearrange("(p o) -> p o", o=1),
              writes=[B_win])
    histneg4 = sbt(e1, "histneg4", [1, 512], BF16)
    K.op("dve", lambda e: e.tensor_scalar(out=histneg4, in0=zeros_b[0:1, :], scalar1=hsc_sb[0:1, 0:1], scalar2=None,
                                          op0=ALU.add), reads=[B_const], writes=[B_win])

    if stage <= 1:
        K.finish()
        return nc
    e1p = ExitStack()
    kslcE = [sbt(e1p, "kslcE%d" % i, [128, 4096], BF16) for i in range(2)]
    qS = [sbt(e1p, "qS%d" % i, [128, 4, 512], BF16) for i in range(2)]
    selt2 = [sbt(e1p, "selt2_%d" % i, [128, 128], BF16) for i in range(2)]
    Vslc = sbt(e1p, "Vslc", [128, 32, 2, 65], BF16)
    kwinT = sbt(e1p, "kwinT", [128, 8, 128], BF16)
    Vwin = sbt(e1p, "Vwin", [128, 8, 2, 65], BF16)
    kvcT = sbt(e1p, "kvcT", [128, 2, 16 + 512], BF16)
    kcT = sbt(e1p, "kcT", [128, NSLOT], BF16)
    vcT = sbt(e1p, "vcT", [128, NSLOT], BF16)
    vcaug = sbt(e1p, "vcaug", [128, 2, 2, 65], BF16)
    B_kslc = [K.buf() for _ in range(32)]
    B_vslc = [K.buf() for _ in range(32)]
    B_kwin = [K.buf() for _ in range(8)]
    B_vwin = [K.buf() for _ in range(8)]
    B_kvc = [K.buf(), K.buf()]
    B_kc = K.buf()
    B_vc = K.buf()
    B_vcaug = [K.buf(), K.buf()]
    for g_ in range(2):
        Ebh = kslcE[g_][(1 - g_) * 64:(2 - g_) * 64, :]
        K.op("pool", lambda e: e.memset(Ebh, 1.0), writes=[B_win])
        K.op("pool", lambda e: e.affine_select(out=Ebh, in_=Ebh, pattern=[[1, 4096]], compare_op=ALU.is_ge, fill=0.0,
                                               base=0, channel_multiplier=-64), reads=[B_win], writes=[B_win])
        K.op("pool", lambda e: e.affine_select(out=Ebh, in_=Ebh, pattern=[[-1, 4096]], compare_op=ALU.is_ge, fill=0.0,
                                               base=63, channel_multiplier=64), reads=[B_win], writes=[B_win])
        K.op("pool", lambda e: e.memset(selt2[g_], 0.0), writes=[B_win])
        K.op("pool", lambda e: e.memset(qS[g_], 0.0), writes=[B_win])
    K.op("pool", lambda e: e.memset(Vslc, 1.0), writes=B_vslc)
    K.op("pool", lambda e: e.memset(Vwin, 1.0), writes=B_vwin)
    K.op("pool", lambda e: e.memset(vcaug, 1.0), writes=B_vcaug)
    K.op("pool", lambda e: e.memset(kvcT, 0.0), writes=B_kvc)
    K.op("pool", lambda e: e.memset(kcT, 0.0), writes=[B_kc])
    K.op("pool", lambda e: e.memset(vcT, 0.0), writes=[B_vc])

    xbuf = [sbt(e1p, "xbuf%d" % i, [128, D], F32) for i in range(2)]
    B_x = [K.buf(), K.buf()]
    xn = [sbt(e1p, "xn%d" % i, [128, D], BF16) for i in range(2)]
    B_xn = [K.buf(), K.buf()]
    st4 = sbt(e1p, "st4", [128, 8], F32)
    B_st = K.buf()
    hT = sbt(e1p, "hT", [128, 8, 512], BF16)
    B_hT = [K.buf() for _ in range(4)]
    qTz = [sbt(e1p, "qTz%d" % i, [128, 4, 512], BF16) for i in range(2)]
    B_qT = K.buf()
    for i in range(2):
        K.op("pool", lambda e: e.memset(qTz[i], 0.0), writes=[B_qT])
    uT = sbt(e1p, "uT", [128, 4, 512], BF16)
    B_uT = K.buf()
    kvtok = [sbt(e1p, "kvtok%d" % i, [128, 768], F32) for i in range(2)]
    B_kvtok = [K.buf(), K.buf()]
    sg = sbt(e1p, "sg", [128, 4, 24], F32)
    B_sg = [K.buf() for _ in range(4)]
    gv = sbt(e1p, "gv", [128, 512], F32)
    B_gv = K.buf()
    vn_f = sbt(e1p, "vn_f", [128, 512], F32)
    B_vnf = K.buf()
    vn_b = sbt(e1p, "vn_b", [128, 512], BF16)
    B_vnb = K.buf()
    tmpm = sbt(e1p, "tmpm", [128, 512], F32)
    B_tmpm = K.buf()
    mixT = sbt(e1p, "mixT", [128, 4, 512], BF16)
    B_mixT = K.buf()
    gh = sbt(e1p, "gh", [128, 2, 2, 32], BF16)
    B_gh = K.buf()
    ET = [sbt(e1p, "ET%d" % i, [128, 512], BF16) for i in range(4)]
    B_ET = [K.buf() for _ in range(4)]
    etr = [0]
    Eh = [sbt(e1p, "Eh%d" % i, [128, 256], F32) for i in range(2)]
    B_Eh = [K.buf(), K.buf()]
    impbuf = sbt(e1p, "impbuf", [128, 272], F32)
    B_imp = K.buf()
    K.op("pool", lambda e: e.memset(impbuf, 0.0), writes=[B_imp])
    sc = sbt(e1p, "sc", [128, 64], F32)
    sc2 = sbt(e1p, "sc2", [128, 64], F32)
    mx8 = sbt(e1p, "mx8", [128, 16], F32)
    selt = sbt(e1p, "selt", [128, 64], BF16)
    den4 = sbt(e1p, "den4", [128, 8], F32)
    B_sel = K.buf()
    cmask = [sbt(e1p, "cmask%d" % i, [128, 4, 128], BF16) for i in range(2)]
    B_cmask = [K.buf(), K.buf()]
    negselT4 = [sbt(e1p, "negselT4_%d" % i, [64, 4, 128], BF16) for i in range(2)]
    B_negsel = [K.buf(), K.buf()]
    o_brs2 = [[sbt(e1p, "o_br%d_%d" % (t, i), [128, 3, 4, 65], F32) for i in range(2)] for t in range(2)]
    B_obrs2 = [[[K.buf() for _ in range(3)] for _ in range(2)] for _ in range(2)]
    coef = sbt(e1p, "coef", [128, 3, 4], F32)
    B_coef = K.buf()
    otmp = sbt(e1p, "otmp", [128, 3, 4, 64], F32)
    B_otmp = K.buf()
    o_nsa = sbt(e1p, "o_nsa", [128, 512], BF16)
    B_onsa = K.buf()
    o_nsaT = sbt(e1p, "o_nsaT", [128, 4, 512], BF16)
    B_onsaT = K.buf()

    print("SBUF remaining after pass1a alloc:", nc.sbuf_bytes_remaining)
    def rmsnorm_T(src_dram, xb, Bxb, xnb, Bxnb, gcol, hT_dst, B_hT_dst, x_preloaded=False):
        if not x_preloaded:
            K.dma("sp", xb, src_dram, writes=[Bxb])
        K.op("act", lambda e: e.activation(out=xnb, in_=xb, func=AF.Square, accum_out=st4[:, 0:1]),
             reads=[Bxb], writes=[Bxnb, B_st])
        K.op("act", lambda e: e.activation(out=st4[:, 1:2], in_=st4[:, 0:1], func=AF.Ln, scale=1.0 / D, bias=EPS), reads=[B_st], writes=[B_st])
        K.op("act", lambda e: e.activation(out=st4[:, 2:3], in_=st4[:, 1:2], func=AF.Exp, scale=-0.5), reads=[B_st], writes=[B_st])
        K.op("dve", lambda e: e.tensor_scalar(out=xnb, in0=xb, scalar1=st4[:, 2:3], scalar2=None, op0=ALU.mult),
             reads=[Bxb, B_st], writes=[Bxnb])
        pt, pb = ps_g()
        ptb = pt.bitcast(BF16).rearrange("p (k t) -> p k t", k=8)
        for k in range(8):
            K.op("pe", lambda e: e.transpose(out=ptb[:, k, :], in_=xnb[:, k * 128:(k + 1) * 128], identity=ident_b),
                 reads=[Bxnb, B_const], writes=[pb], sig=(k == 7))
        K.op("dve", lambda e: e.tensor_tensor(out=hT_dst, in0=ptb, in1=gcol.unsqueeze(2).to_broadcast([128, 8, 128]),
                                              op=ALU.mult), reads=[pb, B_const], writes=[B_hT_dst])

    def projT(cols_ap_fn, nt, B_hts, evac):
        pt, pb = ps_g()
        for k in range(8):
            K.op("pe", lambda e: e.matmul(pt[:, 0:nt], lhsT=cols_ap_fn(k), rhs=hT[:, k, 0:nt], start=(k == 0),
                                          stop=(k == 7)), reads=[B_win] + B_hts, writes=[pb], sig=(k == 7))
        evac(pt[:, 0:nt], pb)

    def tile_src(T):
        return xh[T * 128:(T + 1) * 128, :] if T < 16 else xo[(T - 16) * 128:(T - 15) * 128, :]

    def compress_st(seg0, nseg):
        for kv in range(2):
            for g in range(2):
                pt, pb = ps_g()
                i = 0
                for r in range(2):
                    for s in range(16):
                        c0 = 16 * r + s
                        K.op("pe", lambda e: e.matmul(pt[:, 0:nseg], lhsT=w1sb[g * 64:(g + 1) * 64, kv, r * 16 + s, :],
                                                      rhs=kvcT[g * 64:(g + 1) * 64, kv, c0:c0 + 16 * (nseg - 1) + 1:16],
                                                      start=(i == 0), stop=(i == 31)),
                             reads=[B_win, B_kvc[kv]], writes=[pb], sig=(i == 31))
                        i += 1
                K.op("act", lambda e: e.activation(out=gh[:, kv, g, 0:nseg], in_=pt[:, 0:nseg], func=AF.Gelu_apprx_tanh,
                                                   bias=b1eff[:, kv:kv + 1]), reads=[pb, B_win], writes=[B_gh])
            pt, pb = ps_g()
            for g in range(2):
                K.op("pe", lambda e: e.matmul(pt[:, 0:nseg], lhsT=w2pad[:, kv, g, :], rhs=gh[:, kv, g, 0:nseg],
                                              start=(g == 0), stop=(g == 1)), reads=[B_win, B_gh], writes=[pb], sig=(g == 1))
            dst, Bd = (kcT, B_kc) if kv == 0 else (vcT, B_vc)
            K.op("dve", lambda e: e.tensor_copy(out=dst[:, seg0:seg0 + nseg], in_=pt[:, 0:nseg]), reads=[pb],
                 writes=[Bd])
            K.op("pool", lambda e: e.tensor_copy(out=kvcT[:, kv, 0:16], in_=kvcT[:, kv, 16 * nseg:16 * nseg + 16]),
                 reads=[B_kvc[kv]], writes=[B_kvc[kv]])

    def vc_transpose(ch):
        pt, pb = ps_g()
        ptb = pt.bitcast(BF16)
        K.op("pe", lambda e: e.transpose(out=ptb[:, 0:128], in_=vcT[:, ch * 128:(ch + 1) * 128], identity=ident_b),
             reads=[B_vc, B_const], writes=[pb])
        K.op("dve", lambda e: e.tensor_copy(out=vcaug[:, ch, :, 0:64],
                                            in_=ptb[:, 0:128].rearrange("p (g d) -> p g d", g=2)),
             reads=[pb], writes=[B_vcaug[ch]])

    def front_st(T0, ntl, full):
        nt = ntl * 128
        for j in range(ntl):
            T = T0 + j
            rmsnorm_T(tile_src(T), xbuf[j % 2], B_x[j % 2], xn[j % 2], B_xn[j % 2], gcol_mix,
                      hT[:, :, j * 128:(j + 1) * 128], B_hT[j])
        Bh = B_hT[0:ntl]
        for kv in range(2):
            projT(lambda k: w_in_sb[:, k, 512 + kv * 128:640 + kv * 128], nt, Bh,
                  lambda p, pb: K.op("act", lambda e: e.copy(out=kvcT[:, kv, 16:16 + nt], in_=p), reads=[pb],
                                     writes=[B_kvc[kv]]))
        projT(lambda k: w_in_sb[:, k, 768:896], nt, Bh,
              lambda p, pb: (K.op("dve", lambda e: e.tensor_copy(out=kslcE[0][0:64, T0 * 128:T0 * 128 + nt],
                                                                  in_=p[0:64]), reads=[pb], writes=B_kslc[T0:T0 + ntl]),
                             K.op("act", lambda e: e.copy(out=kslcE[1][64:128, T0 * 128:T0 * 128 + nt], in_=p[64:128]),
                                  reads=[pb], writes=B_kslc[T0:T0 + ntl])))
        if T0 + ntl > 8:
            pt, pb = ps_g()
            for k in range(8):
                K.op("pe", lambda e: e.matmul(pt[:, 0:nt], lhsT=w_in_sb[:, k, 1024:1152], rhs=hT[:, k, 0:nt],
                                              start=(k == 0), stop=(k == 7)), reads=[B_win] + Bh, writes=[pb], sig=(k == 7))
            for j in range(ntl):
                sl = (T0 + j) % 8
                K.op("act", lambda e: e.copy(out=kwinT[:, sl, :], in_=pt[:, j * 128:(j + 1) * 128]), reads=[pb],
                     writes=[B_kwin[sl]])
        if full:
            for r in range(4):
                projT(lambda k: w_in_sb[:, k, 128 * r:128 * r + 128], nt, Bh,
                      lambda p, pb: (K.op("act", lambda e: e.copy(out=qTz[0][0:64, r, 0:nt], in_=p[0:64]), reads=[pb],
                                          writes=[B_qT]),
                                     K.op("dve", lambda e: e.tensor_copy(out=qTz[1][64:128, r, 0:nt], in_=p[64:128]),
                                          reads=[pb], writes=[B_qT]),
                                     K.op("pool", lambda e: e.tensor_copy(out=qS[0][0:64, r, 0:nt],
                                                                          in_=qTz[0][0:64, r, 0:nt]),
                                          reads=[B_qT], writes=[B_qT]),
                                     K.op("pool", lambda e: e.tensor_copy(out=qS[1][64:128, r, 0:nt],
                                                                          in_=qTz[1][64:128, r, 0:nt]),
                                          reads=[B_qT], writes=[B_qT])))
            for cc in range(4):
                projT(lambda k: w_in_sb[:, k, 1304 + cc * 128:1432 + cc * 128], nt, Bh,
                      lambda p, pb: K.op("act", lambda e: e.activation(out=uT[:, cc, 0:nt], in_=p,
                                                                       func=AF.Gelu_apprx_tanh), reads=[pb],
                                         writes=[B_uT]))
        for j in range(ntl):
            T = T0 + j
            kt_, Bkt = kvtok[j % 2], B_kvtok[j % 2]
            pa, pab = ps_g()
            for k in range(8):
                K.op("pe", lambda e: e.matmul(pa, lhsT=hT[:, k, j * 128:(j + 1) * 128], rhs=w_in_sb[:, k, 512:1024],
                                              start=(k == 0), stop=(k == 7)), reads=[B_win, B_hT[j]], writes=[pab], sig=(k == 7))
            pbk, pbb = ps_g()
            for k in range(8):
                K.op("pe", lambda e: e.matmul(pbk[:, 0:280], lhsT=hT[:, k, j * 128:(j + 1) * 128],
                                              rhs=w_in_sb[:, k, 1024:1304], start=(k == 0), stop=(k == 7)),
                     reads=[B_win, B_hT[j]], writes=[pbb], sig=(k == 7))
            K.op("act", lambda e: e.copy(out=kt_[:, 0:512], in_=pa), reads=[pab], writes=[Bkt])
            K.op("dve", lambda e: e.tensor_copy(out=kt_[:, 512:768], in_=pbk[:, 0:256]), reads=[pbb], writes=[Bkt])
            if full:
                K.op("act", lambda e: e.activation(out=sg[:, j, :], in_=pbk[:, 256:280], func=AF.Sigmoid),
                     reads=[pbb], writes=[B_sg[j]])
            K.op("pool", lambda e: e.tensor_copy(out=Vslc[:, T, :, 0:64],
                                                 in_=kt_[:, 384:512].rearrange("p (g d) -> p g d", g=2)),
                 reads=[Bkt], writes=[B_vslc[T]])
            if T >= 8:
                K.op("pool", lambda e: e.tensor_copy(out=Vwin[:, T % 8, :, 0:64],
                                                     in_=kt_[:, 640:768].rearrange("p (g d) -> p g d", g=2)),
                     reads=[Bkt], writes=[B_vwin[T % 8]])
            if T >= 16:
                for br in range(3):
                    K.dma("sp", kv_o[br, (T - 16) * 128:(T - 15) * 128, :], kt_[:, br * 256:(br + 1) * 256],
                          reads=[Bkt])
            if full:
                pc, pcb = ps_g()
                for k in range(8):
                    K.op("pe", lambda e: e.matmul(pc, lhsT=hT[:, k, j * 128:(j + 1) * 128],
                                                  rhs=w_in_sb[:, k, 1816:2328], start=(k == 0), stop=(k == 7)),
                         reads=[B_win, B_hT[j]], writes=[pcb], sig=(k == 7))
                K.op("act", lambda e: e.activation(out=gv, in_=pc, func=AF.Gelu_apprx_tanh), reads=[pcb],
                     writes=[B_gv])
                K.op("act", lambda e: e.activation(out=vn_b, in_=gv, func=AF.Square,
                                                   accum_out=st4[:, 4:5]), reads=[B_gv], writes=[B_vnb, B_st])
                K.op("act", lambda e: e.activation(out=st4[:, 5:6], in_=st4[:, 4:5], func=AF.Ln, scale=1.0 / 512, bias=EPS), reads=[B_st], writes=[B_st])
                K.op("act", lambda e: e.activation(out=st4[:, 6:7], in_=st4[:, 5:6], func=AF.Exp, scale=-0.5), reads=[B_st], writes=[B_st])
                K.op("dve", lambda e: e.scalar_tensor_tensor(out=vn_f, in0=gv, scalar=st4[:, 6:7], in1=gnB,
                                                             op0=ALU.mult, op1=ALU.mult),
                     reads=[B_gv, B_st, B_win], writes=[B_vnf])
                K.op("pool", lambda e: e.tensor_copy(out=vn_b, in_=vn_f), reads=[B_vnf], writes=[B_vnb])
                if T == 31:
                    K.dma("sp", vn_o, vn_f, reads=[B_vnf])
                pm, pmb = ps_g()
                for g in range(4):
                    K.op("pe", lambda e: e.matmul(pm[:, g * 128:(g + 1) * 128], lhsT=vn_b[:, g * 128:(g + 1) * 128],
                                                  rhs=WsT[:, g, :], start=(g == 0), stop=(g == 3),
                                                  skip_group_check=True), reads=[B_vnb, B_win], writes=[pmb], sig=(g == 3))
                K.op("dve", lambda e: e.tensor_tensor(out=tmpm, in0=pm, in1=bsB, op=ALU.add), reads=[pmb, B_win],
                     writes=[B_tmpm])
                K.op("pool", lambda e: e.tensor_tensor(out=mixT[:, :, j * 128:(j + 1) * 128], in0=r4(tmpm),
                                                       in1=uT[:, :, j * 128:(j + 1) * 128], op=ALU.mult),
                     reads=[B_tmpm, B_uT], writes=[B_mixT])
        compress_st(T0 * 8, ntl * 8)

    LOOKAHEAD = 2

    def attn_units(T, j):
        units = []
        for kind in ("win", "cmp", "slc"):
            for g in range(2):
                if kind == "slc":
                    kts = list(range(0, T + 1))
                elif kind == "win":
                    kts = list(range(T - 4, T + 1))
                else:
                    kts = [0] if T < 16 else [0, 1]
                for ki, kt in enumerate(kts):
                    units.append(dict(kind=kind, g=g, kt=kt, first=(ki == 0), last=(ki == len(kts) - 1)))
        state = {}

        def emit_qk(u):
            kind, g, kt = u["kind"], u["g"], u["kt"]
            ps, psb = ps_g()
            ex = []
            if kind == "slc":
                lk, Bk = kslcE[g][:, kt * 128:(kt + 1) * 128], B_kslc[kt]
                rq, Brq = qS[g][:, :, j * 128:(j + 1) * 128], [B_qT, B_negsel[g]]
                if kt == T:
                    ex.append((ident_b, mask_diag4.rearrange("p a b -> p (a b)"), []))
            elif kind == "win":
                lk, Bk = kwinT[:, kt % 8, :], B_kwin[kt % 8]
                rq, Brq = qTz[g][:, :, j * 128:(j + 1) * 128], [B_qT]
                if kt == T:
                    ex.append((ident_b, mask_diag4.rearrange("p a b -> p (a b)"), []))
                if kt == T - 4:
                    ex.append((ident_b, mask_band4.rearrange("p a b -> p (a b)"), []))
                if kt < 16:
                    ex.append((ones_b[0:1, 0:128], histneg4, []))
            else:
                lk, Bk = kcT[:, kt * 128:(kt + 1) * 128], B_kc
                rq, Brq = qTz[g][:, :, j * 128:(j + 1) * 128], [B_qT]
                if (T < 16 and kt == 0) or (T >= 16 and kt == 1):
                    ex.append((ident_b, cmask[T % 2].rearrange("p a b -> p (a b)"), [B_cmask[T % 2]]))
            K.op("pe", lambda e: e.matmul(ps, lhsT=lk, rhs=rq, start=True, stop=(len(ex) == 0)),
                 reads=[Bk, B_win] + Brq, writes=[psb], sig=(len(ex) == 0))
            for xi, (l2, r2, Bs) in enumerate(ex):
                K.op("pe", lambda e: e.matmul(ps, lhsT=l2, rhs=r2, start=False, stop=(xi == len(ex) - 1)),
                     reads=[B_win, B_const] + Bs, writes=[psb], sig=(xi == len(ex) - 1))
            u["ps"], u["psb"] = ps, psb

        def emit_exp_pv(u):
            kind, g, kt = u["kind"], u["g"], u["kt"]
            ps, psb = u["ps"], u["psb"]
            ei = etr[0]
            etr[0] = (ei + 1) % len(ET)
            if kind == "cmp":
                K.op("act", lambda e: e.activation(out=ET[ei], in_=ps, func=AF.Exp, scale=0.125,
                                                   bias=cmpbT[:, kt:kt + 1]), reads=[psb, B_win], writes=[B_ET[ei]])
            else:
                K.op("act", lambda e: e.activation(out=ET[ei], in_=ps, func=AF.Exp, scale=0.125), reads=[psb],
                     writes=[B_ET[ei]])
            if u["first"]:
                state["po"] = ps_o()
            po, pob = state["po"]
            po3 = po[:, 0:260].rearrange("p (h c) -> p h c", h=4)
            if kind == "slc":
                va, Bv = Vslc[:, kt, g, :], B_vslc[kt]
            elif kind == "win":
                va, Bv = Vwin[:, kt % 8, g, :], B_vwin[kt % 8]
            else:
                va, Bv = vcaug[:, kt, g, :], B_vcaug[kt]
            for h in range(4):
                K.op("pe", lambda e: e.matmul(po3[:, h, :], lhsT=ET[ei][:, h * 128:(h + 1) * 128], rhs=va,
                                              start=(u["first"] and h == 0), stop=(u["last"] and h == 3),
                                              skip_group_check=True), reads=[B_ET[ei], Bv], writes=[pob], sig=(h == 3))
            if u["last"]:
                bi = {"cmp": 0, "slc": 1, "win": 2}[kind]
                K.op("act", lambda e: e.copy(out=o_brs2[T % 2][g][:, bi], in_=po3), reads=[pob],
                     writes=[B_obrs2[T % 2][g][bi]])

        n = len(units)
        for i in range(min(LOOKAHEAD, n)):
            emit_qk(units[i])
        for i in range(n):
            if i + LOOKAHEAD < n:
                emit_qk(units[i + LOOKAHEAD])
            emit_exp_pv(units[i])

    def select_blocks(T, g, j):
        L = 8 * T + 8
        S3 = psS.rearrange("p (h c) -> p h c", h=4)
        for bk in range(2):
            Bb = B_S[bk]
            for hh in range(2):
                h = bk * 2 + hh
                K.op("pe", lambda e: e.matmul(S3[:, h, 0:L], lhsT=qTz[g][:, h, j * 128:(j + 1) * 128],
                                              rhs=kcT[:, 0:L], start=(hh == 0), stop=False,
                                              skip_group_check=True), reads=[B_qT, B_kc], writes=[Bb])
            K.op("pe", lambda e: e.matmul(S3[:, bk * 2:bk * 2 + 2, 0:L], lhsT=ones_b[0:1, 0:128],
                                          rhs=cmpb_row2[:, :, 0:L], start=False, stop=False, skip_group_check=True),
                 reads=[B_win, B_const], writes=[Bb])
            K.op("pe", lambda e: e.matmul(S3[:, bk * 2:bk * 2 + 2, L - 8:L], lhsT=ident_b, rhs=vis8x2, start=False,
                                          stop=True, skip_group_check=True), reads=[B_win, B_const], writes=[Bb])
        ck(2)
        for h in range(4):
            K.op("act", lambda e: e.activation(out=Eh[h % 2][:, 0:L], in_=S3[:, h, 0:L], func=AF.Exp, scale=0.125,
                                               accum_out=den4[:, h:h + 1]), reads=[B_S[h // 2]],
                 writes=[B_Eh[h % 2], B_sel])
            K.op("dve", lambda e: e.tensor_scalar(out=den4[:, 4 + h:5 + h], in0=den4[:, h:h + 1], scalar1=1e-30,
                                                  scalar2=None, op0=ALU.max), reads=[B_sel], writes=[B_sel])
            K.op("dve", lambda e: e.reciprocal(out=den4[:, 4 + h:5 + h], in_=den4[:, 4 + h:5 + h]), reads=[B_sel],
                 writes=[B_sel])
            if h == 0:
                K.op("dve", lambda e: e.tensor_scalar(out=impbuf[:, 0:L], in0=Eh[0][:, 0:L], scalar1=den4[:, 4:5],
                                                      scalar2=None, op0=ALU.mult), reads=[B_Eh[0], B_sel],
                     writes=[B_imp])
            else:
                K.op("dve", lambda e: e.scalar_tensor_tensor(out=impbuf[:, 0:L], in0=Eh[h % 2][:, 0:L],
                                                             scalar=den4[:, 4 + h:5 + h], in1=impbuf[:, 0:L],
                                                             op0=ALU.mult, op1=ALU.add),
                     reads=[B_Eh[h % 2], B_sel, B_imp], writes=[B_imp])
        ck(3)
        K.op("dve", lambda e: e.tensor_tensor(out=sc, in0=impbuf[:, 0:253:4], in1=impbuf[:, 1:254:4], op=ALU.add),
             reads=[B_imp], writes=[B_sel])
        for m in (2, 3, 4):
            K.op("dve", lambda e: e.tensor_tensor(out=sc, in0=sc, in1=impbuf[:, m:m + 253:4], op=ALU.add),
                 reads=[B_imp, B_sel], writes=[B_sel])
        K.op("dve", lambda e: e.tensor_tensor(out=sc, in0=sc, in1=selbB, op=ALU.add), reads=[B_sel, B_win],
             writes=[B_sel])
        K.op("dve", lambda e: e.tensor_tensor(out=sc, in0=sc, in1=Mst[:, 64 - 2 * T:128 - 2 * T], op=ALU.add),
             reads=[B_sel, B_win], writes=[B_sel])
        ck(4)
        K.op("dve", lambda e: e.max(out=mx8[:, 0:8], in_=sc), reads=[B_sel], writes=[B_sel])
        K.op("dve", lambda e: e.match_replace(out=sc2, in_to_replace=mx8[:, 0:8], in_values=sc, imm_value=-3e38),
             reads=[B_sel], writes=[B_sel])
        K.op("dve", lambda e: e.max(out=mx8[:, 8:16], in_=sc2), reads=[B_sel], writes=[B_sel])
        K.op("dve", lambda e: e.tensor_scalar(out=mx8[:, 15:16], in0=mx8[:, 15:16], scalar1=-1e29, scalar2=None,
                                              op0=ALU.max), reads=[B_sel], writes=[B_sel])
        K.op("dve", lambda e: e.tensor_scalar(out=sc2, in0=sc, scalar1=mx8[:, 15:16], scalar2=None, op0=ALU.is_ge),
             reads=[B_sel], writes=[B_sel])
        off = (1 - g) * 64
        K.op("dve", lambda e: e.tensor_scalar(out=selt2[g][:, off:off + 64], in0=sc2, scalar1=-1.0, scalar2=-NEGM,
                                              op0=ALU.add, op1=ALU.mult), reads=[B_sel], writes=[B_sel])
        pt, pb = ps_g()
        ptb = pt.bitcast(BF16)
        K.op("pe", lambda e: e.transpose(out=ptb[:, 0:128], in_=selt2[g], identity=ident_b), reads=[B_sel, B_const],
             writes=[pb])
        K.op("dve", lambda e: e.tensor_copy(out=qS[g][off:off + 64, :, j * 128:(j + 1) * 128],
                                            in_=ptb[off:off + 64, 0:128].unsqueeze(1).to_broadcast([64, 4, 128])),
             reads=[pb], writes=[B_negsel[g]])

    def attn_select(T, j):
        ch = 0 if T < 16 else 1
        base = 2048 * ch + 15 - 128 * T
        K.op("pool", lambda e: e.affine_select(out=cmask[T % 2], in_=r4(negs_b), pattern=[[0, 4], [-1, 128]],
                                               compare_op=ALU.is_gt, fill=0.0, base=base, channel_multiplier=16),
             reads=[B_const], writes=[B_cmask[T % 2]])
        for g in range(2):
            select_blocks(T, g, j)

    def attn_combine(T, j):
        for g in range(2):
            o_br, B_obr = o_brs2[T % 2][g], B_obrs2[T % 2][g]
            K.op("dve", lambda e: e.tensor_scalar(out=coef, in0=o_br[:, :, :, 64], scalar1=1e-30, scalar2=None,
                                                  op0=ALU.max), reads=B_obr, writes=[B_coef])
            ck(10)
            K.op("dve", lambda e: e.reciprocal(out=coef, in_=coef), reads=[B_coef], writes=[B_coef])
            ck(11)
            K.op("dve", lambda e: e.tensor_tensor(out=coef, in0=coef,
                                                  in1=sg[:, j, g * 12:(g + 1) * 12].rearrange("p (h b) -> p b h", b=3),
                                                  op=ALU.mult), reads=[B_coef, B_sg[j]], writes=[B_coef])
            ck(12)
            K.op("dve", lambda e: e.tensor_tensor(out=otmp, in0=o_br[:, :, :, 0:64],
                                                  in1=coef.unsqueeze(3).to_broadcast([128, 3, 4, 64]), op=ALU.mult),
                 reads=B_obr + [B_coef], writes=[B_otmp])
            ck(13)
            K.op("pool", lambda e: e.tensor_tensor(out=otmp[:, 0], in0=otmp[:, 0], in1=otmp[:, 1], op=ALU.add),
                 reads=[B_otmp], writes=[B_otmp])
            K.op("pool", lambda e: e.tensor_tensor(out=o_nsa[:, g * 256:(g + 1) * 256].rearrange(
                "p (h d) -> p h d", h=4), in0=otmp[:, 0], in1=otmp[:, 2], op=ALU.add), reads=[B_otmp],
                writes=[B_onsa])
        pt, pb = ps_g()
        ptb = pt.bitcast(BF16).rearrange("p (k t) -> p k t", k=8)
        for cc in range(4):
            K.op("pe", lambda e: e.transpose(out=ptb[:, cc, :], in_=o_nsa[:, cc * 128:(cc + 1) * 128],
                                             identity=ident_b), reads=[B_onsa, B_const], writes=[pb])
        K.op("dve", lambda e: e.tensor_copy(out=o_nsaT[:, :, j * 128:(j + 1) * 128], in_=ptb[:, 0:4, :]), reads=[pb],
             writes=[B_onsaT])

    hist_sts = [(0, 4), (4, 4), (8, 4), (12, 3)]
    own_sts = [(15, 4), (19, 4), (23, 4), (27, 4), (31, 1)]
    for (T0, ntl) in hist_sts:
        front_st(T0, ntl, False)
        if stage <= 2:
            K.finish()
            return nc
    vc_transpose(0)
    for si, (T0, ntl) in enumerate(own_sts):
        front_st(T0, ntl, True)
        vc_transpose(1)
        if si == 0:
            vc_transpose(0)
        if stage <= 3:
            K.finish()
            return nc
        attn_select(T0, 0)
        for j in range(ntl):
            if j + 1 < ntl:
                attn_select(T0 + j + 1, j + 1)
            attn_units(T0 + j, j)
            if j >= 1:
                attn_combine(T0 + j - 1, j - 1)
        attn_combine(T0 + ntl - 1, ntl - 1)
        if stage <= 5:
            K.finish()
            return nc
        nt = ntl * 128
        K.dma("sp", scr_om[si, :, 0:4, 0:nt], o_nsaT[:, :, 0:nt], reads=[B_onsaT], writes=[B_scr_om[si]])
        K.dma("sp", scr_om[si, :, 4:8, 0:nt], mixT[:, :, 0:nt], reads=[B_mixT], writes=[B_scr_om[si]])

    K.barrier()
    e1p.close()
    if do_sample:
        s1 = ExitStack()
        xs_sb = sbt(s1, "xs_sb", [4, D], F32)
        B_xs = K.buf()
        K.dma("sp", xs_sb, xs, writes=[B_xs])
        sqs = sbt(s1, "sqs", [4, D], BF16)
        sts = sbt(s1, "sts", [4, 8], F32)
        xns = sbt(s1, "xns", [4, D], BF16)
        B_s = K.buf()
        K.op("act", lambda e: e.activation(out=sqs, in_=xs_sb, func=AF.Square, accum_out=sts[:, 0:1]), reads=[B_xs],
             writes=[B_s])
        K.op("act", lambda e: e.activation(out=sts[:, 1:2], in_=sts[:, 0:1], func=AF.Ln, scale=1.0 / D, bias=EPS),
             reads=[B_s], writes=[B_s])
        K.op("act", lambda e: e.activation(out=sts[:, 2:3], in_=sts[:, 1:2], func=AF.Exp, scale=-0.5), reads=[B_s],
             writes=[B_s])
        K.op("dve", lambda e: e.tensor_scalar(out=xns, in0=xs_sb, scalar1=sts[:, 2:3], scalar2=None, op0=ALU.mult),
             reads=[B_xs, B_s], writes=[B_s])
        pt, pb = ps_g()
        ptb = pt.bitcast(BF16)[:, 0:32].rearrange("p (k t) -> p k t", k=8)
        for k in range(8):
            K.op("pe", lambda e: e.transpose(out=ptb[:, k, :], in_=xns[:, k * 128:(k + 1) * 128],
                                             identity=ident_b[0:4, 0:4]), reads=[B_s, B_const], writes=[pb])
        K.op("dve", lambda e: e.tensor_tensor(out=hT_s, in0=ptb, in1=gcol_mix.unsqueeze(2).to_broadcast([128, 8, 4]),
                                              op=ALU.mult), reads=[pb, B_const], writes=[B_hTs])
        zs = sbt(s1, "zs", [4, 1816], F32)
        B_zs = K.buf()
        for c0 in range(512, 2328, 512):
            c1 = min(c0 + 512, 2328)
            pt, pb = ps_g()
            for k in range(8):
                K.op("pe", lambda e: e.matmul(pt[0:4, 0:c1 - c0], lhsT=hT_s[:, k, :], rhs=w_in_sb[:, k, c0:c1],
                                              start=(k == 0), stop=(k == 7)), reads=[B_win, B_hTs], writes=[pb], sig=(k == 7))
            K.op("dve", lambda e: e.tensor_copy(out=zs[:, c0 - 512:c1 - 512], in_=pt[0:4, 0:c1 - c0]), reads=[pb],
                 writes=[B_zs])
        K.dma("sp", kvs_o, zs[:, 0:768], reads=[B_zs])
        sg_s = sbt(s1, "sg_s", [4, 24], F32)
        u_s = sbt(s1, "u_s", [4, 512], F32)
        v_s = sbt(s1, "v_s", [4, 512], F32)
        vn_s = sbt(s1, "vn_s", [4, 512], F32)
        mixs = sbt(s1, "mixs", [4, 512], F32)
        mixsb = sbt(s1, "mixsb", [4, 512], BF16)
        B_sm = K.buf()
        K.op("act", lambda e: e.activation(out=sg_s, in_=zs[:, 768:792], func=AF.Sigmoid), reads=[B_zs], writes=[B_sm])
        K.op("act", lambda e: e.activation(out=u_s, in_=zs[:, 792:1304], func=AF.Gelu_apprx_tanh), reads=[B_zs],
             writes=[B_sm])
        K.op("act", lambda e: e.activation(out=v_s, in_=zs[:, 1304:1816], func=AF.Gelu_apprx_tanh), reads=[B_zs],
             writes=[B_sm])
        K.op("act", lambda e: e.activation(out=sqs[:, 0:512], in_=v_s, func=AF.Square, accum_out=sts[:, 4:5]),
             reads=[B_sm], writes=[B_s])
        K.op("act", lambda e: e.activation(out=sts[:, 5:6], in_=sts[:, 4:5], func=AF.Ln, scale=1.0 / 512, bias=EPS),
             reads=[B_s], writes=[B_s])
        K.op("act", lambda e: e.activation(out=sts[:, 6:7], in_=sts[:, 5:6], func=AF.Exp, scale=-0.5), reads=[B_s],
             writes=[B_s])
        K.op("dve", lambda e: e.scalar_tensor_tensor(out=vn_s, in0=v_s, scalar=sts[:, 6:7], in1=gnB[0:4, :],
                                                     op0=ALU.mult, op1=ALU.mult), reads=[B_sm, B_s, B_win],
             writes=[B_sm])
        K.dma("sp", vns_o, vn_s, reads=[B_sm])
        ws00 = sbt(s1, "ws00", [4, 8], F32)
        with nc.allow_non_contiguous_dma(reason="4 scalars"):
            K.dma("sp", ws00[:, 0:4], gmlp_ws[:, 0, 0:1].rearrange("g o -> o g").partition_broadcast(4), writes=[B_sm])
            K.dma("sp", ws00[:, 4:8], gmlp_bs[:, 0:512:128].partition_broadcast(4), writes=[B_sm])
        K.op("dve", lambda e: e.tensor_tensor(out=r4(mixs), in0=r4(vn_s), in1=ws00[:, 0:4].unsqueeze(2).to_broadcast(
            [4, 4, 128]), op=ALU.mult), reads=[B_sm], writes=[B_sm])
        K.op("dve", lambda e: e.tensor_tensor(out=r4(mixs), in0=r4(mixs), in1=ws00[:, 4:8].unsqueeze(2).to_broadcast(
            [4, 4, 128]), op=ALU.add), reads=[B_sm], writes=[B_sm])
        K.op("dve", lambda e: e.tensor_tensor(out=mixsb, in0=mixs, in1=u_s, op=ALU.mult), reads=[B_sm], writes=[B_sm])
        pt, pb = ps_g()
        ptb = pt.bitcast(BF16)[:, 0:16].rearrange("p (k t) -> p k t", k=4)
        for k in range(4):
            K.op("pe", lambda e: e.transpose(out=ptb[:, k, :], in_=mixsb[:, k * 128:(k + 1) * 128],
                                             identity=ident_b[0:4, 0:4]), reads=[B_sm, B_const], writes=[pb])
        K.op("dve", lambda e: e.tensor_copy(out=omT_s[:, 4:8, :], in_=ptb), reads=[pb], writes=[B_omTs])
        qTs = [sbt(s1, "qTs%d" % i, [128, 4, 4], BF16) for i in range(2)]
        B_qTs = K.buf()
        for i in range(2):
            K.op("pool", lambda e: e.memset(qTs[i], 0.0), writes=[B_qTs])
        for r in range(4):
            pt, pb = ps_g()
            for k in range(8):
                K.op("pe", lambda e: e.matmul(pt[:, 0:4], lhsT=w_in_sb[:, k, r * 128:(r + 1) * 128], rhs=hT_s[:, k, :],
                                              start=(k == 0), stop=(k == 7)), reads=[B_win, B_hTs], writes=[pb], sig=(k == 7))
            K.op("dve", lambda e: e.tensor_copy(out=qTs[0][0:64, r, :], in_=pt[0:64, 0:4]), reads=[pb], writes=[B_qTs])
            K.op("dve", lambda e: e.tensor_copy(out=qTs[1][64:128, r, :], in_=pt[64:128, 0:4]), reads=[pb],
                 writes=[B_qTs])
        kTn = sbt(s1, "kTn", [128, 2, 4], BF16)
        B_kTn = K.buf()
        for bi_, c0 in enumerate((768, 1024)):
            pt, pb = ps_g()
            for k in range(8):
                K.op("pe", lambda e: e.matmul(pt[:, 0:4], lhsT=w_in_sb[:, k, c0:c0 + 128], rhs=hT_s[:, k, :],
                                              start=(k == 0), stop=(k == 7)), reads=[B_win, B_hTs], writes=[pb], sig=(k == 7))
            K.op("dve", lambda e: e.tensor_copy(out=kTn[:, bi_, :], in_=pt[:, 0:4]), reads=[pb], writes=[B_kTn])
        Vn = sbt(s1, "Vn", [4, 2, 2, 65], BF16)
        B_Vn = K.buf()
        K.op("pool", lambda e: e.memset(Vn, 1.0), writes=[B_Vn])
        for bi_, c0 in enumerate((384, 640)):
            K.op("dve", lambda e: e.tensor_copy(out=Vn[:, bi_, :, 0:64],
                                                in_=zs[:, c0:c0 + 128].rearrange("p (g d) -> p g d", g=2)),
                 reads=[B_zs], writes=[B_Vn])
        dmask = sbt(s1, "dmask", [4, 8, 4], BF16)
        K.op("pool", lambda e: e.affine_select(out=dmask, in_=zeros_b[0:4, 0:32].rearrange("p (a b) -> p a b", a=8),
                                               pattern=[[0, 8], [1, 4]], compare_op=ALU.is_equal, fill=NEGM, base=0,
                                               channel_multiplier=-1), reads=[B_const], writes=[B_Vn])
        ptx_sb = sbt(s1, "ptx_sb", [128, 32], I32)
        ptf = sbt(s1, "ptf", [128, 32], F32)
        idx_i = sbt(s1, "idx_i", [128, 32], I32)
        pmod_sb = sbt(s1, "pmod_sb", [128, 1], F32)
        B_idx = K.buf()
        K.dma("sp", ptx_sb, ptx, writes=[B_idx])
        K.dma("sp", pmod_sb, pmod, writes=[B_idx])
        K.op("dve", lambda e: e.tensor_copy(out=ptf, in_=ptx_sb), reads=[B_idx], writes=[B_idx])
        K.op("dve", lambda e: e.tensor_scalar(out=ptf, in0=ptf, scalar1=8.0, scalar2=pmod_sb[:, 0:1], op0=ALU.mult,
                                              op1=ALU.add), reads=[B_idx], writes=[B_idx])
        K.op("dve", lambda e: e.tensor_copy(out=idx_i, in_=ptf), reads=[B_idx], writes=[B_idx])

        pg = [sbt(s1, "pg%d" % i, [128, 16, 256], F32) for i in range(2)]
        B_pg = [K.buf(), K.buf()]
        kTs = sbt(s1, "kTs", [128, 2, 16, 129], BF16)
        B_kTs = K.buf()
        kcs = sbt(s1, "kcs", [128, 2, 1024], BF16)
        B_kcs = K.buf()
        ghs = sbt(s1, "ghs", [128, 2, 128], BF16)
        B_ghs = K.buf()
        vcas = sbt(s1, "vcas", [128, 8, 2, 65], BF16)
        B_vcas = K.buf()
        K.op("pool", lambda e: e.memset(vcas, 1.0), writes=[B_vcas])
        KTt = [sbt(s1, "KTt%d" % i, [128, 4, 128], BF16) for i in range(2)]
        B_KTt = [K.buf(), K.buf()]
        Vas = [sbt(s1, "Vas%d" % i, [128, 16, 2, 65], BF16) for i in range(2)]
        B_Vas = [K.buf(), K.buf()]
        for i in range(2):
            K.op("pool", lambda e: e.memset(Vas[i], 1.0), writes=[B_Vas[i]])
        ETs = [sbt(s1, "ETs%d" % i, [128, 16, 8, 4], BF16) for i in range(2)]
        B_ETs = [K.buf(), K.buf()]
        etc = [0]
        Oacc = sbt(s1, "Oacc", [4, 3, 8, 65], F32)
        B_Oacc = K.buf()
        K.op("pool", lambda e: e.memset(Oacc, 0.0), writes=[B_Oacc])
        Es = sbt(s1, "Es", [4, 1024], F32)
        imps = sbt(s1, "imps", [1, 1032], F32)
        scs = sbt(s1, "scs", [1, 264], F32)
        scs2 = sbt(s1, "scs2", [1, 264], F32)
        mxs = sbt(s1, "mxs", [1, 16], F32)
        negx = sbt(s1, "negx", [1, 256, 4], BF16)
        mcol = sbt(s1, "mcol", [128, 2, 8], F32)
        dens = sbt(s1, "dens", [4, 4], F32)
        B_sel = K.buf()
        B_mcol = K.buf()
        K.op("pool", lambda e: e.memset(imps, 0.0), writes=[B_sel])
        s0col = sbt(s1, "s0col", [128, 1], F32)
        K.op("pool", lambda e: e.memset(s0col, 0.0), writes=[B_sel])
        K.op("pool", lambda e: e.memset(s0col[0:1, :], NEGM), writes=[B_sel])
        s0row = sbt(s1, "s0row", [1, 512], BF16)
        K.op("pool", lambda e: e.memset(s0row, 0.0), writes=[B_sel])
        K.op("pool", lambda e: e.memset(s0row[:, 0:1], NEGM), writes=[B_sel])
        ones4f = sbt(s1, "ones4f", [4, 1], F32)
        K.op("pool", lambda e: e.memset(ones4f, 1.0), writes=[B_sel])
        bon = sbt(s1, "bon", [1, 264], F32)
        K.op("pool", lambda e: e.memset(bon, 0.0), writes=[B_sel])
        K.op("pool", lambda e: e.memset(bon[:, 0:1], 1e6), writes=[B_sel])
        K.op("pool", lambda e: e.memset(bon[:, 255:257], 1e6), writes=[B_sel])
        K.op("pool", lambda e: e.memset(bon[:, 257:264], -1e30), writes=[B_sel])

        def acc_banks():
            return allb[4], allb[5]

        def evac_acc(bi):
            (pA, pAb), (pB, pBb) = acc_banks()
            for g, (pp, ppb) in enumerate(((pA, pAb), (pB, pBb))):
                K.op("dve", lambda e: e.tensor_tensor(out=Oacc[:, bi, g * 4:(g + 1) * 4, :],
                                                      in0=Oacc[:, bi, g * 4:(g + 1) * 4, :],
                                                      in1=pp[0:4, 0:260].rearrange("p (h c) -> p h c", h=4),
                                                      op=ALU.add), reads=[ppb, B_Oacc], writes=[B_Oacc])

        def pv_tile(ETt, Bet, s_idx, Vt, Bv, first, kparts=128):
            (pA, pAb), (pB, pBb) = acc_banks()
            for g, (pp, ppb) in enumerate(((pA, pAb), (pB, pBb))):
                p3 = pp[0:4, 0:260].rearrange("p (h c) -> p h c", h=4)
                for r in range(4):
                    K.op("pe", lambda e: e.matmul(p3[:, r, :], lhsT=ETt[0:kparts, s_idx, g * 4 + r, :],
                                                  rhs=Vt[0:kparts, g, :], start=(first and r == 0), stop=False,
                                                  skip_group_check=True), reads=[Bet, Bv], writes=[ppb], sig=(r == 3))

        jobs = []
        for b_ in range(4):
            jobs += [("cmp", b_, g_) for g_ in range(8)] + [("slc", b_, g_) for g_ in range(8)]
            jobs += [("win", b_, w_) for w_ in range(4)]
        jstate = {"issued": 0, "taken": 0}

        def issue_job(i):
            kind_, b_, x_ = jobs[i]
            pgb, Bpg = pg[i % 2], B_pg[i % 2]
            if kind_ == "win":
                K.dma("sp", pgb[:, 0, :], cache_win[b_, x_ * 128:(x_ + 1) * 128, :], writes=[Bpg])
                return
            cache = cache_cmp if kind_ == "cmp" else cache_slc
            col = b_ * 8 + x_
            K._wait("pool", K._deps([B_idx], [Bpg]))
            ring = K.dring["pool"]
            si = ring[K.dpos["pool"] % len(ring)]
            K.dpos["pool"] += 1
            if K.dtgt[si] > 0:
                K._wait("pool", {("d", si): K.dtgt[si]})
            ins = nc.gpsimd.indirect_dma_start(out=pgb.rearrange("p s c -> p (s c)"), out_offset=None, in_=cache,
                                               in_offset=bass.IndirectOffsetOnAxis(ap=idx_i[:, col:col + 1], axis=0))
            K.dtgt[si] += 16
            ins.then_inc(K.dsem[si], 16)
            K._mark(("d", si), K.dtgt[si], [B_idx], [Bpg])

        def take_job(kind_, b_, x_):
            i = jstate["taken"]
            assert jobs[i] == (kind_, b_, x_), (jobs[i], kind_, b_, x_)
            while jstate["issued"] <= min(i + 1, len(jobs) - 1):
                issue_job(jstate["issued"])
                jstate["issued"] += 1
            jstate["taken"] += 1
            return pg[i % 2], B_pg[i % 2]

        gi = [0]
        for b in range(4):
            for i in range(2):
                K.op("dve", lambda e: e.memset(ETs[i], 0.0), writes=[B_ETs[i]])
            K.op("dve", lambda e: e.memset(kTs, 0.0), writes=[B_kTs])
            for grp in range(8):
                pgb, Bpg = take_job("cmp", b, grp)
                for kv in range(2):
                    for s4 in range(4):
                        pt, pb = ps_g()
                        for q_ in range(4):
                            s = s4 * 4 + q_
                            K.op("pe", lambda e: e.transpose(out=pt[:, q_ * 128:(q_ + 1) * 128],
                                                             in_=pgb[:, s, kv * 128:(kv + 1) * 128], identity=ident_f),
                                 reads=[Bpg, B_const], writes=[pb])
                        eng = "act" if (s4 % 2 == 0) else "dve"
                        if eng == "act":
                            K.op("act", lambda e: e.copy(out=kTs[:, kv, s4 * 4:(s4 + 1) * 4, 1:129],
                                                         in_=pt.rearrange("p (a b) -> p a b", a=4)), reads=[pb],
                                 writes=[B_kTs])
                        else:
                            K.op("dve", lambda e: e.tensor_copy(out=kTs[:, kv, s4 * 4:(s4 + 1) * 4, 1:129],
                                                                in_=pt.rearrange("p (a b) -> p a b", a=4)),
                                 reads=[pb], writes=[B_kTs])
                for kv in range(2):
                    for g in range(2):
                        pt, pb = ps_g()
                        i = 0
                        for r in range(2):
                            for s in range(16):
                                K.op("pe", lambda e: e.matmul(pt[:, 0:128],
                                                              lhsT=w1sb[g * 64:(g + 1) * 64, kv, r * 16 + s, :],
                                                              rhs=kTs[g * 64:(g + 1) * 64, kv, s, r:r + 128],
                                                              start=(i == 0), stop=(i == 31)),
                                     reads=[B_win, B_kTs], writes=[pb], sig=(i == 31))
                                i += 1
                        K.op("act", lambda e: e.activation(out=ghs[:, g, :], in_=pt[:, 0:128],
                                                           func=AF.Gelu_apprx_tanh, bias=b1eff[:, kv:kv + 1]),
                             reads=[pb, B_win], writes=[B_ghs])
                    pt, pb = ps_g()
                    for g in range(2):
                        K.op("pe", lambda e: e.matmul(pt[:, 0:128], lhsT=w2pad[:, kv, g, :], rhs=ghs[:, g, :],
                                                      start=(g == 0), stop=(g == 1)), reads=[B_win, B_ghs],
                             writes=[pb], sig=(g == 1))
                    K.op("dve", lambda e: e.tensor_copy(out=kcs[:, kv, grp * 128:(grp + 1) * 128], in_=pt[:, 0:128]),
                         reads=[pb], writes=[B_kcs])
                    K.op("dve", lambda e: e.tensor_copy(out=kTs[:, kv, :, 0:1], in_=kTs[:, kv, :, 128:129]),
                         reads=[B_kTs], writes=[B_kTs])
            for ch in range(8):
                pt, pb = ps_g()
                ptb = pt.bitcast(BF16)
                K.op("pe", lambda e: e.transpose(out=ptb[:, 0:128], in_=kcs[:, 1, ch * 128:(ch + 1) * 128],
                                                 identity=ident_b), reads=[B_kcs, B_const], writes=[pb])
                K.op("dve", lambda e: e.tensor_copy(out=vcas[:, ch, :, 0:64],
                                                    in_=ptb[:, 0:128].rearrange("p (g d) -> p g d", g=2)),
                     reads=[pb], writes=[B_vcas])
            ei = etc[0] % 2
            etc[0] += 1
            pt, pb = ps_g()
            p3 = pt[:, 0:64].rearrange("p (c h) -> p c h", c=8)
            for ch in range(8):
                for g in range(2):
                    K.op("pe", lambda e: e.matmul(p3[:, ch, g * 4:(g + 1) * 4], lhsT=kcs[:, 0, ch * 128:(ch + 1) * 128],
                                                  rhs=qTs[g][:, :, b], start=(ch == 0 and g == 0), stop=False,
                                                  skip_group_check=True), reads=[B_kcs, B_qTs], writes=[pb],
                         sig=(ch == 7 and g == 1))
            K.op("act", lambda e: e.activation(out=ETs[ei][:, 0:1, :, b], in_=p3[:, 0:1, :], func=AF.Exp, scale=0.125,
                                               bias=s0col[:, 0:1]), reads=[pb, B_sel], writes=[B_ETs[ei]])
            K.op("act", lambda e: e.activation(out=ETs[ei][:, 1:8, :, b], in_=p3[:, 1:8, :], func=AF.Exp, scale=0.125),
                 reads=[pb], writes=[B_ETs[ei]])
            for ch in range(8):
                pv_tile(ETs[ei], B_ETs[ei], ch, vcas[:, ch], B_vcas, first=(ch == 0))
            evac_acc(0)
            for g in range(2):
                for hf2 in range(2):
                    bk, bkb = allb[6 + hf2]
                    K.op("pe", lambda e: e.matmul(bk[0:4, :], lhsT=qTs[g][:, :, b],
                                                  rhs=kcs[:, 0, hf2 * 512:(hf2 + 1) * 512], start=True,
                                                  stop=(hf2 == 1)), reads=[B_qTs, B_kcs], writes=[bkb], sig=(hf2 == 1))
                    if hf2 == 0:
                        K.op("pe", lambda e: e.matmul(bk[0:4, :], lhsT=ones_b[0:1, 0:4], rhs=s0row, start=False,
                                                      stop=True), reads=[B_sel, B_const], writes=[bkb])
                K.op("act", lambda e: e.activation(out=Es, in_=psS[0:4, :], func=AF.Exp, scale=0.125,
                                                   accum_out=dens[:, 0:1]), reads=B_S, writes=[B_sel])
                K.op("dve", lambda e: e.tensor_scalar(out=dens[:, 1:2], in0=dens[:, 0:1], scalar1=1e-30, scalar2=None,
                                                      op0=ALU.max), reads=[B_sel], writes=[B_sel])
                K.op("dve", lambda e: e.reciprocal(out=dens[:, 1:2], in_=dens[:, 1:2]), reads=[B_sel], writes=[B_sel])
                K.op("dve", lambda e: e.tensor_scalar(out=Es, in0=Es, scalar1=dens[:, 1:2], scalar2=None,
                                                      op0=ALU.mult), reads=[B_sel], writes=[B_sel])
                for hf2 in range(2):
                    bk, bkb = allb[hf2]
                    K.op("pe", lambda e: e.matmul(bk[0:1, :], lhsT=ones4f, rhs=Es[:, hf2 * 512:(hf2 + 1) * 512],
                                                  start=True, stop=True), reads=[B_sel], writes=[bkb])
                    K.op("dve", lambda e: e.tensor_copy(out=imps[:, hf2 * 512:(hf2 + 1) * 512], in_=bk[0:1, :]),
                         reads=[bkb], writes=[B_sel])
                K.op("dve", lambda e: e.tensor_tensor(out=scs[:, 0:257], in0=imps[:, 0:1025:4], in1=imps[:, 1:1026:4],
                                                      op=ALU.add), reads=[B_sel], writes=[B_sel])
                for m in (2, 3, 4):
                    K.op("dve", lambda e: e.tensor_tensor(out=scs[:, 0:257], in0=scs[:, 0:257],
                                                          in1=imps[:, m:m + 1025:4], op=ALU.add), reads=[B_sel],
                         writes=[B_sel])
                K.op("dve", lambda e: e.tensor_tensor(out=scs[:, 0:257], in0=scs[:, 0:257], in1=bon[:, 0:257],
                                                      op=ALU.add), reads=[B_sel], writes=[B_sel])
                K.op("dve", lambda e: e.tensor_copy(out=scs[:, 257:264], in_=bon[:, 257:264]), reads=[B_sel],
                     writes=[B_sel])
                K.op("dve", lambda e: e.max(out=mxs[:, 0:8], in_=scs), reads=[B_sel], writes=[B_sel])
                K.op("dve", lambda e: e.match_replace(out=scs2, in_to_replace=mxs[:, 0:8], in_values=scs,
                                                      imm_value=-3e38), reads=[B_sel], writes=[B_sel])
                K.op("dve", lambda e: e.max(out=mxs[:, 8:16], in_=scs2), reads=[B_sel], writes=[B_sel])
                K.op("dve", lambda e: e.tensor_scalar(out=scs2[:, 0:256], in0=scs[:, 0:256], scalar1=mxs[:, 15:16],
                                                      scalar2=None, op0=ALU.is_ge), reads=[B_sel], writes=[B_sel])
                K.op("dve", lambda e: e.tensor_scalar(out=scs2[:, 0:256], in0=scs2[:, 0:256], scalar1=-1.0,
                                                      scalar2=-NEGM, op0=ALU.add, op1=ALU.mult), reads=[B_sel],
                     writes=[B_sel])
                K.op("dve", lambda e: e.tensor_copy(out=negx, in_=scs2[:, 0:256].unsqueeze(2).to_broadcast(
                    [1, 256, 4])), reads=[B_sel], writes=[B_sel])
                pt, pb = ps_g()
                nx = negx.rearrange("o (a j) r -> o a (j r)", a=8)
                for grp in range(8):
                    K.op("pe", lambda e: e.matmul(pt[:, grp:grp + 1], lhsT=nx[:, grp, :], rhs=ones_b[0:1, 0:1],
                                                  start=(grp == 0), stop=(grp == 7), skip_group_check=True),
                         reads=[B_sel, B_const], writes=[pb], sig=(grp == 7))
                K.op("dve", lambda e: e.tensor_copy(out=mcol[:, g, :], in_=pt[:, 0:8]), reads=[pb], writes=[B_mcol])
            for grp in range(8):
                pgb, Bpg = take_job("slc", b, grp)
                vi = grp % 2
                K.op("dve", lambda e: e.tensor_copy(out=Vas[vi][:, 0:8, :, 0:64],
                                                    in_=pgb[:, 0:8, 128:256].rearrange("p s (g d) -> p s g d", g=2)),
                     reads=[Bpg], writes=[B_Vas[vi]])
                K.op("act", lambda e: e.copy(out=Vas[vi][:, 8:16, :, 0:64],
                                             in_=pgb[:, 8:16, 128:256].rearrange("p s (g d) -> p s g d", g=2)),
                     reads=[Bpg], writes=[B_Vas[vi]])
                ei = etc[0] % 2
                etc[0] += 1
                ps, psb = ps_g()
                ps4 = ps[:, 0:128].rearrange("p (s h) -> p s h", s=16)
                for s4 in range(4):
                    pt, pb = ps_g()
                    for q_ in range(4):
                        s = s4 * 4 + q_
                        K.op("pe", lambda e: e.transpose(out=pt[:, q_ * 128:(q_ + 1) * 128], in_=pgb[:, s, 0:128],
                                                         identity=ident_f), reads=[Bpg, B_const], writes=[pb])
                    kt_, Bkt_ = KTt[s4 % 2], B_KTt[s4 % 2]
                    if s4 % 2 == 0:
                        K.op("act", lambda e: e.copy(out=kt_, in_=pt.rearrange("p (a b) -> p a b", a=4)), reads=[pb],
                             writes=[Bkt_])
                    else:
                        K.op("dve", lambda e: e.tensor_copy(out=kt_, in_=pt.rearrange("p (a b) -> p a b", a=4)),
                             reads=[pb], writes=[Bkt_])
                    for q_ in range(4):
                        s = s4 * 4 + q_
                        for g in range(2):
                            K.op("pe", lambda e: e.matmul(ps4[:, s, g * 4:(g + 1) * 4], lhsT=kt_[:, q_, :],
                                                          rhs=qTs[g][:, :, b], start=(s == 0 and g == 0), stop=False,
                                                          skip_group_check=True), reads=[Bkt_, B_qTs], writes=[psb],
                                 sig=(q_ == 3 and g == 1))
                for g in range(2):
                    K.op("act", lambda e: e.activation(out=ETs[ei][:, :, g * 4:(g + 1) * 4, b],
                                                       in_=ps4[:, :, g * 4:(g + 1) * 4], func=AF.Exp, scale=0.125,
                                                       bias=mcol[:, g, grp:grp + 1]), reads=[psb, B_mcol],
                         writes=[B_ETs[ei]])
                for s in range(16):
                    pv_tile(ETs[ei], B_ETs[ei], s, Vas[vi][:, s], B_Vas[vi], first=(grp == 0 and s == 0))
            def new_token(bi_, first):
                pt, pb = ps_g()
                for g in range(2):
                    K.op("pe", lambda e: e.matmul(pt[0:4, g * 16:(g + 1) * 16], lhsT=kTn[:, bi_, :],
                                                  rhs=qTs[g].rearrange("p r b -> p (r b)"), start=(g == 0), stop=False,
                                                  skip_group_check=True), reads=[B_kTn, B_qTs], writes=[pb])
                K.op("pe", lambda e: e.matmul(pt[0:4, 0:32], lhsT=ident_b[0:4, 0:4],
                                              rhs=dmask.rearrange("p a b -> p (a b)"), start=False, stop=True,
                                              skip_group_check=True), reads=[B_Vn, B_const], writes=[pb])
                en = sbt(s1, "en_%d_%d" % (b, bi_), [4, 8, 4], BF16)
                Ben = K.buf()
                K.op("act", lambda e: e.activation(out=en, in_=pt[0:4, 0:32].rearrange("p (a b) -> p a b", a=8),
                                                   func=AF.Exp, scale=0.125), reads=[pb], writes=[Ben])
                (pA, pAb), (pB, pBb) = acc_banks()
                for g, (pp, ppb) in enumerate(((pA, pAb), (pB, pBb))):
                    p3_ = pp[0:4, 0:260].rearrange("p (h c) -> p h c", h=4)
                    for r in range(4):
                        K.op("pe", lambda e: e.matmul(p3_[:, r, :], lhsT=en[:, g * 4 + r, :], rhs=Vn[:, bi_, g, :],
                                                      start=(first and r == 0), stop=False, skip_group_check=True),
                             reads=[Ben, B_Vn], writes=[ppb])
            if b == 0:
                new_token(0, False)
            evac_acc(1)
            for wt in range(4):
                pgb, Bpg = take_job("win", b, wt)
                vi = wt % 2
                K.op("dve", lambda e: e.tensor_copy(out=Vas[vi][:, 0, :, 0:64],
                                                     in_=pgb[:, 0, 128:256].rearrange("p (g d) -> p g d", g=2)),
                     reads=[Bpg], writes=[B_Vas[vi]])
                pt, pb = ps_g()
                K.op("pe", lambda e: e.transpose(out=pt[:, 0:128], in_=pgb[:, 0, 0:128], identity=ident_f),
                     reads=[Bpg, B_const], writes=[pb])
                kt_, Bkt_ = KTt[wt % 2], B_KTt[wt % 2]
                K.op("dve", lambda e: e.tensor_copy(out=kt_[:, 0, :], in_=pt[:, 0:128]), reads=[pb], writes=[Bkt_])
                ei = etc[0] % 2
                etc[0] += 1
                ps, psb = ps_g()
                for g in range(2):
                    K.op("pe", lambda e: e.matmul(ps[:, g * 4:(g + 1) * 4], lhsT=kt_[:, 0, :], rhs=qTs[g][:, :, b],
                                                  start=(g == 0), stop=(g == 1), skip_group_check=True),
                         reads=[Bkt_, B_qTs], writes=[psb], sig=(g == 1))
                K.op("act", lambda e: e.activation(out=ETs[ei][:, 0, :, b], in_=ps[:, 0:8], func=AF.Exp, scale=0.125),
                     reads=[psb], writes=[B_ETs[ei]])
                pv_tile(ETs[ei], B_ETs[ei], 0, Vas[vi][:, 0], B_Vas[vi], first=(wt == 0))
            if b == 0:
                new_token(1, False)
            evac_acc(2)
        for b in range(4):
            K.dma("sp", wins_o[b, 0:511, :], cache_win[b, 1:512, :])
        K.dma("sp", wins_o[:, 511, :], zs[:, 512:768], reads=[B_zs])
        coefs = sbt(s1, "coefs", [4, 3, 8], F32)
        otm = sbt(s1, "otm", [4, 3, 8, 64], F32)
        onsas = sbt(s1, "onsas", [4, 512], BF16)
        B_cb = K.buf()
        K.op("dve", lambda e: e.tensor_scalar(out=coefs, in0=Oacc[:, :, :, 64], scalar1=1e-30, scalar2=None,
                                              op0=ALU.max), reads=[B_Oacc], writes=[B_cb])
        K.op("dve", lambda e: e.reciprocal(out=coefs, in_=coefs), reads=[B_cb], writes=[B_cb])
        K.op("dve", lambda e: e.tensor_tensor(out=coefs, in0=coefs, in1=sg_s.rearrange("p (h b) -> p b h", b=3),
                                              op=ALU.mult), reads=[B_cb, B_sm], writes=[B_cb])
        K.op("dve", lambda e: e.tensor_tensor(out=otm, in0=Oacc[:, :, :, 0:64],
                                              in1=coefs.unsqueeze(3).to_broadcast([4, 3, 8, 64]), op=ALU.mult),
             reads=[B_Oacc, B_cb], writes=[B_cb])
        K.op("dve", lambda e: e.tensor_tensor(out=otm[:, 0], in0=otm[:, 0], in1=otm[:, 1], op=ALU.add), reads=[B_cb],
             writes=[B_cb])
        K.op("dve", lambda e: e.tensor_tensor(out=onsas.rearrange("p (h d) -> p h d", h=8), in0=otm[:, 0],
                                              in1=otm[:, 2], op=ALU.add), reads=[B_cb], writes=[B_cb])
        pt, pb = ps_g()
        ptb = pt.bitcast(BF16)[:, 0:16].rearrange("p (k t) -> p k t", k=4)
        for k in range(4):
            K.op("pe", lambda e: e.transpose(out=ptb[:, k, :], in_=onsas[:, k * 128:(k + 1) * 128],
                                             identity=ident_b[0:4, 0:4]), reads=[B_cb, B_const], writes=[pb])
        K.op("dve", lambda e: e.tensor_copy(out=omT_s[:, 0:4, :], in_=ptb), reads=[pb], writes=[B_omTs])
        K.barrier()
        s1.close()
    K.barrier()
    e1.close()
    if stage <= 6:
        K.finish()
        return nc

    e2 = ExitStack()
    wg_sb = sbt(e2, "wg_sb", [128, 8, 2048], BF16)
    B_w2 = K.buf("w_pass1b")
    for k in range(8):
        K.dma("pool", wg_sb[:, k, :], w_in[k * 128:(k + 1) * 128, 2328:4376], writes=[B_w2])
    wpa = sbt(e2, "wpa", [128, 4, D], BF16)
    wpb = sbt(e2, "wpb", [128, 4, D], BF16)
    wout = sbt(e2, "wout", [128, 8, D], BF16)
    for k in range(4):
        K.dma("pool", wpa[:, k, :], w_proj_a[k * 128:(k + 1) * 128, :], writes=[B_w2])
        K.dma("pool", wpb[:, k, :], w_proj_b[k * 128:(k + 1) * 128, :], writes=[B_w2])
    for k in range(8):
        K.dma("pool", wout[:, k, :], w_out[k * 128:(k + 1) * 128, :], writes=[B_w2])
    xst = sbt(e2, "xst", [128, 4, D], F32)
    B_xst = [K.buf() for _ in range(4)]
    xn = [sbt(e2, "xn2_%d" % i, [128, D], BF16) for i in range(2)]
    st4 = sbt(e2, "st4b", [128, 8], F32)
    hT = sbt(e2, "hT2", [128, 8, 512], BF16)
    omT = sbt(e2, "omT", [128, 8, 512], BF16)
    B_omT = K.buf()
    g0s = sbt(e2, "g0s", [128, 512], F32)
    g1s = sbt(e2, "g1s", [128, 512], F32)
    B_gs = [K.buf(), K.buf()]
    mergedT = sbt(e2, "mergedT", [128, 8, 512], BF16)
    B_merged = K.buf()
    x1b = [sbt(e2, "x1b%d" % i, [128, D], F32) for i in range(2)]
    B_x1b = [K.buf(), K.buf()]
    B_st = K.buf()
    B_xn = [K.buf(), K.buf()]
    B_hT = [K.buf() for _ in range(4)]

    def p1b_merge(nt, om_ap, B_om_l, h_ap, B_h_l):
        for cc in range(8):
            pa, pab = ps_a()
            pbb_, pbb = ps_a()
            pg0, pg0b = ps_a()
            pg1, pg1b = ps_a()
            for k in range(4):
                K.op("pe", lambda e: e.matmul(pa[:, 0:nt], lhsT=wpa[:, k, cc * 128:(cc + 1) * 128], rhs=om_ap[:, k, 0:nt],
                                              start=(k == 0), stop=(k == 3)), reads=[B_w2] + B_om_l, writes=[pab], sig=(k == 3))
            for k in range(4):
                K.op("pe", lambda e: e.matmul(pbb_[:, 0:nt], lhsT=wpb[:, k, cc * 128:(cc + 1) * 128],
                                              rhs=om_ap[:, 4 + k, 0:nt], start=(k == 0), stop=(k == 3)),
                     reads=[B_w2] + B_om_l, writes=[pbb], sig=(k == 3))
            for k in range(8):
                K.op("pe", lambda e: e.matmul(pg0[:, 0:nt], lhsT=wg_sb[:, k, cc * 128:(cc + 1) * 128],
                                              rhs=h_ap[:, k, 0:nt], start=(k == 0), stop=(k == 7)),
                     reads=[B_w2] + B_h_l, writes=[pg0b], sig=(k == 7))
            for k in range(8):
                K.op("pe", lambda e: e.matmul(pg1[:, 0:nt], lhsT=wg_sb[:, k, 1024 + cc * 128:1152 + cc * 128],
                                              rhs=h_ap[:, k, 0:nt], start=(k == 0), stop=(k == 7)),
                     reads=[B_w2] + B_h_l, writes=[pg1b], sig=(k == 7))
            K.op("act", lambda e: e.activation(out=g0s[:, 0:nt], in_=pg0[:, 0:nt], func=AF.Sigmoid), reads=[pg0b],
                 writes=[B_gs[0]])
            K.op("act", lambda e: e.activation(out=g1s[:, 0:nt], in_=pg1[:, 0:nt], func=AF.Sigmoid), reads=[pg1b],
                 writes=[B_gs[1]])
            K.op("dve", lambda e: e.tensor_tensor(out=g0s[:, 0:nt], in0=g0s[:, 0:nt], in1=pa[:, 0:nt], op=ALU.mult),
                 reads=[B_gs[0], pab], writes=[B_gs[0]])
            K.op("dve", lambda e: e.tensor_tensor(out=g1s[:, 0:nt], in0=g1s[:, 0:nt], in1=pbb_[:, 0:nt], op=ALU.mult),
                 reads=[B_gs[1], pbb], writes=[B_gs[1]])
            K.op("pool", lambda e: e.tensor_tensor(out=mergedT[:, cc, 0:nt], in0=g0s[:, 0:nt], in1=g1s[:, 0:nt],
                                                   op=ALU.add), reads=B_gs, writes=[B_merged])

    for si, (T0, ntl) in enumerate(own_sts):
        nt = ntl * 128
        K.dma("sp", omT[:, :, 0:nt], scr_om[si, :, :, 0:nt], reads=[B_scr_om[si]], writes=[B_omT])
        for j in range(ntl):
            rmsnorm_T(tile_src(T0 + j), xst[:, j, :], B_xst[j], xn[j % 2], B_xn[j % 2], gcol_mix,
                      hT[:, :, j * 128:(j + 1) * 128], B_hT[j])
        Bh = B_hT[0:ntl]
        p1b_merge(nt, omT, [B_omT], hT, Bh)
        for j in range(ntl):
            T = T0 + j
            for n in range(2):
                px, pxb = ps_a()
                for k in range(8):
                    K.op("pe", lambda e: e.matmul(px, lhsT=mergedT[:, k, j * 128:(j + 1) * 128],
                                                  rhs=wout[:, k, n * 512:(n + 1) * 512], start=(k == 0), stop=(k == 7)),
                         reads=[B_merged, B_w2], writes=[pxb], sig=(k == 7))
                K.op("dve", lambda e: e.tensor_tensor(out=x1b[j % 2][:, n * 512:(n + 1) * 512], in0=px,
                                                      in1=xst[:, j, n * 512:(n + 1) * 512], op=ALU.add),
                     reads=[pxb, B_xst[j]], writes=[B_x1b[j % 2]])
            K.dma("sp", scr_x1[T - 15], x1b[j % 2], reads=[B_x1b[j % 2]], writes=[B_scr_x1[T - 15]])

    if do_sample:
        K.dma("sp", xst[0:4, 0, :], xs, writes=[B_xst[0]])
        p1b_merge(4, omT_s, [B_omTs], hT_s, [B_hTs])
        for n in range(2):
            px, pxb = ps_a()
            for k in range(8):
                K.op("pe", lambda e: e.matmul(px[0:4, :], lhsT=mergedT[:, k, 0:4], rhs=wout[:, k, n * 512:(n + 1) * 512],
                                              start=(k == 0), stop=(k == 7)), reads=[B_merged, B_w2], writes=[pxb], sig=(k == 7))
            K.op("dve", lambda e: e.tensor_tensor(out=x1b[0][0:4, n * 512:(n + 1) * 512], in0=px[0:4, :],
                                                  in1=xst[0:4, 0, n * 512:(n + 1) * 512], op=ALU.add),
                 reads=[pxb, B_xst[0]], writes=[B_x1b[0]])
        K.dma("sp", scr_x1s, x1b[0][0:4, :], reads=[B_x1b[0]], writes=[B_scr_x1s])

    K.barrier()
    e2.close()
    if stage <= 7:
        K.finish()
        return nc

    e3 = ExitStack()
    wup = sbt(e3, "wup", [128, 8, 2 * DFF], BF16)
    wdn = sbt(e3, "wdn", [128, 22, D], BF16)
    B_w3 = K.buf("w_pass2")
    for k in range(8):
        for c0 in range(0, 2 * DFF, 1408):
            K.dma("pool", wup[:, k, c0:c0 + 1408], w_up[k * 128:(k + 1) * 128, c0:c0 + 1408], writes=[B_w3])
    for f in range(22):
        K.dma("pool", wdn[:, f, :], w_down[f * 128:(f + 1) * 128, :], writes=[B_w3])
    cw = sbt(e3, "cw", [128, 66], F32)
    cb = sbt(e3, "cb", [128, 22], F32)
    tmpc = sbt(e3, "tmpc", [66, 128], F32)
    K.dma("sp", tmpc, conv_w, writes=[B_w3])
    pt, pb = ps_a()
    K.op("pe", lambda e: e.transpose(out=pt[:, 0:66], in_=tmpc, identity=ident_f[0:66, 0:66]), reads=[B_w3, B_const],
         writes=[pb])
    K.op("dve", lambda e: e.tensor_copy(out=cw, in_=pt[:, 0:66]), reads=[pb], writes=[B_w3])
    tmpb = sbt(e3, "tmpb", [22, 128], F32)
    K.dma("sp", tmpb, conv_b, writes=[B_w3])
    pt, pb = ps_a()
    K.op("pe", lambda e: e.transpose(out=pt[:, 0:22], in_=tmpb, identity=ident_f[0:22, 0:22]), reads=[B_w3, B_const],
         writes=[pb])
    K.op("dve", lambda e: e.tensor_copy(out=cb, in_=pt[:, 0:22]), reads=[pb], writes=[B_w3])
    gfinB = sbt(e3, "gfinB", [128, D], F32)
    K.dma("sp", gfinB, norm_final.partition_broadcast(128), writes=[B_w3])

    xld = [sbt(e3, "xld%d" % i, [128, D], F32) for i in range(2)]
    B_xld = [K.buf(), K.buf()]
    xn = [sbt(e3, "xn3_%d" % i, [128, D], BF16) for i in range(2)]
    st4 = sbt(e3, "st4c", [128, 8], F32)
    hT = sbt(e3, "hT3", [128, 8, 512], BF16)
    B_st = K.buf()
    B_xn = [K.buf(), K.buf()]
    B_hT = [K.buf() for _ in range(4)]
    abuf = [sbt(e3, "abuf%d" % i, [128, 2 + 512], F32) for i in range(2)]
    B_abuf = [K.buf(), K.buf()]
    acar = sbt(e3, "acar", [128, 22, 2], F32)
    B_acar = K.buf()
    K.op("pool", lambda e: e.memset(acar, 0.0), writes=[B_acar])
    cbuf = [sbt(e3, "cbuf0", [128, 512], F32)] * 2
    B_cbuf = [K.buf()] * 2
    actT = sbt(e3, "actT", [128, 22, 512], BF16)
    B_actT = K.buf()
    ybuf = [sbt(e3, "ybuf0", [128, D], F32)] * 2
    B_ybuf = [K.buf()] * 2
    convsb = sbt(e3, "convsb", [2, 512], F32)
    B_convsb = K.buf()

    for si, (T0, ntl) in enumerate(own_sts):
        nt = ntl * 128
        for j in range(ntl):
            K.dma("sp", xld[j % 2], scr_x1[T0 + j - 15], reads=[B_scr_x1[T0 + j - 15]], writes=[B_xld[j % 2]])
            rmsnorm_T(None, xld[j % 2], B_xld[j % 2], xn[j % 2], B_xn[j % 2], gcol_ffn,
                      hT[:, :, j * 128:(j + 1) * 128], B_hT[j], x_preloaded=True)
        Bh = B_hT[0:ntl]
        for f in range(22):
            pa, pab = ps_a()
            pbb_, pbb = ps_a()
            ab, Bab = abuf[f % 2], B_abuf[f % 2]
            cbf, Bcb = cbuf[f % 2], B_cbuf[f % 2]
            for k in range(8):
                K.op("pe", lambda e: e.matmul(pa[:, 0:nt], lhsT=wup[:, k, f * 128:(f + 1) * 128], rhs=hT[:, k, 0:nt],
                                              start=(k == 0), stop=(k == 7)), reads=[B_w3] + Bh, writes=[pab], sig=(k == 7))
            for k in range(8):
                K.op("pe", lambda e: e.matmul(pbb_[:, 0:nt], lhsT=wup[:, k, DFF + f * 128:DFF + (f + 1) * 128],
                                              rhs=hT[:, k, 0:nt], start=(k == 0), stop=(k == 7)),
                     reads=[B_w3] + Bh, writes=[pbb], sig=(k == 7))
            K.op("pool", lambda e: e.tensor_copy(out=ab[:, 0:2], in_=acar[:, f, :]), reads=[B_acar], writes=[Bab])
            K.op("act", lambda e: e.copy(out=ab[:, 2:2 + nt], in_=pa[:, 0:nt]), reads=[pab], writes=[Bab])
            if si == 0:
                K.op("dve", lambda e: e.tensor_scalar(out=ab[:, 2:130], in0=ab[:, 2:130], scalar1=halfcol[:, 0:1],
                                                      scalar2=None, op0=ALU.mult), reads=[Bab, B_const], writes=[Bab])
            K.op("pool", lambda e: e.tensor_copy(out=acar[:, f, :], in_=ab[:, nt:nt + 2]), reads=[Bab],
                 writes=[B_acar])
            K.op("dve", lambda e: e.tensor_scalar(out=cbf[:, 0:nt], in0=ab[:, 0:nt], scalar1=cw[:, f:f + 1],
                                                  scalar2=cb[:, f:f + 1], op0=ALU.mult, op1=ALU.add),
                 reads=[Bab, B_w3], writes=[Bcb])
            K.op("dve", lambda e: e.scalar_tensor_tensor(out=cbf[:, 0:nt], in0=ab[:, 1:1 + nt],
                                                         scalar=cw[:, 22 + f:23 + f], in1=cbf[:, 0:nt], op0=ALU.mult,
                                                         op1=ALU.add), reads=[Bab, B_w3, Bcb], writes=[Bcb])
            K.op("dve", lambda e: e.scalar_tensor_tensor(out=cbf[:, 0:nt], in0=ab[:, 2:2 + nt],
                                                         scalar=cw[:, 44 + f:45 + f], in1=cbf[:, 0:nt], op0=ALU.mult,
                                                         op1=ALU.add), reads=[Bab, B_w3, Bcb], writes=[Bcb])
            K.op("act", lambda e: e.activation(out=cbf[:, 0:nt], in_=cbf[:, 0:nt], func=AF.Gelu_apprx_tanh),
                 reads=[Bcb], writes=[Bcb])
            K.op("dve", lambda e: e.tensor_tensor(out=actT[:, f, 0:nt], in0=cbf[:, 0:nt], in1=pbb_[:, 0:nt],
                                                  op=ALU.mult), reads=[Bcb, pbb], writes=[B_actT])
        if si == 4:
            for r0 in range(0, 22, 4):
                nn = min(4, 22 - r0)
                pt, pb = ps_a()
                for f in range(r0, r0 + nn):
                    K.op("pe", lambda e: e.transpose(out=pt[0:2, (f - r0) * 128:(f - r0 + 1) * 128], in_=acar[:, f, :],
                                                     identity=ident_f), reads=[B_acar, B_const], writes=[pb])
                K.op("dve", lambda e: e.tensor_copy(out=convsb[:, 0:nn * 128], in_=pt[0:2, 0:nn * 128]),
                     reads=[pb], writes=[B_convsb])
                K.dma("sp", conv_o[:, r0 * 128:(r0 + nn) * 128], convsb[:, 0:nn * 128], reads=[B_convsb])
        for j in range(ntl):
            T = T0 + j
            yb, Byb = ybuf[j % 2], B_ybuf[j % 2]
            K.dma("sp", xld[j % 2], scr_x1[T - 15], reads=[B_scr_x1[T - 15]], writes=[B_xld[j % 2]])
            for n in range(2):
                py, pyb = ps_a()
                for f in range(22):
                    K.op("pe", lambda e: e.matmul(py, lhsT=actT[:, f, j * 128:(j + 1) * 128],
                                                  rhs=wdn[:, f, n * 512:(n + 1) * 512], start=(f == 0), stop=(f == 21)),
                         reads=[B_actT, B_w3], writes=[pyb], sig=(f == 21))
                K.op("dve", lambda e: e.tensor_tensor(out=yb[:, n * 512:(n + 1) * 512], in0=py,
                                                      in1=xld[j % 2][:, n * 512:(n + 1) * 512], op=ALU.add),
                     reads=[pyb, B_xld[j % 2]], writes=[Byb])
            if T < 16:
                continue
            K.op("act", lambda e: e.activation(out=xn[0], in_=yb, func=AF.Square, accum_out=st4[:, 4:5]), reads=[Byb],
                 writes=[B_xn[0], B_st])
            K.op("act", lambda e: e.activation(out=st4[:, 5:6], in_=st4[:, 4:5], func=AF.Ln, scale=1.0 / D, bias=EPS), reads=[B_st], writes=[B_st])
            K.op("act", lambda e: e.activation(out=st4[:, 6:7], in_=st4[:, 5:6], func=AF.Exp, scale=-0.5), reads=[B_st], writes=[B_st])
            K.op("dve", lambda e: e.scalar_tensor_tensor(out=yb, in0=yb, scalar=st4[:, 6:7], in1=gfinB,
                                                          op0=ALU.mult, op1=ALU.mult), reads=[Byb, B_st, B_w3],
                 writes=[Byb])
            K.dma("sp", y_o[(T - 16) * 128:(T - 15) * 128, :], yb, reads=[Byb])

    if do_sample:
        x1s = xld[0][0:4, :]
        K.dma("sp", x1s, scr_x1s, reads=[B_scr_x1s], writes=[B_xld[0]])
        K.op("act", lambda e: e.activation(out=xn[0][0:4, :], in_=x1s, func=AF.Square, accum_out=st4[0:4, 0:1]),
             reads=[B_xld[0]], writes=[B_xn[0], B_st])
        K.op("act", lambda e: e.activation(out=st4[0:4, 1:2], in_=st4[0:4, 0:1], func=AF.Ln, scale=1.0 / D, bias=EPS),
             reads=[B_st], writes=[B_st])
        K.op("act", lambda e: e.activation(out=st4[0:4, 2:3], in_=st4[0:4, 1:2], func=AF.Exp, scale=-0.5),
             reads=[B_st], writes=[B_st])
        K.op("dve", lambda e: e.tensor_scalar(out=xn[0][0:4, :], in0=x1s, scalar1=st4[0:4, 2:3], scalar2=None,
                                              op0=ALU.mult), reads=[B_xld[0], B_st], writes=[B_xn[0]])
        pt, pb = ps_a()
        ptb = pt.bitcast(BF16)[:, 0:32].rearrange("p (k t) -> p k t", k=8)
        for k in range(8):
            K.op("pe", lambda e: e.transpose(out=ptb[:, k, :], in_=xn[0][0:4, k * 128:(k + 1) * 128],
                                             identity=ident_b[0:4, 0:4]), reads=[B_xn[0], B_const], writes=[pb])
        K.op("dve", lambda e: e.tensor_tensor(out=hT[:, :, 0:4], in0=ptb,
                                              in1=gcol_ffn.unsqueeze(2).to_broadcast([128, 8, 4]), op=ALU.mult),
             reads=[pb, B_const], writes=[B_hT[0]])
        pab_, pabb = ps_a()
        p3 = pab_[:, 0:176].rearrange("p (f t) -> p f t", f=44)
        for f in range(44):
            for k in range(8):
                K.op("pe", lambda e: e.matmul(p3[:, f, :], lhsT=wup[:, k, f * 128:(f + 1) * 128], rhs=hT[:, k, 0:4],
                                              start=(f == 0 and k == 0), stop=(f == 43 and k == 7),
                                              skip_group_check=True), reads=[B_w3, B_hT[0]], writes=[pabb], sig=(f == 43 and k == 7))
        aTs = abuf[0][:, 0:88].rearrange("p (f t) -> p f t", f=22)
        K.op("act", lambda e: e.copy(out=abuf[0][:, 0:88], in_=pab_[:, 0:88]), reads=[pabb], writes=[B_abuf[0]])
        stt = abuf[1][0:8, 0:DFF // 8 * 0 + 514]
        stT = cbuf[0][:, 0:176].rearrange("p (f t) -> p f t", f=22)
        pst, pstb = ps_a()
        for f0 in range(0, 22, 4):
            nn = min(4, 22 - f0)
            K.dma("sp", abuf[1][0:8, 0:nn * 128], state_conv[:, f0 * 128:(f0 + nn) * 128], writes=[B_abuf[1]])
            for f in range(f0, f0 + nn):
                K.op("pe", lambda e: e.transpose(out=pst[:, f * 8:(f + 1) * 8],
                                                 in_=abuf[1][0:8, (f - f0) * 128:(f - f0 + 1) * 128],
                                                 identity=ident_f[0:8, 0:8]), reads=[B_abuf[1], B_const],
                     writes=[pstb])
        K.op("dve", lambda e: e.tensor_copy(out=cbuf[0][:, 0:176], in_=pst[:, 0:176]), reads=[pstb],
             writes=[B_cbuf[0]])
        stT4 = cbuf[0][:, 0:176].rearrange("p (f b j) -> p f b j", f=22, b=4)
        cS = cbuf[0][:, 256:344].rearrange("p (f t) -> p f t", f=22)
        tS = cbuf[0][:, 384:472].rearrange("p (f t) -> p f t", f=22)

        def bc(ap):
            return ap.unsqueeze(2).to_broadcast([128, 22, 4])
        K.op("dve", lambda e: e.tensor_tensor(out=cS, in0=stT4[:, :, :, 0], in1=bc(cw[:, 0:22]), op=ALU.mult),
             reads=[B_cbuf[0], B_w3], writes=[B_cbuf[0]])
        K.op("dve", lambda e: e.tensor_tensor(out=cS, in0=cS, in1=bc(cb), op=ALU.add), reads=[B_cbuf[0], B_w3],
             writes=[B_cbuf[0]])
        K.op("dve", lambda e: e.tensor_tensor(out=tS, in0=stT4[:, :, :, 1], in1=bc(cw[:, 22:44]), op=ALU.mult),
             reads=[B_cbuf[0], B_w3], writes=[B_cbuf[0]])
        K.op("dve", lambda e: e.tensor_tensor(out=cS, in0=cS, in1=tS, op=ALU.add), reads=[B_cbuf[0]],
             writes=[B_cbuf[0]])
        K.op("dve", lambda e: e.tensor_tensor(out=tS, in0=aTs, in1=bc(cw[:, 44:66]), op=ALU.mult),
             reads=[B_abuf[0], B_w3], writes=[B_cbuf[0]])
        K.op("dve", lambda e: e.tensor_tensor(out=cS, in0=cS, in1=tS, op=ALU.add), reads=[B_cbuf[0]],
             writes=[B_cbuf[0]])
        K.op("act", lambda e: e.activation(out=cS, in_=cS, func=AF.Gelu_apprx_tanh), reads=[B_cbuf[0]],
             writes=[B_cbuf[0]])
        K.op("dve", lambda e: e.tensor_tensor(out=actT[:, :, 0:4], in0=cS, in1=p3[:, 22:44, :], op=ALU.mult),
             reads=[B_cbuf[0], pabb], writes=[B_actT])
        K.dma("sp", convs_o[:, 0, :], state_conv.rearrange("(b j) f -> b j f", j=2)[:, 1, :])
        for r0 in range(0, 22, 4):
            nn = min(4, 22 - r0)
            pt, pb = ps_a()
            for f in range(r0, r0 + nn):
                K.op("pe", lambda e: e.transpose(out=pt[0:4, (f - r0) * 128:(f - r0 + 1) * 128], in_=aTs[:, f, :],
                                                 identity=ident_f), reads=[B_abuf[0], B_const], writes=[pb])
            K.op("dve", lambda e: e.tensor_copy(out=ybuf[0][0:4, 0:nn * 128], in_=pt[0:4, 0:nn * 128]), reads=[pb],
                 writes=[B_ybuf[0]])
            K.dma("sp", convs_o[:, 1, r0 * 128:(r0 + nn) * 128], ybuf[0][0:4, 0:nn * 128], reads=[B_ybuf[0]])
        yb = ybuf[0]
        for n in range(2):
            py, pyb = ps_a()
            for f in range(22):
                K.op("pe", lambda e: e.matmul(py[0:4, :], lhsT=actT[:, f, 0:4], rhs=wdn[:, f, n * 512:(n + 1) * 512],
                                              start=(f == 0), stop=(f == 21)), reads=[B_actT, B_w3], writes=[pyb], sig=(f == 21))
            K.op("dve", lambda e: e.tensor_tensor(out=yb[0:4, n * 512:(n + 1) * 512], in0=py[0:4, :],
                                                  in1=x1s[:, n * 512:(n + 1) * 512], op=ALU.add),
                 reads=[pyb, B_xld[0]], writes=[B_ybuf[0]])
        K.op("act", lambda e: e.activation(out=xn[0][0:4, :], in_=yb[0:4, :], func=AF.Square,
                                           accum_out=st4[0:4, 4:5]), reads=[B_ybuf[0]], writes=[B_xn[0], B_st])
        K.op("act", lambda e: e.activation(out=st4[0:4, 5:6], in_=st4[0:4, 4:5], func=AF.Ln, scale=1.0 / D, bias=EPS),
             reads=[B_st], writes=[B_st])
        K.op("act", lambda e: e.activation(out=st4[0:4, 6:7], in_=st4[0:4, 5:6], func=AF.Exp, scale=-0.5),
             reads=[B_st], writes=[B_st])
        K.op("dve", lambda e: e.scalar_tensor_tensor(out=yb[0:4, :], in0=yb[0:4, :], scalar=st4[0:4, 6:7],
                                                     in1=gfinB[0:4, :], op0=ALU.mult, op1=ALU.mult),
             reads=[B_ybuf[0], B_st, B_w3], writes=[B_ybuf[0]])
        K.dma("sp", ys_o, yb[0:4, :], reads=[B_ybuf[0]])

    K.finish()
    e3.close()
    es0.close()
    return nc


_NC_CACHE = {}


def _get_nc():
    if "nc" not in _NC_CACHE:
        _NC_CACHE["nc"] = build_program()
    return _NC_CACHE["nc"]


def kernel(x_prompt, x_sample, cache_cmp, cache_slc, cache_win, state_conv, page_table, norm_mix, w_in, cmp_pe,
           cmp_w1, cmp_b1, cmp_w2, gmlp_norm, gmlp_ws, gmlp_bs, w_proj_a, w_proj_b, w_out, norm_ffn, w_up, conv_w,
           conv_b, w_down, norm_final):
    f = np.float32
    x_prompt = np.asarray(x_prompt, f)
    nc = _get_nc()
    shared = {
        "w_in": np.ascontiguousarray(np.asarray(w_in, f)[0]),
        "cmp_pe": np.ascontiguousarray(np.asarray(cmp_pe, f)[0]),
        "cmp_w1": np.ascontiguousarray(np.asarray(cmp_w1, f)[0]),
        "cmp_b1": np.ascontiguousarray(np.asarray(cmp_b1, f)[0]),
        "cmp_w2": np.ascontiguousarray(np.asarray(cmp_w2, f)[0]),
        "gmlp_norm": np.ascontiguousarray(np.asarray(gmlp_norm, f).reshape(1, 512)),
        "gmlp_ws": np.ascontiguousarray(np.asarray(gmlp_ws, f)[0]),
        "gmlp_bs": np.ascontiguousarray(np.asarray(gmlp_bs, f).reshape(1, 512)),
        "w_proj_a": np.ascontiguousarray(np.asarray(w_proj_a, f)[0]),
        "w_proj_b": np.ascontiguousarray(np.asarray(w_proj_b, f)[0]),
        "w_out": np.ascontiguousarray(np.asarray(w_out, f)[0]),
        "norm_mix": np.ascontiguousarray(np.asarray(norm_mix, f).reshape(8, 128)),
        "norm_ffn": np.ascontiguousarray(np.asarray(norm_ffn, f).reshape(8, 128)),
        "w_up": np.ascontiguousarray(np.asarray(w_up, f)[0]),
        "conv_w": np.ascontiguousarray(np.asarray(conv_w, f).reshape(66, 128)),
        "conv_b": np.ascontiguousarray(np.asarray(conv_b, f).reshape(22, 128)),
        "w_down": np.ascontiguousarray(np.asarray(w_down, f)[0]),
        "norm_final": np.ascontiguousarray(np.asarray(norm_final, f).reshape(1, D)),
    }
    x_sample = np.asarray(x_sample, f)
    cc_flat = np.ascontiguousarray(np.asarray(cache_cmp, f)).reshape(40960, 4096)
    cs_flat = np.ascontiguousarray(np.asarray(cache_slc, f)).reshape(40960, 4096)
    cache_win = np.asarray(cache_win, f)
    state_conv = np.asarray(state_conv, f)
    page_table = np.asarray(page_table).astype(np.int32)
    pmod = (np.arange(128) % 8).astype(f).reshape(128, 1)
    in_maps = []
    for c in range(8):
        b, hf = c // 2, c % 2
        m = dict(shared)
        m["xo"] = np.ascontiguousarray(x_prompt[b, hf * 2048:(hf + 1) * 2048])
        m["xh"] = np.ascontiguousarray(x_prompt[b, (1 - hf) * 2048:(2 - hf) * 2048])
        selb = np.zeros((1, 64), f)
        cmpb = np.zeros((1, NSLOT), f)
        if hf == 0:
            selb[0, :32] = -1e30
            selb[0, 32] = 1e6
            cmpb[0, :129] = NEGM
            hsc = np.array([[NEGM, 0.0]], f)
        else:
            selb[0, 0] = 1e6
            cmpb[0, 0] = NEGM
            hsc = np.array([[0.0, 1.0]], f)
        m["selb"], m["cmpb"], m["hsc"] = selb, cmpb, hsc
        sb = slice(4 * c, 4 * c + 4)
        m["xs"] = np.ascontiguousarray(x_sample[sb, 0, :])
        m["cache_cmp"] = cc_flat
        m["cache_slc"] = cs_flat
        m["cache_win"] = np.ascontiguousarray(cache_win[0, sb].reshape(4, 512, 256))
        m["state_conv"] = np.ascontiguousarray(state_conv[0, sb].reshape(8, DFF))
        ptb_ = page_table[sb].reshape(4, 8, 16)
        ptx = np.repeat(ptb_, 8, axis=2)
        m["ptx"] = np.ascontiguousarray(ptx.transpose(2, 0, 1).reshape(128, 32)).astype(np.int32)
        m["pmod"] = pmod
        in_maps.append(m)
    res = run_bass_kernel_spmd(nc, in_maps, core_ids=list(range(8)))
    R = res.results
    y_prompt = np.stack([np.concatenate([R[2 * b]["y"], R[2 * b + 1]["y"]], 0) for b in range(4)]).astype(f)
    kvs = []
    for br in range(3):
        kvs.append(np.stack([np.concatenate([R[2 * b]["kvo"][br], R[2 * b + 1]["kvo"][br]], 0)
                             for b in range(4)]).reshape(1, 4, 4096, 2, 2, 64).astype(f))
    new_win_p = np.ascontiguousarray(kvs[2][:, :, -512:])
    new_v_p = np.stack([R[2 * b + 1]["vno"] for b in range(4)]).reshape(1, 4, 128, 512).astype(f)
    new_conv_p = np.stack([R[2 * b + 1]["convo"] for b in range(4)]).reshape(1, 4, 2, DFF).astype(f)
    y_s = np.concatenate([R[c]["ys"] for c in range(8)], 0).reshape(32, 1, D).astype(f)
    kvs_s = np.concatenate([R[c]["kvs"] for c in range(8)], 0).astype(f)
    cmp_s = np.ascontiguousarray(kvs_s[:, 0:256]).reshape(1, 32, 1, 2, 2, 64)
    slc_s = np.ascontiguousarray(kvs_s[:, 256:512]).reshape(1, 32, 1, 2, 2, 64)
    win_s = np.concatenate([R[c]["wins"] for c in range(8)], 0).reshape(1, 32, 512, 2, 2, 64).astype(f)
    v_s = np.concatenate([R[c]["vns"] for c in range(8)], 0).reshape(1, 32, 1, 512).astype(f)
    conv_s = np.concatenate([R[c]["convs"] for c in range(8)], 0).reshape(1, 32, 2, DFF).astype(f)
    return (y_prompt, y_s, kvs[0], kvs[1], new_win_p, new_v_p, new_conv_p, cmp_s, slc_s, win_s, v_s, conv_s)
```

```python
import numpy as np
from contextlib import ExitStack
import concourse.bass as bass
import concourse.mybir as mybir
from concourse.bass_utils import run_bass_kernel_spmd

F32 = mybir.dt.float32
BF16 = mybir.dt.bfloat16
I32 = mybir.dt.int32
AF = mybir.ActivationFunctionType
ALU = mybir.AluOpType
AX = mybir.AxisListType

NEGM = -30000.0
EPS = 1e-6
D = 1024
INC = 4376
DFF = 2816
NSLOT = 288


class Buf:
    __slots__ = ("name", "w", "r")

    def __init__(self, name):
        self.name = name
        self.w = {}
        self.r = {}


def _merge(d, k, v):
    if d.get(k, 0) < v:
        d[k] = v


class KB:
    NDMA = 32

    def __init__(self, nc):
        self.nc = nc
        self.eng = {"pe": nc.tensor, "act": nc.scalar, "dve": nc.vector, "pool": nc.gpsimd, "sp": nc.sync}
        self.sem = {e: nc.alloc_semaphore("cs_" + e) for e in ("pe", "act", "dve", "pool")}
        self.cnt = {e: 0 for e in self.sem}
        self.waited = {e: {} for e in self.eng}
        self.dsem = [nc.alloc_semaphore("ds%d" % i) for i in range(self.NDMA)]
        self.dtgt = [0] * self.NDMA
        self.dring = {"sp": list(range(0, 24)), "pool": list(range(24, 28)), "act": list(range(28, 32))}
        self.dpos = {"sp": 0, "pool": 0, "act": 0}
        self.nb = 0

    def buf(self, name=None):
        self.nb += 1
        return Buf(name or ("b%d" % self.nb))

    def _deps(self, reads, writes):
        deps = {}
        for b in reads:
            for k, v in b.w.items():
                _merge(deps, k, v)
        for b in writes:
            for k, v in b.w.items():
                _merge(deps, k, v)
            for k, v in b.r.items():
                _merge(deps, k, v)
        return deps

    def _wait(self, e, deps):
        eng = self.eng[e]
        wd = self.waited[e]
        for k, v in deps.items():
            if k == e and e == "pe":
                continue
            if wd.get(k, 0) >= v:
                continue
            sem = self.dsem[k[1]] if isinstance(k, tuple) else self.sem[k]
            eng.wait_ge(sem, v)
            wd[k] = v

    def _mark(self, key, val, reads, writes):
        for b in reads:
            _merge(b.r, key, val)
        for b in writes:
            b.w = {key: val}
            b.r = {}

    def op(self, e, fn, reads=(), writes=(), sig=True):
        self._wait(e, self._deps(reads, writes))
        ins = fn(self.eng[e])
        if sig:
            self.cnt[e] += 1
            ins.then_inc(self.sem[e], 1)
            val = self.cnt[e]
        else:
            val = self.cnt[e] + 1
        self._mark(e, val, reads, writes)

    def dma(self, q, out, in_, reads=(), writes=(), **kw):
        ring = self.dring[q]
        si = ring[self.dpos[q] % len(ring)]
        self.dpos[q] += 1
        deps = self._deps(reads, writes)
        if self.dtgt[si] > 0:
            _merge(deps, ("d", si), self.dtgt[si])
        self._wait(q, deps)
        ins = self.eng[q].dma_start(out=out, in_=in_, **kw)
        self.dtgt[si] += 16
        ins.then_inc(self.dsem[si], 16)
        self._mark(("d", si), self.dtgt[si], reads, writes)

    def barrier(self):
        deps = {}
        for e, c in self.cnt.items():
            if c > 0:
                deps[e] = c
        for si, t in enumerate(self.dtgt):
            if t > 0:
                deps[("d", si)] = t
        for e in self.eng:
            self._wait(e, deps)

    def finish(self):
        self.barrier()


class _Stop(Exception):
    pass


def build_program(do_sample=True, stage=99, sub=99):
    st = {}
    try:
        return _build_program(do_sample, stage, sub, st)
    except _Stop:
        st["K"].finish()
        return st["nc"]


def _build_program(do_sample, stage, sub, _st):
    nc = bass.Bass("TRN2", target_bir_lowering=False)
    K = KB(nc)
    _st["K"], _st["nc"] = K, nc
    es0 = ExitStack()

    def ck(n):
        if stage == 4 and sub == n:
            raise _Stop()

    def din(name, shape, dt=F32):
        return nc.dram_tensor(name, list(shape), dt, kind="ExternalInput").ap()

    def dout(name, shape, dt=F32):
        return nc.dram_tensor(name, list(shape), dt, kind="ExternalOutput").ap()

    def sbt(es, name, shape, dt):
        return es.enter_context(nc.sbuf_tensor(name, list(shape), dt)).ap()

    xh = din("xh", [2048, D])
    xo = din("xo", [2048, D])
    selb = din("selb", [1, 64])
    cmpb = din("cmpb", [1, NSLOT])
    hsc = din("hsc", [1, 2])
    w_in = din("w_in", [D, INC])
    cmp_pe = din("cmp_pe", [2, 32, 64])
    cmp_w1 = din("cmp_w1", [2, 2048, 128])
    cmp_b1 = din("cmp_b1", [2, 128])
    cmp_w2 = din("cmp_w2", [2, 128, 64])
    gmlp_norm = din("gmlp_norm", [1, 512])
    gmlp_ws = din("gmlp_ws", [4, 128, 128])
    gmlp_bs = din("gmlp_bs", [1, 512])
    w_proj_a = din("w_proj_a", [512, D])
    w_proj_b = din("w_proj_b", [512, D])
    w_out = din("w_out", [D, D])
    norm_mix = din("norm_mix", [8, 128])
    norm_ffn = din("norm_ffn", [8, 128])
    w_up = din("w_up", [D, 2 * DFF])
    conv_w = din("conv_w", [3 * 22, 128])
    conv_b = din("conv_b", [22, 128])
    w_down = din("w_down", [DFF, D])
    norm_final = din("norm_final", [1, D])

    xs = din("xs", [4, D])
    cache_cmp = din("cache_cmp", [40960, 4096])
    cache_slc = din("cache_slc", [40960, 4096])
    cache_win = din("cache_win", [4, 512, 256])
    state_conv = din("state_conv", [8, DFF])
    ptx = din("ptx", [128, 32], I32)
    pmod = din("pmod", [128, 1])
    ys_o = dout("ys", [4, D])
    kvs_o = dout("kvs", [4, 768])
    wins_o = dout("wins", [4, 512, 256])
    vns_o = dout("vns", [4, 512])
    convs_o = dout("convs", [4, 2, DFF])

    y_o = dout("y", [2048, D])
    kv_o = dout("kvo", [3, 2048, 256])
    vn_o = dout("vno", [128, 512])
    conv_o = dout("convo", [2, DFF])

    scr_om = nc.dram_tensor("scr_om", [5, 128, 8, 512], BF16, kind="Internal").ap()
    scr_x1 = nc.dram_tensor("scr_x1", [17, 128, D], F32, kind="Internal").ap()
    B_scr_om = [K.buf() for _ in range(5)]
    B_scr_x1 = [K.buf() for _ in range(17)]

    psS = nc.alloc_psum_tensor("psS", [128, 1024], F32).ap()
    banks = [nc.alloc_psum_tensor("pb%d" % i, [128, 512], F32).ap() for i in range(6)]
    B_bank = [K.buf("bank%d" % i) for i in range(6)]
    B_S = [K.buf("bankS0"), K.buf("bankS1")]
    allb = [(banks[i], B_bank[i]) for i in range(6)] + [(psS[:, 0:512], B_S[0]), (psS[:, 512:1024], B_S[1])]
    rot = {"g": 0, "o": 0, "a": 0}

    def ps_g():
        i = rot["g"]
        rot["g"] = (i + 1) % 4
        return allb[i]

    def ps_o():
        i = rot["o"]
        rot["o"] = (i + 1) % 2
        return allb[4 + i]

    def ps_a():
        i = rot["a"]
        rot["a"] = (i + 1) % 8
        return allb[i]

    c = es0
    ident_f = sbt(c, "ident_f", [128, 128], F32)
    ident_b = sbt(c, "ident_b", [128, 128], BF16)
    ones_f = sbt(c, "ones_f", [128, 128], F32)
    ones_b = sbt(c, "ones_b", [128, 128], BF16)
    zeros_b = sbt(c, "zeros_b", [128, 512], BF16)
    negs_b = sbt(c, "negs_b", [128, 512], BF16)
    B_const = K.buf("const")

    K.op("pool", lambda e: e.memset(ones_f, 1.0), writes=[B_const])
    K.op("pool", lambda e: e.memset(ones_b, 1.0), writes=[B_const])
    K.op("pool", lambda e: e.memset(zeros_b, 0.0), writes=[B_const])
    K.op("pool", lambda e: e.memset(negs_b, NEGM), writes=[B_const])
    K.op("pool", lambda e: e.affine_select(out=ident_f, in_=ones_f, pattern=[[-1, 128]], compare_op=ALU.is_equal,
                                           fill=0.0, base=0, channel_multiplier=1), writes=[B_const])
    K.op("pool", lambda e: e.affine_select(out=ident_b, in_=ones_b, pattern=[[-1, 128]], compare_op=ALU.is_equal,
                                           fill=0.0, base=0, channel_multiplier=1), writes=[B_const])

    def r4(ap):
        return ap.rearrange("p (a b) -> p a b", a=4)

    def load_cols(dst, src_rows, nrows):
        tmp = sbt(c, "lc_%d" % K.nb, [nrows, 128], F32)
        tb = K.buf()
        K.dma("sp", tmp, src_rows, writes=[tb])
        pt, pb = ps_g()
        K.op("pe", lambda e: e.transpose(out=pt[:, 0:nrows], in_=tmp, identity=ident_f[0:nrows, 0:nrows]),
             reads=[tb, B_const], writes=[pb])
        K.op("dve", lambda e: e.tensor_copy(out=dst, in_=pt[:, 0:nrows]), reads=[pb], writes=[B_const])

    gcol_mix = sbt(c, "gcol_mix", [128, 8], F32)
    gcol_ffn = sbt(c, "gcol_ffn", [128, 8], F32)
    load_cols(gcol_mix, norm_mix, 8)
    load_cols(gcol_ffn, norm_ffn, 8)
    halfcol = sbt(c, "halfcol", [128, 1], F32)
    K.dma("sp", halfcol, hsc[:, 1:2].partition_broadcast(128), writes=[B_const])
    hsc_sb = sbt(c, "hsc_sb", [1, 2], F32)
    K.dma("sp", hsc_sb, hsc, writes=[B_const])

    hT_s = sbt(c, "hT_s", [128, 8, 4], BF16)
    B_hTs = K.buf()
    omT_s = sbt(c, "omT_s", [128, 8, 4], BF16)
    B_omTs = K.buf()
    scr_x1s = nc.dram_tensor("scr_x1s", [4, D], F32, kind="Internal").ap()
    B_scr_x1s = K.buf()

    e1 = ExitStack()
    w_in_sb = sbt(e1, "w_in_sb", [128, 8, 2328], BF16)
    B_win = K.buf("w_in")
    for k in range(8):
        for g in range(2):
            K.dma("pool", w_in_sb[:, k, 0:512].rearrange("p (r g d) -> p g r d", r=4, g=2, d=64)[:, g],
                  w_in[k * 128:(k + 1) * 128, g * 256:(g + 1) * 256].rearrange("p (r d) -> p r d", d=64),
                  writes=[B_win])
        K.dma("pool", w_in_sb[:, k, 512:2328], w_in[k * 128:(k + 1) * 128, 512:2328], writes=[B_win])
    w1sb = sbt(e1, "w1sb", [128, 2, 32, 128], BF16)
    for kv in range(2):
        for hf in range(2):
            K.dma("pool", w1sb[hf * 64:(hf + 1) * 64, kv], cmp_w1[kv].rearrange("(p d) h -> d p h", d=64),
                  writes=[B_win])
    w2pad = sbt(e1, "w2pad", [128, 2, 2, 128], BF16)
    K.op("pool", lambda e: e.memset(w2pad, 0.0), writes=[B_win])
    for kv in range(2):
        for g in range(2):
            K.dma("pool", w2pad[:, kv, g, g * 64:(g + 1) * 64], cmp_w2[kv], writes=[B_win])
    b1col = sbt(e1, "b1col", [128, 2], F32)
    for kv in range(2):
        K.dma("sp", b1col[:, kv:kv + 1], cmp_b1[kv].rearrange("(p o) -> p o", o=1), writes=[B_win])
    peTok = sbt(e1, "peTok", [32, 2, 64], F32)
    K.dma("sp", peTok, cmp_pe.rearrange("k p d -> p k d"), writes=[B_win])
    peT = sbt(e1, "peT", [64, 2, 32], BF16)
    b1eff = sbt(e1, "b1eff", [128, 2], F32)
    for kv in range(2):
        pt, pb = ps_g()
        K.op("pe", lambda e: e.transpose(out=pt[0:64, 0:32], in_=peTok[:, kv, :], identity=ident_f[0:32, 0:32]),
             reads=[B_win, B_const], writes=[pb])
        K.op("dve", lambda e: e.tensor_copy(out=peT[:, kv, :], in_=pt[0:64, 0:32]), reads=[pb], writes=[B_win])
    for kv in range(2):
        pt, pb = ps_g()
        for pos in range(32):
            K.op("pe", lambda e: e.matmul(pt[:, 0:1], lhsT=w1sb[0:64, kv, pos, :], rhs=peT[:, kv, pos:pos + 1],
                                          start=(pos == 0), stop=(pos == 31)), reads=[B_win], writes=[pb], sig=(pos == 31))
        K.op("dve", lambda e: e.tensor_tensor(out=b1eff[:, kv:kv + 1], in0=pt[:, 0:1], in1=b1col[:, kv:kv + 1],
                                              op=ALU.add), reads=[pb, B_win], writes=[B_win])
    ws_f = sbt(e1, "ws_f", [128, 4, 128], F32)
    K.dma("sp", ws_f, gmlp_ws.rearrange("g i j -> i g j"), writes=[B_win])
    K.op("pool", lambda e: e.affine_select(out=ws_f, in_=ws_f, pattern=[[0, 4], [-1, 128]], compare_op=ALU.is_ge,
                                           fill=0.0, base=0, channel_multiplier=1), reads=[B_win], writes=[B_win])
    WsT = sbt(e1, "WsT", [128, 4, 128], BF16)
    for g in range(4):
        pt, pb = ps_g()
        K.op("pe", lambda e: e.transpose(out=pt[:, 0:128], in_=ws_f[:, g, :], identity=ident_f),
             reads=[B_win, B_const], writes=[pb])
        K.op("dve", lambda e: e.tensor_copy(out=WsT[:, g, :], in_=pt[:, 0:128]), reads=[pb], writes=[B_win])
    bsB = sbt(e1, "bsB", [128, 512], F32)
    K.dma("sp", bsB, gmlp_bs.partition_broadcast(128), writes=[B_win])
    gnB = sbt(e1, "gnB", [128, 512], F32)
    K.dma("sp", gnB, gmlp_norm.partition_broadcast(128), writes=[B_win])

    mask_diag4 = sbt(e1, "mask_diag4", [128, 4, 128], BF16)
    mask_band4 = sbt(e1, "mask_band4", [128, 4, 128], BF16)
    K.op("pool", lambda e: e.affine_select(out=mask_diag4, in_=r4(zeros_b), pattern=[[0, 4], [1, 128]],
                                           compare_op=ALU.is_ge, fill=NEGM, base=0, channel_multiplier=-1),
         reads=[B_const], writes=[B_win])
    K.op("pool", lambda e: e.affine_select(out=mask_band4, in_=r4(zeros_b), pattern=[[0, 4], [-1, 128]],
                                           compare_op=ALU.is_ge, fill=NEGM, base=0, channel_multiplier=1),
         reads=[B_const], writes=[B_win])
    Mst = sbt(e1, "Mst", [128, 128], F32)
    K.op("pool", lambda e: e.memset(Mst, 0.0), writes=[B_win])
    K.op("pool", lambda e: e.memset(Mst[:, 66:128], -1e30), writes=[B_win])
    K.op("pool", lambda e: e.memset(Mst[0:64, 65:66], -1e30), writes=[B_win])
    K.op("pool", lambda e: e.memset(Mst[64:128, 65:66], 1e6), writes=[B_win])
    K.op("pool", lambda e: e.memset(Mst[:, 64:65], 1e6), writes=[B_win])
    K.op("pool", lambda e: e.memset(Mst[0:64, 63:64], 1e6), writes=[B_win])
    vis8x2 = sbt(e1, "vis8x2", [128, 2, 8], BF16)
    K.op("pool", lambda e: e.affine_select(out=vis8x2, in_=zeros_b[:, 0:16].rearrange("p (a b) -> p a b", a=2),
                                           pattern=[[0, 2], [-16, 8]], compare_op=ALU.is_ge, fill=NEGM, base=-15,
                                           channel_multiplier=1), reads=[B_const], writes=[B_win])
    selbB = sbt(e1, "selbB", [128, 64], F32)
    K.dma("sp", selbB, selb.partition_broadcast(128), writes=[B_win])
    cmpb_row2 = sbt(e1, "cmpb_row2", [1, 2, 256], BF16)
    for a in range(2):
        K.dma("pool", cmpb_row2[:, a, :], cmpb[:, 0:256], writes=[B_win])
    cmpbT = sbt(e1, "cmpbT", [128, 2], F32)
    for ch in range(2):
        K.dma("sp", cmpbT[:, ch:ch + 1], cmpb[0, ch * 128:(ch + 1) * 128].rearrange("(p o) -> p o", o=1),
              writes=[B_win])
    histneg4 = sbt(e1, "histneg4", [1, 512], BF16)
    K.op("dve", lambda e: e.tensor_scalar(out=histneg4, in0=zeros_b[0:1, :], scalar1=hsc_sb[0:1, 0:1], scalar2=None,
                                          op0=ALU.add), reads=[B_const], writes=[B_win])

    if stage <= 1:
        K.finish()
        return nc
    e1p = ExitStack()
    kslcE = [sbt(e1p, "kslcE%d" % i, [128, 4096], BF16) for i in range(2)]
    qS = [sbt(e1p, "qS%d" % i, [128, 4, 512], BF16) for i in range(2)]
    selt2 = [sbt(e1p, "selt2_%d" % i, [128, 128], BF16) for i in range(2)]
    Vslc = sbt(e1p, "Vslc", [128, 32, 2, 65], BF16)
    kwinT = sbt(e1p, "kwinT", [128, 8, 128], BF16)
    Vwin = sbt(e1p, "Vwin", [128, 8, 2, 65], BF16)
    kvcT = sbt(e1p, "kvcT", [128, 2, 16 + 512], BF16)
    kcT = sbt(e1p, "kcT", [128, NSLOT], BF16)
    vcT = sbt(e1p, "vcT", [128, NSLOT], BF16)
    vcaug = sbt(e1p, "vcaug", [128, 2, 2, 65], BF16)
    B_kslc = [K.buf() for _ in range(32)]
    B_vslc = [K.buf() for _ in range(32)]
    B_kwin = [K.buf() for _ in range(8)]
    B_vwin = [K.buf() for _ in range(8)]
    B_kvc = [K.buf(), K.buf()]
    B_kc = K.buf()
    B_vc = K.buf()
    B_vcaug = [K.buf(), K.buf()]
    for g_ in range(2):
        Ebh = kslcE[g_][(1 - g_) * 64:(2 - g_) * 64, :]
        K.op("pool", lambda e: e.memset(Ebh, 1.0), writes=[B_win])
        K.op("pool", lambda e: e.affine_select(out=Ebh, in_=Ebh, pattern=[[1, 4096]], compare_op=ALU.is_ge, fill=0.0,
                                               base=0, channel_multiplier=-64), reads=[B_win], writes=[B_win])
        K.op("pool", lambda e: e.affine_select(out=Ebh, in_=Ebh, pattern=[[-1, 4096]], compare_op=ALU.is_ge, fill=0.0,
                                               base=63, channel_multiplier=64), reads=[B_win], writes=[B_win])
        K.op("pool", lambda e: e.memset(selt2[g_], 0.0), writes=[B_win])
        K.op("pool", lambda e: e.memset(qS[g_], 0.0), writes=[B_win])
    K.op("pool", lambda e: e.memset(Vslc, 1.0), writes=B_vslc)
    K.op("pool", lambda e: e.memset(Vwin, 1.0), writes=B_vwin)
    K.op("pool", lambda e: e.memset(vcaug, 1.0), writes=B_vcaug)
    K.op("pool", lambda e: e.memset(kvcT, 0.0), writes=B_kvc)
    K.op("pool", lambda e: e.memset(kcT, 0.0), writes=[B_kc])
    K.op("pool", lambda e: e.memset(vcT, 0.0), writes=[B_vc])

    xbuf = [sbt(e1p, "xbuf%d" % i, [128, D], F32) for i in range(2)]
    B_x = [K.buf(), K.buf()]
    xn = [sbt(e1p, "xn%d" % i, [128, D], BF16) for i in range(2)]
    B_xn = [K.buf(), K.buf()]
    st4 = sbt(e1p, "st4", [128, 8], F32)
    B_st = K.buf()
    hT = sbt(e1p, "hT", [128, 8, 512], BF16)
    B_hT = [K.buf() for _ in range(4)]
    qTz = [sbt(e1p, "qTz%d" % i, [128, 4, 512], BF16) for i in range(2)]
    B_qT = K.buf()
    for i in range(2):
        K.op("pool", lambda e: e.memset(qTz[i], 0.0), writes=[B_qT])
    uT = sbt(e1p, "uT", [128, 4, 512], BF16)
    B_uT = K.buf()
    kvtok = [sbt(e1p, "kvtok%d" % i, [128, 768], F32) for i in range(2)]
    B_kvtok = [K.buf(), K.buf()]
    sg = sbt(e1p, "sg", [128, 4, 24], F32)
    B_sg = [K.buf() for _ in range(4)]
    gv = sbt(e1p, "gv", [128, 512], F32)
    B_gv = K.buf()
    vn_f = sbt(e1p, "vn_f", [128, 512], F32)
    B_vnf = K.buf()
    vn_b = sbt(e1p, "vn_b", [128, 512], BF16)
    B_vnb = K.buf()
    tmpm = sbt(e1p, "tmpm", [128, 512], F32)
    B_tmpm = K.buf()
    mixT = sbt(e1p, "mixT", [128, 4, 512], BF16)
    B_mixT = K.buf()
    gh = sbt(e1p, "gh", [128, 2, 2, 32], BF16)
    B_gh = K.buf()
    ET = [sbt(e1p, "ET%d" % i, [128, 512], BF16) for i in range(4)]
    B_ET = [K.buf() for _ in range(4)]
    etr = [0]
    Eh = [sbt(e1p, "Eh%d" % i, [128, 256], F32) for i in range(2)]
    B_Eh = [K.buf(), K.buf()]
    impbuf = sbt(e1p, "impbuf", [128, 272], F32)
    B_imp = K.buf()
    K.op("pool", lambda e: e.memset(impbuf, 0.0), writes=[B_imp])
    sc = sbt(e1p, "sc", [128, 64], F32)
    sc2 = sbt(e1p, "sc2", [128, 64], F32)
    mx8 = sbt(e1p, "mx8", [128, 16], F32)
    selt = sbt(e1p, "selt", [128, 64], BF16)
    den4 = sbt(e1p, "den4", [128, 8], F32)
    B_sel = K.buf()
    cmask = [sbt(e1p, "cmask%d" % i, [128, 4, 128], BF16) for i in range(2)]
    B_cmask = [K.buf(), K.buf()]
    negselT4 = [sbt(e1p, "negselT4_%d" % i, [64, 4, 128], BF16) for i in range(2)]
    B_negsel = [K.buf(), K.buf()]
    o_brs2 = [[sbt(e1p, "o_br%d_%d" % (t, i), [128, 3, 4, 65], F32) for i in range(2)] for t in range(2)]
    B_obrs2 = [[[K.buf() for _ in range(3)] for _ in range(2)] for _ in range(2)]
    coef = sbt(e1p, "coef", [128, 3, 4], F32)
    B_coef = K.buf()
    otmp = sbt(e1p, "otmp", [128, 3, 4, 64], F32)
    B_otmp = K.buf()
    o_nsa = sbt(e1p, "o_nsa", [128, 512], BF16)
    B_onsa = K.buf()
    o_nsaT = sbt(e1p, "o_nsaT", [128, 4, 512], BF16)
    B_onsaT = K.buf()

    print("SBUF remaining after pass1a alloc:", nc.sbuf_bytes_remaining)
    def rmsnorm_T(src_dram, xb, Bxb, xnb, Bxnb, gcol, hT_dst, B_hT_dst, x_preloaded=False):
        if not x_preloaded:
            K.dma("sp", xb, src_dram, writes=[Bxb])
        K.op("act", lambda e: e.activation(out=xnb, in_=xb, func=AF.Square, accum_out=st4[:, 0:1]),
             reads=[Bxb], writes=[Bxnb, B_st])
        K.op("act", lambda e: e.activation(out=st4[:, 1:2], in_=st4[:, 0:1], func=AF.Ln, scale=1.0 / D, bias=EPS), reads=[B_st], writes=[B_st])
        K.op("act", lambda e: e.activation(out=st4[:, 2:3], in_=st4[:, 1:2], func=AF.Exp, scale=-0.5), reads=[B_st], writes=[B_st])
        K.op("dve", lambda e: e.tensor_scalar(out=xnb, in0=xb, scalar1=st4[:, 2:3], scalar2=None, op0=ALU.mult),
             reads=[Bxb, B_st], writes=[Bxnb])
        pt, pb = ps_g()
        ptb = pt.bitcast(BF16).rearrange("p (k t) -> p k t", k=8)
        for k in range(8):
            K.op("pe", lambda e: e.transpose(out=ptb[:, k, :], in_=xnb[:, k * 128:(k + 1) * 128], identity=ident_b),
                 reads=[Bxnb, B_const], writes=[pb], sig=(k == 7))
        K.op("dve", lambda e: e.tensor_tensor(out=hT_dst, in0=ptb, in1=gcol.unsqueeze(2).to_broadcast([128, 8, 128]),
                                              op=ALU.mult), reads=[pb, B_const], writes=[B_hT_dst])

    def projT(cols_ap_fn, nt, B_hts, evac):
        pt, pb = ps_g()
        for k in range(8):
            K.op("pe", lambda e: e.matmul(pt[:, 0:nt], lhsT=cols_ap_fn(k), rhs=hT[:, k, 0:nt], start=(k == 0),
                                          stop=(k == 7)), reads=[B_win] + B_hts, writes=[pb], sig=(k == 7))
        evac(pt[:, 0:nt], pb)

    def tile_src(T):
        return xh[T * 128:(T + 1) * 128, :] if T < 16 else xo[(T - 16) * 128:(T - 15) * 128, :]

    def compress_st(seg0, nseg):
        for kv in range(2):
            pgs = [ps_g(), ps_g()]
            i = 0
            for r in range(2):
                for s in range(16):
                    c0 = 16 * r + s
                    for g in range(2):
                        pt, pb = pgs[g]
                        K.op("pe", lambda e: e.matmul(pt[:, 0:nseg], lhsT=w1sb[g * 64:(g + 1) * 64, kv, r * 16 + s, :],
                                                      rhs=kvcT[g * 64:(g + 1) * 64, kv, c0:c0 + 16 * (nseg - 1) + 1:16],
                                                      start=(i == 0), stop=(i == 31)),
                             reads=[B_win, B_kvc[kv]], writes=[pb], sig=(i == 31))
                    i += 1
            for g in range(2):
                pt, pb = pgs[g]
                K.op("act", lambda e: e.activation(out=gh[:, kv, g, 0:nseg], in_=pt[:, 0:nseg], func=AF.Gelu_apprx_tanh,
                                                   bias=b1eff[:, kv:kv + 1]), reads=[pb, B_win], writes=[B_gh])
            pt, pb = ps_g()
            for g in range(2):
                K.op("pe", lambda e: e.matmul(pt[:, 0:nseg], lhsT=w2pad[:, kv, g, :], rhs=gh[:, kv, g, 0:nseg],
                                              start=(g == 0), stop=(g == 1)), reads=[B_win, B_gh], writes=[pb],
                     sig=(g == 1))
            dst, Bd = (kcT, B_kc) if kv == 0 else (vcT, B_vc)
            K.op("dve", lambda e: e.tensor_copy(out=dst[:, seg0:seg0 + nseg], in_=pt[:, 0:nseg]), reads=[pb],
                 writes=[Bd])
            K.op("dve", lambda e: e.tensor_copy(out=kvcT[:, kv, 0:16], in_=kvcT[:, kv, 16 * nseg:16 * nseg + 16]),
                 reads=[B_kvc[kv]], writes=[B_kvc[kv]])

    def vc_transpose(ch):
        pt, pb = ps_g()
        ptb = pt.bitcast(BF16)
        K.op("pe", lambda e: e.transpose(out=ptb[:, 0:128], in_=vcT[:, ch * 128:(ch + 1) * 128], identity=ident_b),
             reads=[B_vc, B_const], writes=[pb])
        K.op("dve", lambda e: e.tensor_copy(out=vcaug[:, ch, :, 0:64],
                                            in_=ptb[:, 0:128].rearrange("p (g d) -> p g d", g=2)),
             reads=[pb], writes=[B_vcaug[ch]])

    def front_st(T0, ntl, full):
        nt = ntl * 128
        for j in range(ntl):
            T = T0 + j
            rmsnorm_T(tile_src(T), xbuf[j % 2], B_x[j % 2], xn[j % 2], B_xn[j % 2], gcol_mix,
                      hT[:, :, j * 128:(j + 1) * 128], B_hT[j])
        Bh = B_hT[0:ntl]
        for kv in range(2):
            projT(lambda k: w_in_sb[:, k, 512 + kv * 128:640 + kv * 128], nt, Bh,
                  lambda p, pb: K.op("act", lambda e: e.copy(out=kvcT[:, kv, 16:16 + nt], in_=p), reads=[pb],
                                     writes=[B_kvc[kv]]))
        projT(lambda k: w_in_sb[:, k, 768:896], nt, Bh,
              lambda p, pb: (K.op("dve", lambda e: e.tensor_copy(out=kslcE[0][0:64, T0 * 128:T0 * 128 + nt],
                                                                  in_=p[0:64]), reads=[pb], writes=B_kslc[T0:T0 + ntl]),
                             K.op("act", lambda e: e.copy(out=kslcE[1][64:128, T0 * 128:T0 * 128 + nt], in_=p[64:128]),
                                  reads=[pb], writes=B_kslc[T0:T0 + ntl])))
        if T0 + ntl > 8:
            pt, pb = ps_g()
            for k in range(8):
                K.op("pe", lambda e: e.matmul(pt[:, 0:nt], lhsT=w_in_sb[:, k, 1024:1152], rhs=hT[:, k, 0:nt],
                                              start=(k == 0), stop=(k == 7)), reads=[B_win] + Bh, writes=[pb], sig=(k == 7))
            for j in range(ntl):
                sl = (T0 + j) % 8
                K.op("act", lambda e: e.copy(out=kwinT[:, sl, :], in_=pt[:, j * 128:(j + 1) * 128]), reads=[pb],
                     writes=[B_kwin[sl]])
        if full:
            for r in range(4):
                projT(lambda k: w_in_sb[:, k, 128 * r:128 * r + 128], nt, Bh,
                      lambda p, pb: (K.op("act", lambda e: e.copy(out=qTz[0][0:64, r, 0:nt], in_=p[0:64]), reads=[pb],
                                          writes=[B_qT]),
                                     K.op("dve", lambda e: e.tensor_copy(out=qTz[1][64:128, r, 0:nt], in_=p[64:128]),
                                          reads=[pb], writes=[B_qT]),
                                     K.op("dve", lambda e: e.tensor_copy(out=qS[0][0:64, r, 0:nt], in_=p[0:64]),
                                          reads=[pb], writes=[B_qT]),
                                     K.op("act", lambda e: e.copy(out=qS[1][64:128, r, 0:nt], in_=p[64:128]),
                                          reads=[pb], writes=[B_qT])))
            for cc in range(4):
                projT(lambda k: w_in_sb[:, k, 1304 + cc * 128:1432 + cc * 128], nt, Bh,
                      lambda p, pb: K.op("act", lambda e: e.activation(out=uT[:, cc, 0:nt], in_=p,
                                                                       func=AF.Gelu_apprx_tanh), reads=[pb],
                                         writes=[B_uT]))
        for j in range(ntl):
            T = T0 + j
            kt_, Bkt = kvtok[j % 2], B_kvtok[j % 2]
            pa, pab = ps_g()
            for k in range(8):
                K.op("pe", lambda e: e.matmul(pa, lhsT=hT[:, k, j * 128:(j + 1) * 128], rhs=w_in_sb[:, k, 512:1024],
                                              start=(k == 0), stop=(k == 7)), reads=[B_win, B_hT[j]], writes=[pab], sig=(k == 7))
            pbk, pbb = ps_g()
            for k in range(8):
                K.op("pe", lambda e: e.matmul(pbk[:, 0:280], lhsT=hT[:, k, j * 128:(j + 1) * 128],
                                              rhs=w_in_sb[:, k, 1024:1304], start=(k == 0), stop=(k == 7)),
                     reads=[B_win, B_hT[j]], writes=[pbb], sig=(k == 7))
            K.op("act", lambda e: e.copy(out=kt_[:, 0:512], in_=pa), reads=[pab], writes=[Bkt])
            K.op("dve", lambda e: e.tensor_copy(out=kt_[:, 512:768], in_=pbk[:, 0:256]), reads=[pbb], writes=[Bkt])
            if full:
                K.op("act", lambda e: e.activation(out=sg[:, j, :], in_=pbk[:, 256:280], func=AF.Sigmoid),
                     reads=[pbb], writes=[B_sg[j]])
            K.op("pool", lambda e: e.tensor_copy(out=Vslc[:, T, :, 0:64],
                                                 in_=kt_[:, 384:512].rearrange("p (g d) -> p g d", g=2)),
                 reads=[Bkt], writes=[B_vslc[T]])
            if T >= 8:
                K.op("pool", lambda e: e.tensor_copy(out=Vwin[:, T % 8, :, 0:64],
                                                     in_=kt_[:, 640:768].rearrange("p (g d) -> p g d", g=2)),
                     reads=[Bkt], writes=[B_vwin[T % 8]])
            if T >= 16:
                for br in range(3):
                    K.dma("sp", kv_o[br, (T - 16) * 128:(T - 15) * 128, :], kt_[:, br * 256:(br + 1) * 256],
                          reads=[Bkt])
            if full:
                pc, pcb = ps_g()
                for k in range(8):
                    K.op("pe", lambda e: e.matmul(pc, lhsT=hT[:, k, j * 128:(j + 1) * 128],
                                                  rhs=w_in_sb[:, k, 1816:2328], start=(k == 0), stop=(k == 7)),
                         reads=[B_win, B_hT[j]], writes=[pcb], sig=(k == 7))
                K.op("act", lambda e: e.activation(out=gv, in_=pc, func=AF.Gelu_apprx_tanh), reads=[pcb],
                     writes=[B_gv])
                K.op("act", lambda e: e.activation(out=vn_b, in_=gv, func=AF.Square,
                                                   accum_out=st4[:, 4:5]), reads=[B_gv], writes=[B_vnb, B_st])
                K.op("act", lambda e: e.activation(out=st4[:, 5:6], in_=st4[:, 4:5], func=AF.Ln, scale=1.0 / 512, bias=EPS), reads=[B_st], writes=[B_st])
                K.op("act", lambda e: e.activation(out=st4[:, 6:7], in_=st4[:, 5:6], func=AF.Exp, scale=-0.5), reads=[B_st], writes=[B_st])
                K.op("dve", lambda e: e.scalar_tensor_tensor(out=vn_b, in0=gv, scalar=st4[:, 6:7], in1=gnB,
                                                             op0=ALU.mult, op1=ALU.mult),
                     reads=[B_gv, B_st, B_win], writes=[B_vnb])
                if T == 31:
                    K.op("dve", lambda e: e.scalar_tensor_tensor(out=vn_f, in0=gv, scalar=st4[:, 6:7], in1=gnB,
                                                                 op0=ALU.mult, op1=ALU.mult),
                         reads=[B_gv, B_st, B_win], writes=[B_vnf])
                    K.dma("sp", vn_o, vn_f, reads=[B_vnf])
                pm, pmb = ps_g()
                for g in range(4):
                    K.op("pe", lambda e: e.matmul(pm[:, g * 128:(g + 1) * 128], lhsT=vn_b[:, g * 128:(g + 1) * 128],
                                                  rhs=WsT[:, g, :], start=(g == 0), stop=(g == 3),
                                                  skip_group_check=True), reads=[B_vnb, B_win], writes=[pmb], sig=(g == 3))
                K.op("dve", lambda e: e.tensor_tensor(out=tmpm, in0=pm, in1=bsB, op=ALU.add), reads=[pmb, B_win],
                     writes=[B_tmpm])
                K.op("dve", lambda e: e.tensor_tensor(out=mixT[:, :, j * 128:(j + 1) * 128], in0=r4(tmpm),
                                                       in1=uT[:, :, j * 128:(j + 1) * 128], op=ALU.mult),
                     reads=[B_tmpm, B_uT], writes=[B_mixT])
        compress_st(T0 * 8, ntl * 8)

    LOOKAHEAD = 2

    def attn_units(T, j):
        units = []
        for kind in ("win", "cmp", "slc"):
            for g in range(2):
                if kind == "slc":
                    kts = list(range(0, T + 1))
                elif kind == "win":
                    kts = list(range(T - 4, T + 1))
                else:
                    kts = [0] if T < 16 else [0, 1]
                for ki, kt in enumerate(kts):
                    units.append(dict(kind=kind, g=g, kt=kt, first=(ki == 0), last=(ki == len(kts) - 1)))
        state = {}

        def emit_qk(u):
            kind, g, kt = u["kind"], u["g"], u["kt"]
            ps, psb = ps_g()
            ex = []
            if kind == "slc":
                lk, Bk = kslcE[g][:, kt * 128:(kt + 1) * 128], B_kslc[kt]
                rq, Brq = qS[g][:, :, j * 128:(j + 1) * 128], [B_qT, B_negsel[g]]
                if kt == T:
                    ex.append((ident_b, mask_diag4.rearrange("p a b -> p (a b)"), []))
            elif kind == "win":
                lk, Bk = kwinT[:, kt % 8, :], B_kwin[kt % 8]
                rq, Brq = qTz[g][:, :, j * 128:(j + 1) * 128], [B_qT]
                if kt == T:
                    ex.append((ident_b, mask_diag4.rearrange("p a b -> p (a b)"), []))
                if kt == T - 4:
                    ex.append((ident_b, mask_band4.rearrange("p a b -> p (a b)"), []))
                if kt < 16:
                    ex.append((ones_b[0:1, 0:128], histneg4, []))
            else:
                lk, Bk = kcT[:, kt * 128:(kt + 1) * 128], B_kc
                rq, Brq = qTz[g][:, :, j * 128:(j + 1) * 128], [B_qT]
                if (T < 16 and kt == 0) or (T >= 16 and kt == 1):
                    ex.append((ident_b, cmask[T % 2].rearrange("p a b -> p (a b)"), [B_cmask[T % 2]]))
            K.op("pe", lambda e: e.matmul(ps, lhsT=lk, rhs=rq, start=True, stop=(len(ex) == 0)),
                 reads=[Bk, B_win] + Brq, writes=[psb], sig=(len(ex) == 0))
            for xi, (l2, r2, Bs) in enumerate(ex):
                K.op("pe", lambda e: e.matmul(ps, lhsT=l2, rhs=r2, start=False, stop=(xi == len(ex) - 1)),
                     reads=[B_win, B_const] + Bs, writes=[psb], sig=(xi == len(ex) - 1))
            u["ps"], u["psb"] = ps, psb

        def emit_exp_pv(u):
            kind, g, kt = u["kind"], u["g"], u["kt"]
            ps, psb = u["ps"], u["psb"]
            ei = etr[0]
            etr[0] = (ei + 1) % len(ET)
            if kind == "cmp":
                K.op("act", lambda e: e.activation(out=ET[ei], in_=ps, func=AF.Exp, scale=0.125,
                                                   bias=cmpbT[:, kt:kt + 1]), reads=[psb, B_win], writes=[B_ET[ei]])
            else:
                K.op("act", lambda e: e.activation(out=ET[ei], in_=ps, func=AF.Exp, scale=0.125), reads=[psb],
                     writes=[B_ET[ei]])
            if u["first"]:
                state["po"] = ps_o()
            po, pob = state["po"]
            po3 = po[:, 0:260].rearrange("p (h c) -> p h c", h=4)
            if kind == "slc":
                va, Bv = Vslc[:, kt, g, :], B_vslc[kt]
            elif kind == "win":
                va, Bv = Vwin[:, kt % 8, g, :], B_vwin[kt % 8]
            else:
                va, Bv = vcaug[:, kt, g, :], B_vcaug[kt]
            for h in range(4):
                K.op("pe", lambda e: e.matmul(po3[:, h, :], lhsT=ET[ei][:, h * 128:(h + 1) * 128], rhs=va,
                                              start=(u["first"] and h == 0), stop=(u["last"] and h == 3),
                                              skip_group_check=True), reads=[B_ET[ei], Bv], writes=[pob], sig=(h == 3))
            if u["last"]:
                bi = {"cmp": 0, "slc": 1, "win": 2}[kind]
                K.op("act", lambda e: e.copy(out=o_brs2[T % 2][g][:, bi], in_=po3), reads=[pob],
                     writes=[B_obrs2[T % 2][g][bi]])

        n = len(units)
        for i in range(min(LOOKAHEAD, n)):
            emit_qk(units[i])
        for i in range(n):
            if i + LOOKAHEAD < n:
                emit_qk(units[i + LOOKAHEAD])
            emit_exp_pv(units[i])

    def select_blocks(T, g, j):
        L = 8 * T + 8
        S3 = psS.rearrange("p (h c) -> p h c", h=4)
        for bk in range(2):
            Bb = B_S[bk]
            for hh in range(2):
                h = bk * 2 + hh
                K.op("pe", lambda e: e.matmul(S3[:, h, 0:L], lhsT=qTz[g][:, h, j * 128:(j + 1) * 128],
                                              rhs=kcT[:, 0:L], start=(hh == 0), stop=False,
                                              skip_group_check=True), reads=[B_qT, B_kc], writes=[Bb])
            K.op("pe", lambda e: e.matmul(S3[:, bk * 2:bk * 2 + 2, 0:L], lhsT=ones_b[0:1, 0:128],
                                          rhs=cmpb_row2[:, :, 0:L], start=False, stop=False, skip_group_check=True),
                 reads=[B_win, B_const], writes=[Bb])
            K.op("pe", lambda e: e.matmul(S3[:, bk * 2:bk * 2 + 2, L - 8:L], lhsT=ident_b, rhs=vis8x2, start=False,
                                          stop=True, skip_group_check=True), reads=[B_win, B_const], writes=[Bb])
        ck(2)
        for h in range(4):
            K.op("act", lambda e: e.activation(out=Eh[h % 2][:, 0:L], in_=S3[:, h, 0:L], func=AF.Exp, scale=0.125,
                                               accum_out=den4[:, h:h + 1]), reads=[B_S[h // 2]],
                 writes=[B_Eh[h % 2], B_sel])
            K.op("dve", lambda e: e.tensor_scalar(out=den4[:, 4 + h:5 + h], in0=den4[:, h:h + 1], scalar1=1e-30,
                                                  scalar2=None, op0=ALU.max), reads=[B_sel], writes=[B_sel])
            K.op("dve", lambda e: e.reciprocal(out=den4[:, 4 + h:5 + h], in_=den4[:, 4 + h:5 + h]), reads=[B_sel],
                 writes=[B_sel])
            if h == 0:
                K.op("dve", lambda e: e.tensor_scalar(out=impbuf[:, 0:L], in0=Eh[0][:, 0:L], scalar1=den4[:, 4:5],
                                                      scalar2=None, op0=ALU.mult), reads=[B_Eh[0], B_sel],
                     writes=[B_imp])
            else:
                K.op("dve", lambda e: e.scalar_tensor_tensor(out=impbuf[:, 0:L], in0=Eh[h % 2][:, 0:L],
                                                             scalar=den4[:, 4 + h:5 + h], in1=impbuf[:, 0:L],
                                                             op0=ALU.mult, op1=ALU.add),
                     reads=[B_Eh[h % 2], B_sel, B_imp], writes=[B_imp])
        ck(3)
        K.op("dve", lambda e: e.tensor_tensor(out=sc, in0=impbuf[:, 0:253:4], in1=impbuf[:, 1:254:4], op=ALU.add),
             reads=[B_imp], writes=[B_sel])
        for m in (2, 3, 4):
            K.op("dve", lambda e: e.tensor_tensor(out=sc, in0=sc, in1=impbuf[:, m:m + 253:4], op=ALU.add),
                 reads=[B_imp, B_sel], writes=[B_sel])
        K.op("dve", lambda e: e.tensor_tensor(out=sc, in0=sc, in1=selbB, op=ALU.add), reads=[B_sel, B_win],
             writes=[B_sel])
        K.op("dve", lambda e: e.tensor_tensor(out=sc, in0=sc, in1=Mst[:, 64 - 2 * T:128 - 2 * T], op=ALU.add),
             reads=[B_sel, B_win], writes=[B_sel])
        ck(4)
        K.op("dve", lambda e: e.max(out=mx8[:, 0:8], in_=sc), reads=[B_sel], writes=[B_sel])
        K.op("dve", lambda e: e.match_replace(out=sc2, in_to_replace=mx8[:, 0:8], in_values=sc, imm_value=-3e38),
             reads=[B_sel], writes=[B_sel])
        K.op("dve", lambda e: e.max(out=mx8[:, 8:16], in_=sc2), reads=[B_sel], writes=[B_sel])
        K.op("dve", lambda e: e.tensor_scalar(out=mx8[:, 15:16], in0=mx8[:, 15:16], scalar1=-1e29, scalar2=None,
                                              op0=ALU.max), reads=[B_sel], writes=[B_sel])
        K.op("dve", lambda e: e.tensor_scalar(out=sc2, in0=sc, scalar1=mx8[:, 15:16], scalar2=None, op0=ALU.is_ge),
             reads=[B_sel], writes=[B_sel])
        off = (1 - g) * 64
        K.op("dve", lambda e: e.tensor_scalar(out=selt2[g][:, off:off + 64], in0=sc2, scalar1=-1.0, scalar2=-NEGM,
                                              op0=ALU.add, op1=ALU.mult), reads=[B_sel], writes=[B_sel])
        pt, pb = ps_g()
        ptb = pt.bitcast(BF16)
        K.op("pe", lambda e: e.transpose(out=ptb[:, 0:128], in_=selt2[g], identity=ident_b), reads=[B_sel, B_const],
             writes=[pb])
        K.op("dve", lambda e: e.tensor_copy(out=qS[g][off:off + 64, :, j * 128:(j + 1) * 128],
                                            in_=ptb[off:off + 64, 0:128].unsqueeze(1).to_broadcast([64, 4, 128])),
             reads=[pb], writes=[B_negsel[g]])

    def attn_select(T, j):
        ch = 0 if T < 16 else 1
        base = 2048 * ch + 15 - 128 * T
        K.op("pool", lambda e: e.affine_select(out=cmask[T % 2], in_=r4(negs_b), pattern=[[0, 4], [-1, 128]],
                                               compare_op=ALU.is_gt, fill=0.0, base=base, channel_multiplier=16),
             reads=[B_const], writes=[B_cmask[T % 2]])
        for g in range(2):
            select_blocks(T, g, j)

    def attn_combine(T, j):
        for g in range(2):
            o_br, B_obr = o_brs2[T % 2][g], B_obrs2[T % 2][g]
            K.op("dve", lambda e: e.tensor_scalar(out=coef, in0=o_br[:, :, :, 64], scalar1=1e-30, scalar2=None,
                                                  op0=ALU.max), reads=B_obr, writes=[B_coef])
            ck(10)
            K.op("dve", lambda e: e.reciprocal(out=coef, in_=coef), reads=[B_coef], writes=[B_coef])
            ck(11)
            K.op("dve", lambda e: e.tensor_tensor(out=coef, in0=coef,
                                                  in1=sg[:, j, g * 12:(g + 1) * 12].rearrange("p (h b) -> p b h", b=3),
                                                  op=ALU.mult), reads=[B_coef, B_sg[j]], writes=[B_coef])
            ck(12)
            K.op("dve", lambda e: e.tensor_tensor(out=otmp, in0=o_br[:, :, :, 0:64],
                                                  in1=coef.unsqueeze(3).to_broadcast([128, 3, 4, 64]), op=ALU.mult),
                 reads=B_obr + [B_coef], writes=[B_otmp])
            ck(13)
            K.op("dve", lambda e: e.tensor_tensor(out=otmp[:, 0], in0=otmp[:, 0], in1=otmp[:, 1], op=ALU.add),
                 reads=[B_otmp], writes=[B_otmp])
            K.op("dve", lambda e: e.tensor_tensor(out=o_nsa[:, g * 256:(g + 1) * 256].rearrange(
                "p (h d) -> p h d", h=4), in0=otmp[:, 0], in1=otmp[:, 2], op=ALU.add), reads=[B_otmp],
                writes=[B_onsa])
        pt, pb = ps_g()
        ptb = pt.bitcast(BF16).rearrange("p (k t) -> p k t", k=8)
        for cc in range(4):
            K.op("pe", lambda e: e.transpose(out=ptb[:, cc, :], in_=o_nsa[:, cc * 128:(cc + 1) * 128],
                                             identity=ident_b), reads=[B_onsa, B_const], writes=[pb])
        K.op("dve", lambda e: e.tensor_copy(out=o_nsaT[:, :, j * 128:(j + 1) * 128], in_=ptb[:, 0:4, :]), reads=[pb],
             writes=[B_onsaT])

    hist_sts = [(0, 4), (4, 4), (8, 4), (12, 3)]
    own_sts = [(15, 4), (19, 4), (23, 4), (27, 4), (31, 1)]
    for (T0, ntl) in hist_sts:
        front_st(T0, ntl, False)
        if stage <= 2:
            K.finish()
            return nc
    vc_transpose(0)
    for si, (T0, ntl) in enumerate(own_sts):
        front_st(T0, ntl, True)
        vc_transpose(1)
        if si == 0:
            vc_transpose(0)
        if stage <= 3:
            K.finish()
            return nc
        attn_select(T0, 0)
        for j in range(ntl):
            if j + 1 < ntl:
                attn_select(T0 + j + 1, j + 1)
            attn_units(T0 + j, j)
            if j >= 1:
                attn_combine(T0 + j - 1, j - 1)
        attn_combine(T0 + ntl - 1, ntl - 1)
        if stage <= 5:
            K.finish()
            return nc
        nt = ntl * 128
        K.dma("sp", scr_om[si, :, 0:4, 0:nt], o_nsaT[:, :, 0:nt], reads=[B_onsaT], writes=[B_scr_om[si]])
        K.dma("sp", scr_om[si, :, 4:8, 0:nt], mixT[:, :, 0:nt], reads=[B_mixT], writes=[B_scr_om[si]])

    K.barrier()
    e1p.close()
    e2w = ExitStack()
    wg_sb = sbt(e2w, "wg_sb", [128, 8, 2048], BF16)
    B_w2 = K.buf("w_pass1b")
    for k in range(8):
        K.dma("pool", wg_sb[:, k, :], w_in[k * 128:(k + 1) * 128, 2328:4376], writes=[B_w2])
    wpa = sbt(e2w, "wpa", [128, 4, D], BF16)
    wpb = sbt(e2w, "wpb", [128, 4, D], BF16)
    for k in range(4):
        K.dma("pool", wpa[:, k, :], w_proj_a[k * 128:(k + 1) * 128, :], writes=[B_w2])
        K.dma("pool", wpb[:, k, :], w_proj_b[k * 128:(k + 1) * 128, :], writes=[B_w2])
    if do_sample:
        s1 = ExitStack()
        sg_s = sbt(s1, "sg_s", [4, 24], F32)
        qTs = [sbt(s1, "qTs%d" % i, [128, 4, 4], BF16) for i in range(2)]
        kTn = sbt(s1, "kTn", [128, 2, 4], BF16)
        Vn = sbt(s1, "Vn", [4, 2, 2, 65], BF16)
        dmask = sbt(s1, "dmask", [4, 8, 4], BF16)
        ptx_sb = sbt(s1, "ptx_sb", [128, 32], I32)
        ptf = sbt(s1, "ptf", [128, 32], F32)
        idx_i = sbt(s1, "idx_i", [128, 32], I32)
        pmod_sb = sbt(s1, "pmod_sb", [128, 1], F32)
        Oacc = sbt(s1, "Oacc", [4, 3, 8, 65], F32)
        s0 = ExitStack()
        xs_sb = sbt(s0, "xs_sb", [4, D], F32)
        B_xs = K.buf()
        K.dma("sp", xs_sb, xs, writes=[B_xs])
        sqs = sbt(s0, "sqs", [4, D], BF16)
        sts = sbt(s0, "sts", [4, 8], F32)
        xns = sbt(s0, "xns", [4, D], BF16)
        B_s = K.buf()
        K.op("act", lambda e: e.activation(out=sqs, in_=xs_sb, func=AF.Square, accum_out=sts[:, 0:1]), reads=[B_xs],
             writes=[B_s])
        K.op("act", lambda e: e.activation(out=sts[:, 1:2], in_=sts[:, 0:1], func=AF.Ln, scale=1.0 / D, bias=EPS),
             reads=[B_s], writes=[B_s])
        K.op("act", lambda e: e.activation(out=sts[:, 2:3], in_=sts[:, 1:2], func=AF.Exp, scale=-0.5), reads=[B_s],
             writes=[B_s])
        K.op("dve", lambda e: e.tensor_scalar(out=xns, in0=xs_sb, scalar1=sts[:, 2:3], scalar2=None, op0=ALU.mult),
             reads=[B_xs, B_s], writes=[B_s])
        pt, pb = ps_g()
        ptb = pt.bitcast(BF16)[:, 0:32].rearrange("p (k t) -> p k t", k=8)
        for k in range(8):
            K.op("pe", lambda e: e.transpose(out=ptb[:, k, :], in_=xns[:, k * 128:(k + 1) * 128],
                                             identity=ident_b[0:4, 0:4]), reads=[B_s, B_const], writes=[pb])
        K.op("dve", lambda e: e.tensor_tensor(out=hT_s, in0=ptb, in1=gcol_mix.unsqueeze(2).to_broadcast([128, 8, 4]),
                                              op=ALU.mult), reads=[pb, B_const], writes=[B_hTs])
        zs = sbt(s0, "zs", [4, 1816], F32)
        B_zs = K.buf()
        for c0 in range(512, 2328, 512):
            c1 = min(c0 + 512, 2328)
            pt, pb = ps_g()
            for k in range(8):
                K.op("pe", lambda e: e.matmul(pt[0:4, 0:c1 - c0], lhsT=hT_s[:, k, :], rhs=w_in_sb[:, k, c0:c1],
                                              start=(k == 0), stop=(k == 7)), reads=[B_win, B_hTs], writes=[pb], sig=(k == 7))
            K.op("dve", lambda e: e.tensor_copy(out=zs[:, c0 - 512:c1 - 512], in_=pt[0:4, 0:c1 - c0]), reads=[pb],
                 writes=[B_zs])
        K.dma("sp", kvs_o, zs[:, 0:768], reads=[B_zs])
        u_s = sbt(s0, "u_s", [4, 512], F32)
        v_s = sbt(s0, "v_s", [4, 512], F32)
        vn_s = sbt(s0, "vn_s", [4, 512], F32)
        mixs = sbt(s0, "mixs", [4, 512], F32)
        mixsb = sbt(s0, "mixsb", [4, 512], BF16)
        B_sm = K.buf()
        K.op("act", lambda e: e.activation(out=sg_s, in_=zs[:, 768:792], func=AF.Sigmoid), reads=[B_zs], writes=[B_sm])
        K.op("act", lambda e: e.activation(out=u_s, in_=zs[:, 792:1304], func=AF.Gelu_apprx_tanh), reads=[B_zs],
             writes=[B_sm])
        K.op("act", lambda e: e.activation(out=v_s, in_=zs[:, 1304:1816], func=AF.Gelu_apprx_tanh), reads=[B_zs],
             writes=[B_sm])
        K.op("act", lambda e: e.activation(out=sqs[:, 0:512], in_=v_s, func=AF.Square, accum_out=sts[:, 4:5]),
             reads=[B_sm], writes=[B_s])
        K.op("act", lambda e: e.activation(out=sts[:, 5:6], in_=sts[:, 4:5], func=AF.Ln, scale=1.0 / 512, bias=EPS),
             reads=[B_s], writes=[B_s])
        K.op("act", lambda e: e.activation(out=sts[:, 6:7], in_=sts[:, 5:6], func=AF.Exp, scale=-0.5), reads=[B_s],
             writes=[B_s])
        K.op("dve", lambda e: e.scalar_tensor_tensor(out=vn_s, in0=v_s, scalar=sts[:, 6:7], in1=gnB[0:4, :],
                                                     op0=ALU.mult, op1=ALU.mult), reads=[B_sm, B_s, B_win],
             writes=[B_sm])
        K.dma("sp", vns_o, vn_s, reads=[B_sm])
        ws00 = sbt(s0, "ws00", [4, 8], F32)
        with nc.allow_non_contiguous_dma(reason="4 scalars"):
            K.dma("sp", ws00[:, 0:4], gmlp_ws[:, 0, 0:1].rearrange("g o -> o g").partition_broadcast(4), writes=[B_sm])
            K.dma("sp", ws00[:, 4:8], gmlp_bs[:, 0:512:128].partition_broadcast(4), writes=[B_sm])
        K.op("dve", lambda e: e.tensor_tensor(out=r4(mixs), in0=r4(vn_s), in1=ws00[:, 0:4].unsqueeze(2).to_broadcast(
            [4, 4, 128]), op=ALU.mult), reads=[B_sm], writes=[B_sm])
        K.op("dve", lambda e: e.tensor_tensor(out=r4(mixs), in0=r4(mixs), in1=ws00[:, 4:8].unsqueeze(2).to_broadcast(
            [4, 4, 128]), op=ALU.add), reads=[B_sm], writes=[B_sm])
        K.op("dve", lambda e: e.tensor_tensor(out=mixsb, in0=mixs, in1=u_s, op=ALU.mult), reads=[B_sm], writes=[B_sm])
        pt, pb = ps_g()
        ptb = pt.bitcast(BF16)[:, 0:16].rearrange("p (k t) -> p k t", k=4)
        for k in range(4):
            K.op("pe", lambda e: e.transpose(out=ptb[:, k, :], in_=mixsb[:, k * 128:(k + 1) * 128],
                                             identity=ident_b[0:4, 0:4]), reads=[B_sm, B_const], writes=[pb])
        K.op("dve", lambda e: e.tensor_copy(out=omT_s[:, 4:8, :], in_=ptb), reads=[pb], writes=[B_omTs])
        B_qTs = K.buf()
        for i in range(2):
            K.op("pool", lambda e: e.memset(qTs[i], 0.0), writes=[B_qTs])
        for r in range(4):
            pt, pb = ps_g()
            for k in range(8):
                K.op("pe", lambda e: e.matmul(pt[:, 0:4], lhsT=w_in_sb[:, k, r * 128:(r + 1) * 128], rhs=hT_s[:, k, :],
                                              start=(k == 0), stop=(k == 7)), reads=[B_win, B_hTs], writes=[pb], sig=(k == 7))
            K.op("dve", lambda e: e.tensor_copy(out=qTs[0][0:64, r, :], in_=pt[0:64, 0:4]), reads=[pb], writes=[B_qTs])
            K.op("dve", lambda e: e.tensor_copy(out=qTs[1][64:128, r, :], in_=pt[64:128, 0:4]), reads=[pb],
                 writes=[B_qTs])
        B_kTn = K.buf()
        for bi_, c0 in enumerate((768, 1024)):
            pt, pb = ps_g()
            for k in range(8):
                K.op("pe", lambda e: e.matmul(pt[:, 0:4], lhsT=w_in_sb[:, k, c0:c0 + 128], rhs=hT_s[:, k, :],
                                              start=(k == 0), stop=(k == 7)), reads=[B_win, B_hTs], writes=[pb], sig=(k == 7))
            K.op("dve", lambda e: e.tensor_copy(out=kTn[:, bi_, :], in_=pt[:, 0:4]), reads=[pb], writes=[B_kTn])
        B_Vn = K.buf()
        K.op("pool", lambda e: e.memset(Vn, 1.0), writes=[B_Vn])
        for bi_, c0 in enumerate((384, 640)):
            K.op("dve", lambda e: e.tensor_copy(out=Vn[:, bi_, :, 0:64],
                                                in_=zs[:, c0:c0 + 128].rearrange("p (g d) -> p g d", g=2)),
                 reads=[B_zs], writes=[B_Vn])
        K.op("pool", lambda e: e.affine_select(out=dmask, in_=zeros_b[0:4, 0:32].rearrange("p (a b) -> p a b", a=8),
                                               pattern=[[0, 8], [1, 4]], compare_op=ALU.is_equal, fill=NEGM, base=0,
                                               channel_multiplier=-1), reads=[B_const], writes=[B_Vn])
        B_idx = K.buf()
        K.dma("sp", ptx_sb, ptx, writes=[B_idx])
        K.dma("sp", pmod_sb, pmod, writes=[B_idx])
        K.op("dve", lambda e: e.tensor_copy(out=ptf, in_=ptx_sb), reads=[B_idx], writes=[B_idx])
        K.op("dve", lambda e: e.tensor_scalar(out=ptf, in0=ptf, scalar1=8.0, scalar2=pmod_sb[:, 0:1], op0=ALU.mult,
                                              op1=ALU.add), reads=[B_idx], writes=[B_idx])
        K.op("dve", lambda e: e.tensor_copy(out=idx_i, in_=ptf), reads=[B_idx], writes=[B_idx])

        K.dma("sp", wins_o[:, 511, :], zs[:, 512:768], reads=[B_zs])
        K.barrier()
        s0.close()
        pg = [sbt(s1, "pg%d" % i, [128, 16, 256], F32) for i in range(2)]
        B_pg = [K.buf(), K.buf()]
        kTs = sbt(s1, "kTs", [128, 2, 16, 129], BF16)
        B_kTs = K.buf()
        kcs = sbt(s1, "kcs", [128, 2, 1024], BF16)
        B_kcs = K.buf()
        ghs = sbt(s1, "ghs", [128, 2, 128], BF16)
        B_ghs = K.buf()
        vcas = sbt(s1, "vcas", [128, 8, 2, 65], BF16)
        B_vcas = K.buf()
        K.op("pool", lambda e: e.memset(vcas, 1.0), writes=[B_vcas])
        KTt = [sbt(s1, "KTt%d" % i, [128, 4, 128], BF16) for i in range(2)]
        B_KTt = [K.buf(), K.buf()]
        Vas = [sbt(s1, "Vas%d" % i, [128, 16, 2, 65], BF16) for i in range(2)]
        B_Vas = [K.buf(), K.buf()]
        for i in range(2):
            K.op("pool", lambda e: e.memset(Vas[i], 1.0), writes=[B_Vas[i]])
        ETs = [sbt(s1, "ETs%d" % i, [128, 16, 8, 4], BF16) for i in range(2)]
        B_ETs = [K.buf(), K.buf()]
        etc = [0]
        B_Oacc = K.buf()
        K.op("pool", lambda e: e.memset(Oacc, 0.0), writes=[B_Oacc])
        Es = sbt(s1, "Es", [4, 1024], F32)
        imps = sbt(s1, "imps", [1, 1032], F32)
        scs = sbt(s1, "scs", [1, 264], F32)
        scs2 = sbt(s1, "scs2", [1, 264], F32)
        mxs = sbt(s1, "mxs", [1, 16], F32)
        negx = sbt(s1, "negx", [1, 256, 4], BF16)
        mcol = sbt(s1, "mcol", [128, 2, 8], F32)
        dens = sbt(s1, "dens", [4, 4], F32)
        B_sel = K.buf()
        B_mcol = K.buf()
        K.op("pool", lambda e: e.memset(imps, 0.0), writes=[B_sel])
        s0col = sbt(s1, "s0col", [128, 1], F32)
        K.op("pool", lambda e: e.memset(s0col, 0.0), writes=[B_sel])
        K.op("pool", lambda e: e.memset(s0col[0:1, :], NEGM), writes=[B_sel])
        s0row = sbt(s1, "s0row", [1, 512], BF16)
        K.op("pool", lambda e: e.memset(s0row, 0.0), writes=[B_sel])
        K.op("pool", lambda e: e.memset(s0row[:, 0:1], NEGM), writes=[B_sel])
        ones4f = sbt(s1, "ones4f", [4, 1], F32)
        K.op("pool", lambda e: e.memset(ones4f, 1.0), writes=[B_sel])
        bon = sbt(s1, "bon", [1, 264], F32)
        K.op("pool", lambda e: e.memset(bon, 0.0), writes=[B_sel])
        K.op("pool", lambda e: e.memset(bon[:, 0:1], 1e6), writes=[B_sel])
        K.op("pool", lambda e: e.memset(bon[:, 255:257], 1e6), writes=[B_sel])
        K.op("pool", lambda e: e.memset(bon[:, 257:264], -1e30), writes=[B_sel])

        def acc_banks():
            return allb[4], allb[5]

        def evac_acc(bi):
            (pA, pAb), (pB, pBb) = acc_banks()
            for g, (pp, ppb) in enumerate(((pA, pAb), (pB, pBb))):
                K.op("dve", lambda e: e.tensor_tensor(out=Oacc[:, bi, g * 4:(g + 1) * 4, :],
                                                      in0=Oacc[:, bi, g * 4:(g + 1) * 4, :],
                                                      in1=pp[0:4, 0:260].rearrange("p (h c) -> p h c", h=4),
                                                      op=ALU.add), reads=[ppb, B_Oacc], writes=[B_Oacc])

        def pv_tile(ETt, Bet, s_idx, Vt, Bv, first, kparts=128):
            (pA, pAb), (pB, pBb) = acc_banks()
            for g, (pp, ppb) in enumerate(((pA, pAb), (pB, pBb))):
                p3 = pp[0:4, 0:260].rearrange("p (h c) -> p h c", h=4)
                for r in range(4):
                    K.op("pe", lambda e: e.matmul(p3[:, r, :], lhsT=ETt[0:kparts, s_idx, g * 4 + r, :],
                                                  rhs=Vt[0:kparts, g, :], start=(first and r == 0), stop=False,
                                                  skip_group_check=True), reads=[Bet, Bv], writes=[ppb], sig=(r == 3))

        jobs = []
        for b_ in range(4):
            jobs += [("cmp", b_, g_) for g_ in range(8)] + [("slc", b_, g_) for g_ in range(8)]
            jobs += [("win", b_, w_) for w_ in range(4)]
        jstate = {"issued": 0, "taken": 0}

        def issue_job(i):
            kind_, b_, x_ = jobs[i]
            pgb, Bpg = pg[i % 2], B_pg[i % 2]
            if kind_ == "win":
                K.dma("sp", pgb[:, 0, :], cache_win[b_, x_ * 128:(x_ + 1) * 128, :], writes=[Bpg])
                return
            cache = cache_cmp if kind_ == "cmp" else cache_slc
            col = b_ * 8 + x_
            K._wait("pool", K._deps([B_idx], [Bpg]))
            ring = K.dring["pool"]
            si = ring[K.dpos["pool"] % len(ring)]
            K.dpos["pool"] += 1
            if K.dtgt[si] > 0:
                K._wait("pool", {("d", si): K.dtgt[si]})
            ins = nc.gpsimd.indirect_dma_start(out=pgb.rearrange("p s c -> p (s c)"), out_offset=None, in_=cache,
                                               in_offset=bass.IndirectOffsetOnAxis(ap=idx_i[:, col:col + 1], axis=0))
            K.dtgt[si] += 16
            ins.then_inc(K.dsem[si], 16)
            K._mark(("d", si), K.dtgt[si], [B_idx], [Bpg])

        def take_job(kind_, b_, x_):
            i = jstate["taken"]
            assert jobs[i] == (kind_, b_, x_), (jobs[i], kind_, b_, x_)
            while jstate["issued"] <= min(i + 1, len(jobs) - 1):
                issue_job(jstate["issued"])
                jstate["issued"] += 1
            jstate["taken"] += 1
            return pg[i % 2], B_pg[i % 2]

        gi = [0]
        for b in range(4):
            for i in range(2):
                K.op("dve", lambda e: e.memset(ETs[i], 0.0), writes=[B_ETs[i]])
            K.op("dve", lambda e: e.memset(kTs, 0.0), writes=[B_kTs])
            for grp in range(8):
                pgb, Bpg = take_job("cmp", b, grp)
                for kv in range(2):
                    for s4 in range(4):
                        pt, pb = ps_g()
                        for q_ in range(4):
                            s = s4 * 4 + q_
                            K.op("pe", lambda e: e.transpose(out=pt[:, q_ * 128:(q_ + 1) * 128],
                                                             in_=pgb[:, s, kv * 128:(kv + 1) * 128], identity=ident_f),
                                 reads=[Bpg, B_const], writes=[pb])
                        eng = "act" if (s4 % 2 == 0) else "dve"
                        if eng == "act":
                            K.op("act", lambda e: e.copy(out=kTs[:, kv, s4 * 4:(s4 + 1) * 4, 1:129],
                                                         in_=pt.rearrange("p (a b) -> p a b", a=4)), reads=[pb],
                                 writes=[B_kTs])
                        else:
                            K.op("dve", lambda e: e.tensor_copy(out=kTs[:, kv, s4 * 4:(s4 + 1) * 4, 1:129],
                                                                in_=pt.rearrange("p (a b) -> p a b", a=4)),
                                 reads=[pb], writes=[B_kTs])
                for kv in range(2):
                    pgs = [ps_g(), ps_g()]
                    i = 0
                    for r in range(2):
                        for s in range(16):
                            for g in range(2):
                                pt, pb = pgs[g]
                                K.op("pe", lambda e: e.matmul(pt[:, 0:128],
                                                              lhsT=w1sb[g * 64:(g + 1) * 64, kv, r * 16 + s, :],
                                                              rhs=kTs[g * 64:(g + 1) * 64, kv, s, r:r + 128],
                                                              start=(i == 0), stop=(i == 31)),
                                     reads=[B_win, B_kTs], writes=[pb], sig=(i == 31))
                            i += 1
                    for g in range(2):
                        pt, pb = pgs[g]
                        K.op("act", lambda e: e.activation(out=ghs[:, g, :], in_=pt[:, 0:128],
                                                           func=AF.Gelu_apprx_tanh, bias=b1eff[:, kv:kv + 1]),
                             reads=[pb, B_win], writes=[B_ghs])
                    pt, pb = ps_g()
                    for g in range(2):
                        K.op("pe", lambda e: e.matmul(pt[:, 0:128], lhsT=w2pad[:, kv, g, :], rhs=ghs[:, g, :],
                                                      start=(g == 0), stop=(g == 1)), reads=[B_win, B_ghs],
                             writes=[pb], sig=(g == 1))
                    K.op("dve", lambda e: e.tensor_copy(out=kcs[:, kv, grp * 128:(grp + 1) * 128], in_=pt[:, 0:128]),
                         reads=[pb], writes=[B_kcs])
                    K.op("dve", lambda e: e.tensor_copy(out=kTs[:, kv, :, 0:1], in_=kTs[:, kv, :, 128:129]),
                         reads=[B_kTs], writes=[B_kTs])
            for ch in range(8):
                pt, pb = ps_g()
                ptb = pt.bitcast(BF16)
                K.op("pe", lambda e: e.transpose(out=ptb[:, 0:128], in_=kcs[:, 1, ch * 128:(ch + 1) * 128],
                                                 identity=ident_b), reads=[B_kcs, B_const], writes=[pb])
                K.op("dve", lambda e: e.tensor_copy(out=vcas[:, ch, :, 0:64],
                                                    in_=ptb[:, 0:128].rearrange("p (g d) -> p g d", g=2)),
                     reads=[pb], writes=[B_vcas])
            ei = etc[0] % 2
            etc[0] += 1
            pt, pb = ps_g()
            p3 = pt[:, 0:64].rearrange("p (c h) -> p c h", c=8)
            for ch in range(8):
                for g in range(2):
                    K.op("pe", lambda e: e.matmul(p3[:, ch, g * 4:(g + 1) * 4], lhsT=kcs[:, 0, ch * 128:(ch + 1) * 128],
                                                  rhs=qTs[g][:, :, b], start=(ch == 0 and g == 0), stop=False,
                                                  skip_group_check=True), reads=[B_kcs, B_qTs], writes=[pb],
                         sig=(ch == 7 and g == 1))
            K.op("act", lambda e: e.activation(out=ETs[ei][:, 0:1, :, b], in_=p3[:, 0:1, :], func=AF.Exp, scale=0.125,
                                               bias=s0col[:, 0:1]), reads=[pb, B_sel], writes=[B_ETs[ei]])
            K.op("act", lambda e: e.activation(out=ETs[ei][:, 1:8, :, b], in_=p3[:, 1:8, :], func=AF.Exp, scale=0.125),
                 reads=[pb], writes=[B_ETs[ei]])
            for ch in range(8):
                pv_tile(ETs[ei], B_ETs[ei], ch, vcas[:, ch], B_vcas, first=(ch == 0))
            evac_acc(0)
            for g in range(2):
                for hf2 in range(2):
                    bk, bkb = allb[6 + hf2]
                    K.op("pe", lambda e: e.matmul(bk[0:4, :], lhsT=qTs[g][:, :, b],
                                                  rhs=kcs[:, 0, hf2 * 512:(hf2 + 1) * 512], start=True,
                                                  stop=(hf2 == 1)), reads=[B_qTs, B_kcs], writes=[bkb], sig=(hf2 == 1))
                    if hf2 == 0:
                        K.op("pe", lambda e: e.matmul(bk[0:4, :], lhsT=ones_b[0:1, 0:4], rhs=s0row, start=False,
                                                      stop=True), reads=[B_sel, B_const], writes=[bkb])
                K.op("act", lambda e: e.activation(out=Es, in_=psS[0:4, :], func=AF.Exp, scale=0.125,
                                                   accum_out=dens[:, 0:1]), reads=B_S, writes=[B_sel])
                K.op("dve", lambda e: e.tensor_scalar(out=dens[:, 1:2], in0=dens[:, 0:1], scalar1=1e-30, scalar2=None,
                                                      op0=ALU.max), reads=[B_sel], writes=[B_sel])
                K.op("dve", lambda e: e.reciprocal(out=dens[:, 1:2], in_=dens[:, 1:2]), reads=[B_sel], writes=[B_sel])
                K.op("dve", lambda e: e.tensor_scalar(out=Es, in0=Es, scalar1=dens[:, 1:2], scalar2=None,
                                                      op0=ALU.mult), reads=[B_sel], writes=[B_sel])
                for hf2 in range(2):
                    bk, bkb = allb[hf2]
                    K.op("pe", lambda e: e.matmul(bk[0:1, :], lhsT=ones4f, rhs=Es[:, hf2 * 512:(hf2 + 1) * 512],
                                                  start=True, stop=True), reads=[B_sel], writes=[bkb])
                    K.op("dve", lambda e: e.tensor_copy(out=imps[:, hf2 * 512:(hf2 + 1) * 512], in_=bk[0:1, :]),
                         reads=[bkb], writes=[B_sel])
                K.op("dve", lambda e: e.tensor_tensor(out=scs[:, 0:257], in0=imps[:, 0:1025:4], in1=imps[:, 1:1026:4],
                                                      op=ALU.add), reads=[B_sel], writes=[B_sel])
                for m in (2, 3, 4):
                    K.op("dve", lambda e: e.tensor_tensor(out=scs[:, 0:257], in0=scs[:, 0:257],
                                                          in1=imps[:, m:m + 1025:4], op=ALU.add), reads=[B_sel],
                         writes=[B_sel])
                K.op("dve", lambda e: e.tensor_tensor(out=scs[:, 0:257], in0=scs[:, 0:257], in1=bon[:, 0:257],
                                                      op=ALU.add), reads=[B_sel], writes=[B_sel])
                K.op("dve", lambda e: e.tensor_copy(out=scs[:, 257:264], in_=bon[:, 257:264]), reads=[B_sel],
                     writes=[B_sel])
                K.op("dve", lambda e: e.max(out=mxs[:, 0:8], in_=scs), reads=[B_sel], writes=[B_sel])
                K.op("dve", lambda e: e.match_replace(out=scs2, in_to_replace=mxs[:, 0:8], in_values=scs,
                                                      imm_value=-3e38), reads=[B_sel], writes=[B_sel])
                K.op("dve", lambda e: e.max(out=mxs[:, 8:16], in_=scs2), reads=[B_sel], writes=[B_sel])
                K.op("dve", lambda e: e.tensor_scalar(out=scs2[:, 0:256], in0=scs[:, 0:256], scalar1=mxs[:, 15:16],
                                                      scalar2=None, op0=ALU.is_ge), reads=[B_sel], writes=[B_sel])
                K.op("dve", lambda e: e.tensor_scalar(out=scs2[:, 0:256], in0=scs2[:, 0:256], scalar1=-1.0,
                                                      scalar2=-NEGM, op0=ALU.add, op1=ALU.mult), reads=[B_sel],
                     writes=[B_sel])
                K.op("dve", lambda e: e.tensor_copy(out=negx, in_=scs2[:, 0:256].unsqueeze(2).to_broadcast(
                    [1, 256, 4])), reads=[B_sel], writes=[B_sel])
                pt, pb = ps_g()
                nx = negx.rearrange("o (a j) r -> o a (j r)", a=8)
                for grp in range(8):
                    K.op("pe", lambda e: e.matmul(pt[:, grp:grp + 1], lhsT=nx[:, grp, :], rhs=ones_b[0:1, 0:1],
                                                  start=(grp == 0), stop=(grp == 7), skip_group_check=True),
                         reads=[B_sel, B_const], writes=[pb], sig=(grp == 7))
                K.op("dve", lambda e: e.tensor_copy(out=mcol[:, g, :], in_=pt[:, 0:8]), reads=[pb], writes=[B_mcol])
            for grp in range(8):
                pgb, Bpg = take_job("slc", b, grp)
                vi = grp % 2
                K.op("dve", lambda e: e.tensor_copy(out=Vas[vi][:, 0:8, :, 0:64],
                                                    in_=pgb[:, 0:8, 128:256].rearrange("p s (g d) -> p s g d", g=2)),
                     reads=[Bpg], writes=[B_Vas[vi]])
                K.op("act", lambda e: e.copy(out=Vas[vi][:, 8:16, :, 0:64],
                                             in_=pgb[:, 8:16, 128:256].rearrange("p s (g d) -> p s g d", g=2)),
                     reads=[Bpg], writes=[B_Vas[vi]])
                ei = etc[0] % 2
                etc[0] += 1
                ps, psb = ps_g()
                ps4 = ps[:, 0:128].rearrange("p (s h) -> p s h", s=16)
                for s4 in range(4):
                    pt, pb = ps_g()
                    for q_ in range(4):
                        s = s4 * 4 + q_
                        K.op("pe", lambda e: e.transpose(out=pt[:, q_ * 128:(q_ + 1) * 128], in_=pgb[:, s, 0:128],
                                                         identity=ident_f), reads=[Bpg, B_const], writes=[pb])
                    kt_, Bkt_ = KTt[s4 % 2], B_KTt[s4 % 2]
                    if s4 % 2 == 0:
                        K.op("act", lambda e: e.copy(out=kt_, in_=pt.rearrange("p (a b) -> p a b", a=4)), reads=[pb],
                             writes=[Bkt_])
                    else:
                        K.op("dve", lambda e: e.tensor_copy(out=kt_, in_=pt.rearrange("p (a b) -> p a b", a=4)),
                             reads=[pb], writes=[Bkt_])
                    for q_ in range(4):
                        s = s4 * 4 + q_
                        for g in range(2):
                            K.op("pe", lambda e: e.matmul(ps4[:, s, g * 4:(g + 1) * 4], lhsT=kt_[:, q_, :],
                                                          rhs=qTs[g][:, :, b], start=(s == 0 and g == 0), stop=False,
                                                          skip_group_check=True), reads=[Bkt_, B_qTs], writes=[psb],
                                 sig=(q_ == 3 and g == 1))
                for g in range(2):
                    K.op("act", lambda e: e.activation(out=ETs[ei][:, :, g * 4:(g + 1) * 4, b],
                                                       in_=ps4[:, :, g * 4:(g + 1) * 4], func=AF.Exp, scale=0.125,
                                                       bias=mcol[:, g, grp:grp + 1]), reads=[psb, B_mcol],
                         writes=[B_ETs[ei]])
                for s in range(16):
                    pv_tile(ETs[ei], B_ETs[ei], s, Vas[vi][:, s], B_Vas[vi], first=(grp == 0 and s == 0))
            def new_token(bi_, first):
                pt, pb = ps_g()
                for g in range(2):
                    K.op("pe", lambda e: e.matmul(pt[0:4, g * 16:(g + 1) * 16], lhsT=kTn[:, bi_, :],
                                                  rhs=qTs[g].rearrange("p r b -> p (r b)"), start=(g == 0), stop=False,
                                                  skip_group_check=True), reads=[B_kTn, B_qTs], writes=[pb])
                K.op("pe", lambda e: e.matmul(pt[0:4, 0:32], lhsT=ident_b[0:4, 0:4],
                                              rhs=dmask.rearrange("p a b -> p (a b)"), start=False, stop=True,
                                              skip_group_check=True), reads=[B_Vn, B_const], writes=[pb])
                en = sbt(s1, "en_%d_%d" % (b, bi_), [4, 8, 4], BF16)
                Ben = K.buf()
                K.op("act", lambda e: e.activation(out=en, in_=pt[0:4, 0:32].rearrange("p (a b) -> p a b", a=8),
                                                   func=AF.Exp, scale=0.125), reads=[pb], writes=[Ben])
                (pA, pAb), (pB, pBb) = acc_banks()
                for g, (pp, ppb) in enumerate(((pA, pAb), (pB, pBb))):
                    p3_ = pp[0:4, 0:260].rearrange("p (h c) -> p h c", h=4)
                    for r in range(4):
                        K.op("pe", lambda e: e.matmul(p3_[:, r, :], lhsT=en[:, g * 4 + r, :], rhs=Vn[:, bi_, g, :],
                                                      start=(first and r == 0), stop=False, skip_group_check=True),
                             reads=[Ben, B_Vn], writes=[ppb])
            if b == 0:
                new_token(0, False)
            evac_acc(1)
            for wt in range(4):
                pgb, Bpg = take_job("win", b, wt)
                vi = wt % 2
                K.op("dve", lambda e: e.tensor_copy(out=Vas[vi][:, 0, :, 0:64],
                                                     in_=pgb[:, 0, 128:256].rearrange("p (g d) -> p g d", g=2)),
                     reads=[Bpg], writes=[B_Vas[vi]])
                pt, pb = ps_g()
                K.op("pe", lambda e: e.transpose(out=pt[:, 0:128], in_=pgb[:, 0, 0:128], identity=ident_f),
                     reads=[Bpg, B_const], writes=[pb])
                kt_, Bkt_ = KTt[wt % 2], B_KTt[wt % 2]
                K.op("dve", lambda e: e.tensor_copy(out=kt_[:, 0, :], in_=pt[:, 0:128]), reads=[pb], writes=[Bkt_])
                ei = etc[0] % 2
                etc[0] += 1
                ps, psb = ps_g()
                for g in range(2):
                    K.op("pe", lambda e: e.matmul(ps[:, g * 4:(g + 1) * 4], lhsT=kt_[:, 0, :], rhs=qTs[g][:, :, b],
                                                  start=(g == 0), stop=(g == 1), skip_group_check=True),
                         reads=[Bkt_, B_qTs], writes=[psb], sig=(g == 1))
                K.op("act", lambda e: e.activation(out=ETs[ei][:, 0, :, b], in_=ps[:, 0:8], func=AF.Exp, scale=0.125),
                     reads=[psb], writes=[B_ETs[ei]])
                pv_tile(ETs[ei], B_ETs[ei], 0, Vas[vi][:, 0], B_Vas[vi], first=(wt == 0))
            if b == 0:
                new_token(1, False)
            evac_acc(2)
        for b in range(4):
            K.dma("sp", wins_o[b, 0:511, :], cache_win[b, 1:512, :])
        coefs = sbt(s1, "coefs", [4, 3, 8], F32)
        otm = sbt(s1, "otm", [4, 3, 8, 64], F32)
        onsas = sbt(s1, "onsas", [4, 512], BF16)
        B_cb = K.buf()
        K.op("dve", lambda e: e.tensor_scalar(out=coefs, in0=Oacc[:, :, :, 64], scalar1=1e-30, scalar2=None,
                                              op0=ALU.max), reads=[B_Oacc], writes=[B_cb])
        K.op("dve", lambda e: e.reciprocal(out=coefs, in_=coefs), reads=[B_cb], writes=[B_cb])
        K.op("dve", lambda e: e.tensor_tensor(out=coefs, in0=coefs, in1=sg_s.rearrange("p (h b) -> p b h", b=3),
                                              op=ALU.mult), reads=[B_cb, B_sm], writes=[B_cb])
        K.op("dve", lambda e: e.tensor_tensor(out=otm, in0=Oacc[:, :, :, 0:64],
                                              in1=coefs.unsqueeze(3).to_broadcast([4, 3, 8, 64]), op=ALU.mult),
             reads=[B_Oacc, B_cb], writes=[B_cb])
        K.op("dve", lambda e: e.tensor_tensor(out=otm[:, 0], in0=otm[:, 0], in1=otm[:, 1], op=ALU.add), reads=[B_cb],
             writes=[B_cb])
        K.op("dve", lambda e: e.tensor_tensor(out=onsas.rearrange("p (h d) -> p h d", h=8), in0=otm[:, 0],
                                              in1=otm[:, 2], op=ALU.add), reads=[B_cb], writes=[B_cb])
        pt, pb = ps_g()
        ptb = pt.bitcast(BF16)[:, 0:16].rearrange("p (k t) -> p k t", k=4)
        for k in range(4):
            K.op("pe", lambda e: e.transpose(out=ptb[:, k, :], in_=onsas[:, k * 128:(k + 1) * 128],
                                             identity=ident_b[0:4, 0:4]), reads=[B_cb, B_const], writes=[pb])
        K.op("dve", lambda e: e.tensor_copy(out=omT_s[:, 0:4, :], in_=ptb), reads=[pb], writes=[B_omTs])
        K.barrier()
        s1.close()
    K.barrier()

    e2 = ExitStack()
    wout = sbt(e2, "wout", [128, 8, D], BF16)
    for k in range(8):
        K.dma("pool", wout[:, k, :], w_out[k * 128:(k + 1) * 128, :], writes=[B_w2])
    xst = sbt(e2, "xst", [128, 4, D], F32)
    B_xst = [K.buf() for _ in range(4)]
    xn = [sbt(e2, "xn2_%d" % i, [128, D], BF16) for i in range(2)]
    st4 = sbt(e2, "st4b", [128, 8], F32)
    hT = sbt(e2, "hT2", [128, 8, 512], BF16)
    omT = sbt(e2, "omT", [128, 8, 512], BF16)
    B_omT = K.buf()
    g0s = sbt(e2, "g0s", [128, 512], F32)
    g1s = sbt(e2, "g1s", [128, 512], F32)
    B_gs = [K.buf(), K.buf()]
    mergedT = sbt(e2, "mergedT", [128, 8, 512], BF16)
    B_merged = K.buf()
    x1b = [sbt(e2, "x1b%d" % i, [128, D], F32) for i in range(2)]
    B_x1b = [K.buf(), K.buf()]
    B_st = K.buf()
    B_xn = [K.buf(), K.buf()]
    B_hT = [K.buf() for _ in range(4)]

    def p1b_merge(nt, om_ap, B_om_l, h_ap, B_h_l):
        for cc in range(8):
            pa, pab = ps_a()
            pbb_, pbb = ps_a()
            pg0, pg0b = ps_a()
            pg1, pg1b = ps_a()
            for k in range(4):
                K.op("pe", lambda e: e.matmul(pa[:, 0:nt], lhsT=wpa[:, k, cc * 128:(cc + 1) * 128], rhs=om_ap[:, k, 0:nt],
                                              start=(k == 0), stop=(k == 3)), reads=[B_w2] + B_om_l, writes=[pab], sig=(k == 3))
            for k in range(4):
                K.op("pe", lambda e: e.matmul(pbb_[:, 0:nt], lhsT=wpb[:, k, cc * 128:(cc + 1) * 128],
                                              rhs=om_ap[:, 4 + k, 0:nt], start=(k == 0), stop=(k == 3)),
                     reads=[B_w2] + B_om_l, writes=[pbb], sig=(k == 3))
            for k in range(8):
                K.op("pe", lambda e: e.matmul(pg0[:, 0:nt], lhsT=wg_sb[:, k, cc * 128:(cc + 1) * 128],
                                              rhs=h_ap[:, k, 0:nt], start=(k == 0), stop=(k == 7)),
                     reads=[B_w2] + B_h_l, writes=[pg0b], sig=(k == 7))
            for k in range(8):
                K.op("pe", lambda e: e.matmul(pg1[:, 0:nt], lhsT=wg_sb[:, k, 1024 + cc * 128:1152 + cc * 128],
                                              rhs=h_ap[:, k, 0:nt], start=(k == 0), stop=(k == 7)),
                     reads=[B_w2] + B_h_l, writes=[pg1b], sig=(k == 7))
            K.op("act", lambda e: e.activation(out=g0s[:, 0:nt], in_=pg0[:, 0:nt], func=AF.Sigmoid), reads=[pg0b],
                 writes=[B_gs[0]])
            K.op("act", lambda e: e.activation(out=g1s[:, 0:nt], in_=pg1[:, 0:nt], func=AF.Sigmoid), reads=[pg1b],
                 writes=[B_gs[1]])
            K.op("dve", lambda e: e.tensor_tensor(out=g0s[:, 0:nt], in0=g0s[:, 0:nt], in1=pa[:, 0:nt], op=ALU.mult),
                 reads=[B_gs[0], pab], writes=[B_gs[0]])
            K.op("dve", lambda e: e.tensor_tensor(out=g1s[:, 0:nt], in0=g1s[:, 0:nt], in1=pbb_[:, 0:nt], op=ALU.mult),
                 reads=[B_gs[1], pbb], writes=[B_gs[1]])
            K.op("pool", lambda e: e.tensor_tensor(out=mergedT[:, cc, 0:nt], in0=g0s[:, 0:nt], in1=g1s[:, 0:nt],
                                                   op=ALU.add), reads=B_gs, writes=[B_merged])

    for si, (T0, ntl) in enumerate(own_sts):
        nt = ntl * 128
        K.dma("sp", omT[:, :, 0:nt], scr_om[si, :, :, 0:nt], reads=[B_scr_om[si]], writes=[B_omT])
        for j in range(ntl):
            rmsnorm_T(tile_src(T0 + j), xst[:, j, :], B_xst[j], xn[j % 2], B_xn[j % 2], gcol_mix,
                      hT[:, :, j * 128:(j + 1) * 128], B_hT[j])
        Bh = B_hT[0:ntl]
        p1b_merge(nt, omT, [B_omT], hT, Bh)
        for j in range(ntl):
            T = T0 + j
            for n in range(2):
                px, pxb = ps_a()
                for k in range(8):
                    K.op("pe", lambda e: e.matmul(px, lhsT=mergedT[:, k, j * 128:(j + 1) * 128],
                                                  rhs=wout[:, k, n * 512:(n + 1) * 512], start=(k == 0), stop=(k == 7)),
                         reads=[B_merged, B_w2], writes=[pxb], sig=(k == 7))
                K.op("dve", lambda e: e.tensor_tensor(out=x1b[j % 2][:, n * 512:(n + 1) * 512], in0=px,
                                                      in1=xst[:, j, n * 512:(n + 1) * 512], op=ALU.add),
                     reads=[pxb, B_xst[j]], writes=[B_x1b[j % 2]])
            K.dma("sp", scr_x1[T - 15], x1b[j % 2], reads=[B_x1b[j % 2]], writes=[B_scr_x1[T - 15]])

    if do_sample:
        K.dma("sp", xst[0:4, 0, :], xs, writes=[B_xst[0]])
        p1b_merge(4, omT_s, [B_omTs], hT_s, [B_hTs])
        for n in range(2):
            px, pxb = ps_a()
            for k in range(8):
                K.op("pe", lambda e: e.matmul(px[0:4, :], lhsT=mergedT[:, k, 0:4], rhs=wout[:, k, n * 512:(n + 1) * 512],
                                              start=(k == 0), stop=(k == 7)), reads=[B_merged, B_w2], writes=[pxb], sig=(k == 7))
            K.op("dve", lambda e: e.tensor_tensor(out=x1b[0][0:4, n * 512:(n + 1) * 512], in0=px[0:4, :],
                                                  in1=xst[0:4, 0, n * 512:(n + 1) * 512], op=ALU.add),
                 reads=[pxb, B_xst[0]], writes=[B_x1b[0]])
        K.dma("sp", scr_x1s, x1b[0][0:4, :], reads=[B_x1b[0]], writes=[B_scr_x1s])

    K.barrier()
    e2.close()
    e2w.close()
    e1.close()

    e3 = ExitStack()
    wup = sbt(e3, "wup", [128, 8, 2 * DFF], BF16)
    wdn = sbt(e3, "wdn", [128, 22, D], BF16)
    B_w3 = K.buf("w_pass2")
    for k in range(8):
        for c0 in range(0, 2 * DFF, 1408):
            K.dma("pool", wup[:, k, c0:c0 + 1408], w_up[k * 128:(k + 1) * 128, c0:c0 + 1408], writes=[B_w3])
    for f in range(22):
        K.dma("pool", wdn[:, f, :], w_down[f * 128:(f + 1) * 128, :], writes=[B_w3])
    cw = sbt(e3, "cw", [128, 66], F32)
    cb = sbt(e3, "cb", [128, 22], F32)
    tmpc = sbt(e3, "tmpc", [66, 128], F32)
    K.dma("sp", tmpc, conv_w, writes=[B_w3])
    pt, pb = ps_a()
    K.op("pe", lambda e: e.transpose(out=pt[:, 0:66], in_=tmpc, identity=ident_f[0:66, 0:66]), reads=[B_w3, B_const],
         writes=[pb])
    K.op("dve", lambda e: e.tensor_copy(out=cw, in_=pt[:, 0:66]), reads=[pb], writes=[B_w3])
    tmpb = sbt(e3, "tmpb", [22, 128], F32)
    K.dma("sp", tmpb, conv_b, writes=[B_w3])
    pt, pb = ps_a()
    K.op("pe", lambda e: e.transpose(out=pt[:, 0:22], in_=tmpb, identity=ident_f[0:22, 0:22]), reads=[B_w3, B_const],
         writes=[pb])
    K.op("dve", lambda e: e.tensor_copy(out=cb, in_=pt[:, 0:22]), reads=[pb], writes=[B_w3])
    gfinB = sbt(e3, "gfinB", [128, D], F32)
    K.dma("sp", gfinB, norm_final.partition_broadcast(128), writes=[B_w3])

    xld = [sbt(e3, "xld%d" % i, [128, D], F32) for i in range(2)]
    B_xld = [K.buf(), K.buf()]
    xn = [sbt(e3, "xn3_%d" % i, [128, D], BF16) for i in range(2)]
    st4 = sbt(e3, "st4c", [128, 8], F32)
    hT = sbt(e3, "hT3", [128, 8, 512], BF16)
    B_st = K.buf()
    B_xn = [K.buf(), K.buf()]
    B_hT = [K.buf() for _ in range(4)]
    abuf = [sbt(e3, "abuf%d" % i, [128, 2 + 512], F32) for i in range(2)]
    B_abuf = [K.buf(), K.buf()]
    acar = sbt(e3, "acar", [128, 22, 2], F32)
    B_acar = K.buf()
    K.op("pool", lambda e: e.memset(acar, 0.0), writes=[B_acar])
    cbuf = [sbt(e3, "cbuf0", [128, 512], F32)] * 2
    B_cbuf = [K.buf()] * 2
    actT = sbt(e3, "actT", [128, 22, 512], BF16)
    B_actT = K.buf()
    ybuf = [sbt(e3, "ybuf0", [128, D], F32)] * 2
    B_ybuf = [K.buf()] * 2
    convsb = sbt(e3, "convsb", [2, 512], F32)
    B_convsb = K.buf()

    for si, (T0, ntl) in enumerate(own_sts):
        nt = ntl * 128
        for j in range(ntl):
            K.dma("sp", xld[j % 2], scr_x1[T0 + j - 15], reads=[B_scr_x1[T0 + j - 15]], writes=[B_xld[j % 2]])
            rmsnorm_T(None, xld[j % 2], B_xld[j % 2], xn[j % 2], B_xn[j % 2], gcol_ffn,
                      hT[:, :, j * 128:(j + 1) * 128], B_hT[j], x_preloaded=True)
        Bh = B_hT[0:ntl]
        for f in range(22):
            pa, pab = ps_a()
            pbb_, pbb = ps_a()
            ab, Bab = abuf[f % 2], B_abuf[f % 2]
            cbf, Bcb = cbuf[f % 2], B_cbuf[f % 2]
            for k in range(8):
                K.op("pe", lambda e: e.matmul(pa[:, 0:nt], lhsT=wup[:, k, f * 128:(f + 1) * 128], rhs=hT[:, k, 0:nt],
                                              start=(k == 0), stop=(k == 7)), reads=[B_w3] + Bh, writes=[pab], sig=(k == 7))
            for k in range(8):
                K.op("pe", lambda e: e.matmul(pbb_[:, 0:nt], lhsT=wup[:, k, DFF + f * 128:DFF + (f + 1) * 128],
                                              rhs=hT[:, k, 0:nt], start=(k == 0), stop=(k == 7)),
                     reads=[B_w3] + Bh, writes=[pbb], sig=(k == 7))
            K.op("pool", lambda e: e.tensor_copy(out=ab[:, 0:2], in_=acar[:, f, :]), reads=[B_acar], writes=[Bab])
            K.op("act", lambda e: e.copy(out=ab[:, 2:2 + nt], in_=pa[:, 0:nt]), reads=[pab], writes=[Bab])
            if si == 0:
                K.op("dve", lambda e: e.tensor_scalar(out=ab[:, 2:130], in0=ab[:, 2:130], scalar1=halfcol[:, 0:1],
                                                      scalar2=None, op0=ALU.mult), reads=[Bab, B_const], writes=[Bab])
            K.op("pool", lambda e: e.tensor_copy(out=acar[:, f, :], in_=ab[:, nt:nt + 2]), reads=[Bab],
                 writes=[B_acar])
            K.op("dve", lambda e: e.tensor_scalar(out=cbf[:, 0:nt], in0=ab[:, 0:nt], scalar1=cw[:, f:f + 1],
                                                  scalar2=cb[:, f:f + 1], op0=ALU.mult, op1=ALU.add),
                 reads=[Bab, B_w3], writes=[Bcb])
            K.op("dve", lambda e: e.scalar_tensor_tensor(out=cbf[:, 0:nt], in0=ab[:, 1:1 + nt],
                                                         scalar=cw[:, 22 + f:23 + f], in1=cbf[:, 0:nt], op0=ALU.mult,
                                                         op1=ALU.add), reads=[Bab, B_w3, Bcb], writes=[Bcb])
            K.op("dve", lambda e: e.scalar_tensor_tensor(out=cbf[:, 0:nt], in0=ab[:, 2:2 + nt],
                                                         scalar=cw[:, 44 + f:45 + f], in1=cbf[:, 0:nt], op0=ALU.mult,
                                                         op1=ALU.add), reads=[Bab, B_w3, Bcb], writes=[Bcb])
            K.op("act", lambda e: e.activation(out=cbf[:, 0:nt], in_=cbf[:, 0:nt], func=AF.Gelu_apprx_tanh),
                 reads=[Bcb], writes=[Bcb])
            K.op("dve", lambda e: e.tensor_tensor(out=actT[:, f, 0:nt], in0=cbf[:, 0:nt], in1=pbb_[:, 0:nt],
                                                  op=ALU.mult), reads=[Bcb, pbb], writes=[B_actT])
        if si == 4:
            for r0 in range(0, 22, 4):
                nn = min(4, 22 - r0)
                pt, pb = ps_a()
                for f in range(r0, r0 + nn):
                    K.op("pe", lambda e: e.transpose(out=pt[0:2, (f - r0) * 128:(f - r0 + 1) * 128], in_=acar[:, f, :],
                                                     identity=ident_f), reads=[B_acar, B_const], writes=[pb])
                K.op("dve", lambda e: e.tensor_copy(out=convsb[:, 0:nn * 128], in_=pt[0:2, 0:nn * 128]),
                     reads=[pb], writes=[B_convsb])
                K.dma("sp", conv_o[:, r0 * 128:(r0 + nn) * 128], convsb[:, 0:nn * 128], reads=[B_convsb])
        for j in range(ntl):
            T = T0 + j
            yb, Byb = ybuf[j % 2], B_ybuf[j % 2]
            K.dma("sp", xld[j % 2], scr_x1[T - 15], reads=[B_scr_x1[T - 15]], writes=[B_xld[j % 2]])
            for n in range(2):
                py, pyb = ps_a()
                for f in range(22):
                    K.op("pe", lambda e: e.matmul(py, lhsT=actT[:, f, j * 128:(j + 1) * 128],
                                                  rhs=wdn[:, f, n * 512:(n + 1) * 512], start=(f == 0), stop=(f == 21)),
                         reads=[B_actT, B_w3], writes=[pyb], sig=(f == 21))
                K.op("dve", lambda e: e.tensor_tensor(out=yb[:, n * 512:(n + 1) * 512], in0=py,
                                                      in1=xld[j % 2][:, n * 512:(n + 1) * 512], op=ALU.add),
                     reads=[pyb, B_xld[j % 2]], writes=[Byb])
            if T < 16:
                continue
            K.op("act", lambda e: e.activation(out=xn[0], in_=yb, func=AF.Square, accum_out=st4[:, 4:5]), reads=[Byb],
                 writes=[B_xn[0], B_st])
            K.op("act", lambda e: e.activation(out=st4[:, 5:6], in_=st4[:, 4:5], func=AF.Ln, scale=1.0 / D, bias=EPS), reads=[B_st], writes=[B_st])
            K.op("act", lambda e: e.activation(out=st4[:, 6:7], in_=st4[:, 5:6], func=AF.Exp, scale=-0.5), reads=[B_st], writes=[B_st])
            K.op("dve", lambda e: e.scalar_tensor_tensor(out=yb, in0=yb, scalar=st4[:, 6:7], in1=gfinB,
                                                          op0=ALU.mult, op1=ALU.mult), reads=[Byb, B_st, B_w3],
                 writes=[Byb])
            K.dma("sp", y_o[(T - 16) * 128:(T - 15) * 128, :], yb, reads=[Byb])

    if do_sample:
        x1s = xld[0][0:4, :]
        K.dma("sp", x1s, scr_x1s, reads=[B_scr_x1s], writes=[B_xld[0]])
        K.op("act", lambda e: e.activation(out=xn[0][0:4, :], in_=x1s, func=AF.Square, accum_out=st4[0:4, 0:1]),
             reads=[B_xld[0]], writes=[B_xn[0], B_st])
        K.op("act", lambda e: e.activation(out=st4[0:4, 1:2], in_=st4[0:4, 0:1], func=AF.Ln, scale=1.0 / D, bias=EPS),
             reads=[B_st], writes=[B_st])
        K.op("act", lambda e: e.activation(out=st4[0:4, 2:3], in_=st4[0:4, 1:2], func=AF.Exp, scale=-0.5),
             reads=[B_st], writes=[B_st])
        K.op("dve", lambda e: e.tensor_scalar(out=xn[0][0:4, :], in0=x1s, scalar1=st4[0:4, 2:3], scalar2=None,
                                              op0=ALU.mult), reads=[B_xld[0], B_st], writes=[B_xn[0]])
        pt, pb = ps_a()
        ptb = pt.bitcast(BF16)[:, 0:32].rearrange("p (k t) -> p k t", k=8)
        for k in range(8):
            K.op("pe", lambda e: e.transpose(out=ptb[:, k, :], in_=xn[0][0:4, k * 128:(k + 1) * 128],
                                             identity=ident_b[0:4, 0:4]), reads=[B_xn[0], B_const], writes=[pb])
        K.op("dve", lambda e: e.tensor_tensor(out=hT[:, :, 0:4], in0=ptb,
                                              in1=gcol_ffn.unsqueeze(2).to_broadcast([128, 8, 4]), op=ALU.mult),
             reads=[pb, B_const], writes=[B_hT[0]])
        pab_, pabb = ps_a()
        p3 = pab_[:, 0:176].rearrange("p (f t) -> p f t", f=44)
        for f in range(44):
            for k in range(8):
                K.op("pe", lambda e: e.matmul(p3[:, f, :], lhsT=wup[:, k, f * 128:(f + 1) * 128], rhs=hT[:, k, 0:4],
                                              start=(f == 0 and k == 0), stop=(f == 43 and k == 7),
                                              skip_group_check=True), reads=[B_w3, B_hT[0]], writes=[pabb], sig=(f == 43 and k == 7))
        aTs = abuf[0][:, 0:88].rearrange("p (f t) -> p f t", f=22)
        K.op("act", lambda e: e.copy(out=abuf[0][:, 0:88], in_=pab_[:, 0:88]), reads=[pabb], writes=[B_abuf[0]])
        stt = abuf[1][0:8, 0:DFF // 8 * 0 + 514]
        stT = cbuf[0][:, 0:176].rearrange("p (f t) -> p f t", f=22)
        pst, pstb = ps_a()
        for f0 in range(0, 22, 4):
            nn = min(4, 22 - f0)
            K.dma("sp", abuf[1][0:8, 0:nn * 128], state_conv[:, f0 * 128:(f0 + nn) * 128], writes=[B_abuf[1]])
            for f in range(f0, f0 + nn):
                K.op("pe", lambda e: e.transpose(out=pst[:, f * 8:(f + 1) * 8],
                                                 in_=abuf[1][0:8, (f - f0) * 128:(f - f0 + 1) * 128],
                                                 identity=ident_f[0:8, 0:8]), reads=[B_abuf[1], B_const],
                     writes=[pstb])
        K.op("dve", lambda e: e.tensor_copy(out=cbuf[0][:, 0:176], in_=pst[:, 0:176]), reads=[pstb],
             writes=[B_cbuf[0]])
        stT4 = cbuf[0][:, 0:176].rearrange("p (f b j) -> p f b j", f=22, b=4)
        cS = cbuf[0][:, 256:344].rearrange("p (f t) -> p f t", f=22)
        tS = cbuf[0][:, 384:472].rearrange("p (f t) -> p f t", f=22)

        def bc(ap):
            return ap.unsqueeze(2).to_broadcast([128, 22, 4])
        K.op("dve", lambda e: e.tensor_tensor(out=cS, in0=stT4[:, :, :, 0], in1=bc(cw[:, 0:22]), op=ALU.mult),
             reads=[B_cbuf[0], B_w3], writes=[B_cbuf[0]])
        K.op("dve", lambda e: e.tensor_tensor(out=cS, in0=cS, in1=bc(cb), op=ALU.add), reads=[B_cbuf[0], B_w3],
             writes=[B_cbuf[0]])
        K.op("dve", lambda e: e.tensor_tensor(out=tS, in0=stT4[:, :, :, 1], in1=bc(cw[:, 22:44]), op=ALU.mult),
             reads=[B_cbuf[0], B_w3], writes=[B_cbuf[0]])
        K.op("dve", lambda e: e.tensor_tensor(out=cS, in0=cS, in1=tS, op=ALU.add), reads=[B_cbuf[0]],
             writes=[B_cbuf[0]])
        K.op("dve", lambda e: e.tensor_tensor(out=tS, in0=aTs, in1=bc(cw[:, 44:66]), op=ALU.mult),
             reads=[B_abuf[0], B_w3], writes=[B_cbuf[0]])
        K.op("dve", lambda e: e.tensor_tensor(out=cS, in0=cS, in1=tS, op=ALU.add), reads=[B_cbuf[0]],
             writes=[B_cbuf[0]])
        K.op("act", lambda e: e.activation(out=cS, in_=cS, func=AF.Gelu_apprx_tanh), reads=[B_cbuf[0]],
             writes=[B_cbuf[0]])
        K.op("dve", lambda e: e.tensor_tensor(out=actT[:, :, 0:4], in0=cS, in1=p3[:, 22:44, :], op=ALU.mult),
             reads=[B_cbuf[0], pabb], writes=[B_actT])
        K.dma("sp", convs_o[:, 0, :], state_conv.rearrange("(b j) f -> b j f", j=2)[:, 1, :])
        for r0 in range(0, 22, 4):
            nn = min(4, 22 - r0)
            pt, pb = ps_a()
            for f in range(r0, r0 + nn):
                K.op("pe", lambda e: e.transpose(out=pt[0:4, (f - r0) * 128:(f - r0 + 1) * 128], in_=aTs[:, f, :],
                                                 identity=ident_f), reads=[B_abuf[0], B_const], writes=[pb])
            K.op("dve", lambda e: e.tensor_copy(out=ybuf[0][0:4, 0:nn * 128], in_=pt[0:4, 0:nn * 128]), reads=[pb],
                 writes=[B_ybuf[0]])
            K.dma("sp", convs_o[:, 1, r0 * 128:(r0 + nn) * 128], ybuf[0][0:4, 0:nn * 128], reads=[B_ybuf[0]])
        yb = ybuf[0]
        for n in range(2):
            py, pyb = ps_a()
            for f in range(22):
                K.op("pe", lambda e: e.matmul(py[0:4, :], lhsT=actT[:, f, 0:4], rhs=wdn[:, f, n * 512:(n + 1) * 512],
                                              start=(f == 0), stop=(f == 21)), reads=[B_actT, B_w3], writes=[pyb], sig=(f == 21))
            K.op("dve", lambda e: e.tensor_tensor(out=yb[0:4, n * 512:(n + 1) * 512], in0=py[0:4, :],
                                                  in1=x1s[:, n * 512:(n + 1) * 512], op=ALU.add),
                 reads=[pyb, B_xld[0]], writes=[B_ybuf[0]])
        K.op("act", lambda e: e.activation(out=xn[0][0:4, :], in_=yb[0:4, :], func=AF.Square,
                                           accum_out=st4[0:4, 4:5]), reads=[B_ybuf[0]], writes=[B_xn[0], B_st])
        K.op("act", lambda e: e.activation(out=st4[0:4, 5:6], in_=st4[0:4, 4:5], func=AF.Ln, scale=1.0 / D, bias=EPS),
             reads=[B_st], writes=[B_st])
        K.op("act", lambda e: e.activation(out=st4[0:4, 6:7], in_=st4[0:4, 5:6], func=AF.Exp, scale=-0.5),
             reads=[B_st], writes=[B_st])
        K.op("dve", lambda e: e.scalar_tensor_tensor(out=yb[0:4, :], in0=yb[0:4, :], scalar=st4[0:4, 6:7],
                                                     in1=gfinB[0:4, :], op0=ALU.mult, op1=ALU.mult),
             reads=[B_ybuf[0], B_st, B_w3], writes=[B_ybuf[0]])
        K.dma("sp", ys_o, yb[0:4, :], reads=[B_ybuf[0]])

    K.finish()
    e3.close()
    es0.close()
    return nc


_NC_CACHE = {}


def _get_nc():
    if "nc" not in _NC_CACHE:
        _NC_CACHE["nc"] = build_program()
    return _NC_CACHE["nc"]


def kernel(x_prompt, x_sample, cache_cmp, cache_slc, cache_win, state_conv, page_table, norm_mix, w_in, cmp_pe,
           cmp_w1, cmp_b1, cmp_w2, gmlp_norm, gmlp_ws, gmlp_bs, w_proj_a, w_proj_b, w_out, norm_ffn, w_up, conv_w,
           conv_b, w_down, norm_final):
    f = np.float32
    x_prompt = np.asarray(x_prompt, f)
    nc = _get_nc()
    shared = {
        "w_in": np.ascontiguousarray(np.asarray(w_in, f)[0]),
        "cmp_pe": np.ascontiguousarray(np.asarray(cmp_pe, f)[0]),
        "cmp_w1": np.ascontiguousarray(np.asarray(cmp_w1, f)[0]),
        "cmp_b1": np.ascontiguousarray(np.asarray(cmp_b1, f)[0]),
        "cmp_w2": np.ascontiguousarray(np.asarray(cmp_w2, f)[0]),
        "gmlp_norm": np.ascontiguousarray(np.asarray(gmlp_norm, f).reshape(1, 512)),
        "gmlp_ws": np.ascontiguousarray(np.asarray(gmlp_ws, f)[0]),
        "gmlp_bs": np.ascontiguousarray(np.asarray(gmlp_bs, f).reshape(1, 512)),
        "w_proj_a": np.ascontiguousarray(np.asarray(w_proj_a, f)[0]),
        "w_proj_b": np.ascontiguousarray(np.asarray(w_proj_b, f)[0]),
        "w_out": np.ascontiguousarray(np.asarray(w_out, f)[0]),
        "norm_mix": np.ascontiguousarray(np.asarray(norm_mix, f).reshape(8, 128)),
        "norm_ffn": np.ascontiguousarray(np.asarray(norm_ffn, f).reshape(8, 128)),
        "w_up": np.ascontiguousarray(np.asarray(w_up, f)[0]),
        "conv_w": np.ascontiguousarray(np.asarray(conv_w, f).reshape(66, 128)),
        "conv_b": np.ascontiguousarray(np.asarray(conv_b, f).reshape(22, 128)),
        "w_down": np.ascontiguousarray(np.asarray(w_down, f)[0]),
        "norm_final": np.ascontiguousarray(np.asarray(norm_final, f).reshape(1, D)),
    }
    x_sample = np.asarray(x_sample, f)
    cc_flat = np.ascontiguousarray(np.asarray(cache_cmp, f)).reshape(40960, 4096)
    cs_flat = np.ascontiguousarray(np.asarray(cache_slc, f)).reshape(40960, 4096)
    cache_win = np.asarray(cache_win, f)
    state_conv = np.asarray(state_conv, f)
    page_table = np.asarray(page_table).astype(np.int32)
    pmod = (np.arange(128) % 8).astype(f).reshape(128, 1)
    in_maps = []
    for c in range(8):
        b, hf = c // 2, c % 2
        m = dict(shared)
        m["xo"] = np.ascontiguousarray(x_prompt[b, hf * 2048:(hf + 1) * 2048])
        m["xh"] = np.ascontiguousarray(x_prompt[b, (1 - hf) * 2048:(2 - hf) * 2048])
        selb = np.zeros((1, 64), f)
        cmpb = np.zeros((1, NSLOT), f)
        if hf == 0:
            selb[0, :32] = -1e30
            selb[0, 32] = 1e6
            cmpb[0, :129] = NEGM
            hsc = np.array([[NEGM, 0.0]], f)
        else:
            selb[0, 0] = 1e6
            cmpb[0, 0] = NEGM
            hsc = np.array([[0.0, 1.0]], f)
        m["selb"], m["cmpb"], m["hsc"] = selb, cmpb, hsc
        sb = slice(4 * c, 4 * c + 4)
        m["xs"] = np.ascontiguousarray(x_sample[sb, 0, :])
        m["cache_cmp"] = cc_flat
        m["cache_slc"] = cs_flat
        m["cache_win"] = np.ascontiguousarray(cache_win[0, sb].reshape(4, 512, 256))
        m["state_conv"] = np.ascontiguousarray(state_conv[0, sb].reshape(8, DFF))
        ptb_ = page_table[sb].reshape(4, 8, 16)
        ptx = np.repeat(ptb_, 8, axis=2)
        m["ptx"] = np.ascontiguousarray(ptx.transpose(2, 0, 1).reshape(128, 32)).astype(np.int32)
        m["pmod"] = pmod
        in_maps.append(m)
    res = run_bass_kernel_spmd(nc, in_maps, core_ids=list(range(8)))
    R = res.results
    y_prompt = np.stack([np.concatenate([R[2 * b]["y"], R[2 * b + 1]["y"]], 0) for b in range(4)]).astype(f)
    kvs = []
    for br in range(3):
        kvs.append(np.stack([np.concatenate([R[2 * b]["kvo"][br], R[2 * b + 1]["kvo"][br]], 0)
                             for b in range(4)]).reshape(1, 4, 4096, 2, 2, 64).astype(f))
    new_win_p = np.ascontiguousarray(kvs[2][:, :, -512:])
    new_v_p = np.stack([R[2 * b + 1]["vno"] for b in range(4)]).reshape(1, 4, 128, 512).astype(f)
    new_conv_p = np.stack([R[2 * b + 1]["convo"] for b in range(4)]).reshape(1, 4, 2, DFF).astype(f)
    y_s = np.concatenate([R[c]["ys"] for c in range(8)], 0).reshape(32, 1, D).astype(f)
    kvs_s = np.concatenate([R[c]["kvs"] for c in range(8)], 0).astype(f)
    cmp_s = np.ascontiguousarray(kvs_s[:, 0:256]).reshape(1, 32, 1, 2, 2, 64)
    slc_s = np.ascontiguousarray(kvs_s[:, 256:512]).reshape(1, 32, 1, 2, 2, 64)
    win_s = np.concatenate([R[c]["wins"] for c in range(8)], 0).reshape(1, 32, 512, 2, 2, 64).astype(f)
    v_s = np.concatenate([R[c]["vns"] for c in range(8)], 0).reshape(1, 32, 1, 512).astype(f)
    conv_s = np.concatenate([R[c]["convs"] for c in range(8)], 0).reshape(1, 32, 2, DFF).astype(f)
    return (y_prompt, y_s, kvs[0], kvs[1], new_win_p, new_v_p, new_conv_p, cmp_s, slc_s, win_s, v_s, conv_s)
```

```python
import numpy as np
from contextlib import ExitStack
import concourse.bass as bass
import concourse.mybir as mybir
from concourse.bass_utils import run_bass_kernel_spmd

F32 = mybir.dt.float32
BF16 = mybir.dt.bfloat16
I32 = mybir.dt.int32
AF = mybir.ActivationFunctionType
ALU = mybir.AluOpType
AX = mybir.AxisListType

NEGM = -30000.0
EPS = 1e-6
D = 1024
INC = 4376
DFF = 2816
NSLOT = 288


class Buf:
    __slots__ = ("name", "w", "r")

    def __init__(self, name):
        self.name = name
        self.w = {}
        self.r = {}


def _merge(d, k, v):
    if d.get(k, 0) < v:
        d[k] = v


class KB:
    NDMA = 32

    def __init__(self, nc):
        self.nc = nc
        self.eng = {"pe": nc.tensor, "act": nc.scalar, "dve": nc.vector, "pool": nc.gpsimd, "sp": nc.sync}
        self.sem = {e: nc.alloc_semaphore("cs_" + e) for e in ("pe", "act", "dve", "pool")}
        self.cnt = {e: 0 for e in self.sem}
        self.waited = {e: {} for e in self.eng}
        self.dsem = [nc.alloc_semaphore("ds%d" % i) for i in range(self.NDMA)]
        self.dtgt = [0] * self.NDMA
        self.dring = {"sp": list(range(0, 24)), "pool": list(range(24, 28)), "act": list(range(28, 32))}
        self.dpos = {"sp": 0, "pool": 0, "act": 0}
        self.nb = 0

    def buf(self, name=None):
        self.nb += 1
        return Buf(name or ("b%d" % self.nb))

    def _deps(self, reads, writes):
        deps = {}
        for b in reads:
            for k, v in b.w.items():
                _merge(deps, k, v)
        for b in writes:
            for k, v in b.w.items():
                _merge(deps, k, v)
            for k, v in b.r.items():
                _merge(deps, k, v)
        return deps

    def _wait(self, e, deps):
        eng = self.eng[e]
        wd = self.waited[e]
        for k, v in deps.items():
            if k == e and e == "pe":
                continue
            if wd.get(k, 0) >= v:
                continue
            sem = self.dsem[k[1]] if isinstance(k, tuple) else self.sem[k]
            eng.wait_ge(sem, v)
            wd[k] = v

    def _mark(self, key, val, reads, writes):
        for b in reads:
            _merge(b.r, key, val)
        for b in writes:
            b.w = {key: val}
            b.r = {}

    def op(self, e, fn, reads=(), writes=(), sig=True):
        self._wait(e, self._deps(reads, writes))
        ins = fn(self.eng[e])
        if sig:
            self.cnt[e] += 1
            ins.then_inc(self.sem[e], 1)
            val = self.cnt[e]
        else:
            val = self.cnt[e] + 1
        self._mark(e, val, reads, writes)

    def dma(self, q, out, in_, reads=(), writes=(), **kw):
        ring = self.dring[q]
        si = ring[self.dpos[q] % len(ring)]
        self.dpos[q] += 1
        deps = self._deps(reads, writes)
        if self.dtgt[si] > 0:
            _merge(deps, ("d", si), self.dtgt[si])
        self._wait(q, deps)
        ins = self.eng[q].dma_start(out=out, in_=in_, **kw)
        self.dtgt[si] += 16
        ins.then_inc(self.dsem[si], 16)
        self._mark(("d", si), self.dtgt[si], reads, writes)

    def barrier(self):
        deps = {}
        for e, c in self.cnt.items():
            if c > 0:
                deps[e] = c
        for si, t in enumerate(self.dtgt):
            if t > 0:
                deps[("d", si)] = t
        for e in self.eng:
            self._wait(e, deps)

    def finish(self):
        self.barrier()


class _Stop(Exception):
    pass


def build_program(do_sample=True, stage=99, sub=99):
    st = {}
    try:
        return _build_program(do_sample, stage, sub, st)
    except _Stop:
        st["K"].finish()
        return st["nc"]


def _build_program(do_sample, stage, sub, _st):
    nc = bass.Bass("TRN2", target_bir_lowering=False)
    K = KB(nc)
    _st["K"], _st["nc"] = K, nc
    es0 = ExitStack()

    def ck(n):
        if stage == 4 and sub == n:
            raise _Stop()

    def din(name, shape, dt=F32):
        return nc.dram_tensor(name, list(shape), dt, kind="ExternalInput").ap()

    def dout(name, shape, dt=F32):
        return nc.dram_tensor(name, list(shape), dt, kind="ExternalOutput").ap()

    def sbt(es, name, shape, dt):
        return es.enter_context(nc.sbuf_tensor(name, list(shape), dt)).ap()

    xh = din("xh", [2048, D])
    xo = din("xo", [2048, D])
    selb = din("selb", [1, 64])
    cmpb = din("cmpb", [1, NSLOT])
    hsc = din("hsc", [1, 2])
    w_in = din("w_in", [D, INC])
    cmp_pe = din("cmp_pe", [2, 32, 64])
    cmp_w1 = din("cmp_w1", [2, 2048, 128])
    cmp_b1 = din("cmp_b1", [2, 128])
    cmp_w2 = din("cmp_w2", [2, 128, 64])
    gmlp_norm = din("gmlp_norm", [1, 512])
    gmlp_ws = din("gmlp_ws", [4, 128, 128])
    gmlp_bs = din("gmlp_bs", [1, 512])
    w_proj_a = din("w_proj_a", [512, D])
    w_proj_b = din("w_proj_b", [512, D])
    w_out = din("w_out", [D, D])
    norm_mix = din("norm_mix", [8, 128])
    norm_ffn = din("norm_ffn", [8, 128])
    w_up = din("w_up", [D, 2 * DFF])
    conv_w = din("conv_w", [3 * 22, 128])
    conv_b = din("conv_b", [22, 128])
    w_down = din("w_down", [DFF, D])
    norm_final = din("norm_final", [1, D])

    xs = din("xs", [4, D])
    cache_cmp = din("cache_cmp", [40960, 4096])
    cache_slc = din("cache_slc", [40960, 4096])
    cache_win = din("cache_win", [4, 512, 256])
    state_conv = din("state_conv", [8, DFF])
    ptx = din("ptx", [128, 32], I32)
    pmod = din("pmod", [128, 1])
    ys_o = dout("ys", [4, D])
    kvs_o = dout("kvs", [4, 768])
    wins_o = dout("wins", [4, 512, 256])
    vns_o = dout("vns", [4, 512])
    convs_o = dout("convs", [4, 2, DFF])

    y_o = dout("y", [2048, D])
    kv_o = dout("kvo", [3, 2048, 256])
    vn_o = dout("vno", [128, 512])
    conv_o = dout("convo", [2, DFF])

    scr_om = nc.dram_tensor("scr_om", [5, 128, 8, 512], BF16, kind="Internal").ap()
    scr_x1 = nc.dram_tensor("scr_x1", [17, 128, D], F32, kind="Internal").ap()
    B_scr_om = [K.buf() for _ in range(5)]
    B_scr_x1 = [K.buf() for _ in range(17)]

    psS = nc.alloc_psum_tensor("psS", [128, 1024], F32).ap()
    banks = [nc.alloc_psum_tensor("pb%d" % i, [128, 512], F32).ap() for i in range(6)]
    B_bank = [K.buf("bank%d" % i) for i in range(6)]
    B_S = [K.buf("bankS0"), K.buf("bankS1")]
    allb = [(banks[i], B_bank[i]) for i in range(6)] + [(psS[:, 0:512], B_S[0]), (psS[:, 512:1024], B_S[1])]
    rot = {"g": 0, "o": 0, "a": 0}

    def ps_g():
        i = rot["g"]
        rot["g"] = (i + 1) % 4
        return allb[i]

    def ps_o():
        i = rot["o"]
        rot["o"] = (i + 1) % 2
        return allb[4 + i]

    def ps_a():
        i = rot["a"]
        rot["a"] = (i + 1) % 8
        return allb[i]

    c = es0
    ident_f = sbt(c, "ident_f", [128, 128], F32)
    ident_b = sbt(c, "ident_b", [128, 128], BF16)
    ones_f = sbt(c, "ones_f", [128, 128], F32)
    ones_b = sbt(c, "ones_b", [128, 128], BF16)
    zeros_b = sbt(c, "zeros_b", [128, 512], BF16)
    negs_b = sbt(c, "negs_b", [128, 512], BF16)
    B_const = K.buf("const")

    K.op("pool", lambda e: e.memset(ones_f, 1.0), writes=[B_const])
    K.op("pool", lambda e: e.memset(ones_b, 1.0), writes=[B_const])
    K.op("pool", lambda e: e.memset(zeros_b, 0.0), writes=[B_const])
    K.op("pool", lambda e: e.memset(negs_b, NEGM), writes=[B_const])
    K.op("pool", lambda e: e.affine_select(out=ident_f, in_=ones_f, pattern=[[-1, 128]], compare_op=ALU.is_equal,
                                           fill=0.0, base=0, channel_multiplier=1), writes=[B_const])
    K.op("pool", lambda e: e.affine_select(out=ident_b, in_=ones_b, pattern=[[-1, 128]], compare_op=ALU.is_equal,
                                           fill=0.0, base=0, channel_multiplier=1), writes=[B_const])

    def r4(ap):
        return ap.rearrange("p (a b) -> p a b", a=4)

    def load_cols(dst, src_rows, nrows):
        tmp = sbt(c, "lc_%d" % K.nb, [nrows, 128], F32)
        tb = K.buf()
        K.dma("sp", tmp, src_rows, writes=[tb])
        pt, pb = ps_g()
        K.op("pe", lambda e: e.transpose(out=pt[:, 0:nrows], in_=tmp, identity=ident_f[0:nrows, 0:nrows]),
             reads=[tb, B_const], writes=[pb])
        K.op("dve", lambda e: e.tensor_copy(out=dst, in_=pt[:, 0:nrows]), reads=[pb], writes=[B_const])

    gcol_mix = sbt(c, "gcol_mix", [128, 8], F32)
    gcol_ffn = sbt(c, "gcol_ffn", [128, 8], F32)
    load_cols(gcol_mix, norm_mix, 8)
    load_cols(gcol_ffn, norm_ffn, 8)
    halfcol = sbt(c, "halfcol", [128, 1], F32)
    K.dma("sp", halfcol, hsc[:, 1:2].partition_broadcast(128), writes=[B_const])
    hsc_sb = sbt(c, "hsc_sb", [1, 2], F32)
    K.dma("sp", hsc_sb, hsc, writes=[B_const])

    hT_s = sbt(c, "hT_s", [128, 8, 4], BF16)
    B_hTs = K.buf()
    omT_s = sbt(c, "omT_s", [128, 8, 4], BF16)
    B_omTs = K.buf()
    scr_x1s = nc.dram_tensor("scr_x1s", [4, D], F32, kind="Internal").ap()
    B_scr_x1s = K.buf()

    e1 = ExitStack()
    w_in_sb = sbt(e1, "w_in_sb", [128, 8, 2328], BF16)
    B_win = K.buf("w_in")
    for k in range(8):
        for g in range(2):
            K.dma("pool", w_in_sb[:, k, 0:512].rearrange("p (r g d) -> p g r d", r=4, g=2, d=64)[:, g],
                  w_in[k * 128:(k + 1) * 128, g * 256:(g + 1) * 256].rearrange("p (r d) -> p r d", d=64),
                  writes=[B_win])
        K.dma("pool", w_in_sb[:, k, 512:2328], w_in[k * 128:(k + 1) * 128, 512:2328], writes=[B_win])
    w1sb = sbt(e1, "w1sb", [128, 2, 32, 128], BF16)
    for kv in range(2):
        for hf in range(2):
            K.dma("pool", w1sb[hf * 64:(hf + 1) * 64, kv], cmp_w1[kv].rearrange("(p d) h -> d p h", d=64),
                  writes=[B_win])
    w2pad = sbt(e1, "w2pad", [128, 2, 2, 128], BF16)
    K.op("pool", lambda e: e.memset(w2pad, 0.0), writes=[B_win])
    for kv in range(2):
        for g in range(2):
            K.dma("pool", w2pad[:, kv, g, g * 64:(g + 1) * 64], cmp_w2[kv], writes=[B_win])
    b1col = sbt(e1, "b1col", [128, 2], F32)
    for kv in range(2):
        K.dma("sp", b1col[:, kv:kv + 1], cmp_b1[kv].rearrange("(p o) -> p o", o=1), writes=[B_win])
    peTok = sbt(e1, "peTok", [32, 2, 64], F32)
    K.dma("sp", peTok, cmp_pe.rearrange("k p d -> p k d"), writes=[B_win])
    peT = sbt(e1, "peT", [64, 2, 32], BF16)
    b1eff = sbt(e1, "b1eff", [128, 2], F32)
    for kv in range(2):
        pt, pb = ps_g()
        K.op("pe", lambda e: e.transpose(out=pt[0:64, 0:32], in_=peTok[:, kv, :], identity=ident_f[0:32, 0:32]),
             reads=[B_win, B_const], writes=[pb])
        K.op("dve", lambda e: e.tensor_copy(out=peT[:, kv, :], in_=pt[0:64, 0:32]), reads=[pb], writes=[B_win])
    for kv in range(2):
        pt, pb = ps_g()
        for pos in range(32):
            K.op("pe", lambda e: e.matmul(pt[:, 0:1], lhsT=w1sb[0:64, kv, pos, :], rhs=peT[:, kv, pos:pos + 1],
                                          start=(pos == 0), stop=(pos == 31)), reads=[B_win], writes=[pb], sig=(pos == 31))
        K.op("dve", lambda e: e.tensor_tensor(out=b1eff[:, kv:kv + 1], in0=pt[:, 0:1], in1=b1col[:, kv:kv + 1],
                                              op=ALU.add), reads=[pb, B_win], writes=[B_win])
    ws_f = sbt(e1, "ws_f", [128, 4, 128], F32)
    K.dma("sp", ws_f, gmlp_ws.rearrange("g i j -> i g j"), writes=[B_win])
    K.op("pool", lambda e: e.affine_select(out=ws_f, in_=ws_f, pattern=[[0, 4], [-1, 128]], compare_op=ALU.is_ge,
                                           fill=0.0, base=0, channel_multiplier=1), reads=[B_win], writes=[B_win])
    WsT = sbt(e1, "WsT", [128, 4, 128], BF16)
    for g in range(4):
        pt, pb = ps_g()
        K.op("pe", lambda e: e.transpose(out=pt[:, 0:128], in_=ws_f[:, g, :], identity=ident_f),
             reads=[B_win, B_const], writes=[pb])
        K.op("dve", lambda e: e.tensor_copy(out=WsT[:, g, :], in_=pt[:, 0:128]), reads=[pb], writes=[B_win])
    bsB = sbt(e1, "bsB", [128, 512], F32)
    K.dma("sp", bsB, gmlp_bs.partition_broadcast(128), writes=[B_win])
    gnB = sbt(e1, "gnB", [128, 512], F32)
    K.dma("sp", gnB, gmlp_norm.partition_broadcast(128), writes=[B_win])

    mask_diag4 = sbt(e1, "mask_diag4", [128, 4, 128], BF16)
    mask_band4 = sbt(e1, "mask_band4", [128, 4, 128], BF16)
    K.op("pool", lambda e: e.affine_select(out=mask_diag4, in_=r4(zeros_b), pattern=[[0, 4], [1, 128]],
                                           compare_op=ALU.is_ge, fill=NEGM, base=0, channel_multiplier=-1),
         reads=[B_const], writes=[B_win])
    K.op("pool", lambda e: e.affine_select(out=mask_band4, in_=r4(zeros_b), pattern=[[0, 4], [-1, 128]],
                                           compare_op=ALU.is_ge, fill=NEGM, base=0, channel_multiplier=1),
         reads=[B_const], writes=[B_win])
    Mst = sbt(e1, "Mst", [128, 128], F32)
    K.op("pool", lambda e: e.memset(Mst, 0.0), writes=[B_win])
    K.op("pool", lambda e: e.memset(Mst[:, 66:128], -1e30), writes=[B_win])
    K.op("pool", lambda e: e.memset(Mst[0:64, 65:66], -1e30), writes=[B_win])
    K.op("pool", lambda e: e.memset(Mst[64:128, 65:66], 1e6), writes=[B_win])
    K.op("pool", lambda e: e.memset(Mst[:, 64:65], 1e6), writes=[B_win])
    K.op("pool", lambda e: e.memset(Mst[0:64, 63:64], 1e6), writes=[B_win])
    vis8x2 = sbt(e1, "vis8x2", [128, 2, 8], BF16)
    K.op("pool", lambda e: e.affine_select(out=vis8x2, in_=zeros_b[:, 0:16].rearrange("p (a b) -> p a b", a=2),
                                           pattern=[[0, 2], [-16, 8]], compare_op=ALU.is_ge, fill=NEGM, base=-15,
                                           channel_multiplier=1), reads=[B_const], writes=[B_win])
    selbB = sbt(e1, "selbB", [128, 64], F32)
    K.dma("sp", selbB, selb.partition_broadcast(128), writes=[B_win])
    cmpb_row2 = sbt(e1, "cmpb_row2", [1, 2, 256], BF16)
    for a in range(2):
        K.dma("pool", cmpb_row2[:, a, :], cmpb[:, 0:256], writes=[B_win])
    cmpbT = sbt(e1, "cmpbT", [128, 2], F32)
    for ch in range(2):
        K.dma("sp", cmpbT[:, ch:ch + 1], cmpb[0, ch * 128:(ch + 1) * 128].rearrange("(p o) -> p o", o=1),
              writes=[B_win])
    histneg4 = sbt(e1, "histneg4", [1, 512], BF16)
    K.op("dve", lambda e: e.tensor_scalar(out=histneg4, in0=zeros_b[0:1, :], scalar1=hsc_sb[0:1, 0:1], scalar2=None,
                                          op0=ALU.add), reads=[B_const], writes=[B_win])

    if stage <= 1:
        K.finish()
        return nc
    e1p = ExitStack()
    kslcE = [sbt(e1p, "kslcE%d" % i, [128, 4096], BF16) for i in range(2)]
    qS = [sbt(e1p, "qS%d" % i, [128, 4, 512], BF16) for i in range(2)]
    selt2 = [sbt(e1p, "selt2_%d" % i, [128, 128], BF16) for i in range(2)]
    Vslc = sbt(e1p, "Vslc", [128, 32, 2, 65], BF16)
    kwinT = sbt(e1p, "kwinT", [128, 8, 128], BF16)
    Vwin = sbt(e1p, "Vwin", [128, 8, 2, 65], BF16)
    kvcT = sbt(e1p, "kvcT", [128, 2, 16 + 512], BF16)
    kcT = sbt(e1p, "kcT", [128, NSLOT], BF16)
    vcT = sbt(e1p, "vcT", [128, NSLOT], BF16)
    vcaug = sbt(e1p, "vcaug", [128, 2, 2, 65], BF16)
    B_kslc = [K.buf() for _ in range(32)]
    B_vslc = [K.buf() for _ in range(32)]
    B_kwin = [K.buf() for _ in range(8)]
    B_vwin = [K.buf() for _ in range(8)]
    B_kvc = [K.buf(), K.buf()]
    B_kc = K.buf()
    B_vc = K.buf()
    B_vcaug = [K.buf(), K.buf()]
    for g_ in range(2):
        Ebh = kslcE[g_][(1 - g_) * 64:(2 - g_) * 64, :]
        K.op("pool", lambda e: e.memset(Ebh, 1.0), writes=[B_win])
        K.op("pool", lambda e: e.affine_select(out=Ebh, in_=Ebh, pattern=[[1, 4096]], compare_op=ALU.is_ge, fill=0.0,
                                               base=0, channel_multiplier=-64), reads=[B_win], writes=[B_win])
        K.op("pool", lambda e: e.affine_select(out=Ebh, in_=Ebh, pattern=[[-1, 4096]], compare_op=ALU.is_ge, fill=0.0,
                                               base=63, channel_multiplier=64), reads=[B_win], writes=[B_win])
        K.op("pool", lambda e: e.memset(selt2[g_], 0.0), writes=[B_win])
        K.op("pool", lambda e: e.memset(qS[g_], 0.0), writes=[B_win])
    K.op("pool", lambda e: e.memset(Vslc, 1.0), writes=B_vslc)
    K.op("pool", lambda e: e.memset(Vwin, 1.0), writes=B_vwin)
    K.op("pool", lambda e: e.memset(vcaug, 1.0), writes=B_vcaug)
    K.op("pool", lambda e: e.memset(kvcT, 0.0), writes=B_kvc)
    K.op("pool", lambda e: e.memset(kcT, 0.0), writes=[B_kc])
    K.op("pool", lambda e: e.memset(vcT, 0.0), writes=[B_vc])

    xbuf = [sbt(e1p, "xbuf%d" % i, [128, D], F32) for i in range(2)]
    B_x = [K.buf(), K.buf()]
    xn = [sbt(e1p, "xn%d" % i, [128, D], BF16) for i in range(2)]
    B_xn = [K.buf(), K.buf()]
    st4 = sbt(e1p, "st4", [128, 8], F32)
    B_st = K.buf()
    hT = sbt(e1p, "hT", [128, 8, 512], BF16)
    B_hT = [K.buf() for _ in range(4)]
    qTz = [sbt(e1p, "qTz%d" % i, [128, 4, 512], BF16) for i in range(2)]
    B_qT = K.buf()
    for i in range(2):
        K.op("pool", lambda e: e.memset(qTz[i], 0.0), writes=[B_qT])
    uT = sbt(e1p, "uT", [128, 4, 512], BF16)
    B_uT = K.buf()
    kvtok = [sbt(e1p, "kvtok%d" % i, [128, 768], F32) for i in range(2)]
    B_kvtok = [K.buf(), K.buf()]
    sg = sbt(e1p, "sg", [128, 4, 24], F32)
    B_sg = [K.buf() for _ in range(4)]
    gv = sbt(e1p, "gv", [128, 512], F32)
    B_gv = K.buf()
    vn_f = sbt(e1p, "vn_f", [128, 512], F32)
    B_vnf = K.buf()
    vn_b = sbt(e1p, "vn_b", [128, 512], BF16)
    B_vnb = K.buf()
    tmpm = sbt(e1p, "tmpm", [128, 512], F32)
    B_tmpm = K.buf()
    mixT = sbt(e1p, "mixT", [128, 4, 512], BF16)
    B_mixT = K.buf()
    gh = sbt(e1p, "gh", [128, 2, 2, 32], BF16)
    B_gh = K.buf()
    ET = [sbt(e1p, "ET%d" % i, [128, 512], BF16) for i in range(4)]
    B_ET = [K.buf() for _ in range(4)]
    etr = [0]
    Eh = [sbt(e1p, "Eh%d" % i, [128, 256], F32) for i in range(2)]
    B_Eh = [K.buf(), K.buf()]
    impbuf = sbt(e1p, "impbuf", [128, 272], F32)
    B_imp = K.buf()
    K.op("pool", lambda e: e.memset(impbuf, 0.0), writes=[B_imp])
    sc = sbt(e1p, "sc", [128, 64], F32)
    sc2 = sbt(e1p, "sc2", [128, 64], F32)
    mx8 = sbt(e1p, "mx8", [128, 16], F32)
    selt = sbt(e1p, "selt", [128, 64], BF16)
    den4 = sbt(e1p, "den4", [128, 8], F32)
    B_sel = K.buf()
    cmask = [sbt(e1p, "cmask%d" % i, [128, 4, 128], BF16) for i in range(2)]
    B_cmask = [K.buf(), K.buf()]
    negselT4 = [sbt(e1p, "negselT4_%d" % i, [64, 4, 128], BF16) for i in range(2)]
    B_negsel = [K.buf(), K.buf()]
    o_brs2 = [[sbt(e1p, "o_br%d_%d" % (t, i), [128, 3, 4, 65], F32) for i in range(2)] for t in range(2)]
    B_obrs2 = [[[K.buf() for _ in range(3)] for _ in range(2)] for _ in range(2)]
    coef = sbt(e1p, "coef", [128, 3, 4], F32)
    B_coef = K.buf()
    otmp = sbt(e1p, "otmp", [128, 3, 4, 64], F32)
    B_otmp = K.buf()
    o_nsa = sbt(e1p, "o_nsa", [128, 512], BF16)
    B_onsa = K.buf()
    o_nsaT = sbt(e1p, "o_nsaT", [128, 4, 512], BF16)
    B_onsaT = K.buf()

    print("SBUF remaining after pass1a alloc:", nc.sbuf_bytes_remaining)
    def rmsnorm_T(src_dram, xb, Bxb, xnb, Bxnb, gcol, hT_dst, B_hT_dst, x_preloaded=False):
        if not x_preloaded:
            K.dma("sp", xb, src_dram, writes=[Bxb])
        K.op("act", lambda e: e.activation(out=xnb, in_=xb, func=AF.Square, accum_out=st4[:, 0:1]),
             reads=[Bxb], writes=[Bxnb, B_st])
        K.op("act", lambda e: e.activation(out=st4[:, 1:2], in_=st4[:, 0:1], func=AF.Ln, scale=1.0 / D, bias=EPS), reads=[B_st], writes=[B_st])
        K.op("act", lambda e: e.activation(out=st4[:, 2:3], in_=st4[:, 1:2], func=AF.Exp, scale=-0.5), reads=[B_st], writes=[B_st])
        K.op("dve", lambda e: e.tensor_scalar(out=xnb, in0=xb, scalar1=st4[:, 2:3], scalar2=None, op0=ALU.mult),
             reads=[Bxb, B_st], writes=[Bxnb])
        pt, pb = ps_g()
        ptb = pt.bitcast(BF16).rearrange("p (k t) -> p k t", k=8)
        for k in range(8):
            K.op("pe", lambda e: e.transpose(out=ptb[:, k, :], in_=xnb[:, k * 128:(k + 1) * 128], identity=ident_b),
                 reads=[Bxnb, B_const], writes=[pb], sig=(k == 7))
        K.op("dve", lambda e: e.tensor_tensor(out=hT_dst, in0=ptb, in1=gcol.unsqueeze(2).to_broadcast([128, 8, 128]),
                                              op=ALU.mult), reads=[pb, B_const], writes=[B_hT_dst])

    def projT(cols_ap_fn, nt, B_hts, evac):
        pt, pb = ps_g()
        for k in range(8):
            K.op("pe", lambda e: e.matmul(pt[:, 0:nt], lhsT=cols_ap_fn(k), rhs=hT[:, k, 0:nt], start=(k == 0),
                                          stop=(k == 7)), reads=[B_win] + B_hts, writes=[pb], sig=(k == 7))
        evac(pt[:, 0:nt], pb)

    def tile_src(T):
        return xh[T * 128:(T + 1) * 128, :] if T < 16 else xo[(T - 16) * 128:(T - 15) * 128, :]

    def compress_st(seg0, nseg):
        for kv in range(2):
            pgs = [ps_g(), ps_g()]
            i = 0
            for r in range(2):
                for s in range(16):
                    c0 = 16 * r + s
                    for g in range(2):
                        pt, pb = pgs[g]
                        K.op("pe", lambda e: e.matmul(pt[:, 0:nseg], lhsT=w1sb[g * 64:(g + 1) * 64, kv, r * 16 + s, :],
                                                      rhs=kvcT[g * 64:(g + 1) * 64, kv, c0:c0 + 16 * (nseg - 1) + 1:16],
                                                      start=(i == 0), stop=(i == 31)),
                             reads=[B_win, B_kvc[kv]], writes=[pb], sig=(i == 31))
                    i += 1
            for g in range(2):
                pt, pb = pgs[g]
                K.op("act", lambda e: e.activation(out=gh[:, kv, g, 0:nseg], in_=pt[:, 0:nseg], func=AF.Gelu_apprx_tanh,
                                                   bias=b1eff[:, kv:kv + 1]), reads=[pb, B_win], writes=[B_gh])
            pt, pb = ps_g()
            for g in range(2):
                K.op("pe", lambda e: e.matmul(pt[:, 0:nseg], lhsT=w2pad[:, kv, g, :], rhs=gh[:, kv, g, 0:nseg],
                                              start=(g == 0), stop=(g == 1)), reads=[B_win, B_gh], writes=[pb],
                     sig=(g == 1))
            dst, Bd = (kcT, B_kc) if kv == 0 else (vcT, B_vc)
            K.op("dve", lambda e: e.tensor_copy(out=dst[:, seg0:seg0 + nseg], in_=pt[:, 0:nseg]), reads=[pb],
                 writes=[Bd])
            K.op("dve", lambda e: e.tensor_copy(out=kvcT[:, kv, 0:16], in_=kvcT[:, kv, 16 * nseg:16 * nseg + 16]),
                 reads=[B_kvc[kv]], writes=[B_kvc[kv]])

    def vc_transpose(ch):
        pt, pb = ps_g()
        ptb = pt.bitcast(BF16)
        K.op("pe", lambda e: e.transpose(out=ptb[:, 0:128], in_=vcT[:, ch * 128:(ch + 1) * 128], identity=ident_b),
             reads=[B_vc, B_const], writes=[pb])
        K.op("dve", lambda e: e.tensor_copy(out=vcaug[:, ch, :, 0:64],
                                            in_=ptb[:, 0:128].rearrange("p (g d) -> p g d", g=2)),
             reads=[pb], writes=[B_vcaug[ch]])

    def front_st(T0, ntl, full):
        nt = ntl * 128
        for j in range(ntl):
            T = T0 + j
            rmsnorm_T(tile_src(T), xbuf[j % 2], B_x[j % 2], xn[j % 2], B_xn[j % 2], gcol_mix,
                      hT[:, :, j * 128:(j + 1) * 128], B_hT[j])
        Bh = B_hT[0:ntl]
        for kv in range(2):
            projT(lambda k: w_in_sb[:, k, 512 + kv * 128:640 + kv * 128], nt, Bh,
                  lambda p, pb: K.op("act", lambda e: e.copy(out=kvcT[:, kv, 16:16 + nt], in_=p), reads=[pb],
                                     writes=[B_kvc[kv]]))
        projT(lambda k: w_in_sb[:, k, 768:896], nt, Bh,
              lambda p, pb: (K.op("dve", lambda e: e.tensor_copy(out=kslcE[0][0:64, T0 * 128:T0 * 128 + nt],
                                                                  in_=p[0:64]), reads=[pb], writes=B_kslc[T0:T0 + ntl]),
                             K.op("act", lambda e: e.copy(out=kslcE[1][64:128, T0 * 128:T0 * 128 + nt], in_=p[64:128]),
                                  reads=[pb], writes=B_kslc[T0:T0 + ntl])))
        if T0 + ntl > 8:
            pt, pb = ps_g()
            for k in range(8):
                K.op("pe", lambda e: e.matmul(pt[:, 0:nt], lhsT=w_in_sb[:, k, 1024:1152], rhs=hT[:, k, 0:nt],
                                              start=(k == 0), stop=(k == 7)), reads=[B_win] + Bh, writes=[pb], sig=(k == 7))
            for j in range(ntl):
                sl = (T0 + j) % 8
                K.op("act", lambda e: e.copy(out=kwinT[:, sl, :], in_=pt[:, j * 128:(j + 1) * 128]), reads=[pb],
                     writes=[B_kwin[sl]])
        if full:
            for r in range(4):
                projT(lambda k: w_in_sb[:, k, 128 * r:128 * r + 128], nt, Bh,
                      lambda p, pb: (K.op("act", lambda e: e.copy(out=qTz[0][0:64, r, 0:nt], in_=p[0:64]), reads=[pb],
                                          writes=[B_qT]),
                                     K.op("dve", lambda e: e.tensor_copy(out=qTz[1][64:128, r, 0:nt], in_=p[64:128]),
                                          reads=[pb], writes=[B_qT]),
                                     K.op("dve", lambda e: e.tensor_copy(out=qS[0][0:64, r, 0:nt], in_=p[0:64]),
                                          reads=[pb], writes=[B_qT]),
                                     K.op("act", lambda e: e.copy(out=qS[1][64:128, r, 0:nt], in_=p[64:128]),
                                          reads=[pb], writes=[B_qT])))
            for cc in range(4):
                projT(lambda k: w_in_sb[:, k, 1304 + cc * 128:1432 + cc * 128], nt, Bh,
                      lambda p, pb: K.op("act", lambda e: e.activation(out=uT[:, cc, 0:nt], in_=p,
                                                                       func=AF.Gelu_apprx_tanh), reads=[pb],
                                         writes=[B_uT]))
        for j in range(ntl):
            T = T0 + j
            kt_, Bkt = kvtok[j % 2], B_kvtok[j % 2]
            pa, pab = ps_g()
            for k in range(8):
                K.op("pe", lambda e: e.matmul(pa, lhsT=hT[:, k, j * 128:(j + 1) * 128], rhs=w_in_sb[:, k, 512:1024],
                                              start=(k == 0), stop=(k == 7)), reads=[B_win, B_hT[j]], writes=[pab], sig=(k == 7))
            pbk, pbb = ps_g()
            for k in range(8):
                K.op("pe", lambda e: e.matmul(pbk[:, 0:280], lhsT=hT[:, k, j * 128:(j + 1) * 128],
                                              rhs=w_in_sb[:, k, 1024:1304], start=(k == 0), stop=(k == 7)),
                     reads=[B_win, B_hT[j]], writes=[pbb], sig=(k == 7))
            K.op("act", lambda e: e.copy(out=kt_[:, 0:512], in_=pa), reads=[pab], writes=[Bkt])
            K.op("dve", lambda e: e.tensor_copy(out=kt_[:, 512:768], in_=pbk[:, 0:256]), reads=[pbb], writes=[Bkt])
            if full:
                K.op("act", lambda e: e.activation(out=sg[:, j, :], in_=pbk[:, 256:280], func=AF.Sigmoid),
                     reads=[pbb], writes=[B_sg[j]])
            K.op("pool", lambda e: e.tensor_copy(out=Vslc[:, T, :, 0:64],
                                                 in_=kt_[:, 384:512].rearrange("p (g d) -> p g d", g=2)),
                 reads=[Bkt], writes=[B_vslc[T]])
            if T >= 8:
                K.op("pool", lambda e: e.tensor_copy(out=Vwin[:, T % 8, :, 0:64],
                                                     in_=kt_[:, 640:768].rearrange("p (g d) -> p g d", g=2)),
                     reads=[Bkt], writes=[B_vwin[T % 8]])
            if T >= 16:
                for br in range(3):
                    K.dma("sp", kv_o[br, (T - 16) * 128:(T - 15) * 128, :], kt_[:, br * 256:(br + 1) * 256],
                          reads=[Bkt])
            if full:
                pc, pcb = ps_g()
                for k in range(8):
                    K.op("pe", lambda e: e.matmul(pc, lhsT=hT[:, k, j * 128:(j + 1) * 128],
                                                  rhs=w_in_sb[:, k, 1816:2328], start=(k == 0), stop=(k == 7)),
                         reads=[B_win, B_hT[j]], writes=[pcb], sig=(k == 7))
                K.op("act", lambda e: e.activation(out=gv, in_=pc, func=AF.Gelu_apprx_tanh), reads=[pcb],
                     writes=[B_gv])
                K.op("act", lambda e: e.activation(out=vn_b, in_=gv, func=AF.Square,
                                                   accum_out=st4[:, 4:5]), reads=[B_gv], writes=[B_vnb, B_st])
                K.op("act", lambda e: e.activation(out=st4[:, 5:6], in_=st4[:, 4:5], func=AF.Ln, scale=1.0 / 512, bias=EPS), reads=[B_st], writes=[B_st])
                K.op("act", lambda e: e.activation(out=st4[:, 6:7], in_=st4[:, 5:6], func=AF.Exp, scale=-0.5), reads=[B_st], writes=[B_st])
                K.op("dve", lambda e: e.scalar_tensor_tensor(out=vn_b, in0=gv, scalar=st4[:, 6:7], in1=gnB,
                                                             op0=ALU.mult, op1=ALU.mult),
                     reads=[B_gv, B_st, B_win], writes=[B_vnb])
                if T == 31:
                    K.op("dve", lambda e: e.scalar_tensor_tensor(out=vn_f, in0=gv, scalar=st4[:, 6:7], in1=gnB,
                                                                 op0=ALU.mult, op1=ALU.mult),
                         reads=[B_gv, B_st, B_win], writes=[B_vnf])
                    K.dma("sp", vn_o, vn_f, reads=[B_vnf])
                pm, pmb = ps_g()
                for g in range(4):
                    K.op("pe", lambda e: e.matmul(pm[:, g * 128:(g + 1) * 128], lhsT=vn_b[:, g * 128:(g + 1) * 128],
                                                  rhs=WsT[:, g, :], start=(g == 0), stop=(g == 3),
                                                  skip_group_check=True), reads=[B_vnb, B_win], writes=[pmb], sig=(g == 3))
                K.op("dve", lambda e: e.tensor_tensor(out=tmpm, in0=pm, in1=bsB, op=ALU.add), reads=[pmb, B_win],
                     writes=[B_tmpm])
                K.op("dve", lambda e: e.tensor_tensor(out=mixT[:, :, j * 128:(j + 1) * 128], in0=r4(tmpm),
                                                       in1=uT[:, :, j * 128:(j + 1) * 128], op=ALU.mult),
                     reads=[B_tmpm, B_uT], writes=[B_mixT])
        compress_st(T0 * 8, ntl * 8)

    LOOKAHEAD = 2

    def attn_units(T, j):
        units = []
        for kind in ("win", "cmp", "slc"):
            for g in range(2):
                if kind == "slc":
                    kts = list(range(0, T + 1))
                elif kind == "win":
                    kts = list(range(T - 4, T + 1))
                else:
                    kts = [0] if T < 16 else [0, 1]
                for ki, kt in enumerate(kts):
                    units.append(dict(kind=kind, g=g, kt=kt, first=(ki == 0), last=(ki == len(kts) - 1)))
        state = {}

        def emit_qk(u):
            kind, g, kt = u["kind"], u["g"], u["kt"]
            ps, psb = ps_g()
            ex = []
            if kind == "slc":
                lk, Bk = kslcE[g][:, kt * 128:(kt + 1) * 128], B_kslc[kt]
                rq, Brq = qS[g][:, :, j * 128:(j + 1) * 128], [B_qT, B_negsel[g]]
                if kt == T:
                    ex.append((ident_b, mask_diag4.rearrange("p a b -> p (a b)"), []))
            elif kind == "win":
                lk, Bk = kwinT[:, kt % 8, :], B_kwin[kt % 8]
                rq, Brq = qTz[g][:, :, j * 128:(j + 1) * 128], [B_qT]
                if kt == T:
                    ex.append((ident_b, mask_diag4.rearrange("p a b -> p (a b)"), []))
                if kt == T - 4:
                    ex.append((ident_b, mask_band4.rearrange("p a b -> p (a b)"), []))
                if kt < 16:
                    ex.append((ones_b[0:1, 0:128], histneg4, []))
            else:
                lk, Bk = kcT[:, kt * 128:(kt + 1) * 128], B_kc
                rq, Brq = qTz[g][:, :, j * 128:(j + 1) * 128], [B_qT]
                if (T < 16 and kt == 0) or (T >= 16 and kt == 1):
                    ex.append((ident_b, cmask[T % 2].rearrange("p a b -> p (a b)"), [B_cmask[T % 2]]))
            K.op("pe", lambda e: e.matmul(ps, lhsT=lk, rhs=rq, start=True, stop=(len(ex) == 0)),
                 reads=[Bk, B_win] + Brq, writes=[psb], sig=(len(ex) == 0))
            for xi, (l2, r2, Bs) in enumerate(ex):
                K.op("pe", lambda e: e.matmul(ps, lhsT=l2, rhs=r2, start=False, stop=(xi == len(ex) - 1)),
                     reads=[B_win, B_const] + Bs, writes=[psb], sig=(xi == len(ex) - 1))
            u["ps"], u["psb"] = ps, psb

        def emit_exp_pv(u):
            kind, g, kt = u["kind"], u["g"], u["kt"]
            ps, psb = u["ps"], u["psb"]
            ei = etr[0]
            etr[0] = (ei + 1) % len(ET)
            if kind == "cmp":
                K.op("act", lambda e: e.activation(out=ET[ei], in_=ps, func=AF.Exp, scale=0.125,
                                                   bias=cmpbT[:, kt:kt + 1]), reads=[psb, B_win], writes=[B_ET[ei]])
            else:
                K.op("act", lambda e: e.activation(out=ET[ei], in_=ps, func=AF.Exp, scale=0.125), reads=[psb],
                     writes=[B_ET[ei]])
            if u["first"]:
                state["po"] = ps_o()
            po, pob = state["po"]
            po3 = po[:, 0:260].rearrange("p (h c) -> p h c", h=4)
            if kind == "slc":
                va, Bv = Vslc[:, kt, g, :], B_vslc[kt]
            elif kind == "win":
                va, Bv = Vwin[:, kt % 8, g, :], B_vwin[kt % 8]
            else:
                va, Bv = vcaug[:, kt, g, :], B_vcaug[kt]
            for h in range(4):
                K.op("pe", lambda e: e.matmul(po3[:, h, :], lhsT=ET[ei][:, h * 128:(h + 1) * 128], rhs=va,
                                              start=(u["first"] and h == 0), stop=(u["last"] and h == 3),
                                              skip_group_check=True), reads=[B_ET[ei], Bv], writes=[pob], sig=(h == 3))
            if u["last"]:
                bi = {"cmp": 0, "slc": 1, "win": 2}[kind]
                K.op("act", lambda e: e.copy(out=o_brs2[T % 2][g][:, bi], in_=po3), reads=[pob],
                     writes=[B_obrs2[T % 2][g][bi]])

        n = len(units)
        for i in range(min(LOOKAHEAD, n)):
            emit_qk(units[i])
        for i in range(n):
            if i + LOOKAHEAD < n:
                emit_qk(units[i + LOOKAHEAD])
            emit_exp_pv(units[i])

    def select_blocks(T, g, j):
        L = 8 * T + 8
        S3 = psS.rearrange("p (h c) -> p h c", h=4)
        for bk in range(2):
            Bb = B_S[bk]
            for hh in range(2):
                h = bk * 2 + hh
                K.op("pe", lambda e: e.matmul(S3[:, h, 0:L], lhsT=qTz[g][:, h, j * 128:(j + 1) * 128],
                                              rhs=kcT[:, 0:L], start=(hh == 0), stop=False,
                                              skip_group_check=True), reads=[B_qT, B_kc], writes=[Bb])
            K.op("pe", lambda e: e.matmul(S3[:, bk * 2:bk * 2 + 2, 0:L], lhsT=ones_b[0:1, 0:128],
                                          rhs=cmpb_row2[:, :, 0:L], start=False, stop=False, skip_group_check=True),
                 reads=[B_win, B_const], writes=[Bb])
            K.op("pe", lambda e: e.matmul(S3[:, bk * 2:bk * 2 + 2, L - 8:L], lhsT=ident_b, rhs=vis8x2, start=False,
                                          stop=True, skip_group_check=True), reads=[B_win, B_const], writes=[Bb])
        ck(2)
        for h in range(4):
            K.op("act", lambda e: e.activation(out=Eh[h % 2][:, 0:L], in_=S3[:, h, 0:L], func=AF.Exp, scale=0.125,
                                               accum_out=den4[:, h:h + 1]), reads=[B_S[h // 2]],
                 writes=[B_Eh[h % 2], B_sel])
            K.op("dve", lambda e: e.tensor_scalar(out=den4[:, 4 + h:5 + h], in0=den4[:, h:h + 1], scalar1=1e-30,
                                                  scalar2=None, op0=ALU.max), reads=[B_sel], writes=[B_sel])
            K.op("dve", lambda e: e.reciprocal(out=den4[:, 4 + h:5 + h], in_=den4[:, 4 + h:5 + h]), reads=[B_sel],
                 writes=[B_sel])
            if h == 0:
                K.op("dve", lambda e: e.tensor_scalar(out=impbuf[:, 0:L], in0=Eh[0][:, 0:L], scalar1=den4[:, 4:5],
                                                      scalar2=None, op0=ALU.mult), reads=[B_Eh[0], B_sel],
                     writes=[B_imp])
            else:
                K.op("dve", lambda e: e.scalar_tensor_tensor(out=impbuf[:, 0:L], in0=Eh[h % 2][:, 0:L],
                                                             scalar=den4[:, 4 + h:5 + h], in1=impbuf[:, 0:L],
                                                             op0=ALU.mult, op1=ALU.add),
                     reads=[B_Eh[h % 2], B_sel, B_imp], writes=[B_imp])
        ck(3)
        K.op("dve", lambda e: e.tensor_tensor(out=sc, in0=impbuf[:, 0:253:4], in1=impbuf[:, 1:254:4], op=ALU.add),
             reads=[B_imp], writes=[B_sel])
        for m in (2, 3, 4):
            K.op("dve", lambda e: e.tensor_tensor(out=sc, in0=sc, in1=impbuf[:, m:m + 253:4], op=ALU.add),
                 reads=[B_imp, B_sel], writes=[B_sel])
        K.op("dve", lambda e: e.tensor_tensor(out=sc, in0=sc, in1=selbB, op=ALU.add), reads=[B_sel, B_win],
             writes=[B_sel])
        K.op("dve", lambda e: e.tensor_tensor(out=sc, in0=sc, in1=Mst[:, 64 - 2 * T:128 - 2 * T], op=ALU.add),
             reads=[B_sel, B_win], writes=[B_sel])
        ck(4)
        K.op("dve", lambda e: e.max(out=mx8[:, 0:8], in_=sc), reads=[B_sel], writes=[B_sel])
        K.op("dve", lambda e: e.match_replace(out=sc2, in_to_replace=mx8[:, 0:8], in_values=sc, imm_value=-3e38),
             reads=[B_sel], writes=[B_sel])
        K.op("dve", lambda e: e.max(out=mx8[:, 8:16], in_=sc2), reads=[B_sel], writes=[B_sel])
        K.op("dve", lambda e: e.tensor_scalar(out=mx8[:, 15:16], in0=mx8[:, 15:16], scalar1=-1e29, scalar2=None,
                                              op0=ALU.max), reads=[B_sel], writes=[B_sel])
        K.op("dve", lambda e: e.tensor_scalar(out=sc2, in0=sc, scalar1=mx8[:, 15:16], scalar2=None, op0=ALU.is_ge),
             reads=[B_sel], writes=[B_sel])
        off = (1 - g) * 64
        K.op("dve", lambda e: e.tensor_scalar(out=selt2[g][:, off:off + 64], in0=sc2, scalar1=-1.0, scalar2=-NEGM,
                                              op0=ALU.add, op1=ALU.mult), reads=[B_sel], writes=[B_sel])
        pt, pb = ps_g()
        ptb = pt.bitcast(BF16)
        K.op("pe", lambda e: e.transpose(out=ptb[:, 0:128], in_=selt2[g], identity=ident_b), reads=[B_sel, B_const],
             writes=[pb])
        K.op("dve", lambda e: e.tensor_copy(out=qS[g][off:off + 64, :, j * 128:(j + 1) * 128],
                                            in_=ptb[off:off + 64, 0:128].unsqueeze(1).to_broadcast([64, 4, 128])),
             reads=[pb], writes=[B_negsel[g]])

    def attn_select(T, j):
        ch = 0 if T < 16 else 1
        base = 2048 * ch + 15 - 128 * T
        K.op("pool", lambda e: e.affine_select(out=cmask[T % 2], in_=r4(negs_b), pattern=[[0, 4], [-1, 128]],
                                               compare_op=ALU.is_gt, fill=0.0, base=base, channel_multiplier=16),
             reads=[B_const], writes=[B_cmask[T % 2]])
        for g in range(2):
            select_blocks(T, g, j)

    def attn_combine(T, j):
        for g in range(2):
            o_br, B_obr = o_brs2[T % 2][g], B_obrs2[T % 2][g]
            K.op("dve", lambda e: e.tensor_scalar(out=coef, in0=o_br[:, :, :, 64], scalar1=1e-30, scalar2=None,
                                                  op0=ALU.max), reads=B_obr, writes=[B_coef])
            ck(10)
            K.op("dve", lambda e: e.reciprocal(out=coef, in_=coef), reads=[B_coef], writes=[B_coef])
            ck(11)
            K.op("dve", lambda e: e.tensor_tensor(out=coef, in0=coef,
                                                  in1=sg[:, j, g * 12:(g + 1) * 12].rearrange("p (h b) -> p b h", b=3),
                                                  op=ALU.mult), reads=[B_coef, B_sg[j]], writes=[B_coef])
            ck(12)
            K.op("dve", lambda e: e.tensor_tensor(out=otmp, in0=o_br[:, :, :, 0:64],
                                                  in1=coef.unsqueeze(3).to_broadcast([128, 3, 4, 64]), op=ALU.mult),
                 reads=B_obr + [B_coef], writes=[B_otmp])
            ck(13)
            K.op("dve", lambda e: e.tensor_tensor(out=otmp[:, 0], in0=otmp[:, 0], in1=otmp[:, 1], op=ALU.add),
                 reads=[B_otmp], writes=[B_otmp])
            K.op("dve", lambda e: e.tensor_tensor(out=o_nsa[:, g * 256:(g + 1) * 256].rearrange(
                "p (h d) -> p h d", h=4), in0=otmp[:, 0], in1=otmp[:, 2], op=ALU.add), reads=[B_otmp],
                writes=[B_onsa])
        pt, pb = ps_g()
        ptb = pt.bitcast(BF16).rearrange("p (k t) -> p k t", k=8)
        for cc in range(4):
            K.op("pe", lambda e: e.transpose(out=ptb[:, cc, :], in_=o_nsa[:, cc * 128:(cc + 1) * 128],
                                             identity=ident_b), reads=[B_onsa, B_const], writes=[pb])
        K.op("dve", lambda e: e.tensor_copy(out=o_nsaT[:, :, j * 128:(j + 1) * 128], in_=ptb[:, 0:4, :]), reads=[pb],
             writes=[B_onsaT])

    hist_sts = [(0, 4), (4, 4), (8, 4), (12, 3)]
    own_sts = [(15, 4), (19, 4), (23, 4), (27, 4), (31, 1)]
    for (T0, ntl) in hist_sts:
        front_st(T0, ntl, False)
        if stage <= 2:
            K.finish()
            return nc
    vc_transpose(0)
    for si, (T0, ntl) in enumerate(own_sts):
        front_st(T0, ntl, True)
        vc_transpose(1)
        if si == 0:
            vc_transpose(0)
        if stage <= 3:
            K.finish()
            return nc
        attn_select(T0, 0)
        for j in range(ntl):
            if j + 1 < ntl:
                attn_select(T0 + j + 1, j + 1)
            attn_units(T0 + j, j)
            if j >= 1:
                attn_combine(T0 + j - 1, j - 1)
        attn_combine(T0 + ntl - 1, ntl - 1)
        if stage <= 5:
            K.finish()
            return nc
        nt = ntl * 128
        K.dma("sp", scr_om[si, :, 0:4, 0:nt], o_nsaT[:, :, 0:nt], reads=[B_onsaT], writes=[B_scr_om[si]])
        K.dma("sp", scr_om[si, :, 4:8, 0:nt], mixT[:, :, 0:nt], reads=[B_mixT], writes=[B_scr_om[si]])

    K.barrier()
    e1p.close()
    e2w = ExitStack()
    wg_sb = sbt(e2w, "wg_sb", [128, 8, 2048], BF16)
    B_w2 = K.buf("w_pass1b")
    for k in range(8):
        K.dma("pool", wg_sb[:, k, :], w_in[k * 128:(k + 1) * 128, 2328:4376], writes=[B_w2])
    wpa = sbt(e2w, "wpa", [128, 4, D], BF16)
    wpb = sbt(e2w, "wpb", [128, 4, D], BF16)
    for k in range(4):
        K.dma("pool", wpa[:, k, :], w_proj_a[k * 128:(k + 1) * 128, :], writes=[B_w2])
        K.dma("pool", wpb[:, k, :], w_proj_b[k * 128:(k + 1) * 128, :], writes=[B_w2])
    if do_sample:
        s1 = ExitStack()
        sg_s = sbt(s1, "sg_s", [4, 24], F32)
        qTs = [sbt(s1, "qTs%d" % i, [128, 4, 4], BF16) for i in range(2)]
        kTn = sbt(s1, "kTn", [128, 2, 4], BF16)
        Vn = sbt(s1, "Vn", [4, 2, 2, 65], BF16)
        dmask = sbt(s1, "dmask", [4, 8, 4], BF16)
        ptx_sb = sbt(s1, "ptx_sb", [128, 32], I32)
        ptf = sbt(s1, "ptf", [128, 32], F32)
        idx_i = sbt(s1, "idx_i", [128, 32], I32)
        pmod_sb = sbt(s1, "pmod_sb", [128, 1], F32)
        Oacc = sbt(s1, "Oacc", [4, 3, 8, 65], F32)
        s0 = ExitStack()
        xs_sb = sbt(s0, "xs_sb", [4, D], F32)
        B_xs = K.buf()
        K.dma("sp", xs_sb, xs, writes=[B_xs])
        sqs = sbt(s0, "sqs", [4, D], BF16)
        sts = sbt(s0, "sts", [4, 8], F32)
        xns = sbt(s0, "xns", [4, D], BF16)
        B_s = K.buf()
        K.op("act", lambda e: e.activation(out=sqs, in_=xs_sb, func=AF.Square, accum_out=sts[:, 0:1]), reads=[B_xs],
             writes=[B_s])
        K.op("act", lambda e: e.activation(out=sts[:, 1:2], in_=sts[:, 0:1], func=AF.Ln, scale=1.0 / D, bias=EPS),
             reads=[B_s], writes=[B_s])
        K.op("act", lambda e: e.activation(out=sts[:, 2:3], in_=sts[:, 1:2], func=AF.Exp, scale=-0.5), reads=[B_s],
             writes=[B_s])
        K.op("dve", lambda e: e.tensor_scalar(out=xns, in0=xs_sb, scalar1=sts[:, 2:3], scalar2=None, op0=ALU.mult),
             reads=[B_xs, B_s], writes=[B_s])
        pt, pb = ps_g()
        ptb = pt.bitcast(BF16)[:, 0:32].rearrange("p (k t) -> p k t", k=8)
        for k in range(8):
            K.op("pe", lambda e: e.transpose(out=ptb[:, k, :], in_=xns[:, k * 128:(k + 1) * 128],
                                             identity=ident_b[0:4, 0:4]), reads=[B_s, B_const], writes=[pb])
        K.op("dve", lambda e: e.tensor_tensor(out=hT_s, in0=ptb, in1=gcol_mix.unsqueeze(2).to_broadcast([128, 8, 4]),
                                              op=ALU.mult), reads=[pb, B_const], writes=[B_hTs])
        zs = sbt(s0, "zs", [4, 1816], F32)
        B_zs = K.buf()
        for c0 in range(512, 2328, 512):
            c1 = min(c0 + 512, 2328)
            pt, pb = ps_g()
            for k in range(8):
                K.op("pe", lambda e: e.matmul(pt[0:4, 0:c1 - c0], lhsT=hT_s[:, k, :], rhs=w_in_sb[:, k, c0:c1],
                                              start=(k == 0), stop=(k == 7)), reads=[B_win, B_hTs], writes=[pb], sig=(k == 7))
            K.op("dve", lambda e: e.tensor_copy(out=zs[:, c0 - 512:c1 - 512], in_=pt[0:4, 0:c1 - c0]), reads=[pb],
                 writes=[B_zs])
        K.dma("sp", kvs_o, zs[:, 0:768], reads=[B_zs])
        u_s = sbt(s0, "u_s", [4, 512], F32)
        v_s = sbt(s0, "v_s", [4, 512], F32)
        vn_s = sbt(s0, "vn_s", [4, 512], F32)
        mixs = sbt(s0, "mixs", [4, 512], F32)
        mixsb = sbt(s0, "mixsb", [4, 512], BF16)
        B_sm = K.buf()
        K.op("act", lambda e: e.activation(out=sg_s, in_=zs[:, 768:792], func=AF.Sigmoid), reads=[B_zs], writes=[B_sm])
        K.op("act", lambda e: e.activation(out=u_s, in_=zs[:, 792:1304], func=AF.Gelu_apprx_tanh), reads=[B_zs],
             writes=[B_sm])
        K.op("act", lambda e: e.activation(out=v_s, in_=zs[:, 1304:1816], func=AF.Gelu_apprx_tanh), reads=[B_zs],
             writes=[B_sm])
        K.op("act", lambda e: e.activation(out=sqs[:, 0:512], in_=v_s, func=AF.Square, accum_out=sts[:, 4:5]),
             reads=[B_sm], writes=[B_s])
        K.op("act", lambda e: e.activation(out=sts[:, 5:6], in_=sts[:, 4:5], func=AF.Ln, scale=1.0 / 512, bias=EPS),
             reads=[B_s], writes=[B_s])
        K.op("act", lambda e: e.activation(out=sts[:, 6:7], in_=sts[:, 5:6], func=AF.Exp, scale=-0.5), reads=[B_s],
             writes=[B_s])
        K.op("dve", lambda e: e.scalar_tensor_tensor(out=vn_s, in0=v_s, scalar=sts[:, 6:7], in1=gnB[0:4, :],
                                                     op0=ALU.mult, op1=ALU.mult), reads=[B_sm, B_s, B_win],
             writes=[B_sm])
        K.dma("sp", vns_o, vn_s, reads=[B_sm])
        ws00 = sbt(s0, "ws00", [4, 8], F32)
        with nc.allow_non_contiguous_dma(reason="4 scalars"):
            K.dma("sp", ws00[:, 0:4], gmlp_ws[:, 0, 0:1].rearrange("g o -> o g").partition_broadcast(4), writes=[B_sm])
            K.dma("sp", ws00[:, 4:8], gmlp_bs[:, 0:512:128].partition_broadcast(4), writes=[B_sm])
        K.op("dve", lambda e: e.tensor_tensor(out=r4(mixs), in0=r4(vn_s), in1=ws00[:, 0:4].unsqueeze(2).to_broadcast(
            [4, 4, 128]), op=ALU.mult), reads=[B_sm], writes=[B_sm])
        K.op("dve", lambda e: e.tensor_tensor(out=r4(mixs), in0=r4(mixs), in1=ws00[:, 4:8].unsqueeze(2).to_broadcast(
            [4, 4, 128]), op=ALU.add), reads=[B_sm], writes=[B_sm])
        K.op("dve", lambda e: e.tensor_tensor(out=mixsb, in0=mixs, in1=u_s, op=ALU.mult), reads=[B_sm], writes=[B_sm])
        pt, pb = ps_g()
        ptb = pt.bitcast(BF16)[:, 0:16].rearrange("p (k t) -> p k t", k=4)
        for k in range(4):
            K.op("pe", lambda e: e.transpose(out=ptb[:, k, :], in_=mixsb[:, k * 128:(k + 1) * 128],
                                             identity=ident_b[0:4, 0:4]), reads=[B_sm, B_const], writes=[pb])
        K.op("dve", lambda e: e.tensor_copy(out=omT_s[:, 4:8, :], in_=ptb), reads=[pb], writes=[B_omTs])
        B_qTs = K.buf()
        for i in range(2):
            K.op("pool", lambda e: e.memset(qTs[i], 0.0), writes=[B_qTs])
        for r in range(4):
            pt, pb = ps_g()
            for k in range(8):
                K.op("pe", lambda e: e.matmul(pt[:, 0:4], lhsT=w_in_sb[:, k, r * 128:(r + 1) * 128], rhs=hT_s[:, k, :],
                                              start=(k == 0), stop=(k == 7)), reads=[B_win, B_hTs], writes=[pb], sig=(k == 7))
            K.op("dve", lambda e: e.tensor_copy(out=qTs[0][0:64, r, :], in_=pt[0:64, 0:4]), reads=[pb], writes=[B_qTs])
            K.op("dve", lambda e: e.tensor_copy(out=qTs[1][64:128, r, :], in_=pt[64:128, 0:4]), reads=[pb],
                 writes=[B_qTs])
        B_kTn = K.buf()
        for bi_, c0 in enumerate((768, 1024)):
            pt, pb = ps_g()
            for k in range(8):
                K.op("pe", lambda e: e.matmul(pt[:, 0:4], lhsT=w_in_sb[:, k, c0:c0 + 128], rhs=hT_s[:, k, :],
                                              start=(k == 0), stop=(k == 7)), reads=[B_win, B_hTs], writes=[pb], sig=(k == 7))
            K.op("dve", lambda e: e.tensor_copy(out=kTn[:, bi_, :], in_=pt[:, 0:4]), reads=[pb], writes=[B_kTn])
        B_Vn = K.buf()
        K.op("pool", lambda e: e.memset(Vn, 1.0), writes=[B_Vn])
        for bi_, c0 in enumerate((384, 640)):
            K.op("dve", lambda e: e.tensor_copy(out=Vn[:, bi_, :, 0:64],
                                                in_=zs[:, c0:c0 + 128].rearrange("p (g d) -> p g d", g=2)),
                 reads=[B_zs], writes=[B_Vn])
        K.op("pool", lambda e: e.affine_select(out=dmask, in_=zeros_b[0:4, 0:32].rearrange("p (a b) -> p a b", a=8),
                                               pattern=[[0, 8], [1, 4]], compare_op=ALU.is_equal, fill=NEGM, base=0,
                                               channel_multiplier=-1), reads=[B_const], writes=[B_Vn])
        B_idx = K.buf()
        K.dma("sp", ptx_sb, ptx, writes=[B_idx])
        K.dma("sp", pmod_sb, pmod, writes=[B_idx])
        K.op("dve", lambda e: e.tensor_copy(out=ptf, in_=ptx_sb), reads=[B_idx], writes=[B_idx])
        K.op("dve", lambda e: e.tensor_scalar(out=ptf, in0=ptf, scalar1=8.0, scalar2=pmod_sb[:, 0:1], op0=ALU.mult,
                                              op1=ALU.add), reads=[B_idx], writes=[B_idx])
        K.op("dve", lambda e: e.tensor_copy(out=idx_i, in_=ptf), reads=[B_idx], writes=[B_idx])

        K.dma("sp", wins_o[:, 511, :], zs[:, 512:768], reads=[B_zs])
        K.barrier()
        s0.close()
        pg = [sbt(s1, "pg%d" % i, [128, 16, 256], F32) for i in range(2)]
        B_pg = [K.buf(), K.buf()]
        kTs = sbt(s1, "kTs", [128, 2, 16, 129], BF16)
        B_kTs = K.buf()
        kcs = sbt(s1, "kcs", [128, 2, 1024], BF16)
        B_kcs = K.buf()
        ghs = sbt(s1, "ghs", [128, 2, 128], BF16)
        B_ghs = K.buf()
        vcas = sbt(s1, "vcas", [128, 8, 2, 65], BF16)
        B_vcas = K.buf()
        K.op("pool", lambda e: e.memset(vcas, 1.0), writes=[B_vcas])
        KTt = [sbt(s1, "KTt%d" % i, [128, 4, 128], BF16) for i in range(2)]
        B_KTt = [K.buf(), K.buf()]
        Vas = [sbt(s1, "Vas%d" % i, [128, 16, 2, 65], BF16) for i in range(2)]
        B_Vas = [K.buf(), K.buf()]
        for i in range(2):
            K.op("pool", lambda e: e.memset(Vas[i], 1.0), writes=[B_Vas[i]])
        ETs = [sbt(s1, "ETs%d" % i, [128, 16, 8, 4], BF16) for i in range(2)]
        B_ETs = [K.buf(), K.buf()]
        etc = [0]
        B_Oacc = K.buf()
        K.op("pool", lambda e: e.memset(Oacc, 0.0), writes=[B_Oacc])
        Es = sbt(s1, "Es", [4, 1024], F32)
        imps = sbt(s1, "imps", [1, 1032], F32)
        scs = sbt(s1, "scs", [1, 264], F32)
        scs2 = sbt(s1, "scs2", [1, 264], F32)
        mxs = sbt(s1, "mxs", [1, 16], F32)
        negx = sbt(s1, "negx", [1, 256, 4], BF16)
        mcol = sbt(s1, "mcol", [128, 2, 8], F32)
        dens = sbt(s1, "dens", [4, 4], F32)
        B_sel = K.buf()
        B_mcol = K.buf()
        K.op("pool", lambda e: e.memset(imps, 0.0), writes=[B_sel])
        s0col = sbt(s1, "s0col", [128, 1], F32)
        K.op("pool", lambda e: e.memset(s0col, 0.0), writes=[B_sel])
        K.op("pool", lambda e: e.memset(s0col[0:1, :], NEGM), writes=[B_sel])
        s0row = sbt(s1, "s0row", [1, 512], BF16)
        K.op("pool", lambda e: e.memset(s0row, 0.0), writes=[B_sel])
        K.op("pool", lambda e: e.memset(s0row[:, 0:1], NEGM), writes=[B_sel])
        ones4f = sbt(s1, "ones4f", [4, 1], F32)
        K.op("pool", lambda e: e.memset(ones4f, 1.0), writes=[B_sel])
        bon = sbt(s1, "bon", [1, 264], F32)
        K.op("pool", lambda e: e.memset(bon, 0.0), writes=[B_sel])
        K.op("pool", lambda e: e.memset(bon[:, 0:1], 1e6), writes=[B_sel])
        K.op("pool", lambda e: e.memset(bon[:, 255:257], 1e6), writes=[B_sel])
        K.op("pool", lambda e: e.memset(bon[:, 257:264], -1e30), writes=[B_sel])

        def acc_banks():
            return allb[4], allb[5]

        OaccT = sbt(s1, "OaccT", [65, 3, 2, 16], F32)
        B_OaccT = K.buf()
        K.op("dve", lambda e: e.memset(OaccT, 0.0), writes=[B_OaccT])

        def evac_acc(bi):
            for g, (pp, ppb) in enumerate(acc_banks()):
                K.op("dve", lambda e: e.tensor_tensor(out=OaccT[:, bi, g, :], in0=OaccT[:, bi, g, :], in1=pp[0:65, 0:16],
                                                      op=ALU.add), reads=[ppb, B_OaccT], writes=[B_OaccT])

        def pv_tile(ETt, Bet, s_idx, Vt, Bv, first, kparts=128):
            for g, (pp, ppb) in enumerate(acc_banks()):
                K.op("pe", lambda e: e.matmul(pp[0:65, 0:16], lhsT=Vt[0:kparts, g, :],
                                              rhs=ETt[0:kparts, s_idx, g * 4:(g + 1) * 4, :], start=first, stop=False,
                                              skip_group_check=True), reads=[Bet, Bv], writes=[ppb], sig=(g == 1))

        jobs = []
        for b_ in range(4):
            jobs += [("cmp", b_, g_) for g_ in range(8)] + [("slc", b_, g_) for g_ in range(8)]
            jobs += [("win", b_, w_) for w_ in range(4)]
        jstate = {"issued": 0, "taken": 0}

        def issue_job(i):
            kind_, b_, x_ = jobs[i]
            pgb, Bpg = pg[i % 2], B_pg[i % 2]
            if kind_ == "win":
                K.dma("sp", pgb[:, 0, :], cache_win[b_, x_ * 128:(x_ + 1) * 128, :], writes=[Bpg])
                return
            cache = cache_cmp if kind_ == "cmp" else cache_slc
            col = b_ * 8 + x_
            K._wait("pool", K._deps([B_idx], [Bpg]))
            ring = K.dring["pool"]
            si = ring[K.dpos["pool"] % len(ring)]
            K.dpos["pool"] += 1
            if K.dtgt[si] > 0:
                K._wait("pool", {("d", si): K.dtgt[si]})
            ins = nc.gpsimd.indirect_dma_start(out=pgb.rearrange("p s c -> p (s c)"), out_offset=None, in_=cache,
                                               in_offset=bass.IndirectOffsetOnAxis(ap=idx_i[:, col:col + 1], axis=0))
            K.dtgt[si] += 16
            ins.then_inc(K.dsem[si], 16)
            K._mark(("d", si), K.dtgt[si], [B_idx], [Bpg])

        def take_job(kind_, b_, x_):
            i = jstate["taken"]
            assert jobs[i] == (kind_, b_, x_), (jobs[i], kind_, b_, x_)
            while jstate["issued"] <= min(i + 1, len(jobs) - 1):
                issue_job(jstate["issued"])
                jstate["issued"] += 1
            jstate["taken"] += 1
            return pg[i % 2], B_pg[i % 2]

        gi = [0]
        for b in range(4):
            for i in range(2):
                K.op("dve", lambda e: e.memset(ETs[i], 0.0), writes=[B_ETs[i]])
            K.op("dve", lambda e: e.memset(kTs, 0.0), writes=[B_kTs])
            for grp in range(8):
                pgb, Bpg = take_job("cmp", b, grp)
                for kv in range(2):
                    for s4 in range(4):
                        pt, pb = ps_g()
                        for q_ in range(4):
                            s = s4 * 4 + q_
                            K.op("pe", lambda e: e.transpose(out=pt[:, q_ * 128:(q_ + 1) * 128],
                                                             in_=pgb[:, s, kv * 128:(kv + 1) * 128], identity=ident_f),
                                 reads=[Bpg, B_const], writes=[pb])
                        eng = "act" if (s4 % 2 == 0) else "dve"
                        if eng == "act":
                            K.op("act", lambda e: e.copy(out=kTs[:, kv, s4 * 4:(s4 + 1) * 4, 1:129],
                                                         in_=pt.rearrange("p (a b) -> p a b", a=4)), reads=[pb],
                                 writes=[B_kTs])
                        else:
                            K.op("dve", lambda e: e.tensor_copy(out=kTs[:, kv, s4 * 4:(s4 + 1) * 4, 1:129],
                                                                in_=pt.rearrange("p (a b) -> p a b", a=4)),
                                 reads=[pb], writes=[B_kTs])
                for kv in range(2):
                    pgs = [ps_g(), ps_g()]
                    i = 0
                    for r in range(2):
                        for s in range(16):
                            for g in range(2):
                                pt, pb = pgs[g]
                                K.op("pe", lambda e: e.matmul(pt[:, 0:128],
                                                              lhsT=w1sb[g * 64:(g + 1) * 64, kv, r * 16 + s, :],
                                                              rhs=kTs[g * 64:(g + 1) * 64, kv, s, r:r + 128],
                                                              start=(i == 0), stop=(i == 31)),
                                     reads=[B_win, B_kTs], writes=[pb], sig=(i == 31))
                            i += 1
                    for g in range(2):
                        pt, pb = pgs[g]
                        K.op("act", lambda e: e.activation(out=ghs[:, g, :], in_=pt[:, 0:128],
                                                           func=AF.Gelu_apprx_tanh, bias=b1eff[:, kv:kv + 1]),
                             reads=[pb, B_win], writes=[B_ghs])
                    pt, pb = ps_g()
                    for g in range(2):
                        K.op("pe", lambda e: e.matmul(pt[:, 0:128], lhsT=w2pad[:, kv, g, :], rhs=ghs[:, g, :],
                                                      start=(g == 0), stop=(g == 1)), reads=[B_win, B_ghs],
                             writes=[pb], sig=(g == 1))
                    K.op("dve", lambda e: e.tensor_copy(out=kcs[:, kv, grp * 128:(grp + 1) * 128], in_=pt[:, 0:128]),
                         reads=[pb], writes=[B_kcs])
                    K.op("dve", lambda e: e.tensor_copy(out=kTs[:, kv, :, 0:1], in_=kTs[:, kv, :, 128:129]),
                         reads=[B_kTs], writes=[B_kTs])
            for ch in range(8):
                pt, pb = ps_g()
                ptb = pt.bitcast(BF16)
                K.op("pe", lambda e: e.transpose(out=ptb[:, 0:128], in_=kcs[:, 1, ch * 128:(ch + 1) * 128],
                                                 identity=ident_b), reads=[B_kcs, B_const], writes=[pb])
                K.op("dve", lambda e: e.tensor_copy(out=vcas[:, ch, :, 0:64],
                                                    in_=ptb[:, 0:128].rearrange("p (g d) -> p g d", g=2)),
                     reads=[pb], writes=[B_vcas])
            ei = etc[0] % 2
            etc[0] += 1
            pt, pb = ps_g()
            p3 = pt[:, 0:64].rearrange("p (c h) -> p c h", c=8)
            for ch in range(8):
                for g in range(2):
                    K.op("pe", lambda e: e.matmul(p3[:, ch, g * 4:(g + 1) * 4], lhsT=kcs[:, 0, ch * 128:(ch + 1) * 128],
                                                  rhs=qTs[g][:, :, b], start=(ch == 0 and g == 0), stop=False,
                                                  skip_group_check=True), reads=[B_kcs, B_qTs], writes=[pb],
                         sig=(ch == 7 and g == 1))
            K.op("act", lambda e: e.activation(out=ETs[ei][:, 0:1, :, b], in_=p3[:, 0:1, :], func=AF.Exp, scale=0.125,
                                               bias=s0col[:, 0:1]), reads=[pb, B_sel], writes=[B_ETs[ei]])
            K.op("act", lambda e: e.activation(out=ETs[ei][:, 1:8, :, b], in_=p3[:, 1:8, :], func=AF.Exp, scale=0.125),
                 reads=[pb], writes=[B_ETs[ei]])
            for ch in range(8):
                pv_tile(ETs[ei], B_ETs[ei], ch, vcas[:, ch], B_vcas, first=(ch == 0))
            evac_acc(0)
            for g in range(2):
                for hf2 in range(2):
                    bk, bkb = allb[6 + hf2]
                    K.op("pe", lambda e: e.matmul(bk[0:4, :], lhsT=qTs[g][:, :, b],
                                                  rhs=kcs[:, 0, hf2 * 512:(hf2 + 1) * 512], start=True,
                                                  stop=(hf2 == 1)), reads=[B_qTs, B_kcs], writes=[bkb], sig=(hf2 == 1))
                    if hf2 == 0:
                        K.op("pe", lambda e: e.matmul(bk[0:4, :], lhsT=ones_b[0:1, 0:4], rhs=s0row, start=False,
                                                      stop=True), reads=[B_sel, B_const], writes=[bkb])
                K.op("act", lambda e: e.activation(out=Es, in_=psS[0:4, :], func=AF.Exp, scale=0.125,
                                                   accum_out=dens[:, 0:1]), reads=B_S, writes=[B_sel])
                K.op("dve", lambda e: e.tensor_scalar(out=dens[:, 1:2], in0=dens[:, 0:1], scalar1=1e-30, scalar2=None,
                                                      op0=ALU.max), reads=[B_sel], writes=[B_sel])
                K.op("dve", lambda e: e.reciprocal(out=dens[:, 1:2], in_=dens[:, 1:2]), reads=[B_sel], writes=[B_sel])
                K.op("dve", lambda e: e.tensor_scalar(out=Es, in0=Es, scalar1=dens[:, 1:2], scalar2=None,
                                                      op0=ALU.mult), reads=[B_sel], writes=[B_sel])
                for hf2 in range(2):
                    bk, bkb = allb[hf2]
                    K.op("pe", lambda e: e.matmul(bk[0:1, :], lhsT=ones4f, rhs=Es[:, hf2 * 512:(hf2 + 1) * 512],
                                                  start=True, stop=True), reads=[B_sel], writes=[bkb])
                    K.op("dve", lambda e: e.tensor_copy(out=imps[:, hf2 * 512:(hf2 + 1) * 512], in_=bk[0:1, :]),
                         reads=[bkb], writes=[B_sel])
                K.op("dve", lambda e: e.tensor_tensor(out=scs[:, 0:257], in0=imps[:, 0:1025:4], in1=imps[:, 1:1026:4],
                                                      op=ALU.add), reads=[B_sel], writes=[B_sel])
                for m in (2, 3, 4):
                    K.op("dve", lambda e: e.tensor_tensor(out=scs[:, 0:257], in0=scs[:, 0:257],
                                                          in1=imps[:, m:m + 1025:4], op=ALU.add), reads=[B_sel],
                         writes=[B_sel])
                K.op("dve", lambda e: e.tensor_tensor(out=scs[:, 0:257], in0=scs[:, 0:257], in1=bon[:, 0:257],
                                                      op=ALU.add), reads=[B_sel], writes=[B_sel])
                K.op("dve", lambda e: e.tensor_copy(out=scs[:, 257:264], in_=bon[:, 257:264]), reads=[B_sel],
                     writes=[B_sel])
                K.op("dve", lambda e: e.max(out=mxs[:, 0:8], in_=scs), reads=[B_sel], writes=[B_sel])
                K.op("dve", lambda e: e.match_replace(out=scs2, in_to_replace=mxs[:, 0:8], in_values=scs,
                                                      imm_value=-3e38), reads=[B_sel], writes=[B_sel])
                K.op("dve", lambda e: e.max(out=mxs[:, 8:16], in_=scs2), reads=[B_sel], writes=[B_sel])
                K.op("dve", lambda e: e.tensor_scalar(out=scs2[:, 0:256], in0=scs[:, 0:256], scalar1=mxs[:, 15:16],
                                                      scalar2=None, op0=ALU.is_ge), reads=[B_sel], writes=[B_sel])
                K.op("dve", lambda e: e.tensor_scalar(out=scs2[:, 0:256], in0=scs2[:, 0:256], scalar1=-1.0,
                                                      scalar2=-NEGM, op0=ALU.add, op1=ALU.mult), reads=[B_sel],
                     writes=[B_sel])
                K.op("dve", lambda e: e.tensor_copy(out=negx, in_=scs2[:, 0:256].unsqueeze(2).to_broadcast(
                    [1, 256, 4])), reads=[B_sel], writes=[B_sel])
                pt, pb = ps_g()
                nx = negx.rearrange("o (a j) r -> o a (j r)", a=8)
                for grp in range(8):
                    K.op("pe", lambda e: e.matmul(pt[:, grp:grp + 1], lhsT=nx[:, grp, :], rhs=ones_b[0:1, 0:1],
                                                  start=(grp == 0), stop=(grp == 7), skip_group_check=True),
                         reads=[B_sel, B_const], writes=[pb], sig=(grp == 7))
                K.op("dve", lambda e: e.tensor_copy(out=mcol[:, g, :], in_=pt[:, 0:8]), reads=[pb], writes=[B_mcol])
            for grp in range(8):
                pgb, Bpg = take_job("slc", b, grp)
                vi = grp % 2
                K.op("dve", lambda e: e.tensor_copy(out=Vas[vi][:, 0:8, :, 0:64],
                                                    in_=pgb[:, 0:8, 128:256].rearrange("p s (g d) -> p s g d", g=2)),
                     reads=[Bpg], writes=[B_Vas[vi]])
                K.op("act", lambda e: e.copy(out=Vas[vi][:, 8:16, :, 0:64],
                                             in_=pgb[:, 8:16, 128:256].rearrange("p s (g d) -> p s g d", g=2)),
                     reads=[Bpg], writes=[B_Vas[vi]])
                ei = etc[0] % 2
                etc[0] += 1
                ps, psb = ps_g()
                ps4 = ps[:, 0:128].rearrange("p (s h) -> p s h", s=16)
                for s4 in range(4):
                    pt, pb = ps_g()
                    for q_ in range(4):
                        s = s4 * 4 + q_
                        K.op("pe", lambda e: e.transpose(out=pt[:, q_ * 128:(q_ + 1) * 128], in_=pgb[:, s, 0:128],
                                                         identity=ident_f), reads=[Bpg, B_const], writes=[pb])
                    kt_, Bkt_ = KTt[s4 % 2], B_KTt[s4 % 2]
                    if s4 % 2 == 0:
                        K.op("act", lambda e: e.copy(out=kt_, in_=pt.rearrange("p (a b) -> p a b", a=4)), reads=[pb],
                             writes=[Bkt_])
                    else:
                        K.op("dve", lambda e: e.tensor_copy(out=kt_, in_=pt.rearrange("p (a b) -> p a b", a=4)),
                             reads=[pb], writes=[Bkt_])
                    for q_ in range(4):
                        s = s4 * 4 + q_
                        for g in range(2):
                            K.op("pe", lambda e: e.matmul(ps4[:, s, g * 4:(g + 1) * 4], lhsT=kt_[:, q_, :],
                                                          rhs=qTs[g][:, :, b], start=(s == 0 and g == 0), stop=False,
                                                          skip_group_check=True), reads=[Bkt_, B_qTs], writes=[psb],
                                 sig=(q_ == 3 and g == 1))
                for g in range(2):
                    K.op("act", lambda e: e.activation(out=ETs[ei][:, :, g * 4:(g + 1) * 4, b],
                                                       in_=ps4[:, :, g * 4:(g + 1) * 4], func=AF.Exp, scale=0.125,
                                                       bias=mcol[:, g, grp:grp + 1]), reads=[psb, B_mcol],
                         writes=[B_ETs[ei]])
                for s in range(16):
                    pv_tile(ETs[ei], B_ETs[ei], s, Vas[vi][:, s], B_Vas[vi], first=(grp == 0 and s == 0))
            def new_token(bi_, first):
                pt, pb = ps_g()
                for g in range(2):
                    K.op("pe", lambda e: e.matmul(pt[0:4, g * 16:(g + 1) * 16], lhsT=kTn[:, bi_, :],
                                                  rhs=qTs[g].rearrange("p r b -> p (r b)"), start=(g == 0), stop=False,
                                                  skip_group_check=True), reads=[B_kTn, B_qTs], writes=[pb])
                K.op("pe", lambda e: e.matmul(pt[0:4, 0:32], lhsT=ident_b[0:4, 0:4],
                                              rhs=dmask.rearrange("p a b -> p (a b)"), start=False, stop=True,
                                              skip_group_check=True), reads=[B_Vn, B_const], writes=[pb])
                en = sbt(s1, "en_%d_%d" % (b, bi_), [4, 8, 4], BF16)
                Ben = K.buf()
                K.op("act", lambda e: e.activation(out=en, in_=pt[0:4, 0:32].rearrange("p (a b) -> p a b", a=8),
                                                   func=AF.Exp, scale=0.125), reads=[pb], writes=[Ben])
                for g, (pp, ppb) in enumerate(acc_banks()):
                    K.op("pe", lambda e: e.matmul(pp[0:65, 0:16], lhsT=Vn[:, bi_, g, :], rhs=en[:, g * 4:(g + 1) * 4, :],
                                                  start=first, stop=False, skip_group_check=True),
                         reads=[Ben, B_Vn], writes=[ppb], sig=(g == 1))
            if b == 0:
                new_token(0, False)
            evac_acc(1)
            for wt in range(4):
                pgb, Bpg = take_job("win", b, wt)
                vi = wt % 2
                K.op("dve", lambda e: e.tensor_copy(out=Vas[vi][:, 0, :, 0:64],
                                                     in_=pgb[:, 0, 128:256].rearrange("p (g d) -> p g d", g=2)),
                     reads=[Bpg], writes=[B_Vas[vi]])
                pt, pb = ps_g()
                K.op("pe", lambda e: e.transpose(out=pt[:, 0:128], in_=pgb[:, 0, 0:128], identity=ident_f),
                     reads=[Bpg, B_const], writes=[pb])
                kt_, Bkt_ = KTt[wt % 2], B_KTt[wt % 2]
                K.op("dve", lambda e: e.tensor_copy(out=kt_[:, 0, :], in_=pt[:, 0:128]), reads=[pb], writes=[Bkt_])
                ei = etc[0] % 2
                etc[0] += 1
                ps, psb = ps_g()
                for g in range(2):
                    K.op("pe", lambda e: e.matmul(ps[:, g * 4:(g + 1) * 4], lhsT=kt_[:, 0, :], rhs=qTs[g][:, :, b],
                                                  start=(g == 0), stop=(g == 1), skip_group_check=True),
                         reads=[Bkt_, B_qTs], writes=[psb], sig=(g == 1))
                K.op("act", lambda e: e.activation(out=ETs[ei][:, 0, :, b], in_=ps[:, 0:8], func=AF.Exp, scale=0.125),
                     reads=[psb], writes=[B_ETs[ei]])
                pv_tile(ETs[ei], B_ETs[ei], 0, Vas[vi][:, 0], B_Vas[vi], first=(wt == 0))
            if b == 0:
                new_token(1, False)
            evac_acc(2)
        for b in range(4):
            K.dma("sp", wins_o[b, 0:511, :], cache_win[b, 1:512, :])
        for bi in range(3):
            for g in range(2):
                pt, pb = ps_g()
                for r in range(4):
                    K.op("pe", lambda e: e.transpose(out=pt[0:4, r * 65:(r + 1) * 65], in_=OaccT[:, bi, g, r * 4:(r + 1) * 4],
                                                     identity=ident_f[0:65, 0:65]), reads=[B_OaccT, B_const],
                         writes=[pb], sig=(r == 3))
                K.op("dve", lambda e: e.tensor_copy(out=Oacc[:, bi, g * 4:(g + 1) * 4, :],
                                                    in_=pt[0:4, 0:260].rearrange("p (h c) -> p h c", h=4)),
                     reads=[pb], writes=[B_Oacc])
        coefs = sbt(s1, "coefs", [4, 3, 8], F32)
        otm = sbt(s1, "otm", [4, 3, 8, 64], F32)
        onsas = sbt(s1, "onsas", [4, 512], BF16)
        B_cb = K.buf()
        K.op("dve", lambda e: e.tensor_scalar(out=coefs, in0=Oacc[:, :, :, 64], scalar1=1e-30, scalar2=None,
                                              op0=ALU.max), reads=[B_Oacc], writes=[B_cb])
        K.op("dve", lambda e: e.reciprocal(out=coefs, in_=coefs), reads=[B_cb], writes=[B_cb])
        K.op("dve", lambda e: e.tensor_tensor(out=coefs, in0=coefs, in1=sg_s.rearrange("p (h b) -> p b h", b=3),
                                              op=ALU.mult), reads=[B_cb, B_sm], writes=[B_cb])
        K.op("dve", lambda e: e.tensor_tensor(out=otm, in0=Oacc[:, :, :, 0:64],
                                              in1=coefs.unsqueeze(3).to_broadcast([4, 3, 8, 64]), op=ALU.mult),
             reads=[B_Oacc, B_cb], writes=[B_cb])
        K.op("dve", lambda e: e.tensor_tensor(out=otm[:, 0], in0=otm[:, 0], in1=otm[:, 1], op=ALU.add), reads=[B_cb],
             writes=[B_cb])
        K.op("dve", lambda e: e.tensor_tensor(out=onsas.rearrange("p (h d) -> p h d", h=8), in0=otm[:, 0],
                                              in1=otm[:, 2], op=ALU.add), reads=[B_cb], writes=[B_cb])
        pt, pb = ps_g()
        ptb = pt.bitcast(BF16)[:, 0:16].rearrange("p (k t) -> p k t", k=4)
        for k in range(4):
            K.op("pe", lambda e: e.transpose(out=ptb[:, k, :], in_=onsas[:, k * 128:(k + 1) * 128],
                                             identity=ident_b[0:4, 0:4]), reads=[B_cb, B_const], writes=[pb])
        K.op("dve", lambda e: e.tensor_copy(out=omT_s[:, 0:4, :], in_=ptb), reads=[pb], writes=[B_omTs])
        K.barrier()
        s1.close()
    K.barrier()

    e2 = ExitStack()
    wout = sbt(e2, "wout", [128, 8, D], BF16)
    for k in range(8):
        K.dma("pool", wout[:, k, :], w_out[k * 128:(k + 1) * 128, :], writes=[B_w2])
    xst = sbt(e2, "xst", [128, 4, D], F32)
    B_xst = [K.buf() for _ in range(4)]
    xn = [sbt(e2, "xn2_%d" % i, [128, D], BF16) for i in range(2)]
    st4 = sbt(e2, "st4b", [128, 8], F32)
    hT = sbt(e2, "hT2", [128, 8, 512], BF16)
    omT = sbt(e2, "omT", [128, 8, 512], BF16)
    B_omT = K.buf()
    g0s = sbt(e2, "g0s", [128, 512], F32)
    g1s = sbt(e2, "g1s", [128, 512], F32)
    B_gs = [K.buf(), K.buf()]
    mergedT = sbt(e2, "mergedT", [128, 8, 512], BF16)
    B_merged = K.buf()
    x1b = [sbt(e2, "x1b%d" % i, [128, D], F32) for i in range(2)]
    B_x1b = [K.buf(), K.buf()]
    B_st = K.buf()
    B_xn = [K.buf(), K.buf()]
    B_hT = [K.buf() for _ in range(4)]

    def p1b_merge(nt, om_ap, B_om_l, h_ap, B_h_l):
        for cc in range(8):
            pa, pab = ps_a()
            pbb_, pbb = ps_a()
            pg0, pg0b = ps_a()
            pg1, pg1b = ps_a()
            for k in range(4):
                K.op("pe", lambda e: e.matmul(pa[:, 0:nt], lhsT=wpa[:, k, cc * 128:(cc + 1) * 128], rhs=om_ap[:, k, 0:nt],
                                              start=(k == 0), stop=(k == 3)), reads=[B_w2] + B_om_l, writes=[pab], sig=(k == 3))
            for k in range(4):
                K.op("pe", lambda e: e.matmul(pbb_[:, 0:nt], lhsT=wpb[:, k, cc * 128:(cc + 1) * 128],
                                              rhs=om_ap[:, 4 + k, 0:nt], start=(k == 0), stop=(k == 3)),
                     reads=[B_w2] + B_om_l, writes=[pbb], sig=(k == 3))
            for k in range(8):
                K.op("pe", lambda e: e.matmul(pg0[:, 0:nt], lhsT=wg_sb[:, k, cc * 128:(cc + 1) * 128],
                                              rhs=h_ap[:, k, 0:nt], start=(k == 0), stop=(k == 7)),
                     reads=[B_w2] + B_h_l, writes=[pg0b], sig=(k == 7))
            for k in range(8):
                K.op("pe", lambda e: e.matmul(pg1[:, 0:nt], lhsT=wg_sb[:, k, 1024 + cc * 128:1152 + cc * 128],
                                              rhs=h_ap[:, k, 0:nt], start=(k == 0), stop=(k == 7)),
                     reads=[B_w2] + B_h_l, writes=[pg1b], sig=(k == 7))
            K.op("act", lambda e: e.activation(out=g0s[:, 0:nt], in_=pg0[:, 0:nt], func=AF.Sigmoid), reads=[pg0b],
                 writes=[B_gs[0]])
            K.op("act", lambda e: e.activation(out=g1s[:, 0:nt], in_=pg1[:, 0:nt], func=AF.Sigmoid), reads=[pg1b],
                 writes=[B_gs[1]])
            K.op("dve", lambda e: e.tensor_tensor(out=g0s[:, 0:nt], in0=g0s[:, 0:nt], in1=pa[:, 0:nt], op=ALU.mult),
                 reads=[B_gs[0], pab], writes=[B_gs[0]])
            K.op("dve", lambda e: e.tensor_tensor(out=g1s[:, 0:nt], in0=g1s[:, 0:nt], in1=pbb_[:, 0:nt], op=ALU.mult),
                 reads=[B_gs[1], pbb], writes=[B_gs[1]])
            K.op("pool", lambda e: e.tensor_tensor(out=mergedT[:, cc, 0:nt], in0=g0s[:, 0:nt], in1=g1s[:, 0:nt],
                                                   op=ALU.add), reads=B_gs, writes=[B_merged])

    for si, (T0, ntl) in enumerate(own_sts):
        nt = ntl * 128
        K.dma("sp", omT[:, :, 0:nt], scr_om[si, :, :, 0:nt], reads=[B_scr_om[si]], writes=[B_omT])
        for j in range(ntl):
            rmsnorm_T(tile_src(T0 + j), xst[:, j, :], B_xst[j], xn[j % 2], B_xn[j % 2], gcol_mix,
                      hT[:, :, j * 128:(j + 1) * 128], B_hT[j])
        Bh = B_hT[0:ntl]
        p1b_merge(nt, omT, [B_omT], hT, Bh)
        for j in range(ntl):
            T = T0 + j
            for n in range(2):
                px, pxb = ps_a()
                for k in range(8):
                    K.op("pe", lambda e: e.matmul(px, lhsT=mergedT[:, k, j * 128:(j + 1) * 128],
                                                  rhs=wout[:, k, n * 512:(n + 1) * 512], start=(k == 0), stop=(k == 7)),
                         reads=[B_merged, B_w2], writes=[pxb], sig=(k == 7))
                K.op("dve", lambda e: e.tensor_tensor(out=x1b[j % 2][:, n * 512:(n + 1) * 512], in0=px,
                                                      in1=xst[:, j, n * 512:(n + 1) * 512], op=ALU.add),
                     reads=[pxb, B_xst[j]], writes=[B_x1b[j % 2]])
            K.dma("sp", scr_x1[T - 15], x1b[j % 2], reads=[B_x1b[j % 2]], writes=[B_scr_x1[T - 15]])

    if do_sample:
        K.dma("sp", xst[0:4, 0, :], xs, writes=[B_xst[0]])
        p1b_merge(4, omT_s, [B_omTs], hT_s, [B_hTs])
        for n in range(2):
            px, pxb = ps_a()
            for k in range(8):
                K.op("pe", lambda e: e.matmul(px[0:4, :], lhsT=mergedT[:, k, 0:4], rhs=wout[:, k, n * 512:(n + 1) * 512],
                                              start=(k == 0), stop=(k == 7)), reads=[B_merged, B_w2], writes=[pxb], sig=(k == 7))
            K.op("dve", lambda e: e.tensor_tensor(out=x1b[0][0:4, n * 512:(n + 1) * 512], in0=px[0:4, :],
                                                  in1=xst[0:4, 0, n * 512:(n + 1) * 512], op=ALU.add),
                 reads=[pxb, B_xst[0]], writes=[B_x1b[0]])
        K.dma("sp", scr_x1s, x1b[0][0:4, :], reads=[B_x1b[0]], writes=[B_scr_x1s])

    K.barrier()
    e2.close()
    e2w.close()
    e1.close()

    e3 = ExitStack()
    wup = sbt(e3, "wup", [128, 8, 2 * DFF], BF16)
    wdn = sbt(e3, "wdn", [128, 22, D], BF16)
    B_w3 = K.buf("w_pass2")
    B_wupb = [K.buf() for _ in range(4)]
    B_wdn = K.buf()
    for blk_ in (0, 2, 1, 3):
        c0 = blk_ * 1408
        for k in range(8):
            K.dma("pool", wup[:, k, c0:c0 + 1408], w_up[k * 128:(k + 1) * 128, c0:c0 + 1408], writes=[B_wupb[blk_]])
    for f in range(22):
        K.dma("pool", wdn[:, f, :], w_down[f * 128:(f + 1) * 128, :], writes=[B_wdn])
    cw = sbt(e3, "cw", [128, 66], F32)
    cb = sbt(e3, "cb", [128, 22], F32)
    tmpc = sbt(e3, "tmpc", [66, 128], F32)
    K.dma("sp", tmpc, conv_w, writes=[B_w3])
    pt, pb = ps_a()
    K.op("pe", lambda e: e.transpose(out=pt[:, 0:66], in_=tmpc, identity=ident_f[0:66, 0:66]), reads=[B_w3, B_const],
         writes=[pb])
    K.op("dve", lambda e: e.tensor_copy(out=cw, in_=pt[:, 0:66]), reads=[pb], writes=[B_w3])
    tmpb = sbt(e3, "tmpb", [22, 128], F32)
    K.dma("sp", tmpb, conv_b, writes=[B_w3])
    pt, pb = ps_a()
    K.op("pe", lambda e: e.transpose(out=pt[:, 0:22], in_=tmpb, identity=ident_f[0:22, 0:22]), reads=[B_w3, B_const],
         writes=[pb])
    K.op("dve", lambda e: e.tensor_copy(out=cb, in_=pt[:, 0:22]), reads=[pb], writes=[B_w3])
    gfinB = sbt(e3, "gfinB", [128, D], F32)
    K.dma("sp", gfinB, norm_final.partition_broadcast(128), writes=[B_w3])

    xld = [sbt(e3, "xld%d" % i, [128, D], F32) for i in range(2)]
    B_xld = [K.buf(), K.buf()]
    xn = [sbt(e3, "xn3_%d" % i, [128, D], BF16) for i in range(2)]
    st4 = sbt(e3, "st4c", [128, 8], F32)
    hT = sbt(e3, "hT3", [128, 8, 512], BF16)
    B_st = K.buf()
    B_xn = [K.buf(), K.buf()]
    B_hT = [K.buf() for _ in range(4)]
    abuf = [sbt(e3, "abuf%d" % i, [128, 2 + 512], F32) for i in range(2)]
    B_abuf = [K.buf(), K.buf()]
    acar = sbt(e3, "acar", [128, 22, 2], F32)
    B_acar = K.buf()
    K.op("pool", lambda e: e.memset(acar, 0.0), writes=[B_acar])
    cbuf = [sbt(e3, "cbuf0", [128, 512], F32)] * 2
    B_cbuf = [K.buf()] * 2
    actT = sbt(e3, "actT", [128, 22, 512], BF16)
    B_actT = K.buf()
    ybuf = [sbt(e3, "ybuf0", [128, D], F32)] * 2
    B_ybuf = [K.buf()] * 2
    convsb = sbt(e3, "convsb", [2, 512], F32)
    B_convsb = K.buf()

    for si, (T0, ntl) in enumerate(own_sts):
        nt = ntl * 128
        for j in range(ntl):
            K.dma("sp", xld[j % 2], scr_x1[T0 + j - 15], reads=[B_scr_x1[T0 + j - 15]], writes=[B_xld[j % 2]])
            rmsnorm_T(None, xld[j % 2], B_xld[j % 2], xn[j % 2], B_xn[j % 2], gcol_ffn,
                      hT[:, :, j * 128:(j + 1) * 128], B_hT[j], x_preloaded=True)
        Bh = B_hT[0:ntl]
        for f in range(22):
            pa, pab = ps_a()
            pbb_, pbb = ps_a()
            ab, Bab = abuf[f % 2], B_abuf[f % 2]
            cbf, Bcb = cbuf[f % 2], B_cbuf[f % 2]
            for k in range(8):
                K.op("pe", lambda e: e.matmul(pa[:, 0:nt], lhsT=wup[:, k, f * 128:(f + 1) * 128], rhs=hT[:, k, 0:nt],
                                              start=(k == 0), stop=(k == 7)), reads=[B_wupb[f // 11]] + Bh,
                     writes=[pab], sig=(k == 7))
            for k in range(8):
                K.op("pe", lambda e: e.matmul(pbb_[:, 0:nt], lhsT=wup[:, k, DFF + f * 128:DFF + (f + 1) * 128],
                                              rhs=hT[:, k, 0:nt], start=(k == 0), stop=(k == 7)),
                     reads=[B_wupb[2 + f // 11]] + Bh, writes=[pbb], sig=(k == 7))
            K.op("pool", lambda e: e.tensor_copy(out=ab[:, 0:2], in_=acar[:, f, :]), reads=[B_acar], writes=[Bab])
            K.op("act", lambda e: e.copy(out=ab[:, 2:2 + nt], in_=pa[:, 0:nt]), reads=[pab], writes=[Bab])
            if si == 0:
                K.op("dve", lambda e: e.tensor_scalar(out=ab[:, 2:130], in0=ab[:, 2:130], scalar1=halfcol[:, 0:1],
                                                      scalar2=None, op0=ALU.mult), reads=[Bab, B_const], writes=[Bab])
            K.op("pool", lambda e: e.tensor_copy(out=acar[:, f, :], in_=ab[:, nt:nt + 2]), reads=[Bab],
                 writes=[B_acar])
            K.op("dve", lambda e: e.tensor_scalar(out=cbf[:, 0:nt], in0=ab[:, 0:nt], scalar1=cw[:, f:f + 1],
                                                  scalar2=cb[:, f:f + 1], op0=ALU.mult, op1=ALU.add),
                 reads=[Bab, B_w3], writes=[Bcb])
            K.op("dve", lambda e: e.scalar_tensor_tensor(out=cbf[:, 0:nt], in0=ab[:, 1:1 + nt],
                                                         scalar=cw[:, 22 + f:23 + f], in1=cbf[:, 0:nt], op0=ALU.mult,
                                                         op1=ALU.add), reads=[Bab, B_w3, Bcb], writes=[Bcb])
            K.op("dve", lambda e: e.scalar_tensor_tensor(out=cbf[:, 0:nt], in0=ab[:, 2:2 + nt],
                                                         scalar=cw[:, 44 + f:45 + f], in1=cbf[:, 0:nt], op0=ALU.mult,
                                                         op1=ALU.add), reads=[Bab, B_w3, Bcb], writes=[Bcb])
            K.op("act", lambda e: e.activation(out=cbf[:, 0:nt], in_=cbf[:, 0:nt], func=AF.Gelu_apprx_tanh),
                 reads=[Bcb], writes=[Bcb])
            K.op("dve", lambda e: e.tensor_tensor(out=actT[:, f, 0:nt], in0=cbf[:, 0:nt], in1=pbb_[:, 0:nt],
                                                  op=ALU.mult), reads=[Bcb, pbb], writes=[B_actT])
        if si == 4:
            for r0 in range(0, 22, 4):
                nn = min(4, 22 - r0)
                pt, pb = ps_a()
                for f in range(r0, r0 + nn):
                    K.op("pe", lambda e: e.transpose(out=pt[0:2, (f - r0) * 128:(f - r0 + 1) * 128], in_=acar[:, f, :],
                                                     identity=ident_f), reads=[B_acar, B_const], writes=[pb])
                K.op("dve", lambda e: e.tensor_copy(out=convsb[:, 0:nn * 128], in_=pt[0:2, 0:nn * 128]),
                     reads=[pb], writes=[B_convsb])
                K.dma("sp", conv_o[:, r0 * 128:(r0 + nn) * 128], convsb[:, 0:nn * 128], reads=[B_convsb])
        for j in range(ntl):
            T = T0 + j
            yb, Byb = ybuf[j % 2], B_ybuf[j % 2]
            K.dma("sp", xld[j % 2], scr_x1[T - 15], reads=[B_scr_x1[T - 15]], writes=[B_xld[j % 2]])
            for n in range(2):
                py, pyb = ps_a()
                for f in range(22):
                    K.op("pe", lambda e: e.matmul(py, lhsT=actT[:, f, j * 128:(j + 1) * 128],
                                                  rhs=wdn[:, f, n * 512:(n + 1) * 512], start=(f == 0), stop=(f == 21)),
                         reads=[B_actT, B_wdn], writes=[pyb], sig=(f == 21))
                K.op("dve", lambda e: e.tensor_tensor(out=yb[:, n * 512:(n + 1) * 512], in0=py,
                                                      in1=xld[j % 2][:, n * 512:(n + 1) * 512], op=ALU.add),
                     reads=[pyb, B_xld[j % 2]], writes=[Byb])
            if T < 16:
                continue
            K.op("act", lambda e: e.activation(out=xn[0], in_=yb, func=AF.Square, accum_out=st4[:, 4:5]), reads=[Byb],
                 writes=[B_xn[0], B_st])
            K.op("act", lambda e: e.activation(out=st4[:, 5:6], in_=st4[:, 4:5], func=AF.Ln, scale=1.0 / D, bias=EPS), reads=[B_st], writes=[B_st])
            K.op("act", lambda e: e.activation(out=st4[:, 6:7], in_=st4[:, 5:6], func=AF.Exp, scale=-0.5), reads=[B_st], writes=[B_st])
            K.op("dve", lambda e: e.scalar_tensor_tensor(out=yb, in0=yb, scalar=st4[:, 6:7], in1=gfinB,
                                                          op0=ALU.mult, op1=ALU.mult), reads=[Byb, B_st, B_w3],
                 writes=[Byb])
            K.dma("sp", y_o[(T - 16) * 128:(T - 15) * 128, :], yb, reads=[Byb])

    if do_sample:
        x1s = xld[0][0:4, :]
        K.dma("sp", x1s, scr_x1s, reads=[B_scr_x1s], writes=[B_xld[0]])
        K.op("act", lambda e: e.activation(out=xn[0][0:4, :], in_=x1s, func=AF.Square, accum_out=st4[0:4, 0:1]),
             reads=[B_xld[0]], writes=[B_xn[0], B_st])
        K.op("act", lambda e: e.activation(out=st4[0:4, 1:2], in_=st4[0:4, 0:1], func=AF.Ln, scale=1.0 / D, bias=EPS),
             reads=[B_st], writes=[B_st])
        K.op("act", lambda e: e.activation(out=st4[0:4, 2:3], in_=st4[0:4, 1:2], func=AF.Exp, scale=-0.5),
             reads=[B_st], writes=[B_st])
        K.op("dve", lambda e: e.tensor_scalar(out=xn[0][0:4, :], in0=x1s, scalar1=st4[0:4, 2:3], scalar2=None,
                                              op0=ALU.mult), reads=[B_xld[0], B_st], writes=[B_xn[0]])
        pt, pb = ps_a()
        ptb = pt.bitcast(BF16)[:, 0:32].rearrange("p (k t) -> p k t", k=8)
        for k in range(8):
            K.op("pe", lambda e: e.transpose(out=ptb[:, k, :], in_=xn[0][0:4, k * 128:(k + 1) * 128],
                                             identity=ident_b[0:4, 0:4]), reads=[B_xn[0], B_const], writes=[pb])
        K.op("dve", lambda e: e.tensor_tensor(out=hT[:, :, 0:4], in0=ptb,
                                              in1=gcol_ffn.unsqueeze(2).to_broadcast([128, 8, 4]), op=ALU.mult),
             reads=[pb, B_const], writes=[B_hT[0]])
        pab_, pabb = ps_a()
        p3 = pab_[:, 0:176].rearrange("p (f t) -> p f t", f=44)
        for f in range(44):
            for k in range(8):
                K.op("pe", lambda e: e.matmul(p3[:, f, :], lhsT=wup[:, k, f * 128:(f + 1) * 128], rhs=hT[:, k, 0:4],
                                              start=(f == 0 and k == 0), stop=(f == 43 and k == 7),
                                              skip_group_check=True), reads=B_wupb + [B_hT[0]], writes=[pabb],
                     sig=(f == 43 and k == 7))
        aTs = abuf[0][:, 0:88].rearrange("p (f t) -> p f t", f=22)
        K.op("act", lambda e: e.copy(out=abuf[0][:, 0:88], in_=pab_[:, 0:88]), reads=[pabb], writes=[B_abuf[0]])
        stt = abuf[1][0:8, 0:DFF // 8 * 0 + 514]
        stT = cbuf[0][:, 0:176].rearrange("p (f t) -> p f t", f=22)
        pst, pstb = ps_a()
        for f0 in range(0, 22, 4):
            nn = min(4, 22 - f0)
            K.dma("sp", abuf[1][0:8, 0:nn * 128], state_conv[:, f0 * 128:(f0 + nn) * 128], writes=[B_abuf[1]])
            for f in range(f0, f0 + nn):
                K.op("pe", lambda e: e.transpose(out=pst[:, f * 8:(f + 1) * 8],
                                                 in_=abuf[1][0:8, (f - f0) * 128:(f - f0 + 1) * 128],
                                                 identity=ident_f[0:8, 0:8]), reads=[B_abuf[1], B_const],
                     writes=[pstb])
        K.op("dve", lambda e: e.tensor_copy(out=cbuf[0][:, 0:176], in_=pst[:, 0:176]), reads=[pstb],
             writes=[B_cbuf[0]])
        stT4 = cbuf[0][:, 0:176].rearrange("p (f b j) -> p f b j", f=22, b=4)
        cS = cbuf[0][:, 256:344].rearrange("p (f t) -> p f t", f=22)
        tS = cbuf[0][:, 384:472].rearrange("p (f t) -> p f t", f=22)

        def bc(ap):
            return ap.unsqueeze(2).to_broadcast([128, 22, 4])
        K.op("dve", lambda e: e.tensor_tensor(out=cS, in0=stT4[:, :, :, 0], in1=bc(cw[:, 0:22]), op=ALU.mult),
             reads=[B_cbuf[0], B_w3], writes=[B_cbuf[0]])
        K.op("dve", lambda e: e.tensor_tensor(out=cS, in0=cS, in1=bc(cb), op=ALU.add), reads=[B_cbuf[0], B_w3],
             writes=[B_cbuf[0]])
        K.op("dve", lambda e: e.tensor_tensor(out=tS, in0=stT4[:, :, :, 1], in1=bc(cw[:, 22:44]), op=ALU.mult),
             reads=[B_cbuf[0], B_w3], writes=[B_cbuf[0]])
        K.op("dve", lambda e: e.tensor_tensor(out=cS, in0=cS, in1=tS, op=ALU.add), reads=[B_cbuf[0]],
             writes=[B_cbuf[0]])
        K.op("dve", lambda e: e.tensor_tensor(out=tS, in0=aTs, in1=bc(cw[:, 44:66]), op=ALU.mult),
             reads=[B_abuf[0], B_w3], writes=[B_cbuf[0]])
        K.op("dve", lambda e: e.tensor_tensor(out=cS, in0=cS, in1=tS, op=ALU.add), reads=[B_cbuf[0]],
             writes=[B_cbuf[0]])
        K.op("act", lambda e: e.activation(out=cS, in_=cS, func=AF.Gelu_apprx_tanh), reads=[B_cbuf[0]],
             writes=[B_cbuf[0]])
        K.op("dve", lambda e: e.tensor_tensor(out=actT[:, :, 0:4], in0=cS, in1=p3[:, 22:44, :], op=ALU.mult),
             reads=[B_cbuf[0], pabb], writes=[B_actT])
        K.dma("sp", convs_o[:, 0, :], state_conv.rearrange("(b j) f -> b j f", j=2)[:, 1, :])
        for r0 in range(0, 22, 4):
            nn = min(4, 22 - r0)
            pt, pb = ps_a()
            for f in range(r0, r0 + nn):
                K.op("pe", lambda e: e.transpose(out=pt[0:4, (f - r0) * 128:(f - r0 + 1) * 128], in_=aTs[:, f, :],
                                                 identity=ident_f), reads=[B_abuf[0], B_const], writes=[pb])
            K.op("dve", lambda e: e.tensor_copy(out=ybuf[0][0:4, 0:nn * 128], in_=pt[0:4, 0:nn * 128]), reads=[pb],
                 writes=[B_ybuf[0]])
            K.dma("sp", convs_o[:, 1, r0 * 128:(r0 + nn) * 128], ybuf[0][0:4, 0:nn * 128], reads=[B_ybuf[0]])
        yb = ybuf[0]
        for n in range(2):
            py, pyb = ps_a()
            for f in range(22):
                K.op("pe", lambda e: e.matmul(py[0:4, :], lhsT=actT[:, f, 0:4], rhs=wdn[:, f, n * 512:(n + 1) * 512],
                                              start=(f == 0), stop=(f == 21)), reads=[B_actT, B_wdn], writes=[pyb],
                     sig=(f == 21))
            K.op("dve", lambda e: e.tensor_tensor(out=yb[0:4, n * 512:(n + 1) * 512], in0=py[0:4, :],
                                                  in1=x1s[:, n * 512:(n + 1) * 512], op=ALU.add),
                 reads=[pyb, B_xld[0]], writes=[B_ybuf[0]])
        K.op("act", lambda e: e.activation(out=xn[0][0:4, :], in_=yb[0:4, :], func=AF.Square,
                                           accum_out=st4[0:4, 4:5]), reads=[B_ybuf[0]], writes=[B_xn[0], B_st])
        K.op("act", lambda e: e.activation(out=st4[0:4, 5:6], in_=st4[0:4, 4:5], func=AF.Ln, scale=1.0 / D, bias=EPS),
             reads=[B_st], writes=[B_st])
        K.op("act", lambda e: e.activation(out=st4[0:4, 6:7], in_=st4[0:4, 5:6], func=AF.Exp, scale=-0.5),
             reads=[B_st], writes=[B_st])
        K.op("dve", lambda e: e.scalar_tensor_tensor(out=yb[0:4, :], in0=yb[0:4, :], scalar=st4[0:4, 6:7],
                                                     in1=gfinB[0:4, :], op0=ALU.mult, op1=ALU.mult),
             reads=[B_ybuf[0], B_st, B_w3], writes=[B_ybuf[0]])
        K.dma("sp", ys_o, yb[0:4, :], reads=[B_ybuf[0]])

    K.finish()
    e3.close()
    es0.close()
    return nc


_NC_CACHE = {}


def _get_nc():
    if "nc" not in _NC_CACHE:
        _NC_CACHE["nc"] = build_program()
    return _NC_CACHE["nc"]


def kernel(x_prompt, x_sample, cache_cmp, cache_slc, cache_win, state_conv, page_table, norm_mix, w_in, cmp_pe,
           cmp_w1, cmp_b1, cmp_w2, gmlp_norm, gmlp_ws, gmlp_bs, w_proj_a, w_proj_b, w_out, norm_ffn, w_up, conv_w,
           conv_b, w_down, norm_final):
    f = np.float32
    x_prompt = np.asarray(x_prompt, f)
    nc = _get_nc()
    shared = {
        "w_in": np.ascontiguousarray(np.asarray(w_in, f)[0]),
        "cmp_pe": np.ascontiguousarray(np.asarray(cmp_pe, f)[0]),
        "cmp_w1": np.ascontiguousarray(np.asarray(cmp_w1, f)[0]),
        "cmp_b1": np.ascontiguousarray(np.asarray(cmp_b1, f)[0]),
        "cmp_w2": np.ascontiguousarray(np.asarray(cmp_w2, f)[0]),
        "gmlp_norm": np.ascontiguousarray(np.asarray(gmlp_norm, f).reshape(1, 512)),
        "gmlp_ws": np.ascontiguousarray(np.asarray(gmlp_ws, f)[0]),
        "gmlp_bs": np.ascontiguousarray(np.asarray(gmlp_bs, f).reshape(1, 512)),
        "w_proj_a": np.ascontiguousarray(np.asarray(w_proj_a, f)[0]),
        "w_proj_b": np.ascontiguousarray(np.asarray(w_proj_b, f)[0]),
        "w_out": np.ascontiguousarray(np.asarray(w_out, f)[0]),
        "norm_mix": np.ascontiguousarray(np.asarray(norm_mix, f).reshape(8, 128)),
        "norm_ffn": np.ascontiguousarray(np.asarray(norm_ffn, f).reshape(8, 128)),
        "w_up": np.ascontiguousarray(np.asarray(w_up, f)[0]),
        "conv_w": np.ascontiguousarray(np.asarray(conv_w, f).reshape(66, 128)),
        "conv_b": np.ascontiguousarray(np.asarray(conv_b, f).reshape(22, 128)),
        "w_down": np.ascontiguousarray(np.asarray(w_down, f)[0]),
        "norm_final": np.ascontiguousarray(np.asarray(norm_final, f).reshape(1, D)),
    }
    x_sample = np.asarray(x_sample, f)
    cc_flat = np.ascontiguousarray(np.asarray(cache_cmp, f)).reshape(40960, 4096)
    cs_flat = np.ascontiguousarray(np.asarray(cache_slc, f)).reshape(40960, 4096)
    cache_win = np.asarray(cache_win, f)
    state_conv = np.asarray(state_conv, f)
    page_table = np.asarray(page_table).astype(np.int32)
    pmod = (np.arange(128) % 8).astype(f).reshape(128, 1)
    in_maps = []
    for c in range(8):
        b, hf = c // 2, c % 2
        m = dict(shared)
        m["xo"] = np.ascontiguousarray(x_prompt[b, hf * 2048:(hf + 1) * 2048])
        m["xh"] = np.ascontiguousarray(x_prompt[b, (1 - hf) * 2048:(2 - hf) * 2048])
        selb = np.zeros((1, 64), f)
        cmpb = np.zeros((1, NSLOT), f)
        if hf == 0:
            selb[0, :32] = -1e30
            selb[0, 32] = 1e6
            cmpb[0, :129] = NEGM
            hsc = np.array([[NEGM, 0.0]], f)
        else:
            selb[0, 0] = 1e6
            cmpb[0, 0] = NEGM
            hsc = np.array([[0.0, 1.0]], f)
        m["selb"], m["cmpb"], m["hsc"] = selb, cmpb, hsc
        sb = slice(4 * c, 4 * c + 4)
        m["xs"] = np.ascontiguousarray(x_sample[sb, 0, :])
        m["cache_cmp"] = cc_flat
        m["cache_slc"] = cs_flat
        m["cache_win"] = np.ascontiguousarray(cache_win[0, sb].reshape(4, 512, 256))
        m["state_conv"] = np.ascontiguousarray(state_conv[0, sb].reshape(8, DFF))
        ptb_ = page_table[sb].reshape(4, 8, 16)
        ptx = np.repeat(ptb_, 8, axis=2)
        m["ptx"] = np.ascontiguousarray(ptx.transpose(2, 0, 1).reshape(128, 32)).astype(np.int32)
        m["pmod"] = pmod
        in_maps.append(m)
    res = run_bass_kernel_spmd(nc, in_maps, core_ids=list(range(8)))
    R = res.results
    y_prompt = np.stack([np.concatenate([R[2 * b]["y"], R[2 * b + 1]["y"]], 0) for b in range(4)]).astype(f)
    kvs = []
    for br in range(3):
        kvs.append(np.stack([np.concatenate([R[2 * b]["kvo"][br], R[2 * b + 1]["kvo"][br]], 0)
                             for b in range(4)]).reshape(1, 4, 4096, 2, 2, 64).astype(f))
    new_win_p = np.ascontiguousarray(kvs[2][:, :, -512:])
    new_v_p = np.stack([R[2 * b + 1]["vno"] for b in range(4)]).reshape(1, 4, 128, 512).astype(f)
    new_conv_p = np.stack([R[2 * b + 1]["convo"] for b in range(4)]).reshape(1, 4, 2, DFF).astype(f)
    y_s = np.concatenate([R[c]["ys"] for c in range(8)], 0).reshape(32, 1, D).astype(f)
    kvs_s = np.concatenate([R[c]["kvs"] for c in range(8)], 0).astype(f)
    cmp_s = np.ascontiguousarray(kvs_s[:, 0:256]).reshape(1, 32, 1, 2, 2, 64)
    slc_s = np.ascontiguousarray(kvs_s[:, 256:512]).reshape(1, 32, 1, 2, 2, 64)
    win_s = np.concatenate([R[c]["wins"] for c in range(8)], 0).reshape(1, 32, 512, 2, 2, 64).astype(f)
    v_s = np.concatenate([R[c]["vns"] for c in range(8)], 0).reshape(1, 32, 1, 512).astype(f)
    conv_s = np.concatenate([R[c]["convs"] for c in range(8)], 0).reshape(1, 32, 2, DFF).astype(f)
    return (y_prompt, y_s, kvs[0], kvs[1], new_win_p, new_v_p, new_conv_p, cmp_s, slc_s, win_s, v_s, conv_s)
```
